# Optimizing a Trainium2 kernel written in Bass

```python
import jax
import jax.numpy as jnp
from jax import lax
import numpy as np

D_MODEL = 1024
BATCH = 8
SEQ = 2048
DEPTH = 1

GRID_W = 64
CTX_LEN = 256

LRU_WIDTH = 1280
LRU_BLOCKS = 16
LRU_BLOCK = LRU_WIDTH // LRU_BLOCKS
LRU_CONV = 4
LRU_C = 8.0

RWKV_HEAD = 64
RWKV_WIDTH = 1024
RWKV_HEADS = RWKV_WIDTH // RWKV_HEAD
LORA_W = 64
LORA_A = 64
LORA_G = 160
RWKV_IN = 3 * RWKV_WIDTH + 2 * LORA_W + 2 * LORA_A + LORA_G

N_IN = 2 * LRU_WIDTH + RWKV_IN + 2 * D_MODEL
D_FF = ((8 * D_MODEL // 3 + 255) // 256) * 256

RMS_EPS = 1e-6
GN_EPS = 64e-5
L2_EPS = 1e-12

kernel_name = 'hybrid_rglru_rwkv7_prefix_block'


def _split_points(sizes):
    pts, acc = [], 0
    for s in sizes[:-1]:
        acc += s
        pts.append(acc)
    return pts


def rms_norm(x, g):
    x32 = x.astype(jnp.float32)
    y = x32 * lax.rsqrt(jnp.mean(x32 * x32, axis=-1, keepdims=True) + RMS_EPS)
    return (y * g.astype(jnp.float32)).astype(x.dtype)


def adaln(cvec, w_mod, b_mod):
    m = jax.nn.silu(cvec) @ w_mod + b_mod
    return [t[:, None, :] for t in jnp.split(m, 6, axis=-1)]


def modulate(x, g, shift, scale):
    return rms_norm(x, g) * (1 + scale) + shift


def swiglu(h, w_in, w_out):
    gate, up = jnp.split(h @ w_in, 2, axis=-1)
    return (jax.nn.silu(gate) * up) @ w_out


def to_colmajor(z, rows):
    b, l, ch = z.shape
    return z.reshape(b, rows, GRID_W, ch).transpose(0, 2, 1, 3).reshape(b, l, ch)


def from_colmajor(z, rows):
    b, l, ch = z.shape
    return z.reshape(b, GRID_W, rows, ch).transpose(0, 2, 1, 3).reshape(b, l, ch)


def directional_conv(u, w, bias, reverse):
    L = u.shape[1]
    pad = (0, LRU_CONV - 1) if reverse else (LRU_CONV - 1, 0)
    up = jnp.pad(u, ((0, 0), pad, (0, 0)))
    out = bias
    for j in range(LRU_CONV):
        off = LRU_CONV - 1 - j if reverse else j
        out = out + w[j] * up[:, off:off + L]
    return out


def _lin_combine(e1, e2):
    a1, b1 = e1
    a2, b2 = e2
    return a1 * a2, a2 * b1 + b2


def rglru_scan(u, h0, conv_w, conv_b, wa, ba, wx, bx, lam, reverse):
    B, L, W = u.shape
    xc = directional_conv(u.astype(jnp.float32), conv_w, conv_b, reverse)
    if reverse:
        xc = jnp.flip(xc, axis=1)
    xb = xc.reshape(B, L, LRU_BLOCKS, LRU_BLOCK)
    gate_r = jax.nn.sigmoid(jnp.einsum('blnc,ncd->blnd', xb, wa).reshape(B, L, W) + ba)
    gate_i = jax.nn.sigmoid(jnp.einsum('blnc,ncd->blnd', xb, wx).reshape(B, L, W) + bx)
    log_a = -LRU_C * gate_r * jax.nn.softplus(-lam)
    a = jnp.exp(log_a)
    b = jnp.sqrt(-jnp.expm1(2.0 * log_a)) * (gate_i * xc)
    b = b.at[:, 0].add(a[:, 0] * h0)
    _, h = lax.associative_scan(_lin_combine, (a, b), axis=1)
    h_last = h[:, -1]
    if reverse:
        h = jnp.flip(h, axis=1)
    return h, h_last


def lru_dir(u, h0, p, d, reverse):
    return rglru_scan(u, h0, p['lru_conv_w'][d], p['lru_conv_b'][d], p['lru_wa'][d], p['lru_ba'][d],
                      p['lru_wx'][d], p['lru_bx'][d], p['lru_lambda'][d], reverse)


def centred_shift(z, mu):
    zp = jnp.pad(z, ((0, 0), (1, 1), (0, 0)))
    return z + mu[0] * (zp[:, :-2] - z) + mu[1] * (zp[:, 2:] - z)


def orient(t):
    return jnp.stack([t[0], jnp.flip(t[1], axis=1)])


def heads(t):
    return t.reshape(t.shape[:-1] + (RWKV_HEADS, RWKV_HEAD))


def wkv7_scan(r, w, k, v, a, b, S0):
    def step(S, inp):
        r_t, w_t, k_t, v_t, a_t, b_t = inp
        sa = jnp.einsum('dbhij,dbhj->dbhi', S, a_t)
        S = S * w_t[..., None, :] + sa[..., :, None] * b_t[..., None, :] + v_t[..., :, None] * k_t[..., None, :]
        y = jnp.einsum('dbhij,dbhj->dbhi', S, r_t)
        return S, y
    tm = lambda t: jnp.moveaxis(t, 2, 0)
    S, ys = lax.scan(step, S0, (tm(r), tm(w), tm(k), tm(v), tm(a), tm(b)))
    return jnp.moveaxis(ys, 0, 2), S


def rwkv_branch(zs, S0, p, need_out):
    B, L, _ = zs.shape
    z32 = zs.astype(jnp.float32)
    pts = _split_points([RWKV_WIDTH, RWKV_WIDTH, RWKV_WIDTH, LORA_W, LORA_W, LORA_A, LORA_A, LORA_G])
    r, k, v, wd_f, wd_b, ad_f, ad_b, gd = jnp.split(z32, pts, axis=-1)
    wd = jnp.stack([wd_f, wd_b])
    ad = jnp.stack([ad_f, ad_b])
    w_log = -jax.nn.softplus(-(p['rwkv_w0'][:, None, None, :]
                               + jnp.einsum('dblr,drc->dblc', jnp.tanh(wd), p['rwkv_w2']))) - 0.5
    decay = jnp.exp(-jnp.exp(w_log))
    a = jax.nn.sigmoid(p['rwkv_a0'][:, None, None, :] + jnp.einsum('dblr,drc->dblc', ad, p['rwkv_a2']))
    kk = heads(k * p['rwkv_k_k'])
    kk = kk / jnp.maximum(jnp.sqrt(jnp.sum(kk * kk, axis=-1, keepdims=True)), L2_EPS)
    kd = heads(k[None] * (1 + (a - 1) * p['rwkv_k_a']))
    rr, vv = heads(r), heads(v)
    two = lambda t: jnp.broadcast_to(t[None], (2,) + t.shape)
    ys, S = wkv7_scan(orient(two(rr)), orient(heads(decay)), orient(kd), orient(two(vv)),
                      orient(two(-kk)), orient(kk[None] * heads(a)), S0)
    if not need_out:
        return None, S
    y = orient(ys)
    y = y[0] + y[1]
    mu = jnp.mean(y, axis=-1, keepdims=True)
    var = jnp.mean(jnp.square(y - mu), axis=-1, keepdims=True)
    yn = ((y - mu) * lax.rsqrt(var + GN_EPS)).reshape(B, L, RWKV_WIDTH) * p['rwkv_ln_g'] + p['rwkv_ln_b']
    bonus = jnp.sum(rr[None] * kd * p['rwkv_r_k'], axis=-1, keepdims=True)
    bonus = jnp.sum(bonus * vv[None], axis=0).reshape(B, L, RWKV_WIDTH)
    g = jax.nn.sigmoid(gd) @ p['rwkv_g2']
    return ((yn + bonus) * g).astype(zs.dtype), S


def merge(lru, rwkv, g_lru, g_rwkv, p):
    m = jax.nn.sigmoid(g_lru) * (lru @ p['w_o_lru']) + jax.nn.sigmoid(g_rwkv) * (rwkv @ p['w_o_rwkv'])
    return m @ p['w_out']


def mixer(h_ctx, h_lat, p, rows, need_ctx_out):
    B = h_lat.shape[0]
    pts = _split_points([LRU_WIDTH, LRU_WIDTH, RWKV_IN, D_MODEL, D_MODEL])
    ux_c, uy_c, zr_c, gl_c, gr_c = jnp.split(h_ctx @ p['w_in'], pts, axis=-1)
    ux_l, uy_l, zr_l, gl_l, gr_l = jnp.split(h_lat @ p['w_in'], pts, axis=-1)
    h_zero = jnp.zeros((B, LRU_WIDTH), jnp.float32)
    hf_c, sf = lru_dir(ux_c, h_zero, p, 0, False)
    hb_c, sb = lru_dir(ux_c, h_zero, p, 1, True)
    hf_l, _ = lru_dir(ux_l, sf, p, 0, False)
    hb_l, _ = lru_dir(ux_l, sb, p, 1, True)
    lru_l = ((hf_l + hb_l) * jax.nn.gelu(uy_l.astype(jnp.float32))).astype(h_lat.dtype)
    S_zero = jnp.zeros((2, B, RWKV_HEADS, RWKV_HEAD, RWKV_HEAD), jnp.float32)
    rw_c, S_c = rwkv_branch(centred_shift(zr_c, p['rwkv_mu']), S_zero, p, need_ctx_out)
    rw_l, _ = rwkv_branch(centred_shift(to_colmajor(zr_l, rows), p['rwkv_mu']), S_c, p, True)
    rw_l = from_colmajor(rw_l, rows)
    out_l = merge(lru_l, rw_l, gl_l, gr_l, p)
    out_c = None
    if need_ctx_out:
        lru_c = ((hf_c + hb_c) * jax.nn.gelu(uy_c.astype(jnp.float32))).astype(h_ctx.dtype)
        out_c = merge(lru_c, rw_c, gl_c, gr_c, p)
    return out_c, out_l


def setup_inputs(seed: int = 0) -> dict:
    key = jax.random.key(seed)
    ks = jax.random.split(key, 40)
    nrm = lambda k, shape, s: jax.random.normal(k, shape, jnp.float32) * s
    D, Ld = D_MODEL, DEPTH
    u = jax.random.uniform(ks[15], (Ld, 2, LRU_WIDTH), jnp.float32, 0.9, 0.999)
    sig = u ** (1.0 / LRU_C)
    return {
        'x': nrm(ks[0], (BATCH, SEQ, D), 1.0),
        'c': nrm(ks[1], (BATCH, D), 1.0),
        'ctx': nrm(ks[2], (BATCH, CTX_LEN, D), 1.0),
        'c_ctx': nrm(ks[3], (D,), 1.0),
        'norm_mix_g': 1.0 + nrm(ks[4], (Ld, D), 0.1),
        'norm_ffn_g': 1.0 + nrm(ks[5], (Ld, D), 0.1),
        'w_mod': nrm(ks[6], (Ld, D, 6 * D), 0.5 * D ** -0.5),
        'b_mod': nrm(ks[7], (Ld, 6 * D), 0.02),
        'w_in': nrm(ks[8], (Ld, D, N_IN), D ** -0.5),
        'lru_conv_w': nrm(ks[9], (Ld, 2, LRU_CONV, LRU_WIDTH), 0.5),
        'lru_conv_b': nrm(ks[10], (Ld, 2, LRU_WIDTH), 0.02),
        'lru_wa': nrm(ks[11], (Ld, 2, LRU_BLOCKS, LRU_BLOCK, LRU_BLOCK), LRU_BLOCK ** -0.5),
        'lru_ba': nrm(ks[12], (Ld, 2, LRU_WIDTH), 0.02),
        'lru_wx': nrm(ks[13], (Ld, 2, LRU_BLOCKS, LRU_BLOCK, LRU_BLOCK), LRU_BLOCK ** -0.5),
        'lru_bx': nrm(ks[14], (Ld, 2, LRU_WIDTH), 0.02),
        'lru_lambda': jnp.log(sig) - jnp.log1p(-sig),
        'w_o_lru': nrm(ks[16], (Ld, LRU_WIDTH, D), LRU_WIDTH ** -0.5),
        'rwkv_mu': jax.random.uniform(ks[17], (Ld, 2, RWKV_IN), jnp.float32, 0.1, 0.5),
        'rwkv_w0': jax.random.uniform(ks[18], (Ld, 2, RWKV_WIDTH), jnp.float32, -6.0, 0.0),
        'rwkv_w2': nrm(ks[19], (Ld, 2, LORA_W, RWKV_WIDTH), 0.1),
        'rwkv_a0': nrm(ks[20], (Ld, 2, RWKV_WIDTH), 0.5),
        'rwkv_a2': nrm(ks[21], (Ld, 2, LORA_A, RWKV_WIDTH), 0.1),
        'rwkv_g2': nrm(ks[22], (Ld, LORA_G, RWKV_WIDTH), LORA_G ** -0.5),
        'rwkv_k_k': 0.85 + nrm(ks[23], (Ld, RWKV_WIDTH), 0.1),
        'rwkv_k_a': 1.0 + nrm(ks[24], (Ld, RWKV_WIDTH), 0.1),
        'rwkv_r_k': nrm(ks[25], (Ld, RWKV_HEADS, RWKV_HEAD), 0.1),
        'rwkv_ln_g': 1.0 + nrm(ks[26], (Ld, RWKV_WIDTH), 0.1),
        'rwkv_ln_b': nrm(ks[27], (Ld, RWKV_WIDTH), 0.02),
        'w_o_rwkv': nrm(ks[28], (Ld, RWKV_WIDTH, D), RWKV_WIDTH ** -0.5),
        'w_out': nrm(ks[29], (Ld, D, D), D ** -0.5),
        'w_ffn_in': nrm(ks[30], (Ld, D, 2 * D_FF), D ** -0.5),
        'w_ffn_out': nrm(ks[31], (Ld, D_FF, D), D_FF ** -0.5),
        'norm_final_g': 1.0 + nrm(ks[32], (D,), 0.1),
    }


def reference(x, c, ctx, c_ctx, norm_mix_g, norm_ffn_g, w_mod, b_mod, w_in, lru_conv_w, lru_conv_b,
              lru_wa, lru_ba, lru_wx, lru_bx, lru_lambda, w_o_lru, rwkv_mu, rwkv_w0, rwkv_w2, rwkv_a0,
              rwkv_a2, rwkv_g2, rwkv_k_k, rwkv_k_a, rwkv_r_k, rwkv_ln_g, rwkv_ln_b, w_o_rwkv, w_out,
              w_ffn_in, w_ffn_out, norm_final_g):
    rows = x.shape[1] // GRID_W
    for l in range(DEPTH):
        last = l == DEPTH - 1
        p = {
            'w_in': w_in[l], 'lru_conv_w': lru_conv_w[l], 'lru_conv_b': lru_conv_b[l],
            'lru_wa': lru_wa[l], 'lru_ba': lru_ba[l], 'lru_wx': lru_wx[l], 'lru_bx': lru_bx[l],
            'lru_lambda': lru_lambda[l], 'w_o_lru': w_o_lru[l], 'rwkv_mu': rwkv_mu[l],
            'rwkv_w0': rwkv_w0[l], 'rwkv_w2': rwkv_w2[l], 'rwkv_a0': rwkv_a0[l], 'rwkv_a2': rwkv_a2[l],
            'rwkv_g2': rwkv_g2[l], 'rwkv_k_k': rwkv_k_k[l], 'rwkv_k_a': rwkv_k_a[l],
            'rwkv_r_k': rwkv_r_k[l], 'rwkv_ln_g': rwkv_ln_g[l], 'rwkv_ln_b': rwkv_ln_b[l],
            'w_o_rwkv': w_o_rwkv[l], 'w_out': w_out[l],
        }
        sh_m, sc_m, g_m, sh_f, sc_f, g_f = adaln(c, w_mod[l], b_mod[l])
        csh_m, csc_m, cg_m, csh_f, csc_f, cg_f = adaln(c_ctx[None], w_mod[l], b_mod[l])
        h_lat = modulate(x, norm_mix_g[l], sh_m, sc_m)
        h_ctx = modulate(ctx, norm_mix_g[l], csh_m, csc_m)
        mix_c, mix_l = mixer(h_ctx, h_lat, p, rows, not last)
        x = x + g_m * mix_l
        x = x + g_f * swiglu(modulate(x, norm_ffn_g[l], sh_f, sc_f), w_ffn_in[l], w_ffn_out[l])
        if not last:
            ctx = ctx + cg_m * mix_c
            ctx = ctx + cg_f * swiglu(modulate(ctx, norm_ffn_g[l], csh_f, csc_f), w_ffn_in[l], w_ffn_out[l])
    return rms_norm(x, norm_final_g)
```

```python
import contextlib
import numpy as np
import concourse.bass as bass
import concourse.mybir as mybir
from concourse.bass_utils import run_bass_kernel_spmd

F32 = mybir.dt.float32
BF16 = mybir.dt.bfloat16
AF = mybir.ActivationFunctionType
ALU = mybir.AluOpType

ENGS = ['pe', 'act', 'dve', 'pool', 'sp']
NCTX, NLAT, T = 256, 2048, 2304
D = 1024
LW, NBLK, BLK = 1280, 16, 80
RIN = 3488
DFF = 2816
RMS_EPS, GN_EPS = 1e-6, 64e-5


class _Rec:
    def __init__(self):
        self.call = None

    def __getattr__(self, name):
        def f(*a, **k):
            self.call = (name, a, k)
            return self
        return f


class Prog:
    def __init__(self, nc):
        self.nc = nc
        self.root = contextlib.ExitStack()
        self.stacks = [self.root]
        self.ops = {e: [] for e in ENGS}
        self.cnt = {e: 0 for e in ENGS}
        self.seen = {e: {} for e in ENGS}
        self.last_w = {}
        self.readers = {}
        self.esem = {e: self.root.enter_context(nc.semaphore('s_' + e)) for e in ENGS if e != 'sp'}
        self.dsem = {}
        self.fence = []
        self.ntile = 0

    def sb(self, shape, dt=F32, name=None):
        self.ntile += 1
        return self.stacks[-1].enter_context(self.nc.sbuf_tensor(f'{name or "t"}{self.ntile}', list(shape), dt))

    def sbm(self, shape, dt=F32, name=None):
        self.ntile += 1
        st = contextlib.ExitStack()
        t = st.enter_context(self.nc.sbuf_tensor(f'{name or "t"}{self.ntile}', list(shape), dt))
        return t, st

    def _set_fence(self):
        self.fence = [('E', e, self.cnt[e]) for e in self.esem if self.cnt[e] > 0]
        self.fence += [('D', s_, v) for s_, v in self.dsem.values() if v > 0]

    def free(self, stacks):
        for st in stacks:
            st.close()
        self._set_fence()

    def ps(self, shape, dt=F32, name=None):
        self.ntile += 1
        return self.root.enter_context(self.nc.psum_tensor(f'{name or "p"}{self.ntile}', list(shape), dt))

    @contextlib.contextmanager
    def scope(self):
        st = contextlib.ExitStack()
        self.stacks.append(st)
        try:
            yield
        finally:
            self.stacks.pop()
            st.close()
            self._set_fence()

    def _k(self, k):
        if isinstance(k, tuple):
            return tuple(self._k(x) for x in k)
        if isinstance(k, (str, int)):
            return k
        return id(k)

    @staticmethod
    def _tkey(tok):
        return ('E', tok[1]) if tok[0] == 'E' else ('D', id(tok[1]))

    def _deps(self, eng, r, w):
        deps = {}

        def add(tok):
            if tok is None:
                return
            if tok[0] == 'E' and tok[1] == eng == 'pe':
                return
            k = self._tkey(tok)
            if k not in deps or deps[k][2] < tok[2]:
                deps[k] = tok
        for tok in self.fence:
            add(tok)
        for k in r:
            add(self.last_w.get(k))
        for k in w:
            add(self.last_w.get(k))
            for t in self.readers.get(k, ()):
                add(t)
        out = []
        seen = self.seen[eng]
        for k, tok in deps.items():
            if seen.get(k, 0) >= tok[2]:
                continue
            seen[k] = tok[2]
            out.append(tok)
        return out

    def _commit(self, tok, r, w):
        for k in w:
            self.last_w[k] = tok
            self.readers[k] = []
        for k in r:
            if k in w:
                continue
            self.readers.setdefault(k, []).append(tok)

    def op(self, eng, fn, r=(), w=()):
        r = [self._k(k) for k in r]
        w = [self._k(k) for k in w]
        waits = self._deps(eng, r, w)
        self.cnt[eng] += 1
        tok = ('E', eng, self.cnt[eng])
        rec = _Rec()
        fn(rec)
        name, a, k = rec.call
        self.ops[eng].append((waits, lambda e: getattr(e, name)(*a, **k), tok))
        self._commit(tok, r, w)

    def dma(self, q, out, in_, r=(), w=(), group=None, **kw):
        r = [self._k(k) for k in r]
        w = [self._k(k) for k in w]
        waits = self._deps(q, r, w)
        g = group or ('dma_' + str(w[0] if w else 'x'))
        if g not in self.dsem:
            self.dsem[g] = [self.root.enter_context(self.nc.semaphore('d%d' % len(self.dsem))), 0]
        ent = self.dsem[g]
        ent[1] += 16
        tok = ('D', ent[0], ent[1])
        self.ops[q].append((waits, lambda e: e.dma_start(out=out, in_=in_, **kw), tok))
        self._commit(tok, r, w)

    def wait_group(self, eng, group):
        ent = self.dsem[group]
        self.ops[eng].append(([('D', ent[0], ent[1])], None, None))

    def emit(self):
        engobj = {'pe': 'tensor', 'act': 'scalar', 'dve': 'vector', 'pool': 'gpsimd', 'sp': 'sync'}
        waited = {e: set() for e in ENGS}
        for e in ENGS:
            for waits, fn, tok in self.ops[e]:
                for t in waits:
                    if t[0] == 'E':
                        waited[t[1]].add(t[2])
        rank = {e: {s_: i + 1 for i, s_ in enumerate(sorted(waited[e]))} for e in ENGS}
        with self.nc.Block() as block:
            for e in ENGS:
                ops = self.ops[e]

                def body(eng, ops=ops):
                    for waits, fn, tok in ops:
                        for t in waits:
                            if t[0] == 'E':
                                eng.wait_ge(self.esem[t[1]], rank[t[1]][t[2]])
                            else:
                                eng.wait_ge(t[1], t[2])
                        if fn is not None:
                            ins = fn(eng)
                            if tok[0] == 'D':
                                ins.then_inc(tok[1], 16)
                            elif tok[2] in rank[tok[1]]:
                                ins.then_inc(self.esem[tok[1]], 1)
                getattr(block, engobj[e])(body)


def rsl(start, n, step):
    if step > 0:
        return slice(start, start + n)
    stop = start - n
    return slice(start, stop if stop >= 0 else None, -1)


class Builder:
    def __init__(self, taps=(), stop_after=None):
        self.taps = set(taps)
        self.stop_after = stop_after
        nc = self.nc = bass.Bass("TRN2", target_bir_lowering=False)
        self.P = Prog(nc)
        self.din = {}
        self.tapout = {}
        self._castn = 0

    def inp(self, name, shape):
        self.din[name] = self.nc.dram_tensor(name, list(shape), F32, kind="ExternalInput").ap()
        return self.din[name]

    def mm(self, out, lhsT, rhs, start=True, stop=True, r=(), w=(), **kw):
        self.P.op('pe', lambda e: e.matmul(out, lhsT=lhsT, rhs=rhs, start=start, stop=stop, **kw), r=r, w=w)

    def act(self, out, in_, func, r=(), w=(), **kw):
        self.P.op('act', lambda e: e.activation(out=out, in_=in_, func=func, **kw), r=r, w=w)

    def tap(self, name, tile_ap, shape, r, dt=F32):
        if name not in self.taps:
            return
        o = self.nc.dram_tensor('tap_' + name, list(shape), dt, kind="ExternalOutput").ap()
        self.P.dma('pool', o, tile_ap, r=r, group='out_' + name)
        self.tapout[name] = o

    def cols(self, rows, n, chunk, name):
        P = self.P
        R = len(rows)
        nch = (n + chunk - 1) // chunk
        out = P.sb([chunk, nch, R], name=name)
        with P.scope():
            st = P.sb([R, n], name='colst')
            for i, rw in enumerate(rows):
                P.dma('sp', st[i:i + 1, :], rw.rearrange("(o n) -> o n", o=1), w=[(st, i)], group='colst')
            pp = self.pb[0]
            assert nch * R <= 512
            for c in range(nch):
                cs = min(chunk, n - c * chunk)
                P.op('pe', lambda e, c=c, cs=cs: e.transpose(out=pp[0:cs, c * R:(c + 1) * R], in_=st[0:R, c * chunk:c * chunk + cs],
                                                             identity=self.ident[0:R, 0:R]),
                     r=[(st, i) for i in range(R)] + [self.ident], w=[pp])
            P.op('dve', lambda e: e.tensor_copy(out=out[:].rearrange("p c r -> p (c r)"), in_=pp[0:chunk, 0:nch * R]), r=[pp], w=[out])
        return out

    def build(self):
        nc, P = self.nc, self.P
        inp = self.inp
        xT = inp('xT', [D, NLAT]); ctxT = inp('ctxT', [D, NCTX]); cvec = inp('cvec', [2, D])
        w_mod = inp('w_mod', [D, 6 * D]); b_mod = inp('b_mod', [6 * D])
        nmg = inp('norm_mix_g', [D]); nfg = inp('norm_ffn_g', [D]); nfin = inp('norm_final_g', [D])
        w_in = inp('w_in', [D, 8096])
        lru_conv_w = inp('lru_conv_w', [2, 4, LW]); lru_conv_b = inp('lru_conv_b', [2, LW])
        lru_wa = inp('lru_wa', [2, NBLK, BLK, BLK]); lru_ba = inp('lru_ba', [2, LW])
        lru_wx = inp('lru_wx', [2, NBLK, BLK, BLK]); lru_bx = inp('lru_bx', [2, LW])
        lru_lam = inp('lru_lambda', [2, LW]); w_o_lru = inp('w_o_lru', [LW, D])
        mu = inp('rwkv_mu', [2, RIN]); w0 = inp('rwkv_w0', [2, D]); w2 = inp('rwkv_w2', [2, 64, D])
        a0 = inp('rwkv_a0', [2, D]); a2 = inp('rwkv_a2', [2, 64, D]); g2 = inp('rwkv_g2', [160, D])
        k_k = inp('rwkv_k_k', [D]); k_a = inp('rwkv_k_a', [D]); r_k = inp('rwkv_r_k', [D])
        ln_g = inp('rwkv_ln_g', [D]); ln_b = inp('rwkv_ln_b', [D])
        w_o_rwkv = inp('w_o_rwkv', [D, D]); w_out = inp('w_out', [D, D])
        w_ffn_in = inp('w_ffn_in', [D, 2 * DFF]); w_ffn_out = inp('w_ffn_out', [DFF, D])
        outT = nc.dram_tensor('outT', [D, NLAT], F32, kind="ExternalOutput").ap()

        self.pb = [P.ps([128, 512], F32, name='pb') for _ in range(7)]
        self.pbh = P.ps([128, 1024], BF16, name='pbh')
        pb = self.pb

        ones = P.sb([128, 128], name='ones')
        P.op('dve', lambda e: e.memset(ones[:], 1.0), w=[ones])
        self.ident = ident = P.sb([128, 128], name='ident')
        P.op('pool', lambda e: e.affine_select(out=ident[:], in_=ones[:], pattern=[[-1, 128]], compare_op=ALU.is_equal, fill=0.0,
                                               base=0, channel_multiplier=1), r=[ones], w=[ident])
        identb = P.sb([128, 128], BF16, name='identb')
        P.op('dve', lambda e: e.tensor_copy(out=identb[:], in_=ident[:]), r=[ident], w=[identb])
        bones = P.sb([128, 128], name='bones')
        P.op('dve', lambda e: e.memset(bones[:], 0.0), w=[bones])
        P.op('dve', lambda e: e.memset(bones[0:64, 0:64], 1.0), w=[bones])
        P.op('dve', lambda e: e.memset(bones[64:128, 64:128], 1.0), w=[bones])
        self.ones, self.identb, self.bones = ones, identb, bones

        gains = self.cols([nmg, nfg, nfin], D, 128, 'gains')
        cT = self.cols([cvec[0], cvec[1]], D, 128, 'cT')
        bm = self.cols([b_mod], 6 * D, 128, 'bm')
        mod = P.sb([128, 48, 2], name='mod')
        with P.scope():
            sc = P.sb([128, 8, 2], name='sc')
            self.act(sc[:], cT[:], AF.Silu, r=[cT], w=[sc])
            wm = [P.sb([128, 8, 768], name='wm') for _ in range(2)]
            pm = pb[1]
            wv = w_mod.rearrange("(k p) n -> p k n", p=128)
            for jb in range(8):
                buf = wm[jb % 2]
                for k2 in range(2):
                    P.dma('sp' if k2 == 0 else 'act', buf[:, 4 * k2:4 * k2 + 4, :], wv[:, 4 * k2:4 * k2 + 4, jb * 768:(jb + 1) * 768],
                          w=[(buf, k2)], group='wm%d' % (jb % 2))
                for jj in range(6):
                    j = jb * 6 + jj
                    for k in range(8):
                        self.mm(pm[:, 2 * j:2 * j + 2], buf[:, k, jj * 128:(jj + 1) * 128], sc[:, k, :], start=(k == 0), stop=(k == 7),
                                r=[(buf, 0), (buf, 1), sc], w=[pm])
            for n in range(2):
                P.op('dve', lambda e, n=n: e.tensor_tensor(out=mod[:, :, n], in0=pm[:, 0:96].rearrange("p (j n) -> p j n", n=2)[:, :, n],
                                                           in1=bm[:, :, 0], op=ALU.add), r=[pm, bm], w=[mod])
        self.tap('mod', mod[:], [128, 48, 2], [mod])
        G1 = P.sb([128, 8, 2], name='G1'); G2 = P.sb([128, 8, 1], name='G2')
        for n in range(2):
            P.op('dve', lambda e, n=n: e.scalar_tensor_tensor(out=G1[:, :, n], in0=mod[:, 8:16, n], scalar=1.0, in1=gains[:, :, 0],
                                                              op0=ALU.add, op1=ALU.mult), r=[mod, gains], w=[G1])
        P.op('dve', lambda e: e.scalar_tensor_tensor(out=G2[:, :, 0], in0=mod[:, 32:40, 0], scalar=1.0, in1=gains[:, :, 1],
                                                     op0=ALU.add, op1=ALU.mult), r=[mod, gains], w=[G2])
        self.mod, self.gains = mod, gains

        arena = P.sb([128, 8 * T + 8 * NLAT], BF16, name='arena')
        hT = arena[:, 0:8 * T].rearrange("p (k t) -> p k t", t=T)
        xv = xT.rearrange("(k p) t -> p k t", p=128)
        cv = ctxT.rearrange("(k p) t -> p k t", p=128)
        self.modulate(hT, [(cv, 0, 256, 0, 1)] + [(xv, 512 * i, 512, 256 + 512 * i, 0) for i in range(4)], G1, mod, 0)
        self.tap('hT', hT[:], [128, 8, T], [(hT, i) for i in range(5)], dt=BF16)
        self.hT = hT
        if self.stop_after == 'B':
            return self.finish()
        self.rwT = arena[:, 8 * T:8 * T + 8 * NLAT].rearrange("p (k t) -> p k t", t=NLAT)
        if self.stop_after != 'C':
            with P.scope():
                self.rwkv()
        self.tap('rw', self.rwT[:], [128, 8, NLAT], [self.rwT], dt=BF16)
        if self.stop_after in ('D', 'D0'):
            return self.finish()
        self.lruT, lru_st = P.sbm([128, 10, NLAT], BF16, name='lruT')
        with P.scope():
            self.lru()
        self.tap('lru', self.lruT[:], [128, 10, NLAT], [self.lruT], dt=BF16)
        if self.stop_after == 'C':
            return self.finish()
        self.mT, m_st = P.sbm([128, 8, NLAT], BF16, name='mT')
        with P.scope():
            self.merge()
        P.free([])
        self.x1T = arena[:].bitcast(F32)[:, 0:8 * NLAT].rearrange("p (k t) -> p k t", t=NLAT)
        with P.scope():
            self.resid1()
        P.free([m_st, lru_st])
        self.tap('x1', self.x1T[:], [128, 8, NLAT], [self.x1T])
        if self.stop_after == 'E':
            return self.finish()
        self.h2T = P.sb([128, 8, NLAT], BF16, name='h2T')
        self.modulate(self.h2T, [(None, 512 * i, 512, 512 * i, 0) for i in range(4)], G2, mod, 24, src_sb=self.x1T)
        with P.scope():
            self.ffn()
        with P.scope():
            self.final(outT)
        return self.finish()

    def modulate(self, hT, blocks, G, mod, shift_j0, src_sb=None):
        P, pb = self.P, self.pb
        with P.scope():
            xb = [P.sb([128, 8, 512], name='xb') for _ in range(2)]
            sq = P.sb([128, 8, 512], name='sq')
            rs = P.sb([128, 512], name='rs')
            epst = P.sb([128, 1], name='epst')
            P.op('dve', lambda e: e.memset(epst[:], RMS_EPS), w=[epst])
            for bi, (src, so, n, do, mn) in enumerate(blocks):
                if src_sb is None:
                    x = xb[bi % 2]
                    for k2 in range(2):
                        P.dma('sp' if k2 == 0 else 'act', x[:, 4 * k2:4 * k2 + 4, 0:n], src[:, 4 * k2:4 * k2 + 4, so:so + n],
                              w=[(x, k2)], group='xb%d' % (bi % 2))
                    xr = [(x, 0), (x, 1)]
                    xa = lambda k, x=x, n=n: x[:, k, 0:n]
                    xall = x[:, :, 0:n]
                else:
                    xr = [src_sb]
                    xa = lambda k, so=so, n=n: src_sb[:, k, so:so + n]
                    xall = src_sb[:, :, so:so + n]
                self.act(sq[:, :, 0:n], xall, AF.Square, r=xr, w=[sq])
                pp = pb[bi % 2]
                for k in range(8):
                    self.mm(pp[:, 0:n], self.ones[:], sq[:, k, 0:n], start=(k == 0), stop=(k == 7), r=[sq, self.ones], w=[pp])
                self.act(rs[:, 0:n], pp[:, 0:n], AF.Sqrt, scale=1.0 / D, bias=epst[:], r=[pp, epst], w=[rs])
                P.op('dve', lambda e, n=n: e.reciprocal(out=rs[:, 0:n], in_=rs[:, 0:n]), r=[rs], w=[rs])
                for k in range(8):
                    P.op('dve', lambda e, k=k, n=n, xa=xa: e.tensor_tensor(out=sq[:, k, 0:n], in0=xa(k), in1=rs[:, 0:n], op=ALU.mult),
                         r=xr + [rs], w=[sq])
                    self.act(hT[:, k, do:do + n], sq[:, k, 0:n], AF.Identity, scale=G[:, k, mn:mn + 1], bias=mod[:, shift_j0 + k, mn:mn + 1],
                             r=[sq, G, mod], w=[(hT, bi)])

    def zshift(self, cq, ncol, dsts, zbuf, A, wz, wzb, mixw, segs=((0, 1, 256, 0), (1, 258, 2048, 256))):
        P, pb, hT = self.P, self.pb, self.hT
        w_in = self.din['w_in']
        i = self.zcount = getattr(self, 'zcount', 0) + 1
        wf, wb = wz[i % 2], wzb[i % 2]
        if isinstance(zbuf, list):
            zbuf = zbuf[i % len(zbuf)]
        if isinstance(A, list):
            A = A[i % len(A)]
        c0 = 2560 + 128 * cq
        P.dma('sp', wf[:, :, 0:ncol], w_in.rearrange("(k p) n -> p k n", p=128)[:, :, c0:c0 + ncol], w=[wf], group='wz%d' % (i % 2))
        P.op('pool', lambda e: e.tensor_copy(out=wb[:, :, 0:ncol], in_=wf[:, :, 0:ncol]), r=[wf], w=[wb])
        hk = [(hT, j) for j in range(5)]
        lat = lambda k: hT[:, k, 256:2304].rearrange("p (r c) -> p c r", c=64)
        nblk = 0
        for (seg, zc, n, _) in segs:
            nb = 1 if seg == 0 else 4
            for bi in range(nb):
                pp = pb[(nblk + 5 * i) % 6]; nblk += 1
                bn = 256 if seg == 0 else 512
                for k in range(8):
                    if seg == 0:
                        self.mm(pp[0:ncol, 0:256], wb[:, k, 0:ncol], hT[:, k, 0:256], start=(k == 0), stop=(k == 7), r=[wb] + hk, w=[pp])
                    else:
                        self.mm(pp[0:ncol, 0:512], wb[:, k, 0:ncol], hT[:, k, 256 + 512 * bi:256 + 512 * bi + 512],
                                start=(k == 0), stop=(k == 7), r=[wb] + hk, w=[pp])
                if seg == 0:
                    P.op('act', lambda e: e.copy(out=zbuf[0:ncol, zc:zc + 256], in_=pp[0:ncol, 0:256]), r=[pp], w=[zbuf])
                else:
                    zo = zbuf[0:ncol, zc:zc + 2048].rearrange("p (c r) -> p r c", r=32)[:, 8 * bi:8 * bi + 8, :]
                    P.op('act', lambda e: e.copy(out=zo, in_=pp[0:ncol, 0:512].rearrange("p (r c) -> p r c", c=64)), r=[pp], w=[zbuf])
        for (seg, zc, n, _), dst in zip(segs, dsts):
            if dst is None:
                continue
            dt, do, key = dst
            self.act(A[0:ncol, 0:n], zbuf[0:ncol, zc:zc + n], AF.Identity, scale=mixw[0:ncol, cq, 2:3], r=[zbuf, mixw], w=[A])
            P.op('dve', lambda e, zc=zc, n=n: e.scalar_tensor_tensor(out=A[0:ncol, 0:n], in0=zbuf[0:ncol, zc - 1:zc - 1 + n], scalar=mixw[0:ncol, cq, 0:1],
                                                                     in1=A[0:ncol, 0:n], op0=ALU.mult, op1=ALU.add), r=[zbuf, mixw, A], w=[A])
            P.op('dve', lambda e, zc=zc, n=n, dt=dt, do=do: e.scalar_tensor_tensor(out=dt[0:ncol, do:do + n], in0=zbuf[0:ncol, zc + 1:zc + 1 + n],
                                                                                   scalar=mixw[0:ncol, cq, 1:2], in1=A[0:ncol, 0:n],
                                                                                   op0=ALU.mult, op1=ALU.add), r=[zbuf, mixw, A], w=[key])

    def rwkv(self):
        P, pb, pbh, hT, din = self.P, self.pb, self.pbh, self.hT, self.din
        rwT = self.rwT
        CW = -0.5 * float(np.exp(-0.5))
        mixw = self.cols([din['rwkv_mu'][0], din['rwkv_mu'][1], din['rwkv_mu'][0]], RIN, 128, 'mixw')
        P.op('dve', lambda e: e.tensor_tensor(out=mixw[:, :, 2], in0=mixw[:, :, 0], in1=mixw[:, :, 1], op=ALU.add), r=[mixw], w=[mixw])
        P.op('dve', lambda e: e.tensor_scalar(out=mixw[:, :, 2], in0=mixw[:, :, 2], scalar1=-1.0, scalar2=1.0, op0=ALU.mult, op1=ALU.add), r=[mixw], w=[mixw])
        chp = self.cols([din['rwkv_w0'][0], din['rwkv_w0'][1], din['rwkv_a0'][0], din['rwkv_a0'][1], din['rwkv_k_k'], din['rwkv_k_a'],
                         din['rwkv_r_k'], din['rwkv_ln_g'], din['rwkv_ln_b']], D, 128, 'chp')
        hp2 = P.sb([128, 8, 6], name='hp2')
        P.op('dve', lambda e: e.tensor_scalar(out=hp2[:, :, 0:4], in0=chp[:, :, 0:4], scalar1=0.5, scalar2=None, op0=ALU.mult), r=[chp], w=[hp2])
        P.op('dve', lambda e: e.tensor_scalar(out=hp2[:, :, 4:5], in0=chp[:, :, 5:6], scalar1=0.5, scalar2=None, op0=ALU.mult), r=[chp], w=[hp2])
        P.op('dve', lambda e: e.tensor_scalar(out=hp2[:, :, 5:6], in0=chp[:, :, 5:6], scalar1=-0.5, scalar2=1.0, op0=ALU.mult, op1=ALU.add), r=[chp], w=[hp2])
        gneps = P.sb([128, 1], name='gneps')
        P.op('dve', lambda e: e.memset(gneps[:], GN_EPS), w=[gneps])
        w2s = P.sb([128, D], BF16, name='w2s'); a2s = P.sb([128, D], BF16, name='a2s')
        g2a = P.sb([128, D], BF16, name='g2a'); g2b = P.sb([32, D], BF16, name='g2b')
        with P.scope():
            st = P.sb([128, D], name='lst')
            for src, dst, npart in ((din['rwkv_w2'].rearrange("d r c -> (d r) c"), w2s, 128), (din['rwkv_a2'].rearrange("d r c -> (d r) c"), a2s, 128),
                                    (din['rwkv_g2'][0:128, :], g2a, 128), (din['rwkv_g2'][128:160, :], g2b, 32)):
                P.dma('sp', st[0:npart, :], src, w=[st], group='lst')
                P.op('dve', lambda e, dst=dst, npart=npart: e.tensor_copy(out=dst[0:npart, :], in_=st[0:npart, :]), r=[st], w=[dst])
        msk = {}
        onesb = P.sb([128, 4, 64], BF16, name='onesb')
        P.op('dve', lambda e: e.memset(onesb[:], 1.0), w=[onesb])
        for nm, op, sgn in (('su', ALU.is_gt, -1), ('sl', ALU.is_gt, 1), ('iu', ALU.is_ge, -1), ('id', ALU.is_equal, 1)):
            m = P.sb([128, 4, 64], BF16, name='m' + nm)
            for e_ in range(2):
                P.op('pool', lambda e, m=m, op=op, sgn=sgn, e_=e_: e.affine_select(out=m[64 * e_:64 * e_ + 64], in_=onesb[64 * e_:64 * e_ + 64],
                                                                                   pattern=[[0, 4], [-sgn, 64]], compare_op=op, fill=0.0,
                                                                                   base=0, channel_multiplier=sgn), r=[onesb], w=[m])
            msk[nm] = m
        cmask = P.sb([128, 256], name='cmask')
        P.op('dve', lambda e: e.memset(cmask[:], 1.0), w=[cmask])
        P.op('dve', lambda e: e.memset(cmask[:, 0:256:64], 0.0), w=[cmask])
        twd = P.sb([128, T], BF16, name='twd'); adb = P.sb([128, T], BF16, name='adb')
        sgd1 = P.sb([128, NLAT], BF16, name='sgd1'); sgd2 = P.sb([32, NLAT], BF16, name='sgd2')

        with P.scope():
            zbuf = [P.sb([128, 2307], name='zbuf') for _ in range(2)]; A = [P.sb([128, 2048], name='zA') for _ in range(2)]
            wz = [P.sb([128, 8, 128], name='wz') for _ in range(2)]; wzb = [P.sb([128, 8, 128], BF16, name='wzb') for _ in range(2)]
            for zb in zbuf:
                P.op('pool', lambda e, zb=zb: e.memset(zb[:], 0.0), w=[zb])
            tmp = P.sb([128, T], name='ltmp')
            self.zshift(24, 128, [(tmp, 0, tmp), (tmp, 256, tmp)], zbuf, A, wz, wzb, mixw)
            self.act(twd[:], tmp[:], AF.Tanh, r=[tmp], w=[twd])
            self.zshift(25, 128, [(adb, 0, adb), (adb, 256, adb)], zbuf, A, wz, wzb, mixw)
            for cq, ncol, dst in ((26, 128, sgd1), (27, 32, sgd2)):
                self.zshift(cq, ncol, [None, (tmp, 256, tmp)], zbuf, A, wz, wzb, mixw)
                self.act(tmp[0:ncol, 256:T], tmp[0:ncol, 256:T], AF.Tanh, scale=0.5, r=[tmp], w=[tmp])
                P.op('dve', lambda e, dst=dst, ncol=ncol: e.tensor_scalar(out=dst[0:ncol, :], in0=tmp[0:ncol, 256:T], scalar1=0.5, scalar2=0.5,
                                                                          op0=ALU.mult, op1=ALU.add), r=[tmp], w=[dst])

        self.tap('twd', twd[:], [128, T], [twd], dt=BF16)
        self.tap('adb', adb[:], [128, T], [adb], dt=BF16)
        self.tap('sgd1', sgd1[:], [128, NLAT], [sgd1], dt=BF16)
        for hp in range(8):
            if self.stop_after == 'D0' and hp > 0:
                break
            with P.scope():
                rb = P.sb([128, T], BF16, name='rb'); kb = P.sb([128, T], BF16, name='kb'); vb = P.sb([128, T], BF16, name='vb')
                kkb = P.sb([128, T], BF16, name='kkb')
                y0 = P.sb([128, NLAT], name='y0'); bacc = P.sb([128, NLAT], name='bacc')
                with P.scope():
                    zbufs = [P.sb([128, 2307], name='zbuf') for _ in range(3)]; As = [P.sb([128, 2048], name='zA') for _ in range(2)]
                    wz = [P.sb([128, 8, 128], name='wz') for _ in range(2)]; wzb = [P.sb([128, 8, 128], BF16, name='wzb') for _ in range(2)]
                    for zb in zbufs:
                        P.op('pool', lambda e, zb=zb: e.memset(zb[:], 0.0), w=[zb])
                    for cq, dst in ((8 + hp, kb), (hp, rb), (16 + hp, vb)):
                        self.zshift(cq, 128, [(dst, 0, dst), (dst, 256, dst)], zbufs, As, wz, wzb, mixw)
                    kq = P.sb([128, T], name='kq'); A = P.sb([128, 1024], name='kA')
                    self.act(kq[:, 0:T], kb[:], AF.Identity, scale=chp[:, hp, 4:5], r=[kb, chp], w=[kq])
                    for bi, (o, n) in enumerate([(0, 512), (512, 512), (1024, 512), (1536, 512), (2048, 256)]):
                        P.op('dve', lambda e, o=o, n=n: e.tensor_tensor(out=A[:, 0:n], in0=kq[:, o:o + n], in1=kq[:, o:o + n], op=ALU.mult), r=[kq], w=[A])
                        pp = pb[bi % 5]
                        self.mm(pp[:, 0:n], self.bones[:], A[:, 0:n], r=[A, self.bones], w=[pp])
                        self.act(A[:, 512:512 + n], pp[:, 0:n], AF.Sqrt, r=[pp], w=[A])
                        P.op('dve', lambda e, n=n: e.tensor_scalar(out=A[:, 512:512 + n], in0=A[:, 512:512 + n], scalar1=1e-12, scalar2=None, op0=ALU.max), r=[A], w=[A])
                        P.op('dve', lambda e, n=n: e.reciprocal(out=A[:, 512:512 + n], in_=A[:, 512:512 + n]), r=[A], w=[A])
                        P.op('dve', lambda e, o=o, n=n: e.tensor_tensor(out=kkb[:, o:o + n], in0=kq[:, o:o + n], in1=A[:, 512:512 + n], op=ALU.mult), r=[A, kq], w=[kkb])
                if hp == 0:
                    for nm, t_ in (('rb', rb), ('kb', kb), ('vb', vb), ('kkb', kkb)):
                        self.tap(nm, t_[:], [128, T], [t_], dt=BF16)
                for j in range(8):
                    P.op('pool', lambda e: e.memset(y0[:, 256 * j:256 * j + 256], 0.0), w=[(y0, 256 * j)])
                    P.op('pool', lambda e: e.memset(bacc[:, 256 * j:256 * j + 256], 0.0), w=[(bacc, 256 * j)])
                C = dict(rb=rb, kb=kb, vb=vb, kkb=kkb, y0=y0, bacc=bacc, twd=twd, adb=adb, w2s=w2s, a2s=a2s, chp=chp, hp2=hp2, msk=msk,
                         cmask=cmask, CW=CW, rot=[0])
                with P.scope():
                    self.rw_rounds(hp, C)
                if hp == 0:
                    self.tap('y0', y0[:], [128, NLAT], [y0])
                    self.tap('bacc', bacc[:], [128, NLAT], [bacc])
                with P.scope():
                    yc = P.sb([128, 512], name='yc'); sq = P.sb([128, 512], name='sq2'); rstd = P.sb([128, 512], name='rstd'); tt = P.sb([128, 512], name='tt')
                    for i in range(4):
                        c0 = 512 * i
                        pm, pv, pg = pb[0], pb[1], pb[2]
                        self.mm(pm[:], self.bones[:], y0[:, c0:c0 + 512], r=[y0, self.bones], w=[pm])
                        P.op('dve', lambda e, c0=c0: e.scalar_tensor_tensor(out=yc[:], in0=pm[:], scalar=-1.0 / 64, in1=y0[:, c0:c0 + 512], op0=ALU.mult, op1=ALU.add),
                             r=[pm, y0], w=[yc])
                        P.op('dve', lambda e: e.tensor_tensor(out=sq[:], in0=yc[:], in1=yc[:], op=ALU.mult), r=[yc], w=[sq])
                        self.mm(pv[:], self.bones[:], sq[:], r=[sq, self.bones], w=[pv])
                        self.act(rstd[:], pv[:], AF.Sqrt, scale=1.0 / 64, bias=gneps[:], r=[pv, gneps], w=[rstd])
                        P.op('dve', lambda e: e.reciprocal(out=rstd[:], in_=rstd[:]), r=[rstd], w=[rstd])
                        P.op('dve', lambda e: e.tensor_tensor(out=yc[:], in0=yc[:], in1=rstd[:], op=ALU.mult), r=[yc, rstd], w=[yc])
                        self.act(yc[:], yc[:], AF.Identity, scale=chp[:, hp, 7:8], bias=chp[:, hp, 8:9], r=[yc, chp], w=[yc])
                        P.op('dve', lambda e, c0=c0: e.tensor_tensor(out=tt[:], in0=vb[:, 256 + c0:256 + c0 + 512], in1=bacc[:, c0:c0 + 512], op=ALU.mult), r=[vb, bacc], w=[tt])
                        P.op('dve', lambda e: e.tensor_tensor(out=tt[:], in0=tt[:], in1=yc[:], op=ALU.add), r=[tt, yc], w=[tt])
                        self.mm(pg[:], g2a[:, hp * 128:(hp + 1) * 128], sgd1[:, c0:c0 + 512], start=True, stop=False, r=[g2a, sgd1], w=[pg])
                        self.mm(pg[:], g2b[0:32, hp * 128:(hp + 1) * 128], sgd2[0:32, c0:c0 + 512], start=False, stop=True, r=[g2b, sgd2], w=[pg])
                        P.op('dve', lambda e, i=i: e.tensor_tensor(out=rwT[:, hp, :].rearrange("p (r c) -> p c r", c=64)[:, 16 * i:16 * i + 16, :],
                                                                   in0=tt[:].rearrange("p (c r) -> p c r", r=32), in1=pg[:].rearrange("p (c r) -> p c r", r=32),
                                                                   op=ALU.mult), r=[tt, pg], w=[rwT])

    def rw_alloc_dir(self):
        P = self.P
        S = {}
        S['A1'] = P.sb([128, 256], name='A1'); S['B1'] = P.sb([128, 256], name='B1'); S['C1'] = P.sb([128, 256], name='C1')
        S['s1'] = [{nm: P.sb([128, 256], BF16, name=nm) for nm in ('at', 'bt', 'kt', 'vs')} for _ in range(2)]
        S['rw'] = [{'rt': P.sb([128, 256], BF16, name='rt'), 'wc': P.sb([128, 4], name='wc')} for _ in range(3)]
        S['QX'] = [P.sb([128, 4, 2, 64], BF16, name='QX') for _ in range(2)]
        S['QT'] = [P.sb([128, 4, 64], BF16, name='QT') for _ in range(2)]
        S['AakT'] = P.sb([128, 4, 64], BF16, name='AakT')
        S['slots'] = []
        for _ in range(2):
            sl = {'tokT': P.sb([128, 4, 4, 64], BF16, name='tokT'), 'MT': P.sb([128, 4, 64], BF16, name='MT'), 'Xak': P.sb([128, 4, 64], BF16, name='Xak'),
                  'ArbT': P.sb([128, 4, 64], BF16, name='ArbT'), 'ArkT': P.sb([128, 4, 64], BF16, name='ArkT'), 'AhT': P.sb([128, 4, 64], BF16, name='AhT')}
            S['slots'].append(sl)
        S['Tst'] = P.sb([128, 64], name='Tst'); S['Tw'] = P.sb([128, 64], name='Tw'); S['Tb'] = P.sb([128, 64], BF16, name='Tb')
        S['Ub'] = P.sb([128, 64], BF16, name='Ub')
        for nm in ('Tst', 'Tw', 'Tb'):
            P.op('dve', lambda e, t=S[nm]: e.memset(t[:], 0.0), w=[S[nm]])
        return S

    def gen_S1(self, hp, d, g, S, C):
        P, pb, pbh = self.P, self.pb, self.pbh
        rb, kb, vb, kkb, bacc = C['rb'], C['kb'], C['vb'], C['kkb'], C['bacc']
        twd, adb, w2s, a2s, chp, hp2, msk, cmask, CW = (C[k] for k in ('twd', 'adb', 'w2s', 'a2s', 'chp', 'hp2', 'msk', 'cmask', 'CW'))
        A1, B1, C1 = (S[k] for k in ('A1', 'B1', 'C1'))
        at, bt, kt, vs = (S['s1'][g % 2][k] for k in ('at', 'bt', 'kt', 'vs'))
        rt, wc = S['rw'][g % 3]['rt'], S['rw'][g % 3]['wc']
        if d == 0:
            s0, step = 256 * g, 1
        else:
            s0, step = (0 if g == 0 else 2304 - 256 * g), -1
        nat = slice(s0, s0 + 256)
        loc = lambda t: t[:, rsl(0 if step > 0 else 255, 256, step)]
        hc = slice(hp * 128, (hp + 1) * 128); ds = slice(64 * d, 64 * d + 64)
        rot = C['rot']
        H = lambda e_: slice(64 * e_, 64 * e_ + 64)
        TP = lambda e_: (64 * e_, 64 * e_)

        def bank():
            rot[0] = (rot[0] + 1) % 4
            return pb[(0, 1, 2, 5)[rot[0]]]
        pp = bank()
        self.mm(pp[:, 0:256], w2s[ds, hc], twd[ds, nat], r=[w2s, twd], w=[pp])
        self.act(loc(A1), pp[:, 0:256], AF.Tanh, scale=0.5, bias=hp2[:, hp, d:d + 1], r=[pp, hp2], w=[A1])
        yield
        P.op('dve', lambda e: e.tensor_scalar(out=A1[:], in0=A1[:], scalar1=1.0, scalar2=CW, op0=ALU.add, op1=ALU.mult), r=[A1], w=[A1])
        P.op('dve', lambda e: e.tensor_tensor_scan(out=B1[:], data0=cmask[:, 0:256], data1=A1[:], initial=0.0, op0=ALU.mult, op1=ALU.add), r=[A1, cmask], w=[B1])
        P.op('dve', lambda e: e.tensor_tensor(out=A1[:], in0=B1[:], in1=A1[:], op=ALU.subtract), r=[A1, B1], w=[A1])
        yield
        self.act(C1[:], A1[:], AF.Exp, r=[A1], w=[C1])
        P.op('dve', lambda e: e.scalar_tensor_tensor(out=loc(at), in0=kkb[:, nat], scalar=-1.0, in1=loc(C1), op0=ALU.mult, op1=ALU.mult), r=[kkb, C1], w=[at])
        yield
        self.act(C1[:], B1[:], AF.Exp, r=[B1], w=[C1])
        P.op('dve', lambda e: e.tensor_copy(out=wc[:], in_=C1[:, 63:256:64]), r=[C1], w=[wc])
        P.op('pool', lambda e: e.tensor_tensor(out=loc(rt), in0=rb[:, nat], in1=loc(C1), op=ALU.mult), r=[rb, C1], w=[rt])
        yield
        self.act(C1[:], B1[:], AF.Exp, scale=-1.0, r=[B1], w=[C1])
        pp = bank()
        self.mm(pp[:, 0:256], a2s[ds, hc], adb[ds, nat], r=[a2s, adb], w=[pp])
        self.act(loc(A1), pp[:, 0:256], AF.Tanh, scale=0.5, bias=hp2[:, hp, 2 + d:3 + d], r=[pp, hp2], w=[A1])
        yield
        P.op('dve', lambda e: e.tensor_scalar(out=B1[:], in0=A1[:], scalar1=0.5, scalar2=0.5, op0=ALU.mult, op1=ALU.add), r=[A1], w=[B1])
        P.op('pool', lambda e: e.tensor_tensor(out=loc(B1), in0=loc(B1), in1=kkb[:, nat], op=ALU.mult), r=[B1, kkb], w=[B1])
        P.op('dve', lambda e: e.tensor_tensor(out=bt[:], in0=B1[:], in1=C1[:], op=ALU.mult), r=[B1, C1], w=[bt])
        yield
        P.op('dve', lambda e: e.tensor_scalar(out=A1[:], in0=A1[:], scalar1=hp2[:, hp, 4:5], scalar2=hp2[:, hp, 5:6], op0=ALU.mult, op1=ALU.add), r=[A1, hp2], w=[A1])
        P.op('pool', lambda e: e.tensor_tensor(out=loc(A1), in0=loc(A1), in1=kb[:, nat], op=ALU.mult), r=[A1, kb], w=[A1])
        P.op('dve', lambda e: e.tensor_tensor(out=kt[:], in0=A1[:], in1=C1[:], op=ALU.mult), r=[A1, C1], w=[kt])
        P.op('pool', lambda e: e.tensor_copy(out=loc(vs), in_=vb[:, nat]), r=[vb], w=[vs])
        yield
        if s0 >= 256:
            P.op('dve', lambda e: e.scalar_tensor_tensor(out=B1[:], in0=loc(A1), scalar=chp[:, hp, 6:7], in1=rb[:, nat], op0=ALU.mult, op1=ALU.mult),
                 r=[A1, chp, rb, bt], w=[B1])
            pp = bank()
            self.mm(pp[:, 0:256], self.bones[:], B1[:], r=[B1, self.bones], w=[pp])
            bo = s0 - 256
            P.op('dve', lambda e: e.tensor_tensor(out=bacc[:, bo:bo + 256], in0=bacc[:, bo:bo + 256], in1=pp[:, 0:256], op=ALU.add), r=[pp, (bacc, bo)], w=[(bacc, bo)])
            yield

    def gen_S2(self, hp, d, g, S, C):
        P, pb, pbh = self.P, self.pb, self.pbh
        msk = C['msk']
        at, bt, kt, vs = (S['s1'][g % 2][k] for k in ('at', 'bt', 'kt', 'vs'))
        rt = S['rw'][g % 3]['rt']
        sl = S['slots'][g % 2]
        tokT = sl['tokT']
        rot = C['rot']
        H = lambda e_: slice(64 * e_, 64 * e_ + 64)
        TP = lambda e_: (64 * e_, 64 * e_)

        def bank():
            rot[0] = (rot[0] + 1) % 4
            return pb[(0, 1, 2, 5)[rot[0]]]
        for qi, q in enumerate((at, bt, kt, vs)):
            for c in range(4):
                for e_ in range(2):
                    o = (qi % 2) * 256 + c * 64
                    P.op('pe', lambda e: e.transpose(out=pbh[H(e_), o:o + 64], in_=q[H(e_), 64 * c:64 * c + 64], identity=self.identb[H(e_), H(e_)],
                                                     tile_position=TP(e_)), r=[q, self.identb], w=[pbh])
            if qi % 2 == 1:
                P.op('act', lambda e: e.copy(out=tokT[:, qi - 1:qi + 1, :, :].rearrange("p q c j -> p (q c j)"), in_=pbh[:, 0:512]), r=[pbh], w=[tokT])
                yield
        v3 = lambda p: p[:, 0:256].rearrange("p (c t) -> p c t", t=64)
        cs = lambda q, c, e_: q[H(e_), 64 * c:64 * c + 64]

        def score(L, Rr, mk, dst, dkey):
            pp = bank()
            for c in range(4):
                for e_ in range(2):
                    self.mm(v3(pp)[H(e_), c, :], cs(L, c, e_), cs(Rr, c, e_), r=[L, Rr], w=[pp], tile_position=TP(e_))
            P.op('dve', lambda e: e.tensor_tensor(out=dst, in0=v3(pp), in1=msk[mk][:], op=ALU.mult), r=[pp, msk[mk]], w=[dkey])
        QX, QT = S['QX'][0], S['QT'][0]
        score(bt, at, 'su', QX[:, :, 0, :], QX)
        yield
        score(at, bt, 'sl', QT[:], QT)
        yield
        P.op('pool', lambda e: e.tensor_tensor(out=QX[:, :, 1, :], in0=QX[:, :, 0, :], in1=msk['id'][:], op=ALU.add), r=[QX, msk['id']], w=[QX])
        score(kt, at, 'su', S['AakT'][:], S['AakT'])
        yield
        score(bt, rt, 'iu', sl['ArbT'][:], sl['ArbT'])
        yield
        score(kt, rt, 'iu', sl['ArkT'][:], sl['ArkT'])
        yield
        v4 = lambda p: p[:, :].rearrange("p (c x) -> p c x", x=128)
        for lvl in range(1, 7):
            QXn, QTn = S['QX'][lvl % 2], S['QT'][lvl % 2]
            last = lvl == 6
            if lvl == 1:
                pq = bank()
                for c in range(4):
                    for e_ in range(2):
                        self.mm(v3(pq)[H(e_), c, :], QT[H(e_), c, :], QX[H(e_), c, 0, :], r=[QX, QT], w=[pq], tile_position=TP(e_))
                P.op('act', lambda e: e.copy(out=QXn[:, :, 0, :], in_=v3(pq)), r=[pq], w=[QXn])
                P.op('pool', lambda e: e.tensor_copy(out=QXn[:, :, 1, :], in_=QX[:, :, 1, :]), r=[QX], w=[QXn])
            else:
                ppx = bank()
                for c in range(4):
                    for e_ in range(2):
                        if last:
                            self.mm(v4(ppx)[H(e_), c, 64:128], QT[H(e_), c, :], QX[H(e_), c, 1, :], r=[QX, QT], w=[ppx], tile_position=TP(e_))
                        else:
                            self.mm(v4(ppx)[H(e_), c, :], QT[H(e_), c, :], QX[H(e_), c, :, :].rearrange("p a b -> p (a b)"), r=[QX, QT], w=[ppx],
                                    tile_position=TP(e_))
                dstP = sl['MT'][:] if last else QXn[:, :, 1, :]
                P.op('dve', lambda e: e.tensor_tensor(out=dstP, in0=v4(ppx)[:, :, 64:128], in1=QX[:, :, 1, :], op=ALU.add), r=[ppx, QX], w=[sl['MT'] if last else QXn])
                if not last:
                    P.op('act', lambda e: e.copy(out=QXn[:, :, 0, :], in_=v4(ppx)[:, :, 0:64]), r=[ppx], w=[QXn])
            if not last:
                pqt = bank()
                for c in range(4):
                    for e_ in range(2):
                        self.mm(v3(pqt)[H(e_), c, :], QX[H(e_), c, 0, :], QT[H(e_), c, :], r=[QX, QT], w=[pqt], tile_position=TP(e_))
                P.op('act', lambda e: e.copy(out=QTn[:], in_=v3(pqt)), r=[pqt], w=[QTn])
            QX, QT = QXn, QTn
            yield
        MT = sl['MT']
        pxa = bank()
        for c in range(4):
            for e_ in range(2):
                self.mm(v3(pxa)[H(e_), c, :], S['AakT'][H(e_), c, :], tokT[H(e_), 3, c, :], r=[S['AakT'], tokT], w=[pxa], tile_position=TP(e_))
        P.op('act', lambda e: e.copy(out=sl['Xak'][:], in_=v3(pxa)), r=[pxa], w=[sl['Xak']])
        yield
        pA = bank()
        for c in range(4):
            for e_ in range(2):
                self.mm(v3(pA)[H(e_), c, :], tokT[H(e_), 0, c, :], MT[H(e_), c, :], r=[tokT, MT], w=[pA], tile_position=TP(e_))
        P.op('act', lambda e: e.copy(out=sl['AhT'][:], in_=v3(pA)), r=[pA], w=[sl['AhT']])
        yield

    def gen_Q(self, hp, d, g, S, C):
        P, pb = self.P, self.pb
        y0 = C['y0']
        sl = S['slots'][g % 2]
        tokT, MT, Xak, ArbT, ArkT, AhT = (sl[k] for k in ('tokT', 'MT', 'Xak', 'ArbT', 'ArkT', 'AhT'))
        rt, wc = S['rw'][g % 3]['rt'], S['rw'][g % 3]['wc']
        Tst, Tw, Tb, Ub = S['Tst'], S['Tw'], S['Tb'], S['Ub']
        pU, pT, pY = pb[3], pb[4], pb[6]
        pUv = pU[:, 0:64]
        pTv = pT[:, 0:64]
        pYv = pY[:, 256 * d:256 * d + 256].rearrange("p (c t) -> p c t", t=64)
        H = lambda e_: slice(64 * e_, 64 * e_ + 64)
        TP = lambda e_: (64 * e_, 64 * e_)
        latent = g >= 1
        for c in range(4):
            for e_ in range(2):
                self.mm(pUv[H(e_), :], MT[H(e_), c, :], Xak[H(e_), c, :], start=True, stop=False, r=[MT, Xak], w=[pU], tile_position=TP(e_))
            for e_ in range(2):
                self.mm(pUv[H(e_), :], AhT[H(e_), c, :], Tb[H(e_), :], start=False, stop=True, r=[AhT, Tb], w=[pU], tile_position=TP(e_))
            P.op('act', lambda e: e.copy(out=Ub[:], in_=pUv), r=[pU], w=[Ub])
            yield
            for e_ in range(2):
                self.mm(pTv[H(e_), :], tokT[H(e_), 1, c, :], Ub[H(e_), :], start=True, stop=False, r=[tokT, Ub], w=[pT], tile_position=TP(e_))
            for e_ in range(2):
                self.mm(pTv[H(e_), :], tokT[H(e_), 2, c, :], tokT[H(e_), 3, c, :], start=False, stop=True, r=[tokT], w=[pT], tile_position=TP(e_))
            if latent:
                for e_ in range(2):
                    self.mm(pYv[H(e_), c, :], Tb[H(e_), :], rt[H(e_), 64 * c:64 * c + 64], start=True, stop=False, r=[Tb, rt], w=[pY], tile_position=TP(e_))
                for e_ in range(2):
                    self.mm(pYv[H(e_), c, :], Ub[H(e_), :], ArbT[H(e_), c, :], start=False, stop=False, r=[Ub, ArbT], w=[pY], tile_position=TP(e_))
                for e_ in range(2):
                    self.mm(pYv[H(e_), c, :], tokT[H(e_), 3, c, :], ArkT[H(e_), c, :], start=False, stop=True, r=[tokT, ArkT], w=[pY], tile_position=TP(e_))
            wcc = wc[:, c:c + 1]
            P.op('dve', lambda e: e.scalar_tensor_tensor(out=Tb[:], in0=pTv, scalar=wcc, in1=Tw[:], op0=ALU.mult, op1=ALU.add), r=[pT, wc, Tw], w=[Tb])
            P.op('dve', lambda e: e.scalar_tensor_tensor(out=Tst[:], in0=pTv, scalar=wcc, in1=Tw[:], op0=ALU.mult, op1=ALU.add), r=[pT, wc, Tw], w=[Tst])
            yield
            if c < 3:
                P.op('dve', lambda e: e.tensor_scalar(out=Tw[:], in0=Tst[:], scalar1=wc[:, c + 1:c + 2], scalar2=None, op0=ALU.mult), r=[Tst, wc], w=[Tw])
        if latent:
            g0 = 256 * g
            if d == 0:
                ysl = slice(g0 - 256, g0); yk = (y0, g0 - 256)
            else:
                ysl = rsl(2303 - g0, 256, -1); yk = (y0, 2048 - g0)
            P.op('dve', lambda e: e.tensor_tensor(out=y0[:, ysl], in0=y0[:, ysl], in1=pY[:, 256 * d:256 * d + 256], op=ALU.add), r=[pY, yk], w=[yk])
        yield

    def rw_rounds(self, hp, C):
        P = self.P
        dirs = [self.rw_alloc_dir() for _ in range(2)]
        for R in range(11):
            gens = []
            if 2 <= R:
                for d in range(2):
                    S = dirs[d]
                    wc = S['rw'][(R - 2) % 3]['wc']
                    P.op('dve', lambda e, S=S, wc=wc: e.tensor_scalar(out=S['Tw'][:], in0=S['Tst'][:], scalar1=wc[:, 0:1], scalar2=None, op0=ALU.mult),
                         r=[S['Tst'], wc], w=[S['Tw']])
                    gens.append(self.gen_Q(hp, d, R - 2, S, C))
            if 1 <= R <= 9:
                for d in range(2):
                    gens.append(self.gen_S2(hp, d, R - 1, dirs[d], C))
            if R <= 8:
                for d in range(2):
                    gens.append(self.gen_S1(hp, d, R, dirs[d], C))
            while gens:
                for gn in list(gens):
                    try:
                        next(gn)
                    except StopIteration:
                        gens.remove(gn)

    def rwkv_half(self, hp, d, half, rb, kb, vb, kkb, y0, bacc, Tst, Tb, twd, adb, w2s, a2s, chp, hp2, msk, cmask, CW):
        P, pb, pbh = self.P, self.pb, self.pbh
        h0, W = (0, 1280) if half == 0 else (1280, 1024)
        if d == 0:
            pieces = [(0, 256, 0, 1), (256, 1024, 256, 1)] if half == 0 else [(1280, 1024, 1280, 1)]
        else:
            pieces = [(0, 256, 255, -1), (1280, 1024, 1279, -1)] if half == 0 else [(256, 1024, 2303, -1)]
        sg = lambda t, s0, n, sig0, step, off=0, nn=None: t[:, rsl(sig0 - h0 + step * off, nn if nn is not None else n, step)]
        A1 = P.sb([128, 1280], name='A1'); B1 = P.sb([128, 1280], name='B1'); C1 = P.sb([128, 1280], name='C1')
        rt = P.sb([128, 1280], BF16, name='rt'); at = P.sb([128, 1280], BF16, name='at'); bt = P.sb([128, 1280], BF16, name='bt')
        kt = P.sb([128, 1280], BF16, name='kt'); vs = P.sb([128, 1280], BF16, name='vs')
        wcs = P.sb([128, 20], name='wcs')
        hc = slice(hp * 128, (hp + 1) * 128)
        ds = slice(64 * d, 64 * d + 64)

        def lora(wts, src, bias_col, dst):
            nb = 0
            for (s0, n, sig0, step) in pieces:
                for o in range(0, n, 512):
                    nn = min(512, n - o)
                    pp = pb[nb % 2]; nb += 1
                    self.mm(pp[:, 0:nn], wts[ds, hc], src[ds, s0 + o:s0 + o + nn], r=[wts, src], w=[pp])
                    self.act(sg(dst, s0, n, sig0, step, o, nn), pp[:, 0:nn], AF.Tanh, scale=0.5, bias=hp2[:, hp, bias_col:bias_col + 1], r=[pp, hp2], w=[dst])
        lora(w2s, twd, d, A1)
        P.op('dve', lambda e: e.tensor_scalar(out=A1[:, 0:W], in0=A1[:, 0:W], scalar1=1.0, scalar2=CW, op0=ALU.add, op1=ALU.mult), r=[A1], w=[A1])
        P.op('dve', lambda e: e.tensor_tensor_scan(out=B1[:, 0:W], data0=cmask[:, 0:W], data1=A1[:, 0:W], initial=0.0, op0=ALU.mult, op1=ALU.add),
             r=[A1, cmask], w=[B1])
        P.op('dve', lambda e: e.tensor_tensor(out=A1[:, 0:W], in0=B1[:, 0:W], in1=A1[:, 0:W], op=ALU.subtract), r=[A1, B1], w=[A1])
        self.act(C1[:, 0:W], A1[:, 0:W], AF.Exp, r=[A1], w=[C1])
        for pc in pieces:
            s0, n = pc[0], pc[1]
            P.op('dve', lambda e, pc=pc, s0=s0, n=n: e.scalar_tensor_tensor(out=sg(at, *pc), in0=kkb[:, s0:s0 + n], scalar=-1.0, in1=sg(C1, *pc),
                                                                            op0=ALU.mult, op1=ALU.mult), r=[kkb, C1], w=[at])
        self.act(C1[:, 0:W], B1[:, 0:W], AF.Exp, r=[B1, at], w=[C1])
        P.op('dve', lambda e: e.tensor_copy(out=wcs[:, 0:W // 64], in_=C1[:, 63:W:64]), r=[C1], w=[wcs])
        for pc in pieces:
            s0, n = pc[0], pc[1]
            P.op('dve', lambda e, pc=pc, s0=s0, n=n: e.tensor_tensor(out=sg(rt, *pc), in0=rb[:, s0:s0 + n], in1=sg(C1, *pc), op=ALU.mult), r=[rb, C1], w=[rt])
        self.act(C1[:, 0:W], B1[:, 0:W], AF.Exp, scale=-1.0, r=[B1, rt, wcs], w=[C1])
        lora(a2s, adb, 2 + d, A1)
        P.op('dve', lambda e: e.tensor_scalar(out=B1[:, 0:W], in0=A1[:, 0:W], scalar1=0.5, scalar2=0.5, op0=ALU.mult, op1=ALU.add), r=[A1], w=[B1])
        for pc in pieces:
            s0, n = pc[0], pc[1]
            P.op('dve', lambda e, pc=pc, s0=s0, n=n: e.tensor_tensor(out=sg(B1, *pc), in0=sg(B1, *pc), in1=kkb[:, s0:s0 + n], op=ALU.mult), r=[B1, kkb], w=[B1])
        P.op('dve', lambda e: e.tensor_tensor(out=bt[:, 0:W], in0=B1[:, 0:W], in1=C1[:, 0:W], op=ALU.mult), r=[B1, C1], w=[bt])
        P.op('dve', lambda e: e.tensor_scalar(out=A1[:, 0:W], in0=A1[:, 0:W], scalar1=hp2[:, hp, 4:5], scalar2=hp2[:, hp, 5:6], op0=ALU.mult, op1=ALU.add),
             r=[A1, hp2], w=[A1])
        for pc in pieces:
            s0, n = pc[0], pc[1]
            P.op('dve', lambda e, pc=pc, s0=s0, n=n: e.tensor_tensor(out=sg(A1, *pc), in0=sg(A1, *pc), in1=kb[:, s0:s0 + n], op=ALU.mult), r=[A1, kb], w=[A1])
        P.op('dve', lambda e: e.tensor_tensor(out=kt[:, 0:W], in0=A1[:, 0:W], in1=C1[:, 0:W], op=ALU.mult), r=[A1, C1], w=[kt])
        nb = 0
        for pc in pieces:
            s0, n, sig0, step = pc
            if s0 < 256:
                continue
            P.op('dve', lambda e, pc=pc, s0=s0, n=n: e.scalar_tensor_tensor(out=B1[:, 0:n], in0=sg(A1, *pc), scalar=chp[:, hp, 6:7], in1=rb[:, s0:s0 + n],
                                                                            op0=ALU.mult, op1=ALU.mult), r=[A1, chp, rb, bt], w=[B1])
            for o in range(0, n, 512):
                pp = pb[nb % 2]; nb += 1
                self.mm(pp[:], self.bones[:], B1[:, o:o + 512], r=[B1, self.bones], w=[pp])
                bo = s0 - 256 + o
                if d == 0:
                    P.op('act', lambda e, pp=pp, bo=bo: e.copy(out=bacc[:, bo:bo + 512], in_=pp[:]), r=[pp], w=[bacc])
                else:
                    P.op('dve', lambda e, pp=pp, bo=bo: e.tensor_tensor(out=bacc[:, bo:bo + 512], in0=bacc[:, bo:bo + 512], in1=pp[:], op=ALU.add), r=[pp, bacc], w=[bacc])
        for pc in pieces:
            s0, n = pc[0], pc[1]
            P.op('pool', lambda e, pc=pc, s0=s0, n=n: e.tensor_copy(out=sg(vs, *pc), in_=vb[:, s0:s0 + n]), r=[vb], w=[vs])

        if hp == 0 and half == 0:
            for nm, t_ in (('rt', rt), ('at', at), ('bt', bt), ('kt', kt), ('vs', vs)):
                self.tap(nm + str(d), t_[:], [128, 1280], [t_], dt=BF16)
            self.tap('wcs' + str(d), wcs[:], [128, 20], [wcs])
            self.tap('kd' + str(d), A1[:], [128, 1280], [A1])
        tokT = P.sb([64, 4, 4, 128], BF16, name='tokT')
        Qs = [P.sb([64, 8, 64], BF16, name='Q') for _ in range(2)]; QTs = [P.sb([64, 8, 64], BF16, name='QT') for _ in range(2)]
        Xs = [P.sb([64, 8, 64], BF16, name='X') for _ in range(2)]
        AakT = P.sb([64, 8, 64], BF16, name='AakT'); ArbT = P.sb([64, 8, 64], BF16, name='ArbT'); ArkT = P.sb([64, 8, 64], BF16, name='ArkT')
        Xak = P.sb([64, 8, 64], BF16, name='Xak'); AhT = P.sb([128, 4, 64], BF16, name='AhT')
        Ub = P.sb([64, 2, 64], BF16, name='Ub'); Ts = P.sb([128, 64], name='Ts')
        pU, pT, pY, pA = pb[4], pb[5], pb[6], pb[0]
        pYv = pb[6][:, 0:256].rearrange("p (c t) -> p c t", t=64)
        pAv = pb[0][:, 0:256].rearrange("p (c t) -> p c t", t=64)
        pUv = pb[4][0:64, 0:128].rearrange("p (e v) -> p e v", v=64)
        pTv = pb[5][:, 0:64]
        bank = [0]

        def nextbank():
            bank[0] = (bank[0] + 1) % 2
            return pb[2 + bank[0]]
        v3 = lambda p: p[0:64, :].rearrange("p (i t) -> p i t", t=64)
        for gi in range(W // 256):
            loc = 256 * gi
            g0 = h0 + loc
            latent = g0 >= 256
            for qi, q in enumerate((at, bt, kt, vs)):
                for c in range(4):
                    P.op('pe', lambda e, qi=qi, q=q, c=c: e.transpose(out=pbh[0:64, (qi % 2) * 512 + c * 128:(qi % 2) * 512 + (c + 1) * 128],
                                                                      in_=q[:, loc + 64 * c:loc + 64 * c + 64], identity=self.identb[:]),
                         r=[q, self.identb], w=[pbh])
                if qi % 2 == 1:
                    P.op('act', lambda e, qi=qi: e.copy(out=tokT[:, qi - 1:qi + 1, :, :].rearrange("p q c j -> p (q c j)"), in_=pbh[0:64, :]), r=[pbh], w=[tokT])
            cs = lambda q, c, e_: q[64 * e_:64 * e_ + 64, loc + 64 * c:loc + 64 * c + 64]

            def score(L, Rr, mk, dst):
                pp = nextbank()
                for c in range(4):
                    for e_ in range(2):
                        self.mm(v3(pp)[:, 2 * c + e_, :], cs(L, c, e_), cs(Rr, c, e_), r=[L, Rr], w=[pp])
                P.op('dve', lambda e: e.tensor_tensor(out=dst[:], in0=v3(pp), in1=msk[mk][:], op=ALU.mult), r=[pp, msk[mk]], w=[dst])
            Q, QT, X = Qs[0], QTs[0], Xs[0]
            score(bt, at, 'su', Q)
            score(at, bt, 'sl', QT)
            score(kt, at, 'su', AakT)
            score(bt, rt, 'iu', ArbT)
            score(kt, rt, 'iu', ArkT)
            P.op('dve', lambda e, Q=Q, X=X: e.tensor_tensor(out=X[:], in0=Q[:], in1=msk['id'][:], op=ALU.add), r=[Q, msk['id']], w=[X])
            for lvl in range(2, 7):
                Qn, QTn, Xn = Qs[(lvl + 1) % 2], QTs[(lvl + 1) % 2], Xs[(lvl + 1) % 2]
                if lvl < 6:
                    pq = nextbank()
                    for i in range(8):
                        self.mm(v3(pq)[:, i, :], QT[:, i, :], Q[:, i, :], r=[Q, QT], w=[pq])
                    P.op('act', lambda e, pq=pq, Qn=Qn: e.copy(out=Qn[:], in_=v3(pq)), r=[pq], w=[Qn])
                pqt = nextbank()
                for i in range(8):
                    self.mm(v3(pqt)[:, i, :], Q[:, i, :], QT[:, i, :], r=[Q, QT], w=[pqt])
                P.op('act', lambda e, pqt=pqt, QTn=QTn: e.copy(out=QTn[:], in_=v3(pqt)), r=[pqt], w=[QTn])
                px = nextbank()
                for i in range(8):
                    self.mm(v3(px)[:, i, :], QTn[:, i, :], X[:, i, :], r=[QTn, X], w=[px])
                P.op('dve', lambda e, px=px, X=X, Xn=Xn: e.tensor_tensor(out=Xn[:], in0=v3(px), in1=X[:], op=ALU.add), r=[px, X], w=[Xn])
                Q, QT, X = Qn, QTn, Xn
            MT = X
            pxa = nextbank()
            for c in range(4):
                for e_ in range(2):
                    self.mm(v3(pxa)[:, 2 * c + e_, :], AakT[:, 2 * c + e_, :], tokT[:, 3, c, 64 * e_:64 * e_ + 64], r=[AakT, tokT], w=[pxa])
            P.op('act', lambda e, pxa=pxa: e.copy(out=Xak[:], in_=v3(pxa)), r=[pxa], w=[Xak])
            for c in range(4):
                for e_ in range(2):
                    self.mm(pAv[64 * e_:64 * e_ + 64, c, :], tokT[:, 0, c, 64 * e_:64 * e_ + 64], MT[:, 2 * c + e_, :], r=[tokT, MT], w=[pA],
                            tile_position=(0, 64 * e_))
            P.op('act', lambda e: e.copy(out=AhT[:], in_=pAv), r=[pA], w=[AhT])
            if hp == 0 and half == 0 and d == 0 and gi == 0:
                self.tap('MT', MT[:], [64, 8, 64], [MT], dt=BF16)
                self.tap('AhT', AhT[:], [128, 4, 64], [AhT], dt=BF16)
                self.tap('Xak', Xak[:], [64, 8, 64], [Xak], dt=BF16)
                self.tap('tokT', tokT[:], [64, 4, 4, 128], [tokT], dt=BF16)
                self.tap('ArbT', ArbT[:], [64, 8, 64], [ArbT], dt=BF16)
            for c in range(4):
                for e_ in range(2):
                    i = 2 * c + e_
                    es = slice(64 * e_, 64 * e_ + 64)
                    self.mm(pUv[:, e_, :], MT[:, i, :], Xak[:, i, :], start=True, stop=False, r=[MT, Xak], w=[pU])
                    self.mm(pUv[:, e_, :], AhT[es, c, :], Tb[es, :], start=False, stop=True, r=[AhT, Tb], w=[pU], tile_position=(64 * e_, 0))
                P.op('act', lambda e: e.copy(out=Ub[:], in_=pUv), r=[pU], w=[Ub])
                for e_ in range(2):
                    i = 2 * c + e_
                    es = slice(64 * e_, 64 * e_ + 64)
                    if latent:
                        self.mm(pYv[es, c, :], Tb[es, :], rt[es, loc + 64 * c:loc + 64 * c + 64], start=True, stop=False, r=[Tb, rt], w=[pY],
                                tile_position=(64 * e_, 64 * e_))
                        self.mm(pYv[es, c, :], Ub[:, e_, :], ArbT[:, i, :], start=False, stop=False, r=[Ub, ArbT], w=[pY], tile_position=(0, 64 * e_))
                        self.mm(pYv[es, c, :], tokT[:, 3, c, es], ArkT[:, i, :], start=False, stop=True, r=[tokT, ArkT], w=[pY], tile_position=(0, 64 * e_))
                    self.mm(pTv[es, :], tokT[:, 1, c, es], Ub[:, e_, :], start=True, stop=False, r=[tokT, Ub], w=[pT], tile_position=(0, 64 * e_))
                    self.mm(pTv[es, :], tokT[:, 2, c, es], tokT[:, 3, c, es], start=False, stop=True, r=[tokT], w=[pT], tile_position=(0, 64 * e_))
                P.op('dve', lambda e: e.tensor_tensor(out=Ts[:], in0=pTv, in1=Tst[:], op=ALU.add), r=[pT, Tst], w=[Ts])
                wc = wcs[:, 4 * gi + c:4 * gi + c + 1]
                P.op('dve', lambda e, wc=wc: e.tensor_scalar(out=Tst[:], in0=Ts[:], scalar1=wc, scalar2=None, op0=ALU.mult), r=[Ts, wcs], w=[Tst])
                self.act(Tb[:], Ts[:], AF.Identity, scale=wc, r=[Ts, wcs], w=[Tb])
            if latent:
                if d == 0:
                    P.op('act', lambda e, g0=g0: e.copy(out=y0[:, g0 - 256:g0], in_=pb[6][:, 0:256]), r=[pY], w=[y0])
                else:
                    ysl = rsl(2303 - g0, 256, -1)
                    P.op('dve', lambda e, ysl=ysl: e.tensor_tensor(out=y0[:, ysl], in0=y0[:, ysl], in1=pb[6][:, 0:256], op=ALU.add), r=[pY, y0], w=[y0])

    def wload(self, pool, src, npart, K, ncol, q='sp'):
        P = self.P
        i = pool['i'] = pool['i'] + 1
        wf, wb = pool['f'][i % len(pool['f'])], pool['b'][i % len(pool['b'])]
        P.dma(q, wf[0:npart, 0:K, 0:ncol], src, w=[wf], group=pool['name'] + str(i % len(pool['f'])))
        ceng = 'pool' if (self._castn % 2 == 0) else 'dve'
        self._castn += 1
        P.op(ceng, lambda e: e.tensor_copy(out=wb[0:npart, 0:K, 0:ncol], in_=wf[0:npart, 0:K, 0:ncol]), r=[wf], w=[wb])
        return wb

    def mkpool(self, name, npart, K, ncol, n=2):
        P = self.P
        return {'name': name, 'i': 0, 'f': [P.sb([npart, K, ncol], name=name + 'f') for _ in range(n)],
                'b': [P.sb([npart, K, ncol], BF16, name=name + 'b') for _ in range(n)]}

    def lru(self):
        P, pb, hT, din = self.P, self.pb, self.hT, self.din
        lruT = self.lruT
        cw_, cb_, ba_, bx_, lam_ = din['lru_conv_w'], din['lru_conv_b'], din['lru_ba'], din['lru_bx'], din['lru_lambda']
        rows = [cw_[d, j] for d in range(2) for j in range(4)] + [cb_[0], cb_[1], ba_[0], ba_[1], bx_[0], bx_[1], lam_[0], lam_[1]]
        lp = self.cols(rows, LW, 80, 'lp')
        hb = P.sb([80, 16, 4], name='hb')
        P.op('dve', lambda e: e.tensor_scalar(out=hb[:], in0=lp[:, :, 10:14], scalar1=0.5, scalar2=None, op0=ALU.mult), r=[lp], w=[hb])
        cs = P.sb([80, 16, 4], name='cs')
        one1 = P.sb([80, 1], name='one1')
        P.op('dve', lambda e: e.memset(one1[:], 1.0), w=[one1])
        self.act(cs[:, :, 0:2], lp[:, :, 14:16], AF.Exp, scale=-1.0, r=[lp], w=[cs])
        self.act(cs[:, :, 0:2], cs[:, :, 0:2], AF.Ln, bias=one1[:], r=[cs, one1], w=[cs])
        P.op('dve', lambda e: e.tensor_scalar(out=cs[:, :, 2:4], in0=cs[:, :, 0:2], scalar1=-4.0, scalar2=None, op0=ALU.mult), r=[cs], w=[cs])
        P.op('dve', lambda e: e.tensor_scalar(out=cs[:, :, 0:2], in0=cs[:, :, 0:2], scalar1=-8.0, scalar2=None, op0=ALU.mult), r=[cs], w=[cs])
        q25 = P.sb([80, 1], name='q25')
        P.op('dve', lambda e: e.memset(q25[:], 0.25), w=[q25])
        gwa = P.sb([80, 32, 80], BF16, name='gwa'); gwx = P.sb([80, 32, 80], BF16, name='gwx')
        with P.scope():
            st = P.sb([80, 32, 80], name='gst')
            for src, dst in ((din['lru_wa'], gwa), (din['lru_wx'], gwx)):
                P.dma('sp', st[:], src.rearrange("d n c e -> c (d n) e"), w=[st], group='gst')
                P.op('dve', lambda e, dst=dst: e.tensor_copy(out=dst[:], in_=st[:]), r=[st], w=[dst])
        U = P.sb([80, 2313], name='U'); guy = P.sb([80, NLAT], BF16, name='guy')
        xcs = [P.sb([80, 2313], name='xc') for _ in range(2)]
        xcb = P.sb([80, 2313], BF16, name='xcb')
        thr = P.sb([80, 2313], name='thr'); thi = P.sb([80, 2313], name='thi'); aa = P.sb([80, 2313], name='aa')
        lrub = P.sb([80, NLAT], BF16, name='lrub')
        wp = self.mkpool('lw', 128, 8, 80, n=2)
        P.op('dve', lambda e: e.memset(U[:], 0.0), w=[U])
        for xc in xcs:
            P.op('dve', lambda e, xc=xc: e.memset(xc[:], 0.0), w=[xc])
        P.op('dve', lambda e: e.memset(thr[:], 0.0), w=[thr])
        P.op('dve', lambda e: e.memset(thi[:], 0.0), w=[thi])
        wv = din['w_in'].rearrange("(k p) n -> p k n", p=128)
        hk = [(hT, j) for j in range(5)]
        tbs = [(0, 256, 3)] + [(256 + 512 * i, 512, 262 + 512 * i) for i in range(4)]
        nb = 0
        for n in range(NBLK):
            wx = self.wload(wp, wv[:, :, 80 * n:80 * n + 80], 128, 8, 80)
            for (hc, nn, uc) in tbs:
                pp = pb[nb % 6]; nb += 1
                for k in range(8):
                    self.mm(pp[0:80, 0:nn], wx[:, k, :], hT[:, k, hc:hc + nn], start=(k == 0), stop=(k == 7), r=[wx] + hk, w=[pp])
                P.op('act', lambda e: e.copy(out=U[:, uc:uc + nn], in_=pp[0:80, 0:nn]), r=[pp], w=[U])
            wy = self.wload(wp, wv[:, :, 1280 + 80 * n:1280 + 80 * n + 80], 128, 8, 80, q='act')
            for (hc, nn, uc) in tbs[1:]:
                pp = pb[nb % 6]; nb += 1
                for k in range(8):
                    self.mm(pp[0:80, 0:nn], wy[:, k, :], hT[:, k, hc:hc + nn], start=(k == 0), stop=(k == 7), r=[wy] + hk, w=[pp])
                self.act(guy[:, hc - 256:hc - 256 + nn], pp[0:80, 0:nn], AF.Gelu_apprx_tanh, r=[pp], w=[guy])
            for d in range(2):
                xc = xcs[d]
                sgn = -1 if d == 0 else 1
                cwj = lambda j: lp[:, n, 4 * d + j:4 * d + j + 1]

                def chunk(ci, uc, nn):
                    nonlocal nb
                    cs_ = slice(uc, uc + nn)
                    P.op('dve', lambda e: e.tensor_scalar(out=xc[:, cs_], in0=U[:, cs_], scalar1=cwj(3), scalar2=lp[:, n, 8 + d:9 + d],
                                                          op0=ALU.mult, op1=ALU.add), r=[U, lp], w=[(xc, ci)])
                    for j in range(3):
                        o = uc + sgn * (3 - j)
                        P.op('dve', lambda e: e.scalar_tensor_tensor(out=xc[:, cs_], in0=U[:, o:o + nn], scalar=cwj(j), in1=xc[:, cs_],
                                                                     op0=ALU.mult, op1=ALU.add), r=[U, lp, (xc, ci)], w=[(xc, ci)])
                    yield
                    P.op('pool', lambda e: e.tensor_copy(out=xcb[:, cs_], in_=xc[:, cs_]), r=[(xc, ci)], w=[(xcb, ci)])
                    yield
                    for gw, dst, bcol in ((gwa, thr, d), (gwx, thi, 2 + d)):
                        pp = pb[nb % 6]; nb += 1
                        self.mm(pp[0:80, 0:nn], gw[:, 16 * d + n, :], xcb[:, cs_], r=[gw, (xcb, ci)], w=[pp])
                        self.act(dst[:, cs_], pp[0:80, 0:nn], AF.Tanh, scale=0.5, bias=hb[:, n, bcol:bcol + 1], r=[pp, hb], w=[(dst, ci)])
                    yield
                    self.act(aa[:, cs_], thr[:, cs_], AF.Exp, scale=cs[:, n, 2 + d:3 + d], bias=cs[:, n, 2 + d:3 + d], r=[(thr, ci), cs], w=[(aa, ci)])
                    self.act(thr[:, cs_], thr[:, cs_], AF.Exp, scale=cs[:, n, d:d + 1], bias=cs[:, n, d:d + 1], r=[(thr, ci), cs], w=[(thr, ci)])
                    P.op('dve', lambda e: e.scalar_tensor_tensor(out=thi[:, cs_], in0=thi[:, cs_], scalar=1.0, in1=xc[:, cs_], op0=ALU.add, op1=ALU.mult),
                         r=[(thi, ci), (xc, ci)], w=[(thi, ci)])
                    yield
                    self.act(thr[:, cs_], thr[:, cs_], AF.Sqrt, scale=-0.25, bias=q25[:], r=[(thr, ci), q25], w=[(thr, ci)])
                    yield
                    P.op('dve', lambda e: e.tensor_tensor(out=thi[:, cs_], in0=thi[:, cs_], in1=thr[:, cs_], op=ALU.mult), r=[(thi, ci), (thr, ci)], w=[(thi, ci)])
                    yield
                gens = [chunk(ci, uc, nn) for ci, (hc, nn, uc) in enumerate(tbs)]
                while gens:
                    for gn in list(gens):
                        try:
                            next(gn)
                        except StopIteration:
                            gens.remove(gn)
                allk = lambda t_: [(t_, ci) for ci in range(5)]
                if d == 0:
                    P.op('dve', lambda e: e.tensor_tensor_scan(out=xc[:, 3:259], data0=aa[:, 3:259], data1=thi[:, 3:259], initial=0.0, op0=ALU.mult, op1=ALU.add),
                         r=allk(aa) + allk(thi), w=allk(xc))
                    P.op('dve', lambda e: e.tensor_tensor_scan(out=xc[:, 262:2310], data0=aa[:, 262:2310], data1=thi[:, 262:2310], initial=xc[:, 258:259],
                                                               op0=ALU.mult, op1=ALU.add), r=allk(aa) + allk(thi) + allk(xc), w=allk(xc))
                else:
                    rv = lambda t_, a_, b_: t_[:, rsl(b_ - 1, b_ - a_, -1)]
                    P.op('dve', lambda e: e.tensor_tensor_scan(out=rv(xc, 3, 259), data0=rv(aa, 3, 259), data1=rv(thi, 3, 259), initial=0.0, op0=ALU.mult, op1=ALU.add),
                         r=allk(aa) + allk(thi), w=allk(xc))
                    P.op('dve', lambda e: e.tensor_tensor_scan(out=rv(xc, 262, 2310), data0=rv(aa, 262, 2310), data1=rv(thi, 262, 2310), initial=xc[:, 3:4],
                                                               op0=ALU.mult, op1=ALU.add), r=allk(aa) + allk(thi) + allk(xc), w=allk(xc))
            allk = lambda t_: [(t_, ci) for ci in range(5)]
            P.op('dve', lambda e: e.tensor_tensor(out=aa[:, 0:NLAT], in0=xcs[0][:, 262:2310], in1=xcs[1][:, 262:2310], op=ALU.add), r=allk(xcs[0]) + allk(xcs[1]) + allk(aa), w=allk(aa))
            P.op('dve', lambda e: e.tensor_tensor(out=lrub[:], in0=aa[:, 0:NLAT], in1=guy[:], op=ALU.mult), r=allk(aa) + [guy], w=[lrub])
            p0, c0 = (80 * n) % 128, (80 * n) // 128
            n1 = min(80, 128 - p0)
            P.dma('sp', lruT[p0:p0 + n1, c0, :], lrub[0:n1, :], r=[lrub], w=[lruT], group='lruT')
            if n1 < 80:
                P.dma('sp', lruT[0:80 - n1, c0 + 1, :], lrub[n1:80, :], r=[lrub], w=[lruT], group='lruT')

    def merge(self):
        P, pb, hT, din = self.P, self.pb, self.hT, self.din
        lruT, rwT, mT = self.lruT, self.rwT, self.mT
        wv = din['w_in'].rearrange("(k p) n -> p k n", p=128)
        wol = din['w_o_lru'].rearrange("(k p) n -> p k n", p=128)
        wor = din['w_o_rwkv'].rearrange("(k p) n -> p k n", p=128)
        pl = self.mkpool('wl', 128, 10, 128); pr = self.mkpool('wr', 128, 8, 128); pg = self.mkpool('wg', 128, 8, 128, n=3)
        thl = P.sb([128, 512], name='thl'); thr = P.sb([128, 512], name='thr2'); t1 = P.sb([128, 512], name='t1'); t2 = P.sb([128, 512], name='t2')
        hk = [(hT, j) for j in range(5)]
        for dc in range(8):
            cs_ = slice(dc * 128, dc * 128 + 128)
            wl = self.wload(pl, wol[:, :, cs_], 128, 10, 128)
            wr = self.wload(pr, wor[:, :, cs_], 128, 8, 128, q='act')
            wgl = self.wload(pg, wv[:, :, 6048 + dc * 128:6048 + dc * 128 + 128], 128, 8, 128)
            wgr = self.wload(pg, wv[:, :, 7072 + dc * 128:7072 + dc * 128 + 128], 128, 8, 128, q='act')
            for tb in range(4):
                ts_ = slice(512 * tb, 512 * tb + 512); hs_ = slice(256 + 512 * tb, 256 + 512 * tb + 512)
                p1, p2, p3, p4 = pb[0], pb[1], pb[2], pb[3]
                for c in range(10):
                    self.mm(p1[:], wl[:, c, :], lruT[:, c, ts_], start=(c == 0), stop=(c == 9), r=[wl, lruT], w=[p1])
                for c in range(8):
                    self.mm(p2[:], wr[:, c, :], rwT[:, c, ts_], start=(c == 0), stop=(c == 7), r=[wr, rwT], w=[p2])
                for c in range(8):
                    self.mm(p3[:], wgl[:, c, :], hT[:, c, hs_], start=(c == 0), stop=(c == 7), r=[wgl] + hk, w=[p3])
                for c in range(8):
                    self.mm(p4[:], wgr[:, c, :], hT[:, c, hs_], start=(c == 0), stop=(c == 7), r=[wgr] + hk, w=[p4])
                self.act(thl[:], p3[:], AF.Tanh, scale=0.5, r=[p3], w=[thl])
                self.act(thr[:], p4[:], AF.Tanh, scale=0.5, r=[p4], w=[thr])
                P.op('dve', lambda e: e.scalar_tensor_tensor(out=t1[:], in0=thl[:], scalar=1.0, in1=p1[:], op0=ALU.add, op1=ALU.mult), r=[thl, p1], w=[t1])
                P.op('dve', lambda e: e.scalar_tensor_tensor(out=t2[:], in0=thr[:], scalar=1.0, in1=p2[:], op0=ALU.add, op1=ALU.mult), r=[thr, p2], w=[t2])
                P.op('dve', lambda e: e.tensor_tensor(out=mT[:, dc, ts_], in0=t1[:], in1=t2[:], op=ALU.add), r=[t1, t2], w=[mT])

    def resid1(self):
        P, pb, din, mod = self.P, self.pb, self.din, self.mod
        mT, x1T = self.mT, self.x1T
        hg = P.sb([128, 8], name='hg')
        P.op('dve', lambda e: e.tensor_scalar(out=hg[:], in0=mod[:, 16:24, 0], scalar1=0.5, scalar2=None, op0=ALU.mult), r=[mod], w=[hg])
        wo = P.sb([128, 8, D], BF16, name='wo')
        with P.scope():
            st = P.sb([128, 8, 256], name='wost')
            for j in range(4):
                P.dma('sp', st[:], din['w_out'].rearrange("(k p) n -> p k n", p=128)[:, :, 256 * j:256 * j + 256], w=[st], group='wost')
                P.op('pool', lambda e: e.tensor_copy(out=wo[:, :, 256 * j:256 * j + 256], in_=st[:]), r=[st], w=[wo])
        xv = din['xT'].rearrange("(k p) t -> p k t", p=128)
        for tb in range(4):
            ts_ = slice(512 * tb, 512 * tb + 512)
            P.dma('sp', x1T[:, :, ts_], xv[:, :, ts_], w=[x1T], group='x1ld')
            for dc in range(8):
                pp = pb[(tb * 8 + dc) % 2]
                for c in range(8):
                    self.mm(pp[:], wo[:, c, dc * 128:dc * 128 + 128], mT[:, c, ts_], start=(c == 0), stop=(c == 7), r=[wo, mT], w=[pp])
                P.op('dve', lambda e: e.scalar_tensor_tensor(out=x1T[:, dc, ts_], in0=pp[:], scalar=hg[:, dc:dc + 1], in1=x1T[:, dc, ts_], op0=ALU.mult, op1=ALU.add),
                     r=[pp, hg, x1T], w=[x1T])

    def ffn(self):
        P, pb, din, mod = self.P, self.pb, self.din, self.mod
        x1T, h2T = self.x1T, self.h2T
        wi = din['w_ffn_in'].rearrange("(k p) n -> p k n", p=128)
        wo_ = din['w_ffn_out'].rearrange("(f p) n -> p f n", p=128)
        actT = P.sb([128, 22, 1024], BF16, name='actT')
        pin = self.mkpool('fi', 128, 8, 128, n=3); pout = self.mkpool('fo', 128, 22, 128, n=2)
        sl = P.sb([128, 512], name='sl')
        hk = [(h2T, j) for j in range(4)]
        nb = 0
        for half in range(2):
            for f in range(22):
                wg = self.wload(pin, wi[:, :, f * 128:f * 128 + 128], 128, 8, 128)
                wu = self.wload(pin, wi[:, :, DFF + f * 128:DFF + f * 128 + 128], 128, 8, 128, q='act')
                for t2 in range(2):
                    tok = slice(1024 * half + 512 * t2, 1024 * half + 512 * t2 + 512)
                    pg_, pu_ = pb[nb % 4], pb[(nb + 1) % 4]; nb += 2
                    for k in range(8):
                        self.mm(pg_[:], wg[:, k, :], h2T[:, k, tok], start=(k == 0), stop=(k == 7), r=[wg] + hk, w=[pg_])
                    for k in range(8):
                        self.mm(pu_[:], wu[:, k, :], h2T[:, k, tok], start=(k == 0), stop=(k == 7), r=[wu] + hk, w=[pu_])
                    self.act(sl[:], pg_[:], AF.Silu, r=[pg_], w=[sl])
                    P.op('dve', lambda e: e.tensor_tensor(out=actT[:, f, 512 * t2:512 * t2 + 512], in0=sl[:], in1=pu_[:], op=ALU.mult), r=[sl, pu_], w=[(actT, f)])
            for dc in range(8):
                wo = self.wload(pout, wo_[:, :, dc * 128:dc * 128 + 128], 128, 22, 128)
                for t2 in range(2):
                    tok = slice(1024 * half + 512 * t2, 1024 * half + 512 * t2 + 512)
                    pp = pb[4 + (nb % 2)]; nb += 1
                    for f in range(22):
                        self.mm(pp[:], wo[:, f, :], actT[:, f, 512 * t2:512 * t2 + 512], start=(f == 0), stop=(f == 21), r=[wo, (actT, f)], w=[pp])
                    P.op('dve', lambda e: e.scalar_tensor_tensor(out=x1T[:, dc, tok], in0=pp[:], scalar=mod[:, 40 + dc, 0:1], in1=x1T[:, dc, tok], op0=ALU.mult, op1=ALU.add),
                         r=[pp, mod, x1T], w=[x1T])

    def final(self, outT):
        P, pb, x = self.P, self.pb, self.x1T
        gains = self.gains
        sq = P.sb([128, 8, 512], name='fsq'); rs = P.sb([128, 512], name='frs')
        epst = P.sb([128, 1], name='fepst')
        P.op('dve', lambda e: e.memset(epst[:], RMS_EPS), w=[epst])
        ov = outT.rearrange("(k p) t -> p k t", p=128)
        for tb in range(4):
            ts_ = slice(512 * tb, 512 * tb + 512)
            self.act(sq[:], x[:, :, ts_], AF.Square, r=[x], w=[sq])
            pp = pb[tb % 2]
            for k in range(8):
                self.mm(pp[:], self.ones[:], sq[:, k, :], start=(k == 0), stop=(k == 7), r=[sq, self.ones], w=[pp])
            self.act(rs[:], pp[:], AF.Sqrt, scale=1.0 / D, bias=epst[:], r=[pp, epst], w=[rs])
            P.op('dve', lambda e: e.reciprocal(out=rs[:], in_=rs[:]), r=[rs], w=[rs])
            for k in range(8):
                P.op('dve', lambda e: e.scalar_tensor_tensor(out=sq[:, k, :], in0=x[:, k, ts_], scalar=gains[:, k, 2:3], in1=rs[:], op0=ALU.mult, op1=ALU.mult),
                     r=[x, gains, rs], w=[sq])
            P.op('dve', lambda e: e.tensor_copy(out=epst[:], in_=epst[:]), r=[sq, epst], w=[sq, epst])
            P.dma('sp', ov[:, :, ts_], sq[:], r=[sq], group='out')

    def finish(self):
        P = self.P
        for gname in list(P.dsem):
            if gname.startswith('out'):
                P.wait_group('pool', gname)
        P.emit()
        return self.nc


_CACHE = {}


def _prep(inputs, b):
    f = lambda a: np.ascontiguousarray(a, dtype=np.float32)
    m = {
        'xT': f(inputs['x'][b].T), 'ctxT': f(inputs['ctx'][b].T),
        'cvec': f(np.stack([inputs['c'][b], inputs['c_ctx']])),
        'w_mod': f(inputs['w_mod'][0]), 'b_mod': f(inputs['b_mod'][0]),
        'norm_mix_g': f(inputs['norm_mix_g'][0]), 'norm_ffn_g': f(inputs['norm_ffn_g'][0]), 'norm_final_g': f(inputs['norm_final_g']),
        'w_in': f(inputs['w_in'][0]),
        'lru_conv_w': f(inputs['lru_conv_w'][0]), 'lru_conv_b': f(inputs['lru_conv_b'][0]),
        'lru_wa': f(inputs['lru_wa'][0]), 'lru_ba': f(inputs['lru_ba'][0]), 'lru_wx': f(inputs['lru_wx'][0]), 'lru_bx': f(inputs['lru_bx'][0]),
        'lru_lambda': f(inputs['lru_lambda'][0]), 'w_o_lru': f(inputs['w_o_lru'][0]),
        'rwkv_mu': f(inputs['rwkv_mu'][0]), 'rwkv_w0': f(inputs['rwkv_w0'][0]), 'rwkv_w2': f(inputs['rwkv_w2'][0]),
        'rwkv_a0': f(inputs['rwkv_a0'][0]), 'rwkv_a2': f(inputs['rwkv_a2'][0]), 'rwkv_g2': f(inputs['rwkv_g2'][0]),
        'rwkv_k_k': f(inputs['rwkv_k_k'][0]), 'rwkv_k_a': f(inputs['rwkv_k_a'][0]), 'rwkv_r_k': f(inputs['rwkv_r_k'][0].reshape(-1)),
        'rwkv_ln_g': f(inputs['rwkv_ln_g'][0]), 'rwkv_ln_b': f(inputs['rwkv_ln_b'][0]),
        'w_o_rwkv': f(inputs['w_o_rwkv'][0]), 'w_out': f(inputs['w_out'][0]),
        'w_ffn_in': f(inputs['w_ffn_in'][0]), 'w_ffn_out': f(inputs['w_ffn_out'][0]),
    }
    return m


def kernel(**inputs):
    if 'nc' not in _CACHE:
        _CACHE['nc'] = Builder().build()
    nc = _CACHE['nc']
    shared = _prep(inputs, 0)
    in_maps = []
    for b in range(8):
        m = dict(shared)
        m['xT'] = np.ascontiguousarray(np.asarray(inputs['x'][b], dtype=np.float32).T)
        m['ctxT'] = np.ascontiguousarray(np.asarray(inputs['ctx'][b], dtype=np.float32).T)
        m['cvec'] = np.ascontiguousarray(np.stack([inputs['c'][b], inputs['c_ctx']]).astype(np.float32))
        in_maps.append(m)
    res = run_bass_kernel_spmd(nc, in_maps, core_ids=list(range(8)))
    out = np.stack([np.ascontiguousarray(r['outT'].T) for r in res.results]).astype(np.float32)
    return out
```

```python
import contextlib
import numpy as np
import concourse.bass as bass
import concourse.mybir as mybir
from concourse.bass_utils import run_bass_kernel_spmd

F32 = mybir.dt.float32
BF16 = mybir.dt.bfloat16
AF = mybir.ActivationFunctionType
ALU = mybir.AluOpType

ENGS = ['pe', 'act', 'dve', 'pool', 'sp']
NCTX, NLAT, T = 256, 2048, 2304
D = 1024
LW, NBLK, BLK = 1280, 16, 80
RIN = 3488
DFF = 2816
RMS_EPS, GN_EPS = 1e-6, 64e-5


class _Rec:
    def __init__(self):
        self.call = None

    def __getattr__(self, name):
        def f(*a, **k):
            self.call = (name, a, k)
            return self
        return f


class Prog:
    def __init__(self, nc):
        self.nc = nc
        self.root = contextlib.ExitStack()
        self.stacks = [self.root]
        self.ops = {e: [] for e in ENGS}
        self.cnt = {e: 0 for e in ENGS}
        self.seen = {e: {} for e in ENGS}
        self.last_w = {}
        self.readers = {}
        self.esem = {e: self.root.enter_context(nc.semaphore('s_' + e)) for e in ENGS if e != 'sp'}
        self.dsem = {}
        self.fence = []
        self.ntile = 0

    def sb(self, shape, dt=F32, name=None):
        self.ntile += 1
        return self.stacks[-1].enter_context(self.nc.sbuf_tensor(f'{name or "t"}{self.ntile}', list(shape), dt))

    def sbm(self, shape, dt=F32, name=None):
        self.ntile += 1
        st = contextlib.ExitStack()
        t = st.enter_context(self.nc.sbuf_tensor(f'{name or "t"}{self.ntile}', list(shape), dt))
        return t, st

    def _set_fence(self):
        self.fence = [('E', e, self.cnt[e]) for e in self.esem if self.cnt[e] > 0]
        self.fence += [('D', s_, v) for s_, v in self.dsem.values() if v > 0]

    def free(self, stacks):
        for st in stacks:
            st.close()
        self._set_fence()

    def ps(self, shape, dt=F32, name=None):
        self.ntile += 1
        return self.root.enter_context(self.nc.psum_tensor(f'{name or "p"}{self.ntile}', list(shape), dt))

    @contextlib.contextmanager
    def scope(self):
        st = contextlib.ExitStack()
        self.stacks.append(st)
        try:
            yield
        finally:
            self.stacks.pop()
            st.close()
            self._set_fence()

    def _k(self, k):
        if isinstance(k, tuple):
            return tuple(self._k(x) for x in k)
        if isinstance(k, (str, int)):
            return k
        return id(k)

    @staticmethod
    def _tkey(tok):
        return ('E', tok[1]) if tok[0] == 'E' else ('D', id(tok[1]))

    def _deps(self, eng, r, w):
        deps = {}

        def add(tok):
            if tok is None:
                return
            if tok[0] == 'E' and tok[1] == eng == 'pe':
                return
            k = self._tkey(tok)
            if k not in deps or deps[k][2] < tok[2]:
                deps[k] = tok
        for tok in self.fence:
            add(tok)
        for k in r:
            add(self.last_w.get(k))
        for k in w:
            add(self.last_w.get(k))
            for t in self.readers.get(k, ()):
                add(t)
        out = []
        seen = self.seen[eng]
        for k, tok in deps.items():
            if seen.get(k, 0) >= tok[2]:
                continue
            seen[k] = tok[2]
            out.append(tok)
        return out

    def _commit(self, tok, r, w):
        for k in w:
            self.last_w[k] = tok
            self.readers[k] = []
        for k in r:
            if k in w:
                continue
            self.readers.setdefault(k, []).append(tok)

    def op(self, eng, fn, r=(), w=()):
        r = [self._k(k) for k in r]
        w = [self._k(k) for k in w]
        waits = self._deps(eng, r, w)
        self.cnt[eng] += 1
        tok = ('E', eng, self.cnt[eng])
        rec = _Rec()
        fn(rec)
        name, a, k = rec.call
        self.ops[eng].append((waits, lambda e: getattr(e, name)(*a, **k), tok))
        self._commit(tok, r, w)

    def dma(self, q, out, in_, r=(), w=(), group=None, **kw):
        r = [self._k(k) for k in r]
        w = [self._k(k) for k in w]
        waits = self._deps(q, r, w)
        g = group or ('dma_' + str(w[0] if w else 'x'))
        if g not in self.dsem:
            self.dsem[g] = [self.root.enter_context(self.nc.semaphore('d%d' % len(self.dsem))), 0]
        ent = self.dsem[g]
        ent[1] += 16
        tok = ('D', ent[0], ent[1])
        self.ops[q].append((waits, lambda e: e.dma_start(out=out, in_=in_, **kw), tok))
        self._commit(tok, r, w)

    def wait_group(self, eng, group):
        ent = self.dsem[group]
        self.ops[eng].append(([('D', ent[0], ent[1])], None, None))

    def emit(self):
        engobj = {'pe': 'tensor', 'act': 'scalar', 'dve': 'vector', 'pool': 'gpsimd', 'sp': 'sync'}
        waited = {e: set() for e in ENGS}
        for e in ENGS:
            for waits, fn, tok in self.ops[e]:
                for t in waits:
                    if t[0] == 'E':
                        waited[t[1]].add(t[2])
        rank = {e: {s_: i + 1 for i, s_ in enumerate(sorted(waited[e]))} for e in ENGS}
        with self.nc.Block() as block:
            for e in ENGS:
                ops = self.ops[e]

                def body(eng, ops=ops):
                    for waits, fn, tok in ops:
                        for t in waits:
                            if t[0] == 'E':
                                eng.wait_ge(self.esem[t[1]], rank[t[1]][t[2]])
                            else:
                                eng.wait_ge(t[1], t[2])
                        if fn is not None:
                            ins = fn(eng)
                            if tok[0] == 'D':
                                ins.then_inc(tok[1], 16)
                            elif tok[2] in rank[tok[1]]:
                                ins.then_inc(self.esem[tok[1]], 1)
                getattr(block, engobj[e])(body)


def rsl(start, n, step):
    if step > 0:
        return slice(start, start + n)
    stop = start - n
    return slice(start, stop if stop >= 0 else None, -1)


class Builder:
    def __init__(self, taps=(), stop_after=None):
        self.taps = set(taps)
        self.stop_after = stop_after
        nc = self.nc = bass.Bass("TRN2", target_bir_lowering=False)
        self.P = Prog(nc)
        self.din = {}
        self.tapout = {}
        self._castn = 0

    def inp(self, name, shape):
        self.din[name] = self.nc.dram_tensor(name, list(shape), F32, kind="ExternalInput").ap()
        return self.din[name]

    def mm(self, out, lhsT, rhs, start=True, stop=True, r=(), w=(), **kw):
        self.P.op('pe', lambda e: e.matmul(out, lhsT=lhsT, rhs=rhs, start=start, stop=stop, **kw), r=r, w=w)

    def act(self, out, in_, func, r=(), w=(), **kw):
        self.P.op('act', lambda e: e.activation(out=out, in_=in_, func=func, **kw), r=r, w=w)

    def tap(self, name, tile_ap, shape, r, dt=F32):
        if name not in self.taps:
            return
        o = self.nc.dram_tensor('tap_' + name, list(shape), dt, kind="ExternalOutput").ap()
        self.P.dma('pool', o, tile_ap, r=r, group='out_' + name)
        self.tapout[name] = o

    def cols(self, rows, n, chunk, name):
        P = self.P
        R = len(rows)
        nch = (n + chunk - 1) // chunk
        out = P.sb([chunk, nch, R], name=name)
        with P.scope():
            st = P.sb([R, n], name='colst')
            for i, rw in enumerate(rows):
                P.dma('sp', st[i:i + 1, :], rw.rearrange("(o n) -> o n", o=1), w=[(st, i)], group='colst')
            pp = self.pb[0]
            assert nch * R <= 512
            for c in range(nch):
                cs = min(chunk, n - c * chunk)
                P.op('pe', lambda e, c=c, cs=cs: e.transpose(out=pp[0:cs, c * R:(c + 1) * R], in_=st[0:R, c * chunk:c * chunk + cs],
                                                             identity=self.ident[0:R, 0:R]),
                     r=[(st, i) for i in range(R)] + [self.ident], w=[pp])
            P.op('dve', lambda e: e.tensor_copy(out=out[:].rearrange("p c r -> p (c r)"), in_=pp[0:chunk, 0:nch * R]), r=[pp], w=[out])
        return out

    def build(self):
        nc, P = self.nc, self.P
        inp = self.inp
        xT = inp('xT', [D, NLAT]); ctxT = inp('ctxT', [D, NCTX]); cvec = inp('cvec', [2, D])
        w_mod = inp('w_mod', [D, 6 * D]); b_mod = inp('b_mod', [6 * D])
        nmg = inp('norm_mix_g', [D]); nfg = inp('norm_ffn_g', [D]); nfin = inp('norm_final_g', [D])
        w_in = inp('w_in', [D, 8096])
        lru_conv_w = inp('lru_conv_w', [2, 4, LW]); lru_conv_b = inp('lru_conv_b', [2, LW])
        lru_wa = inp('lru_wa', [2, NBLK, BLK, BLK]); lru_ba = inp('lru_ba', [2, LW])
        lru_wx = inp('lru_wx', [2, NBLK, BLK, BLK]); lru_bx = inp('lru_bx', [2, LW])
        lru_lam = inp('lru_lambda', [2, LW]); w_o_lru = inp('w_o_lru', [LW, D])
        mu = inp('rwkv_mu', [2, RIN]); w0 = inp('rwkv_w0', [2, D]); w2 = inp('rwkv_w2', [2, 64, D])
        a0 = inp('rwkv_a0', [2, D]); a2 = inp('rwkv_a2', [2, 64, D]); g2 = inp('rwkv_g2', [160, D])
        k_k = inp('rwkv_k_k', [D]); k_a = inp('rwkv_k_a', [D]); r_k = inp('rwkv_r_k', [D])
        ln_g = inp('rwkv_ln_g', [D]); ln_b = inp('rwkv_ln_b', [D])
        w_o_rwkv = inp('w_o_rwkv', [D, D]); w_out = inp('w_out', [D, D])
        w_ffn_in = inp('w_ffn_in', [D, 2 * DFF]); w_ffn_out = inp('w_ffn_out', [DFF, D])
        outT = nc.dram_tensor('outT', [D, NLAT], F32, kind="ExternalOutput").ap()

        self.pb = [P.ps([128, 512], F32, name='pb') for _ in range(7)]
        self.pbh = P.ps([128, 1024], BF16, name='pbh')
        pb = self.pb

        ones = P.sb([128, 128], name='ones')
        P.op('dve', lambda e: e.memset(ones[:], 1.0), w=[ones])
        self.ident = ident = P.sb([128, 128], name='ident')
        P.op('pool', lambda e: e.affine_select(out=ident[:], in_=ones[:], pattern=[[-1, 128]], compare_op=ALU.is_equal, fill=0.0,
                                               base=0, channel_multiplier=1), r=[ones], w=[ident])
        identb = P.sb([128, 128], BF16, name='identb')
        P.op('dve', lambda e: e.tensor_copy(out=identb[:], in_=ident[:]), r=[ident], w=[identb])
        bones = P.sb([128, 128], name='bones')
        P.op('dve', lambda e: e.memset(bones[:], 0.0), w=[bones])
        P.op('dve', lambda e: e.memset(bones[0:64, 0:64], 1.0), w=[bones])
        P.op('dve', lambda e: e.memset(bones[64:128, 64:128], 1.0), w=[bones])
        self.ones, self.identb, self.bones = ones, identb, bones

        gains = self.cols([nmg, nfg, nfin], D, 128, 'gains')
        cT = self.cols([cvec[0], cvec[1]], D, 128, 'cT')
        bm = self.cols([b_mod], 6 * D, 128, 'bm')
        mod = P.sb([128, 48, 2], name='mod')
        with P.scope():
            sc = P.sb([128, 8, 2], name='sc')
            self.act(sc[:], cT[:], AF.Silu, r=[cT], w=[sc])
            wm = [P.sb([128, 8, 768], name='wm') for _ in range(2)]
            pm = pb[1]
            wv = w_mod.rearrange("(k p) n -> p k n", p=128)
            for jb in range(8):
                buf = wm[jb % 2]
                for k2 in range(2):
                    P.dma('sp' if k2 == 0 else 'act', buf[:, 4 * k2:4 * k2 + 4, :], wv[:, 4 * k2:4 * k2 + 4, jb * 768:(jb + 1) * 768],
                          w=[(buf, k2)], group='wm%d' % (jb % 2))
                for jj in range(6):
                    j = jb * 6 + jj
                    for k in range(8):
                        self.mm(pm[:, 2 * j:2 * j + 2], buf[:, k, jj * 128:(jj + 1) * 128], sc[:, k, :], start=(k == 0), stop=(k == 7),
                                r=[(buf, 0), (buf, 1), sc], w=[pm])
            for n in range(2):
                P.op('dve', lambda e, n=n: e.tensor_tensor(out=mod[:, :, n], in0=pm[:, 0:96].rearrange("p (j n) -> p j n", n=2)[:, :, n],
                                                           in1=bm[:, :, 0], op=ALU.add), r=[pm, bm], w=[mod])
        self.tap('mod', mod[:], [128, 48, 2], [mod])
        G1 = P.sb([128, 8, 2], name='G1'); G2 = P.sb([128, 8, 1], name='G2')
        for n in range(2):
            P.op('dve', lambda e, n=n: e.scalar_tensor_tensor(out=G1[:, :, n], in0=mod[:, 8:16, n], scalar=1.0, in1=gains[:, :, 0],
                                                              op0=ALU.add, op1=ALU.mult), r=[mod, gains], w=[G1])
        P.op('dve', lambda e: e.scalar_tensor_tensor(out=G2[:, :, 0], in0=mod[:, 32:40, 0], scalar=1.0, in1=gains[:, :, 1],
                                                     op0=ALU.add, op1=ALU.mult), r=[mod, gains], w=[G2])
        self.mod, self.gains = mod, gains

        arena = P.sb([128, 8 * T + 8 * NLAT], BF16, name='arena')
        hT = arena[:, 0:8 * T].rearrange("p (k t) -> p k t", t=T)
        xv = xT.rearrange("(k p) t -> p k t", p=128)
        cv = ctxT.rearrange("(k p) t -> p k t", p=128)
        self.modulate(hT, [(cv, 0, 256, 0, 1)] + [(xv, 512 * i, 512, 256 + 512 * i, 0) for i in range(4)], G1, mod, 0)
        self.tap('hT', hT[:], [128, 8, T], [(hT, i) for i in range(5)], dt=BF16)
        self.hT = hT
        if self.stop_after == 'B':
            return self.finish()
        self.rwT = arena[:, 8 * T:8 * T + 8 * NLAT].rearrange("p (k t) -> p k t", t=NLAT)
        if self.stop_after != 'C':
            with P.scope():
                self.rwkv()
        self.tap('rw', self.rwT[:], [128, 8, NLAT], [self.rwT], dt=BF16)
        if self.stop_after in ('D', 'D0'):
            return self.finish()
        self.lruT, lru_st = P.sbm([128, 10, NLAT], BF16, name='lruT')
        with P.scope():
            self.lru()
        self.tap('lru', self.lruT[:], [128, 10, NLAT], [self.lruT], dt=BF16)
        if self.stop_after == 'C':
            return self.finish()
        self.mT, m_st = P.sbm([128, 8, NLAT], BF16, name='mT')
        with P.scope():
            self.merge()
        P.free([])
        self.x1T = arena[:].bitcast(F32)[:, 0:8 * NLAT].rearrange("p (k t) -> p k t", t=NLAT)
        with P.scope():
            self.resid1()
        P.free([m_st, lru_st])
        self.tap('x1', self.x1T[:], [128, 8, NLAT], [self.x1T])
        if self.stop_after == 'E':
            return self.finish()
        self.h2T = P.sb([128, 8, NLAT], BF16, name='h2T')
        self.modulate(self.h2T, [(None, 512 * i, 512, 512 * i, 0) for i in range(4)], G2, mod, 24, src_sb=self.x1T)
        with P.scope():
            self.ffn()
        with P.scope():
            self.final(outT)
        return self.finish()

    def modulate(self, hT, blocks, G, mod, shift_j0, src_sb=None):
        P, pb = self.P, self.pb
        with P.scope():
            xb = [P.sb([128, 8, 512], name='xb') for _ in range(2)]
            sq = P.sb([128, 8, 512], name='sq')
            rs = P.sb([128, 512], name='rs')
            epst = P.sb([128, 1], name='epst')
            P.op('dve', lambda e: e.memset(epst[:], RMS_EPS), w=[epst])
            for bi, (src, so, n, do, mn) in enumerate(blocks):
                if src_sb is None:
                    x = xb[bi % 2]
                    for k2 in range(2):
                        P.dma('sp' if k2 == 0 else 'act', x[:, 4 * k2:4 * k2 + 4, 0:n], src[:, 4 * k2:4 * k2 + 4, so:so + n],
                              w=[(x, k2)], group='xb%d' % (bi % 2))
                    xr = [(x, 0), (x, 1)]
                    xa = lambda k, x=x, n=n: x[:, k, 0:n]
                    xall = x[:, :, 0:n]
                else:
                    xr = [src_sb]
                    xa = lambda k, so=so, n=n: src_sb[:, k, so:so + n]
                    xall = src_sb[:, :, so:so + n]
                self.act(sq[:, :, 0:n], xall, AF.Square, r=xr, w=[sq])
                pp = pb[bi % 2]
                for k in range(8):
                    self.mm(pp[:, 0:n], self.ones[:], sq[:, k, 0:n], start=(k == 0), stop=(k == 7), r=[sq, self.ones], w=[pp])
                self.act(rs[:, 0:n], pp[:, 0:n], AF.Sqrt, scale=1.0 / D, bias=epst[:], r=[pp, epst], w=[rs])
                P.op('dve', lambda e, n=n: e.reciprocal(out=rs[:, 0:n], in_=rs[:, 0:n]), r=[rs], w=[rs])
                for k in range(8):
                    P.op('dve', lambda e, k=k, n=n, xa=xa: e.tensor_tensor(out=sq[:, k, 0:n], in0=xa(k), in1=rs[:, 0:n], op=ALU.mult),
                         r=xr + [rs], w=[sq])
                    self.act(hT[:, k, do:do + n], sq[:, k, 0:n], AF.Identity, scale=G[:, k, mn:mn + 1], bias=mod[:, shift_j0 + k, mn:mn + 1],
                             r=[sq, G, mod], w=[(hT, bi)])

    def zshift(self, cq, ncol, dsts, zbuf, A, wz, wzb, mixw, segs=((0, 1, 256, 0), (1, 258, 2048, 256))):
        P, pb, hT = self.P, self.pb, self.hT
        w_in = self.din['w_in']
        i = self.zcount = getattr(self, 'zcount', 0) + 1
        wf, wb = wz[i % 2], wzb[i % 2]
        if isinstance(zbuf, list):
            zbuf = zbuf[i % len(zbuf)]
        if isinstance(A, list):
            A = A[i % len(A)]
        c0 = 2560 + 128 * cq
        P.dma('sp', wf[:, :, 0:ncol], w_in.rearrange("(k p) n -> p k n", p=128)[:, :, c0:c0 + ncol], w=[wf], group='wz%d' % (i % 2))
        P.op('pool', lambda e: e.tensor_copy(out=wb[:, :, 0:ncol], in_=wf[:, :, 0:ncol]), r=[wf], w=[wb])
        hk = [(hT, j) for j in range(5)]
        lat = lambda k: hT[:, k, 256:2304].rearrange("p (r c) -> p c r", c=64)
        nblk = 0
        for (seg, zc, n, _) in segs:
            nb = 1 if seg == 0 else 4
            for bi in range(nb):
                pp = pb[(nblk + 5 * i) % 6]; nblk += 1
                bn = 256 if seg == 0 else 512
                for k in range(8):
                    if seg == 0:
                        self.mm(pp[0:ncol, 0:256], wb[:, k, 0:ncol], hT[:, k, 0:256], start=(k == 0), stop=(k == 7), r=[wb] + hk, w=[pp])
                    else:
                        self.mm(pp[0:ncol, 0:512], wb[:, k, 0:ncol], hT[:, k, 256 + 512 * bi:256 + 512 * bi + 512],
                                start=(k == 0), stop=(k == 7), r=[wb] + hk, w=[pp])
                if seg == 0:
                    P.op('act', lambda e: e.copy(out=zbuf[0:ncol, zc:zc + 256], in_=pp[0:ncol, 0:256]), r=[pp], w=[zbuf])
                else:
                    zo = zbuf[0:ncol, zc:zc + 2048].rearrange("p (c r) -> p r c", r=32)[:, 8 * bi:8 * bi + 8, :]
                    P.op('act', lambda e: e.copy(out=zo, in_=pp[0:ncol, 0:512].rearrange("p (r c) -> p r c", c=64)), r=[pp], w=[zbuf])
        for (seg, zc, n, _), dst in zip(segs, dsts):
            if dst is None:
                continue
            dt, do, key = dst
            P.op('dve', lambda e, zc=zc, n=n: e.tensor_scalar(out=A[0:ncol, 0:n], in0=zbuf[0:ncol, zc:zc + n], scalar1=mixw[0:ncol, cq, 2:3],
                                                              scalar2=None, op0=ALU.mult), r=[zbuf, mixw], w=[A])
            P.op('dve', lambda e, zc=zc, n=n: e.scalar_tensor_tensor(out=A[0:ncol, 0:n], in0=zbuf[0:ncol, zc - 1:zc - 1 + n], scalar=mixw[0:ncol, cq, 0:1],
                                                                     in1=A[0:ncol, 0:n], op0=ALU.mult, op1=ALU.add), r=[zbuf, mixw, A], w=[A])
            P.op('dve', lambda e, zc=zc, n=n, dt=dt, do=do: e.scalar_tensor_tensor(out=dt[0:ncol, do:do + n], in0=zbuf[0:ncol, zc + 1:zc + 1 + n],
                                                                                   scalar=mixw[0:ncol, cq, 1:2], in1=A[0:ncol, 0:n],
                                                                                   op0=ALU.mult, op1=ALU.add), r=[zbuf, mixw, A], w=[key])

    def rwkv(self):
        P, pb, pbh, hT, din = self.P, self.pb, self.pbh, self.hT, self.din
        rwT = self.rwT
        CW = -0.5 * float(np.exp(-0.5))
        mixw = self.cols([din['rwkv_mu'][0], din['rwkv_mu'][1], din['rwkv_mu'][0]], RIN, 128, 'mixw')
        P.op('dve', lambda e: e.tensor_tensor(out=mixw[:, :, 2], in0=mixw[:, :, 0], in1=mixw[:, :, 1], op=ALU.add), r=[mixw], w=[mixw])
        P.op('dve', lambda e: e.tensor_scalar(out=mixw[:, :, 2], in0=mixw[:, :, 2], scalar1=-1.0, scalar2=1.0, op0=ALU.mult, op1=ALU.add), r=[mixw], w=[mixw])
        chp = self.cols([din['rwkv_w0'][0], din['rwkv_w0'][1], din['rwkv_a0'][0], din['rwkv_a0'][1], din['rwkv_k_k'], din['rwkv_k_a'],
                         din['rwkv_r_k'], din['rwkv_ln_g'], din['rwkv_ln_b']], D, 128, 'chp')
        hp2 = P.sb([128, 8, 6], name='hp2')
        P.op('dve', lambda e: e.tensor_scalar(out=hp2[:, :, 0:4], in0=chp[:, :, 0:4], scalar1=0.5, scalar2=None, op0=ALU.mult), r=[chp], w=[hp2])
        P.op('dve', lambda e: e.tensor_scalar(out=hp2[:, :, 4:5], in0=chp[:, :, 5:6], scalar1=0.5, scalar2=None, op0=ALU.mult), r=[chp], w=[hp2])
        P.op('dve', lambda e: e.tensor_scalar(out=hp2[:, :, 5:6], in0=chp[:, :, 5:6], scalar1=-0.5, scalar2=1.0, op0=ALU.mult, op1=ALU.add), r=[chp], w=[hp2])
        gneps = P.sb([128, 1], name='gneps')
        P.op('dve', lambda e: e.memset(gneps[:], GN_EPS), w=[gneps])
        w2s = P.sb([128, D], BF16, name='w2s'); a2s = P.sb([128, D], BF16, name='a2s')
        g2a = P.sb([128, D], BF16, name='g2a'); g2b = P.sb([32, D], BF16, name='g2b')
        with P.scope():
            st = P.sb([128, D], name='lst')
            for src, dst, npart in ((din['rwkv_w2'].rearrange("d r c -> (d r) c"), w2s, 128), (din['rwkv_a2'].rearrange("d r c -> (d r) c"), a2s, 128),
                                    (din['rwkv_g2'][0:128, :], g2a, 128), (din['rwkv_g2'][128:160, :], g2b, 32)):
                P.dma('sp', st[0:npart, :], src, w=[st], group='lst')
                P.op('dve', lambda e, dst=dst, npart=npart: e.tensor_copy(out=dst[0:npart, :], in_=st[0:npart, :]), r=[st], w=[dst])
        msk = {}
        onesb = P.sb([128, 4, 64], BF16, name='onesb')
        P.op('dve', lambda e: e.memset(onesb[:], 1.0), w=[onesb])
        for nm, op, sgn in (('su', ALU.is_gt, -1), ('sl', ALU.is_gt, 1), ('iu', ALU.is_ge, -1), ('id', ALU.is_equal, 1)):
            m = P.sb([128, 4, 64], BF16, name='m' + nm)
            for e_ in range(2):
                P.op('pool', lambda e, m=m, op=op, sgn=sgn, e_=e_: e.affine_select(out=m[64 * e_:64 * e_ + 64], in_=onesb[64 * e_:64 * e_ + 64],
                                                                                   pattern=[[0, 4], [-sgn, 64]], compare_op=op, fill=0.0,
                                                                                   base=0, channel_multiplier=sgn), r=[onesb], w=[m])
            msk[nm] = m
        cmask = P.sb([128, 256], name='cmask')
        P.op('dve', lambda e: e.memset(cmask[:], 1.0), w=[cmask])
        P.op('dve', lambda e: e.memset(cmask[:, 0:256:64], 0.0), w=[cmask])
        twd = P.sb([128, T], BF16, name='twd'); adb = P.sb([128, T], BF16, name='adb')
        sgd1 = P.sb([128, NLAT], BF16, name='sgd1'); sgd2 = P.sb([32, NLAT], BF16, name='sgd2')

        with P.scope():
            zbuf = [P.sb([128, 2307], name='zbuf') for _ in range(2)]; A = [P.sb([128, 2048], name='zA') for _ in range(2)]
            wz = [P.sb([128, 8, 128], name='wz') for _ in range(2)]; wzb = [P.sb([128, 8, 128], BF16, name='wzb') for _ in range(2)]
            for zb in zbuf:
                P.op('pool', lambda e, zb=zb: e.memset(zb[:], 0.0), w=[zb])
            tmp = P.sb([128, T], name='ltmp')
            self.zshift(24, 128, [(tmp, 0, tmp), (tmp, 256, tmp)], zbuf, A, wz, wzb, mixw)
            self.act(twd[:], tmp[:], AF.Tanh, r=[tmp], w=[twd])
            self.zshift(25, 128, [(adb, 0, adb), (adb, 256, adb)], zbuf, A, wz, wzb, mixw)
            for cq, ncol, dst in ((26, 128, sgd1), (27, 32, sgd2)):
                self.zshift(cq, ncol, [None, (tmp, 256, tmp)], zbuf, A, wz, wzb, mixw)
                self.act(tmp[0:ncol, 256:T], tmp[0:ncol, 256:T], AF.Tanh, scale=0.5, r=[tmp], w=[tmp])
                P.op('dve', lambda e, dst=dst, ncol=ncol: e.tensor_scalar(out=dst[0:ncol, :], in0=tmp[0:ncol, 256:T], scalar1=0.5, scalar2=0.5,
                                                                          op0=ALU.mult, op1=ALU.add), r=[tmp], w=[dst])

        self.tap('twd', twd[:], [128, T], [twd], dt=BF16)
        self.tap('adb', adb[:], [128, T], [adb], dt=BF16)
        self.tap('sgd1', sgd1[:], [128, NLAT], [sgd1], dt=BF16)
        for hp in range(8):
            if self.stop_after == 'D0' and hp > 0:
                break
            with P.scope():
                rb = P.sb([128, T], BF16, name='rb'); kb = P.sb([128, T], BF16, name='kb'); vb = P.sb([128, T], BF16, name='vb')
                kkb = P.sb([128, T], BF16, name='kkb')
                y0 = P.sb([128, NLAT], name='y0'); bacc = P.sb([128, NLAT], name='bacc')
                with P.scope():
                    zbufs = [P.sb([128, 2307], name='zbuf') for _ in range(3)]; As = [P.sb([128, 2048], name='zA') for _ in range(2)]
                    wz = [P.sb([128, 8, 128], name='wz') for _ in range(2)]; wzb = [P.sb([128, 8, 128], BF16, name='wzb') for _ in range(2)]
                    for zb in zbufs:
                        P.op('pool', lambda e, zb=zb: e.memset(zb[:], 0.0), w=[zb])
                    for cq, dst in ((8 + hp, kb), (hp, rb), (16 + hp, vb)):
                        self.zshift(cq, 128, [(dst, 0, dst), (dst, 256, dst)], zbufs, As, wz, wzb, mixw)
                    kq = P.sb([128, T], name='kq'); A = P.sb([128, 1024], name='kA')
                    self.act(kq[:, 0:T], kb[:], AF.Identity, scale=chp[:, hp, 4:5], r=[kb, chp], w=[kq])
                    for bi, (o, n) in enumerate([(0, 512), (512, 512), (1024, 512), (1536, 512), (2048, 256)]):
                        P.op('dve', lambda e, o=o, n=n: e.tensor_tensor(out=A[:, 0:n], in0=kq[:, o:o + n], in1=kq[:, o:o + n], op=ALU.mult), r=[kq], w=[A])
                        pp = pb[bi % 5]
                        self.mm(pp[:, 0:n], self.bones[:], A[:, 0:n], r=[A, self.bones], w=[pp])
                        self.act(A[:, 512:512 + n], pp[:, 0:n], AF.Sqrt, r=[pp], w=[A])
                        P.op('dve', lambda e, n=n: e.tensor_scalar(out=A[:, 512:512 + n], in0=A[:, 512:512 + n], scalar1=1e-12, scalar2=None, op0=ALU.max), r=[A], w=[A])
                        P.op('dve', lambda e, n=n: e.reciprocal(out=A[:, 512:512 + n], in_=A[:, 512:512 + n]), r=[A], w=[A])
                        P.op('dve', lambda e, o=o, n=n: e.tensor_tensor(out=kkb[:, o:o + n], in0=kq[:, o:o + n], in1=A[:, 512:512 + n], op=ALU.mult), r=[A, kq], w=[kkb])
                if hp == 0:
                    for nm, t_ in (('rb', rb), ('kb', kb), ('vb', vb), ('kkb', kkb)):
                        self.tap(nm, t_[:], [128, T], [t_], dt=BF16)
                for j in range(8):
                    P.op('pool', lambda e: e.memset(y0[:, 256 * j:256 * j + 256], 0.0), w=[(y0, 256 * j)])
                    P.op('pool', lambda e: e.memset(bacc[:, 256 * j:256 * j + 256], 0.0), w=[(bacc, 256 * j)])
                C = dict(rb=rb, kb=kb, vb=vb, kkb=kkb, y0=y0, bacc=bacc, twd=twd, adb=adb, w2s=w2s, a2s=a2s, chp=chp, hp2=hp2, msk=msk,
                         cmask=cmask, CW=CW, rot=[0])
                with P.scope():
                    self.rw_rounds(hp, C)
                if hp == 0:
                    self.tap('y0', y0[:], [128, NLAT], [y0])
                    self.tap('bacc', bacc[:], [128, NLAT], [bacc])
                with P.scope():
                    yc = P.sb([128, 512], name='yc'); sq = P.sb([128, 512], name='sq2'); rstd = P.sb([128, 512], name='rstd'); tt = P.sb([128, 512], name='tt')
                    for i in range(4):
                        c0 = 512 * i
                        pm, pv, pg = pb[0], pb[1], pb[2]
                        self.mm(pm[:], self.bones[:], y0[:, c0:c0 + 512], r=[y0, self.bones], w=[pm])
                        P.op('dve', lambda e, c0=c0: e.scalar_tensor_tensor(out=yc[:], in0=pm[:], scalar=-1.0 / 64, in1=y0[:, c0:c0 + 512], op0=ALU.mult, op1=ALU.add),
                             r=[pm, y0], w=[yc])
                        P.op('dve', lambda e: e.tensor_tensor(out=sq[:], in0=yc[:], in1=yc[:], op=ALU.mult), r=[yc], w=[sq])
                        self.mm(pv[:], self.bones[:], sq[:], r=[sq, self.bones], w=[pv])
                        self.act(rstd[:], pv[:], AF.Sqrt, scale=1.0 / 64, bias=gneps[:], r=[pv, gneps], w=[rstd])
                        P.op('dve', lambda e: e.reciprocal(out=rstd[:], in_=rstd[:]), r=[rstd], w=[rstd])
                        P.op('dve', lambda e: e.tensor_tensor(out=yc[:], in0=yc[:], in1=rstd[:], op=ALU.mult), r=[yc, rstd], w=[yc])
                        self.act(yc[:], yc[:], AF.Identity, scale=chp[:, hp, 7:8], bias=chp[:, hp, 8:9], r=[yc, chp], w=[yc])
                        P.op('dve', lambda e, c0=c0: e.tensor_tensor(out=tt[:], in0=vb[:, 256 + c0:256 + c0 + 512], in1=bacc[:, c0:c0 + 512], op=ALU.mult), r=[vb, bacc], w=[tt])
                        P.op('dve', lambda e: e.tensor_tensor(out=tt[:], in0=tt[:], in1=yc[:], op=ALU.add), r=[tt, yc], w=[tt])
                        self.mm(pg[:], g2a[:, hp * 128:(hp + 1) * 128], sgd1[:, c0:c0 + 512], start=True, stop=False, r=[g2a, sgd1], w=[pg])
                        self.mm(pg[:], g2b[0:32, hp * 128:(hp + 1) * 128], sgd2[0:32, c0:c0 + 512], start=False, stop=True, r=[g2b, sgd2], w=[pg])
                        P.op('dve', lambda e, i=i: e.tensor_tensor(out=rwT[:, hp, :].rearrange("p (r c) -> p c r", c=64)[:, 16 * i:16 * i + 16, :],
                                                                   in0=tt[:].rearrange("p (c r) -> p c r", r=32), in1=pg[:].rearrange("p (c r) -> p c r", r=32),
                                                                   op=ALU.mult), r=[tt, pg], w=[rwT])

    def rw_alloc_dir(self):
        P = self.P
        S = {}
        S['A1'] = P.sb([128, 256], name='A1'); S['B1'] = P.sb([128, 256], name='B1'); S['C1'] = P.sb([128, 256], name='C1')
        S['s1'] = [{nm: P.sb([128, 256], BF16, name=nm) for nm in ('at', 'bt', 'kt', 'vs')} for _ in range(2)]
        S['rw'] = [{'rt': P.sb([128, 256], BF16, name='rt'), 'wc': P.sb([128, 4], name='wc')} for _ in range(4)]
        S['QX'] = [[P.sb([128, 4, 2, 64], BF16, name='QX') for _ in range(2)] for _ in range(2)]
        S['QT'] = [[P.sb([128, 4, 64], BF16, name='QT') for _ in range(2)] for _ in range(2)]
        S['AakT'] = [P.sb([128, 4, 64], BF16, name='AakT') for _ in range(2)]
        S['slots'] = []
        for _ in range(3):
            sl = {'tokT': P.sb([128, 4, 4, 64], BF16, name='tokT'), 'MT': P.sb([128, 4, 64], BF16, name='MT'), 'Xak': P.sb([128, 4, 64], BF16, name='Xak'),
                  'ArbT': P.sb([128, 4, 64], BF16, name='ArbT'), 'ArkT': P.sb([128, 4, 64], BF16, name='ArkT'), 'AhT': P.sb([128, 4, 64], BF16, name='AhT')}
            S['slots'].append(sl)
        S['Tst'] = P.sb([128, 64], name='Tst'); S['Tw'] = P.sb([128, 64], name='Tw'); S['Tb'] = P.sb([128, 64], BF16, name='Tb')
        S['Ub'] = P.sb([128, 64], BF16, name='Ub')
        for nm in ('Tst', 'Tw', 'Tb'):
            P.op('dve', lambda e, t=S[nm]: e.memset(t[:], 0.0), w=[S[nm]])
        return S

    def gen_S1(self, hp, d, g, S, C):
        P, pb, pbh = self.P, self.pb, self.pbh
        rb, kb, vb, kkb, bacc = C['rb'], C['kb'], C['vb'], C['kkb'], C['bacc']
        twd, adb, w2s, a2s, chp, hp2, msk, cmask, CW = (C[k] for k in ('twd', 'adb', 'w2s', 'a2s', 'chp', 'hp2', 'msk', 'cmask', 'CW'))
        A1, B1, C1 = (S[k] for k in ('A1', 'B1', 'C1'))
        at, bt, kt, vs = (S['s1'][g % 2][k] for k in ('at', 'bt', 'kt', 'vs'))
        rt, wc = S['rw'][g % 4]['rt'], S['rw'][g % 4]['wc']
        if d == 0:
            s0, step = 256 * g, 1
        else:
            s0, step = (0 if g == 0 else 2304 - 256 * g), -1
        nat = slice(s0, s0 + 256)
        loc = lambda t: t[:, rsl(0 if step > 0 else 255, 256, step)]
        hc = slice(hp * 128, (hp + 1) * 128); ds = slice(64 * d, 64 * d + 64)
        rot = C['rot']
        H = lambda e_: slice(64 * e_, 64 * e_ + 64)
        TP = lambda e_: (64 * e_, 64 * e_)

        def bank():
            rot[0] = (rot[0] + 1) % 4
            return pb[(0, 1, 2, 5)[rot[0]]]
        pp = bank()
        self.mm(pp[:, 0:256], w2s[ds, hc], twd[ds, nat], r=[w2s, twd], w=[pp])
        self.act(loc(A1), pp[:, 0:256], AF.Tanh, scale=0.5, bias=hp2[:, hp, d:d + 1], r=[pp, hp2], w=[A1])
        yield
        P.op('dve', lambda e: e.tensor_scalar(out=A1[:], in0=A1[:], scalar1=1.0, scalar2=CW, op0=ALU.add, op1=ALU.mult), r=[A1], w=[A1])
        P.op('dve', lambda e: e.tensor_tensor_scan(out=B1[:], data0=cmask[:, 0:256], data1=A1[:], initial=0.0, op0=ALU.mult, op1=ALU.add), r=[A1, cmask], w=[B1])
        P.op('dve', lambda e: e.tensor_tensor(out=A1[:], in0=B1[:], in1=A1[:], op=ALU.subtract), r=[A1, B1], w=[A1])
        yield
        self.act(C1[:], A1[:], AF.Exp, r=[A1], w=[C1])
        P.op('dve', lambda e: e.scalar_tensor_tensor(out=loc(at), in0=kkb[:, nat], scalar=-1.0, in1=loc(C1), op0=ALU.mult, op1=ALU.mult), r=[kkb, C1], w=[at])
        yield
        self.act(C1[:], B1[:], AF.Exp, r=[B1], w=[C1])
        P.op('dve', lambda e: e.tensor_copy(out=wc[:], in_=C1[:, 63:256:64]), r=[C1], w=[wc])
        P.op('dve', lambda e: e.tensor_tensor(out=loc(rt), in0=rb[:, nat], in1=loc(C1), op=ALU.mult), r=[rb, C1], w=[rt])
        yield
        self.act(C1[:], B1[:], AF.Exp, scale=-1.0, r=[B1], w=[C1])
        pp = bank()
        self.mm(pp[:, 0:256], a2s[ds, hc], adb[ds, nat], r=[a2s, adb], w=[pp])
        self.act(loc(A1), pp[:, 0:256], AF.Tanh, scale=0.5, bias=hp2[:, hp, 2 + d:3 + d], r=[pp, hp2], w=[A1])
        yield
        P.op('dve', lambda e: e.tensor_scalar(out=B1[:], in0=A1[:], scalar1=0.5, scalar2=0.5, op0=ALU.mult, op1=ALU.add), r=[A1], w=[B1])
        P.op('dve', lambda e: e.tensor_tensor(out=loc(B1), in0=loc(B1), in1=kkb[:, nat], op=ALU.mult), r=[B1, kkb], w=[B1])
        P.op('dve', lambda e: e.tensor_tensor(out=bt[:], in0=B1[:], in1=C1[:], op=ALU.mult), r=[B1, C1], w=[bt])
        yield
        P.op('dve', lambda e: e.tensor_scalar(out=A1[:], in0=A1[:], scalar1=hp2[:, hp, 4:5], scalar2=hp2[:, hp, 5:6], op0=ALU.mult, op1=ALU.add), r=[A1, hp2], w=[A1])
        P.op('dve', lambda e: e.tensor_tensor(out=loc(A1), in0=loc(A1), in1=kb[:, nat], op=ALU.mult), r=[A1, kb], w=[A1])
        P.op('dve', lambda e: e.tensor_tensor(out=kt[:], in0=A1[:], in1=C1[:], op=ALU.mult), r=[A1, C1], w=[kt])
        P.op('pool', lambda e: e.tensor_copy(out=loc(vs), in_=vb[:, nat]), r=[vb], w=[vs])
        yield
        if s0 >= 256:
            P.op('dve', lambda e: e.scalar_tensor_tensor(out=B1[:], in0=loc(A1), scalar=chp[:, hp, 6:7], in1=rb[:, nat], op0=ALU.mult, op1=ALU.mult),
                 r=[A1, chp, rb, bt], w=[B1])
            pp = bank()
            self.mm(pp[:, 0:256], self.bones[:], B1[:], r=[B1, self.bones], w=[pp])
            bo = s0 - 256
            P.op('dve', lambda e: e.tensor_tensor(out=bacc[:, bo:bo + 256], in0=bacc[:, bo:bo + 256], in1=pp[:, 0:256], op=ALU.add), r=[pp, (bacc, bo)], w=[(bacc, bo)])
            yield

    def gen_S2a(self, hp, d, g, S, C):
        P, pb, pbh = self.P, self.pb, self.pbh
        msk = C['msk']
        at, bt, kt, vs = (S['s1'][g % 2][k] for k in ('at', 'bt', 'kt', 'vs'))
        rt = S['rw'][g % 4]['rt']
        sl = S['slots'][g % 3]
        tokT = sl['tokT']
        rot = C['rot']
        H = lambda e_: slice(64 * e_, 64 * e_ + 64)
        TP = lambda e_: (64 * e_, 64 * e_)

        def bank():
            rot[0] = (rot[0] + 1) % 4
            return pb[(0, 1, 2, 5)[rot[0]]]
        v3 = lambda p: p[:, 0:256].rearrange("p (c t) -> p c t", t=64)
        v4 = lambda p: p[:, :].rearrange("p (c x) -> p c x", x=128)
        QXs, QTs, AakT = S['QX'][g % 2], S['QT'][g % 2], S['AakT'][g % 2]

        def neumann_level(lvl, QX, QT):
            QXn, QTn = QXs[lvl % 2], QTs[lvl % 2]
            last = lvl == 6
            if lvl == 1:
                pq = bank()
                for c in range(4):
                    for e_ in range(2):
                        self.mm(v3(pq)[H(e_), c, :], QT[H(e_), c, :], QX[H(e_), c, 0, :], r=[QX, QT], w=[pq], tile_position=TP(e_))
                P.op('act', lambda e: e.copy(out=QXn[:, :, 0, :], in_=v3(pq)), r=[pq], w=[QXn])
                P.op('pool', lambda e: e.tensor_copy(out=QXn[:, :, 1, :], in_=QX[:, :, 1, :]), r=[QX], w=[QXn])
            else:
                ppx = bank()
                for c in range(4):
                    for e_ in range(2):
                        if last:
                            self.mm(v4(ppx)[H(e_), c, 64:128], QT[H(e_), c, :], QX[H(e_), c, 1, :], r=[QX, QT], w=[ppx], tile_position=TP(e_))
                        else:
                            self.mm(v4(ppx)[H(e_), c, :], QT[H(e_), c, :], QX[H(e_), c, :, :].rearrange("p a b -> p (a b)"), r=[QX, QT], w=[ppx],
                                    tile_position=TP(e_))
                dstP = sl['MT'][:] if last else QXn[:, :, 1, :]
                P.op('dve', lambda e: e.tensor_tensor(out=dstP, in0=v4(ppx)[:, :, 64:128], in1=QX[:, :, 1, :], op=ALU.add), r=[ppx, QX], w=[sl['MT'] if last else QXn])
                if not last:
                    P.op('act', lambda e: e.copy(out=QXn[:, :, 0, :], in_=v4(ppx)[:, :, 0:64]), r=[ppx], w=[QXn])
            if not last:
                pqt = bank()
                for c in range(4):
                    for e_ in range(2):
                        self.mm(v3(pqt)[H(e_), c, :], QX[H(e_), c, 0, :], QT[H(e_), c, :], r=[QX, QT], w=[pqt], tile_position=TP(e_))
                P.op('act', lambda e: e.copy(out=QTn[:], in_=v3(pqt)), r=[pqt], w=[QTn])
            return QXn, QTn
        for qi, q in enumerate((at, bt, kt, vs)):
            for c in range(4):
                for e_ in range(2):
                    o = (qi % 2) * 256 + c * 64
                    P.op('pe', lambda e: e.transpose(out=pbh[H(e_), o:o + 64], in_=q[H(e_), 64 * c:64 * c + 64], identity=self.identb[H(e_), H(e_)],
                                                     tile_position=TP(e_)), r=[q, self.identb], w=[pbh])
            if qi % 2 == 1:
                P.op('act', lambda e: e.copy(out=tokT[:, qi - 1:qi + 1, :, :].rearrange("p q c j -> p (q c j)"), in_=pbh[:, 0:512]), r=[pbh], w=[tokT])
                yield
        cs = lambda q, c, e_: q[H(e_), 64 * c:64 * c + 64]

        def score(L, Rr, mk, dst, dkey):
            pp = bank()
            for c in range(4):
                for e_ in range(2):
                    self.mm(v3(pp)[H(e_), c, :], cs(L, c, e_), cs(Rr, c, e_), r=[L, Rr], w=[pp], tile_position=TP(e_))
            P.op('dve', lambda e: e.tensor_tensor(out=dst, in0=v3(pp), in1=msk[mk][:], op=ALU.mult), r=[pp, msk[mk]], w=[dkey])
        QX, QT = QXs[0], QTs[0]
        score(bt, at, 'su', QX[:, :, 0, :], QX)
        yield
        score(at, bt, 'sl', QT[:], QT)
        yield
        P.op('pool', lambda e: e.tensor_tensor(out=QX[:, :, 1, :], in0=QX[:, :, 0, :], in1=msk['id'][:], op=ALU.add), r=[QX, msk['id']], w=[QX])
        score(kt, at, 'su', AakT[:], AakT)
        yield
        score(bt, rt, 'iu', sl['ArbT'][:], sl['ArbT'])
        yield
        score(kt, rt, 'iu', sl['ArkT'][:], sl['ArkT'])
        yield
        for lvl in (1, 2, 3):
            QX, QT = neumann_level(lvl, QX, QT)
            yield

    def gen_S2b(self, hp, d, g, S, C):
        P, pb, pbh = self.P, self.pb, self.pbh
        msk = C['msk']
        at, bt, kt, vs = (S['s1'][g % 2][k] for k in ('at', 'bt', 'kt', 'vs'))
        rt = S['rw'][g % 4]['rt']
        sl = S['slots'][g % 3]
        tokT = sl['tokT']
        rot = C['rot']
        H = lambda e_: slice(64 * e_, 64 * e_ + 64)
        TP = lambda e_: (64 * e_, 64 * e_)

        def bank():
            rot[0] = (rot[0] + 1) % 4
            return pb[(0, 1, 2, 5)[rot[0]]]
        v3 = lambda p: p[:, 0:256].rearrange("p (c t) -> p c t", t=64)
        v4 = lambda p: p[:, :].rearrange("p (c x) -> p c x", x=128)
        QXs, QTs, AakT = S['QX'][g % 2], S['QT'][g % 2], S['AakT'][g % 2]

        def neumann_level(lvl, QX, QT):
            QXn, QTn = QXs[lvl % 2], QTs[lvl % 2]
            last = lvl == 6
            if lvl == 1:
                pq = bank()
                for c in range(4):
                    for e_ in range(2):
                        self.mm(v3(pq)[H(e_), c, :], QT[H(e_), c, :], QX[H(e_), c, 0, :], r=[QX, QT], w=[pq], tile_position=TP(e_))
                P.op('act', lambda e: e.copy(out=QXn[:, :, 0, :], in_=v3(pq)), r=[pq], w=[QXn])
                P.op('pool', lambda e: e.tensor_copy(out=QXn[:, :, 1, :], in_=QX[:, :, 1, :]), r=[QX], w=[QXn])
            else:
                ppx = bank()
                for c in range(4):
                    for e_ in range(2):
                        if last:
                            self.mm(v4(ppx)[H(e_), c, 64:128], QT[H(e_), c, :], QX[H(e_), c, 1, :], r=[QX, QT], w=[ppx], tile_position=TP(e_))
                        else:
                            self.mm(v4(ppx)[H(e_), c, :], QT[H(e_), c, :], QX[H(e_), c, :, :].rearrange("p a b -> p (a b)"), r=[QX, QT], w=[ppx],
                                    tile_position=TP(e_))
                dstP = sl['MT'][:] if last else QXn[:, :, 1, :]
                P.op('dve', lambda e: e.tensor_tensor(out=dstP, in0=v4(ppx)[:, :, 64:128], in1=QX[:, :, 1, :], op=ALU.add), r=[ppx, QX], w=[sl['MT'] if last else QXn])
                if not last:
                    P.op('act', lambda e: e.copy(out=QXn[:, :, 0, :], in_=v4(ppx)[:, :, 0:64]), r=[ppx], w=[QXn])
            if not last:
                pqt = bank()
                for c in range(4):
                    for e_ in range(2):
                        self.mm(v3(pqt)[H(e_), c, :], QX[H(e_), c, 0, :], QT[H(e_), c, :], r=[QX, QT], w=[pqt], tile_position=TP(e_))
                P.op('act', lambda e: e.copy(out=QTn[:], in_=v3(pqt)), r=[pqt], w=[QTn])
            return QXn, QTn
        QX, QT = QXs[1], QTs[1]
        for lvl in (4, 5, 6):
            QX, QT = neumann_level(lvl, QX, QT)
            yield
        MT = sl['MT']
        pxa = bank()
        for c in range(4):
            for e_ in range(2):
                self.mm(v3(pxa)[H(e_), c, :], AakT[H(e_), c, :], tokT[H(e_), 3, c, :], r=[AakT, tokT], w=[pxa], tile_position=TP(e_))
        P.op('act', lambda e: e.copy(out=sl['Xak'][:], in_=v3(pxa)), r=[pxa], w=[sl['Xak']])
        yield
        pA = bank()
        for c in range(4):
            for e_ in range(2):
                self.mm(v3(pA)[H(e_), c, :], tokT[H(e_), 0, c, :], MT[H(e_), c, :], r=[tokT, MT], w=[pA], tile_position=TP(e_))
        P.op('act', lambda e: e.copy(out=sl['AhT'][:], in_=v3(pA)), r=[pA], w=[sl['AhT']])
        yield

    def gen_Q(self, hp, d, g, S, C):
        P, pb = self.P, self.pb
        y0 = C['y0']
        sl = S['slots'][g % 3]
        tokT, MT, Xak, ArbT, ArkT, AhT = (sl[k] for k in ('tokT', 'MT', 'Xak', 'ArbT', 'ArkT', 'AhT'))
        rt, wc = S['rw'][g % 4]['rt'], S['rw'][g % 4]['wc']
        Tst, Tw, Tb, Ub = S['Tst'], S['Tw'], S['Tb'], S['Ub']
        pU, pT, pY = pb[3], pb[4], pb[6]
        pUv = pU[:, 0:64]
        pTv = pT[:, 0:64]
        pYv = pY[:, 256 * d:256 * d + 256].rearrange("p (c t) -> p c t", t=64)
        H = lambda e_: slice(64 * e_, 64 * e_ + 64)
        TP = lambda e_: (64 * e_, 64 * e_)
        latent = g >= 1
        for c in range(4):
            for e_ in range(2):
                self.mm(pUv[H(e_), :], MT[H(e_), c, :], Xak[H(e_), c, :], start=True, stop=False, r=[MT, Xak], w=[pU], tile_position=TP(e_))
            for e_ in range(2):
                self.mm(pUv[H(e_), :], AhT[H(e_), c, :], Tb[H(e_), :], start=False, stop=True, r=[AhT, Tb], w=[pU], tile_position=TP(e_))
            P.op('act', lambda e: e.copy(out=Ub[:], in_=pUv), r=[pU], w=[Ub])
            yield
            for e_ in range(2):
                self.mm(pTv[H(e_), :], tokT[H(e_), 1, c, :], Ub[H(e_), :], start=True, stop=False, r=[tokT, Ub], w=[pT], tile_position=TP(e_))
            for e_ in range(2):
                self.mm(pTv[H(e_), :], tokT[H(e_), 2, c, :], tokT[H(e_), 3, c, :], start=False, stop=True, r=[tokT], w=[pT], tile_position=TP(e_))
            if latent:
                for e_ in range(2):
                    self.mm(pYv[H(e_), c, :], Tb[H(e_), :], rt[H(e_), 64 * c:64 * c + 64], start=True, stop=False, r=[Tb, rt], w=[pY], tile_position=TP(e_))
                for e_ in range(2):
                    self.mm(pYv[H(e_), c, :], Ub[H(e_), :], ArbT[H(e_), c, :], start=False, stop=False, r=[Ub, ArbT], w=[pY], tile_position=TP(e_))
                for e_ in range(2):
                    self.mm(pYv[H(e_), c, :], tokT[H(e_), 3, c, :], ArkT[H(e_), c, :], start=False, stop=True, r=[tokT, ArkT], w=[pY], tile_position=TP(e_))
            wcc = wc[:, c:c + 1]
            P.op('dve', lambda e: e.scalar_tensor_tensor(out=Tb[:], in0=pTv, scalar=wcc, in1=Tw[:], op0=ALU.mult, op1=ALU.add), r=[pT, wc, Tw], w=[Tb])
            P.op('dve', lambda e: e.scalar_tensor_tensor(out=Tst[:], in0=pTv, scalar=wcc, in1=Tw[:], op0=ALU.mult, op1=ALU.add), r=[pT, wc, Tw], w=[Tst])
            yield
            if c < 3:
                P.op('dve', lambda e: e.tensor_scalar(out=Tw[:], in0=Tst[:], scalar1=wc[:, c + 1:c + 2], scalar2=None, op0=ALU.mult), r=[Tst, wc], w=[Tw])
        if latent:
            g0 = 256 * g
            if d == 0:
                ysl = slice(g0 - 256, g0); yk = (y0, g0 - 256)
            else:
                ysl = rsl(2303 - g0, 256, -1); yk = (y0, 2048 - g0)
            P.op('dve', lambda e: e.tensor_tensor(out=y0[:, ysl], in0=y0[:, ysl], in1=pY[:, 256 * d:256 * d + 256], op=ALU.add), r=[pY, yk], w=[yk])
        yield

    def rw_rounds(self, hp, C):
        P = self.P
        dirs = [self.rw_alloc_dir() for _ in range(2)]
        for R in range(12):
            gens = []
            if 3 <= R:
                for d in range(2):
                    S = dirs[d]
                    wc = S['rw'][(R - 3) % 4]['wc']
                    P.op('dve', lambda e, S=S, wc=wc: e.tensor_scalar(out=S['Tw'][:], in0=S['Tst'][:], scalar1=wc[:, 0:1], scalar2=None, op0=ALU.mult),
                         r=[S['Tst'], wc], w=[S['Tw']])
                    gens.append(self.gen_Q(hp, d, R - 3, S, C))
            if 2 <= R <= 10:
                for d in range(2):
                    gens.append(self.gen_S2b(hp, d, R - 2, dirs[d], C))
            if 1 <= R <= 9:
                for d in range(2):
                    gens.append(self.gen_S2a(hp, d, R - 1, dirs[d], C))
            if R <= 8:
                for d in range(2):
                    gens.append(self.gen_S1(hp, d, R, dirs[d], C))
            while gens:
                for gn in list(gens):
                    try:
                        next(gn)
                    except StopIteration:
                        gens.remove(gn)

    def rwkv_half(self, hp, d, half, rb, kb, vb, kkb, y0, bacc, Tst, Tb, twd, adb, w2s, a2s, chp, hp2, msk, cmask, CW):
        P, pb, pbh = self.P, self.pb, self.pbh
        h0, W = (0, 1280) if half == 0 else (1280, 1024)
        if d == 0:
            pieces = [(0, 256, 0, 1), (256, 1024, 256, 1)] if half == 0 else [(1280, 1024, 1280, 1)]
        else:
            pieces = [(0, 256, 255, -1), (1280, 1024, 1279, -1)] if half == 0 else [(256, 1024, 2303, -1)]
        sg = lambda t, s0, n, sig0, step, off=0, nn=None: t[:, rsl(sig0 - h0 + step * off, nn if nn is not None else n, step)]
        A1 = P.sb([128, 1280], name='A1'); B1 = P.sb([128, 1280], name='B1'); C1 = P.sb([128, 1280], name='C1')
        rt = P.sb([128, 1280], BF16, name='rt'); at = P.sb([128, 1280], BF16, name='at'); bt = P.sb([128, 1280], BF16, name='bt')
        kt = P.sb([128, 1280], BF16, name='kt'); vs = P.sb([128, 1280], BF16, name='vs')
        wcs = P.sb([128, 20], name='wcs')
        hc = slice(hp * 128, (hp + 1) * 128)
        ds = slice(64 * d, 64 * d + 64)

        def lora(wts, src, bias_col, dst):
            nb = 0
            for (s0, n, sig0, step) in pieces:
                for o in range(0, n, 512):
                    nn = min(512, n - o)
                    pp = pb[nb % 2]; nb += 1
                    self.mm(pp[:, 0:nn], wts[ds, hc], src[ds, s0 + o:s0 + o + nn], r=[wts, src], w=[pp])
                    self.act(sg(dst, s0, n, sig0, step, o, nn), pp[:, 0:nn], AF.Tanh, scale=0.5, bias=hp2[:, hp, bias_col:bias_col + 1], r=[pp, hp2], w=[dst])
        lora(w2s, twd, d, A1)
        P.op('dve', lambda e: e.tensor_scalar(out=A1[:, 0:W], in0=A1[:, 0:W], scalar1=1.0, scalar2=CW, op0=ALU.add, op1=ALU.mult), r=[A1], w=[A1])
        P.op('dve', lambda e: e.tensor_tensor_scan(out=B1[:, 0:W], data0=cmask[:, 0:W], data1=A1[:, 0:W], initial=0.0, op0=ALU.mult, op1=ALU.add),
             r=[A1, cmask], w=[B1])
        P.op('dve', lambda e: e.tensor_tensor(out=A1[:, 0:W], in0=B1[:, 0:W], in1=A1[:, 0:W], op=ALU.subtract), r=[A1, B1], w=[A1])
        self.act(C1[:, 0:W], A1[:, 0:W], AF.Exp, r=[A1], w=[C1])
        for pc in pieces:
            s0, n = pc[0], pc[1]
            P.op('dve', lambda e, pc=pc, s0=s0, n=n: e.scalar_tensor_tensor(out=sg(at, *pc), in0=kkb[:, s0:s0 + n], scalar=-1.0, in1=sg(C1, *pc),
                                                                            op0=ALU.mult, op1=ALU.mult), r=[kkb, C1], w=[at])
        self.act(C1[:, 0:W], B1[:, 0:W], AF.Exp, r=[B1, at], w=[C1])
        P.op('dve', lambda e: e.tensor_copy(out=wcs[:, 0:W // 64], in_=C1[:, 63:W:64]), r=[C1], w=[wcs])
        for pc in pieces:
            s0, n = pc[0], pc[1]
            P.op('dve', lambda e, pc=pc, s0=s0, n=n: e.tensor_tensor(out=sg(rt, *pc), in0=rb[:, s0:s0 + n], in1=sg(C1, *pc), op=ALU.mult), r=[rb, C1], w=[rt])
        self.act(C1[:, 0:W], B1[:, 0:W], AF.Exp, scale=-1.0, r=[B1, rt, wcs], w=[C1])
        lora(a2s, adb, 2 + d, A1)
        P.op('dve', lambda e: e.tensor_scalar(out=B1[:, 0:W], in0=A1[:, 0:W], scalar1=0.5, scalar2=0.5, op0=ALU.mult, op1=ALU.add), r=[A1], w=[B1])
        for pc in pieces:
            s0, n = pc[0], pc[1]
            P.op('dve', lambda e, pc=pc, s0=s0, n=n: e.tensor_tensor(out=sg(B1, *pc), in0=sg(B1, *pc), in1=kkb[:, s0:s0 + n], op=ALU.mult), r=[B1, kkb], w=[B1])
        P.op('dve', lambda e: e.tensor_tensor(out=bt[:, 0:W], in0=B1[:, 0:W], in1=C1[:, 0:W], op=ALU.mult), r=[B1, C1], w=[bt])
        P.op('dve', lambda e: e.tensor_scalar(out=A1[:, 0:W], in0=A1[:, 0:W], scalar1=hp2[:, hp, 4:5], scalar2=hp2[:, hp, 5:6], op0=ALU.mult, op1=ALU.add),
             r=[A1, hp2], w=[A1])
        for pc in pieces:
            s0, n = pc[0], pc[1]
            P.op('dve', lambda e, pc=pc, s0=s0, n=n: e.tensor_tensor(out=sg(A1, *pc), in0=sg(A1, *pc), in1=kb[:, s0:s0 + n], op=ALU.mult), r=[A1, kb], w=[A1])
        P.op('dve', lambda e: e.tensor_tensor(out=kt[:, 0:W], in0=A1[:, 0:W], in1=C1[:, 0:W], op=ALU.mult), r=[A1, C1], w=[kt])
        nb = 0
        for pc in pieces:
            s0, n, sig0, step = pc
            if s0 < 256:
                continue
            P.op('dve', lambda e, pc=pc, s0=s0, n=n: e.scalar_tensor_tensor(out=B1[:, 0:n], in0=sg(A1, *pc), scalar=chp[:, hp, 6:7], in1=rb[:, s0:s0 + n],
                                                                            op0=ALU.mult, op1=ALU.mult), r=[A1, chp, rb, bt], w=[B1])
            for o in range(0, n, 512):
                pp = pb[nb % 2]; nb += 1
                self.mm(pp[:], self.bones[:], B1[:, o:o + 512], r=[B1, self.bones], w=[pp])
                bo = s0 - 256 + o
                if d == 0:
                    P.op('act', lambda e, pp=pp, bo=bo: e.copy(out=bacc[:, bo:bo + 512], in_=pp[:]), r=[pp], w=[bacc])
                else:
                    P.op('dve', lambda e, pp=pp, bo=bo: e.tensor_tensor(out=bacc[:, bo:bo + 512], in0=bacc[:, bo:bo + 512], in1=pp[:], op=ALU.add), r=[pp, bacc], w=[bacc])
        for pc in pieces:
            s0, n = pc[0], pc[1]
            P.op('pool', lambda e, pc=pc, s0=s0, n=n: e.tensor_copy(out=sg(vs, *pc), in_=vb[:, s0:s0 + n]), r=[vb], w=[vs])

        if hp == 0 and half == 0:
            for nm, t_ in (('rt', rt), ('at', at), ('bt', bt), ('kt', kt), ('vs', vs)):
                self.tap(nm + str(d), t_[:], [128, 1280], [t_], dt=BF16)
            self.tap('wcs' + str(d), wcs[:], [128, 20], [wcs])
            self.tap('kd' + str(d), A1[:], [128, 1280], [A1])
        tokT = P.sb([64, 4, 4, 128], BF16, name='tokT')
        Qs = [P.sb([64, 8, 64], BF16, name='Q') for _ in range(2)]; QTs = [P.sb([64, 8, 64], BF16, name='QT') for _ in range(2)]
        Xs = [P.sb([64, 8, 64], BF16, name='X') for _ in range(2)]
        AakT = P.sb([64, 8, 64], BF16, name='AakT'); ArbT = P.sb([64, 8, 64], BF16, name='ArbT'); ArkT = P.sb([64, 8, 64], BF16, name='ArkT')
        Xak = P.sb([64, 8, 64], BF16, name='Xak'); AhT = P.sb([128, 4, 64], BF16, name='AhT')
        Ub = P.sb([64, 2, 64], BF16, name='Ub'); Ts = P.sb([128, 64], name='Ts')
        pU, pT, pY, pA = pb[4], pb[5], pb[6], pb[0]
        pYv = pb[6][:, 0:256].rearrange("p (c t) -> p c t", t=64)
        pAv = pb[0][:, 0:256].rearrange("p (c t) -> p c t", t=64)
        pUv = pb[4][0:64, 0:128].rearrange("p (e v) -> p e v", v=64)
        pTv = pb[5][:, 0:64]
        bank = [0]

        def nextbank():
            bank[0] = (bank[0] + 1) % 2
            return pb[2 + bank[0]]
        v3 = lambda p: p[0:64, :].rearrange("p (i t) -> p i t", t=64)
        for gi in range(W // 256):
            loc = 256 * gi
            g0 = h0 + loc
            latent = g0 >= 256
            for qi, q in enumerate((at, bt, kt, vs)):
                for c in range(4):
                    P.op('pe', lambda e, qi=qi, q=q, c=c: e.transpose(out=pbh[0:64, (qi % 2) * 512 + c * 128:(qi % 2) * 512 + (c + 1) * 128],
                                                                      in_=q[:, loc + 64 * c:loc + 64 * c + 64], identity=self.identb[:]),
                         r=[q, self.identb], w=[pbh])
                if qi % 2 == 1:
                    P.op('act', lambda e, qi=qi: e.copy(out=tokT[:, qi - 1:qi + 1, :, :].rearrange("p q c j -> p (q c j)"), in_=pbh[0:64, :]), r=[pbh], w=[tokT])
            cs = lambda q, c, e_: q[64 * e_:64 * e_ + 64, loc + 64 * c:loc + 64 * c + 64]

            def score(L, Rr, mk, dst):
                pp = nextbank()
                for c in range(4):
                    for e_ in range(2):
                        self.mm(v3(pp)[:, 2 * c + e_, :], cs(L, c, e_), cs(Rr, c, e_), r=[L, Rr], w=[pp])
                P.op('dve', lambda e: e.tensor_tensor(out=dst[:], in0=v3(pp), in1=msk[mk][:], op=ALU.mult), r=[pp, msk[mk]], w=[dst])
            Q, QT, X = Qs[0], QTs[0], Xs[0]
            score(bt, at, 'su', Q)
            score(at, bt, 'sl', QT)
            score(kt, at, 'su', AakT)
            score(bt, rt, 'iu', ArbT)
            score(kt, rt, 'iu', ArkT)
            P.op('dve', lambda e, Q=Q, X=X: e.tensor_tensor(out=X[:], in0=Q[:], in1=msk['id'][:], op=ALU.add), r=[Q, msk['id']], w=[X])
            for lvl in range(2, 7):
                Qn, QTn, Xn = Qs[(lvl + 1) % 2], QTs[(lvl + 1) % 2], Xs[(lvl + 1) % 2]
                if lvl < 6:
                    pq = nextbank()
                    for i in range(8):
                        self.mm(v3(pq)[:, i, :], QT[:, i, :], Q[:, i, :], r=[Q, QT], w=[pq])
                    P.op('act', lambda e, pq=pq, Qn=Qn: e.copy(out=Qn[:], in_=v3(pq)), r=[pq], w=[Qn])
                pqt = nextbank()
                for i in range(8):
                    self.mm(v3(pqt)[:, i, :], Q[:, i, :], QT[:, i, :], r=[Q, QT], w=[pqt])
                P.op('act', lambda e, pqt=pqt, QTn=QTn: e.copy(out=QTn[:], in_=v3(pqt)), r=[pqt], w=[QTn])
                px = nextbank()
                for i in range(8):
                    self.mm(v3(px)[:, i, :], QTn[:, i, :], X[:, i, :], r=[QTn, X], w=[px])
                P.op('dve', lambda e, px=px, X=X, Xn=Xn: e.tensor_tensor(out=Xn[:], in0=v3(px), in1=X[:], op=ALU.add), r=[px, X], w=[Xn])
                Q, QT, X = Qn, QTn, Xn
            MT = X
            pxa = nextbank()
            for c in range(4):
                for e_ in range(2):
                    self.mm(v3(pxa)[:, 2 * c + e_, :], AakT[:, 2 * c + e_, :], tokT[:, 3, c, 64 * e_:64 * e_ + 64], r=[AakT, tokT], w=[pxa])
            P.op('act', lambda e, pxa=pxa: e.copy(out=Xak[:], in_=v3(pxa)), r=[pxa], w=[Xak])
            for c in range(4):
                for e_ in range(2):
                    self.mm(pAv[64 * e_:64 * e_ + 64, c, :], tokT[:, 0, c, 64 * e_:64 * e_ + 64], MT[:, 2 * c + e_, :], r=[tokT, MT], w=[pA],
                            tile_position=(0, 64 * e_))
            P.op('act', lambda e: e.copy(out=AhT[:], in_=pAv), r=[pA], w=[AhT])
            if hp == 0 and half == 0 and d == 0 and gi == 0:
                self.tap('MT', MT[:], [64, 8, 64], [MT], dt=BF16)
                self.tap('AhT', AhT[:], [128, 4, 64], [AhT], dt=BF16)
                self.tap('Xak', Xak[:], [64, 8, 64], [Xak], dt=BF16)
                self.tap('tokT', tokT[:], [64, 4, 4, 128], [tokT], dt=BF16)
                self.tap('ArbT', ArbT[:], [64, 8, 64], [ArbT], dt=BF16)
            for c in range(4):
                for e_ in range(2):
                    i = 2 * c + e_
                    es = slice(64 * e_, 64 * e_ + 64)
                    self.mm(pUv[:, e_, :], MT[:, i, :], Xak[:, i, :], start=True, stop=False, r=[MT, Xak], w=[pU])
                    self.mm(pUv[:, e_, :], AhT[es, c, :], Tb[es, :], start=False, stop=True, r=[AhT, Tb], w=[pU], tile_position=(64 * e_, 0))
                P.op('act', lambda e: e.copy(out=Ub[:], in_=pUv), r=[pU], w=[Ub])
                for e_ in range(2):
                    i = 2 * c + e_
                    es = slice(64 * e_, 64 * e_ + 64)
                    if latent:
                        self.mm(pYv[es, c, :], Tb[es, :], rt[es, loc + 64 * c:loc + 64 * c + 64], start=True, stop=False, r=[Tb, rt], w=[pY],
                                tile_position=(64 * e_, 64 * e_))
                        self.mm(pYv[es, c, :], Ub[:, e_, :], ArbT[:, i, :], start=False, stop=False, r=[Ub, ArbT], w=[pY], tile_position=(0, 64 * e_))
                        self.mm(pYv[es, c, :], tokT[:, 3, c, es], ArkT[:, i, :], start=False, stop=True, r=[tokT, ArkT], w=[pY], tile_position=(0, 64 * e_))
                    self.mm(pTv[es, :], tokT[:, 1, c, es], Ub[:, e_, :], start=True, stop=False, r=[tokT, Ub], w=[pT], tile_position=(0, 64 * e_))
                    self.mm(pTv[es, :], tokT[:, 2, c, es], tokT[:, 3, c, es], start=False, stop=True, r=[tokT], w=[pT], tile_position=(0, 64 * e_))
                P.op('dve', lambda e: e.tensor_tensor(out=Ts[:], in0=pTv, in1=Tst[:], op=ALU.add), r=[pT, Tst], w=[Ts])
                wc = wcs[:, 4 * gi + c:4 * gi + c + 1]
                P.op('dve', lambda e, wc=wc: e.tensor_scalar(out=Tst[:], in0=Ts[:], scalar1=wc, scalar2=None, op0=ALU.mult), r=[Ts, wcs], w=[Tst])
                self.act(Tb[:], Ts[:], AF.Identity, scale=wc, r=[Ts, wcs], w=[Tb])
            if latent:
                if d == 0:
                    P.op('act', lambda e, g0=g0: e.copy(out=y0[:, g0 - 256:g0], in_=pb[6][:, 0:256]), r=[pY], w=[y0])
                else:
                    ysl = rsl(2303 - g0, 256, -1)
                    P.op('dve', lambda e, ysl=ysl: e.tensor_tensor(out=y0[:, ysl], in0=y0[:, ysl], in1=pb[6][:, 0:256], op=ALU.add), r=[pY, y0], w=[y0])

    def wload(self, pool, src, npart, K, ncol, q='sp'):
        P = self.P
        i = pool['i'] = pool['i'] + 1
        wf, wb = pool['f'][i % len(pool['f'])], pool['b'][i % len(pool['b'])]
        P.dma(q, wf[0:npart, 0:K, 0:ncol], src, w=[wf], group=pool['name'] + str(i % len(pool['f'])))
        ceng = 'pool' if (self._castn % 2 == 0) else 'dve'
        self._castn += 1
        P.op(ceng, lambda e: e.tensor_copy(out=wb[0:npart, 0:K, 0:ncol], in_=wf[0:npart, 0:K, 0:ncol]), r=[wf], w=[wb])
        return wb

    def mkpool(self, name, npart, K, ncol, n=2):
        P = self.P
        return {'name': name, 'i': 0, 'f': [P.sb([npart, K, ncol], name=name + 'f') for _ in range(n)],
                'b': [P.sb([npart, K, ncol], BF16, name=name + 'b') for _ in range(n)]}

    def lru(self):
        P, pb, hT, din = self.P, self.pb, self.hT, self.din
        lruT = self.lruT
        cw_, cb_, ba_, bx_, lam_ = din['lru_conv_w'], din['lru_conv_b'], din['lru_ba'], din['lru_bx'], din['lru_lambda']
        rows = [cw_[d, j] for d in range(2) for j in range(4)] + [cb_[0], cb_[1], ba_[0], ba_[1], bx_[0], bx_[1], lam_[0], lam_[1]]
        lp = self.cols(rows, LW, 80, 'lp')
        hb = P.sb([80, 16, 4], name='hb')
        P.op('dve', lambda e: e.tensor_scalar(out=hb[:], in0=lp[:, :, 10:14], scalar1=0.5, scalar2=None, op0=ALU.mult), r=[lp], w=[hb])
        cs = P.sb([80, 16, 4], name='cs')
        one1 = P.sb([80, 1], name='one1')
        P.op('dve', lambda e: e.memset(one1[:], 1.0), w=[one1])
        self.act(cs[:, :, 0:2], lp[:, :, 14:16], AF.Exp, scale=-1.0, r=[lp], w=[cs])
        self.act(cs[:, :, 0:2], cs[:, :, 0:2], AF.Ln, bias=one1[:], r=[cs, one1], w=[cs])
        P.op('dve', lambda e: e.tensor_scalar(out=cs[:, :, 2:4], in0=cs[:, :, 0:2], scalar1=-4.0, scalar2=None, op0=ALU.mult), r=[cs], w=[cs])
        P.op('dve', lambda e: e.tensor_scalar(out=cs[:, :, 0:2], in0=cs[:, :, 0:2], scalar1=-8.0, scalar2=None, op0=ALU.mult), r=[cs], w=[cs])
        q25 = P.sb([80, 1], name='q25')
        P.op('dve', lambda e: e.memset(q25[:], 0.25), w=[q25])
        gwa = P.sb([80, 32, 80], BF16, name='gwa'); gwx = P.sb([80, 32, 80], BF16, name='gwx')
        with P.scope():
            st = P.sb([80, 32, 80], name='gst')
            for src, dst in ((din['lru_wa'], gwa), (din['lru_wx'], gwx)):
                P.dma('sp', st[:], src.rearrange("d n c e -> c (d n) e"), w=[st], group='gst')
                P.op('dve', lambda e, dst=dst: e.tensor_copy(out=dst[:], in_=st[:]), r=[st], w=[dst])
        U = P.sb([80, 2313], name='U'); guy = P.sb([80, NLAT], BF16, name='guy')
        xcs = [P.sb([80, 2313], name='xc') for _ in range(2)]
        xcb = P.sb([80, 2313], BF16, name='xcb')
        thr = P.sb([80, 2313], name='thr'); thi = P.sb([80, 2313], name='thi'); aa = P.sb([80, 2313], name='aa')
        lrub = P.sb([80, NLAT], BF16, name='lrub')
        wp = self.mkpool('lw', 128, 8, 80, n=2)
        P.op('dve', lambda e: e.memset(U[:], 0.0), w=[U])
        for xc in xcs:
            P.op('dve', lambda e, xc=xc: e.memset(xc[:], 0.0), w=[xc])
        P.op('dve', lambda e: e.memset(thr[:], 0.0), w=[thr])
        P.op('dve', lambda e: e.memset(thi[:], 0.0), w=[thi])
        wv = din['w_in'].rearrange("(k p) n -> p k n", p=128)
        hk = [(hT, j) for j in range(5)]
        tbs = [(0, 256, 3)] + [(256 + 512 * i, 512, 262 + 512 * i) for i in range(4)]
        nb = 0
        for n in range(NBLK):
            wx = self.wload(wp, wv[:, :, 80 * n:80 * n + 80], 128, 8, 80)
            for (hc, nn, uc) in tbs:
                pp = pb[nb % 6]; nb += 1
                for k in range(8):
                    self.mm(pp[0:80, 0:nn], wx[:, k, :], hT[:, k, hc:hc + nn], start=(k == 0), stop=(k == 7), r=[wx] + hk, w=[pp])
                P.op('act', lambda e: e.copy(out=U[:, uc:uc + nn], in_=pp[0:80, 0:nn]), r=[pp], w=[U])
            wy = self.wload(wp, wv[:, :, 1280 + 80 * n:1280 + 80 * n + 80], 128, 8, 80, q='act')
            for (hc, nn, uc) in tbs[1:]:
                pp = pb[nb % 6]; nb += 1
                for k in range(8):
                    self.mm(pp[0:80, 0:nn], wy[:, k, :], hT[:, k, hc:hc + nn], start=(k == 0), stop=(k == 7), r=[wy] + hk, w=[pp])
                self.act(guy[:, hc - 256:hc - 256 + nn], pp[0:80, 0:nn], AF.Gelu_apprx_tanh, r=[pp], w=[guy])
            L = 2307
            for d in range(2):
                xc = xcs[d]
                sgn = -1 if d == 0 else 1
                cwj = lambda j: lp[:, n, 4 * d + j:4 * d + j + 1]
                P.op('dve', lambda e: e.tensor_scalar(out=xc[:, 3:3 + L], in0=U[:, 3:3 + L], scalar1=cwj(3), scalar2=lp[:, n, 8 + d:9 + d],
                                                      op0=ALU.mult, op1=ALU.add), r=[U, lp], w=[xc])
                for j in range(3):
                    o = 3 + sgn * (3 - j)
                    P.op('dve', lambda e: e.scalar_tensor_tensor(out=xc[:, 3:3 + L], in0=U[:, o:o + L], scalar=cwj(j), in1=xc[:, 3:3 + L],
                                                                 op0=ALU.mult, op1=ALU.add), r=[U, lp, xc], w=[xc])
                P.op('pool', lambda e: e.tensor_copy(out=xcb[:, 3:3 + L], in_=xc[:, 3:3 + L]), r=[xc], w=[xcb])
                for (hc, nn, uc) in tbs:
                    for gw, dst, bcol in ((gwa, thr, d), (gwx, thi, 2 + d)):
                        pp = pb[nb % 6]; nb += 1
                        self.mm(pp[0:80, 0:nn], gw[:, 16 * d + n, :], xcb[:, uc:uc + nn], r=[gw, xcb], w=[pp])
                        self.act(dst[:, uc:uc + nn], pp[0:80, 0:nn], AF.Tanh, scale=0.5, bias=hb[:, n, bcol:bcol + 1], r=[pp, hb], w=[dst])
                self.act(aa[:, 3:3 + L], thr[:, 3:3 + L], AF.Exp, scale=cs[:, n, 2 + d:3 + d], bias=cs[:, n, 2 + d:3 + d], r=[thr, cs], w=[aa])
                self.act(thr[:, 3:3 + L], thr[:, 3:3 + L], AF.Exp, scale=cs[:, n, d:d + 1], bias=cs[:, n, d:d + 1], r=[thr, cs], w=[thr])
                self.act(thr[:, 3:3 + L], thr[:, 3:3 + L], AF.Sqrt, scale=-0.25, bias=q25[:], r=[thr, q25], w=[thr])
                P.op('dve', lambda e: e.scalar_tensor_tensor(out=thi[:, 3:3 + L], in0=thi[:, 3:3 + L], scalar=1.0, in1=xc[:, 3:3 + L], op0=ALU.add, op1=ALU.mult),
                     r=[thi, xc], w=[thi])
                P.op('dve', lambda e: e.tensor_tensor(out=thi[:, 3:3 + L], in0=thi[:, 3:3 + L], in1=thr[:, 3:3 + L], op=ALU.mult), r=[thi, thr], w=[thi])
                if d == 0:
                    P.op('dve', lambda e: e.tensor_tensor_scan(out=xc[:, 3:259], data0=aa[:, 3:259], data1=thi[:, 3:259], initial=0.0, op0=ALU.mult, op1=ALU.add),
                         r=[aa, thi], w=[xc])
                    P.op('dve', lambda e: e.tensor_tensor_scan(out=xc[:, 262:2310], data0=aa[:, 262:2310], data1=thi[:, 262:2310], initial=xc[:, 258:259],
                                                               op0=ALU.mult, op1=ALU.add), r=[aa, thi, xc], w=[xc])
                else:
                    rv = lambda t_, a, b: t_[:, rsl(b - 1, b - a, -1)]
                    P.op('dve', lambda e: e.tensor_tensor_scan(out=rv(xc, 3, 259), data0=rv(aa, 3, 259), data1=rv(thi, 3, 259), initial=0.0, op0=ALU.mult, op1=ALU.add),
                         r=[aa, thi], w=[xc])
                    P.op('dve', lambda e: e.tensor_tensor_scan(out=rv(xc, 262, 2310), data0=rv(aa, 262, 2310), data1=rv(thi, 262, 2310), initial=xc[:, 3:4],
                                                               op0=ALU.mult, op1=ALU.add), r=[aa, thi, xc], w=[xc])
            P.op('dve', lambda e: e.tensor_tensor(out=aa[:, 0:NLAT], in0=xcs[0][:, 262:2310], in1=xcs[1][:, 262:2310], op=ALU.add), r=xcs, w=[aa])
            P.op('dve', lambda e: e.tensor_tensor(out=lrub[:], in0=aa[:, 0:NLAT], in1=guy[:], op=ALU.mult), r=[aa, guy], w=[lrub])
            p0, c0 = (80 * n) % 128, (80 * n) // 128
            n1 = min(80, 128 - p0)
            P.dma('sp', lruT[p0:p0 + n1, c0, :], lrub[0:n1, :], r=[lrub], w=[lruT], group='lruT')
            if n1 < 80:
                P.dma('sp', lruT[0:80 - n1, c0 + 1, :], lrub[n1:80, :], r=[lrub], w=[lruT], group='lruT')

    def merge(self):
        P, pb, hT, din = self.P, self.pb, self.hT, self.din
        lruT, rwT, mT = self.lruT, self.rwT, self.mT
        wv = din['w_in'].rearrange("(k p) n -> p k n", p=128)
        wol = din['w_o_lru'].rearrange("(k p) n -> p k n", p=128)
        wor = din['w_o_rwkv'].rearrange("(k p) n -> p k n", p=128)
        pl = self.mkpool('wl', 128, 10, 128); pr = self.mkpool('wr', 128, 8, 128); pg = self.mkpool('wg', 128, 8, 128, n=3)
        thl = P.sb([128, 512], name='thl'); thr = P.sb([128, 512], name='thr2'); t1 = P.sb([128, 512], name='t1'); t2 = P.sb([128, 512], name='t2')
        hk = [(hT, j) for j in range(5)]
        for dc in range(8):
            cs_ = slice(dc * 128, dc * 128 + 128)
            wl = self.wload(pl, wol[:, :, cs_], 128, 10, 128)
            wr = self.wload(pr, wor[:, :, cs_], 128, 8, 128, q='act')
            wgl = self.wload(pg, wv[:, :, 6048 + dc * 128:6048 + dc * 128 + 128], 128, 8, 128)
            wgr = self.wload(pg, wv[:, :, 7072 + dc * 128:7072 + dc * 128 + 128], 128, 8, 128, q='act')
            for tb in range(4):
                ts_ = slice(512 * tb, 512 * tb + 512); hs_ = slice(256 + 512 * tb, 256 + 512 * tb + 512)
                p1, p2, p3, p4 = pb[0], pb[1], pb[2], pb[3]
                for c in range(10):
                    self.mm(p1[:], wl[:, c, :], lruT[:, c, ts_], start=(c == 0), stop=(c == 9), r=[wl, lruT], w=[p1])
                for c in range(8):
                    self.mm(p2[:], wr[:, c, :], rwT[:, c, ts_], start=(c == 0), stop=(c == 7), r=[wr, rwT], w=[p2])
                for c in range(8):
                    self.mm(p3[:], wgl[:, c, :], hT[:, c, hs_], start=(c == 0), stop=(c == 7), r=[wgl] + hk, w=[p3])
                for c in range(8):
                    self.mm(p4[:], wgr[:, c, :], hT[:, c, hs_], start=(c == 0), stop=(c == 7), r=[wgr] + hk, w=[p4])
                self.act(thl[:], p3[:], AF.Tanh, scale=0.5, r=[p3], w=[thl])
                self.act(thr[:], p4[:], AF.Tanh, scale=0.5, r=[p4], w=[thr])
                P.op('dve', lambda e: e.scalar_tensor_tensor(out=t1[:], in0=thl[:], scalar=1.0, in1=p1[:], op0=ALU.add, op1=ALU.mult), r=[thl, p1], w=[t1])
                P.op('dve', lambda e: e.scalar_tensor_tensor(out=t2[:], in0=thr[:], scalar=1.0, in1=p2[:], op0=ALU.add, op1=ALU.mult), r=[thr, p2], w=[t2])
                P.op('dve', lambda e: e.tensor_tensor(out=mT[:, dc, ts_], in0=t1[:], in1=t2[:], op=ALU.add), r=[t1, t2], w=[mT])

    def resid1(self):
        P, pb, din, mod = self.P, self.pb, self.din, self.mod
        mT, x1T = self.mT, self.x1T
        hg = P.sb([128, 8], name='hg')
        P.op('dve', lambda e: e.tensor_scalar(out=hg[:], in0=mod[:, 16:24, 0], scalar1=0.5, scalar2=None, op0=ALU.mult), r=[mod], w=[hg])
        wo = P.sb([128, 8, D], BF16, name='wo')
        with P.scope():
            st = P.sb([128, 8, 256], name='wost')
            for j in range(4):
                P.dma('sp', st[:], din['w_out'].rearrange("(k p) n -> p k n", p=128)[:, :, 256 * j:256 * j + 256], w=[st], group='wost')
                P.op('pool', lambda e: e.tensor_copy(out=wo[:, :, 256 * j:256 * j + 256], in_=st[:]), r=[st], w=[wo])
        xv = din['xT'].rearrange("(k p) t -> p k t", p=128)
        for tb in range(4):
            ts_ = slice(512 * tb, 512 * tb + 512)
            P.dma('sp', x1T[:, :, ts_], xv[:, :, ts_], w=[x1T], group='x1ld')
            for dc in range(8):
                pp = pb[(tb * 8 + dc) % 2]
                for c in range(8):
                    self.mm(pp[:], wo[:, c, dc * 128:dc * 128 + 128], mT[:, c, ts_], start=(c == 0), stop=(c == 7), r=[wo, mT], w=[pp])
                P.op('dve', lambda e: e.scalar_tensor_tensor(out=x1T[:, dc, ts_], in0=pp[:], scalar=hg[:, dc:dc + 1], in1=x1T[:, dc, ts_], op0=ALU.mult, op1=ALU.add),
                     r=[pp, hg, x1T], w=[x1T])

    def ffn(self):
        P, pb, din, mod = self.P, self.pb, self.din, self.mod
        x1T, h2T = self.x1T, self.h2T
        wi = din['w_ffn_in'].rearrange("(k p) n -> p k n", p=128)
        wo_ = din['w_ffn_out'].rearrange("(f p) n -> p f n", p=128)
        actT = P.sb([128, 22, 1024], BF16, name='actT')
        pin = self.mkpool('fi', 128, 8, 128, n=3); pout = self.mkpool('fo', 128, 22, 128, n=2)
        sl = P.sb([128, 512], name='sl')
        hk = [(h2T, j) for j in range(4)]
        nb = 0
        for half in range(2):
            for f in range(22):
                wg = self.wload(pin, wi[:, :, f * 128:f * 128 + 128], 128, 8, 128)
                wu = self.wload(pin, wi[:, :, DFF + f * 128:DFF + f * 128 + 128], 128, 8, 128, q='act')
                for t2 in range(2):
                    tok = slice(1024 * half + 512 * t2, 1024 * half + 512 * t2 + 512)
                    pg_, pu_ = pb[nb % 4], pb[(nb + 1) % 4]; nb += 2
                    for k in range(8):
                        self.mm(pg_[:], wg[:, k, :], h2T[:, k, tok], start=(k == 0), stop=(k == 7), r=[wg] + hk, w=[pg_])
                    for k in range(8):
                        self.mm(pu_[:], wu[:, k, :], h2T[:, k, tok], start=(k == 0), stop=(k == 7), r=[wu] + hk, w=[pu_])
                    self.act(sl[:], pg_[:], AF.Silu, r=[pg_], w=[sl])
                    P.op('dve', lambda e: e.tensor_tensor(out=actT[:, f, 512 * t2:512 * t2 + 512], in0=sl[:], in1=pu_[:], op=ALU.mult), r=[sl, pu_], w=[(actT, f)])
            for dc in range(8):
                wo = self.wload(pout, wo_[:, :, dc * 128:dc * 128 + 128], 128, 22, 128)
                for t2 in range(2):
                    tok = slice(1024 * half + 512 * t2, 1024 * half + 512 * t2 + 512)
                    pp = pb[4 + (nb % 2)]; nb += 1
                    for f in range(22):
                        self.mm(pp[:], wo[:, f, :], actT[:, f, 512 * t2:512 * t2 + 512], start=(f == 0), stop=(f == 21), r=[wo, (actT, f)], w=[pp])
                    P.op('dve', lambda e: e.scalar_tensor_tensor(out=x1T[:, dc, tok], in0=pp[:], scalar=mod[:, 40 + dc, 0:1], in1=x1T[:, dc, tok], op0=ALU.mult, op1=ALU.add),
                         r=[pp, mod, x1T], w=[x1T])

    def final(self, outT):
        P, pb, x = self.P, self.pb, self.x1T
        gains = self.gains
        sq = P.sb([128, 8, 512], name='fsq'); rs = P.sb([128, 512], name='frs')
        epst = P.sb([128, 1], name='fepst')
        P.op('dve', lambda e: e.memset(epst[:], RMS_EPS), w=[epst])
        ov = outT.rearrange("(k p) t -> p k t", p=128)
        for tb in range(4):
            ts_ = slice(512 * tb, 512 * tb + 512)
            self.act(sq[:], x[:, :, ts_], AF.Square, r=[x], w=[sq])
            pp = pb[tb % 2]
            for k in range(8):
                self.mm(pp[:], self.ones[:], sq[:, k, :], start=(k == 0), stop=(k == 7), r=[sq, self.ones], w=[pp])
            self.act(rs[:], pp[:], AF.Sqrt, scale=1.0 / D, bias=epst[:], r=[pp, epst], w=[rs])
            P.op('dve', lambda e: e.reciprocal(out=rs[:], in_=rs[:]), r=[rs], w=[rs])
            for k in range(8):
                P.op('dve', lambda e: e.scalar_tensor_tensor(out=sq[:, k, :], in0=x[:, k, ts_], scalar=gains[:, k, 2:3], in1=rs[:], op0=ALU.mult, op1=ALU.mult),
                     r=[x, gains, rs], w=[sq])
            P.op('dve', lambda e: e.tensor_copy(out=epst[:], in_=epst[:]), r=[sq, epst], w=[sq, epst])
            P.dma('sp', ov[:, :, ts_], sq[:], r=[sq], group='out')

    def finish(self):
        P = self.P
        for gname in list(P.dsem):
            if gname.startswith('out'):
                P.wait_group('pool', gname)
        P.emit()
        return self.nc


_CACHE = {}


def _prep(inputs, b):
    f = lambda a: np.ascontiguousarray(a, dtype=np.float32)
    m = {
        'xT': f(inputs['x'][b].T), 'ctxT': f(inputs['ctx'][b].T),
        'cvec': f(np.stack([inputs['c'][b], inputs['c_ctx']])),
        'w_mod': f(inputs['w_mod'][0]), 'b_mod': f(inputs['b_mod'][0]),
        'norm_mix_g': f(inputs['norm_mix_g'][0]), 'norm_ffn_g': f(inputs['norm_ffn_g'][0]), 'norm_final_g': f(inputs['norm_final_g']),
        'w_in': f(inputs['w_in'][0]),
        'lru_conv_w': f(inputs['lru_conv_w'][0]), 'lru_conv_b': f(inputs['lru_conv_b'][0]),
        'lru_wa': f(inputs['lru_wa'][0]), 'lru_ba': f(inputs['lru_ba'][0]), 'lru_wx': f(inputs['lru_wx'][0]), 'lru_bx': f(inputs['lru_bx'][0]),
        'lru_lambda': f(inputs['lru_lambda'][0]), 'w_o_lru': f(inputs['w_o_lru'][0]),
        'rwkv_mu': f(inputs['rwkv_mu'][0]), 'rwkv_w0': f(inputs['rwkv_w0'][0]), 'rwkv_w2': f(inputs['rwkv_w2'][0]),
        'rwkv_a0': f(inputs['rwkv_a0'][0]), 'rwkv_a2': f(inputs['rwkv_a2'][0]), 'rwkv_g2': f(inputs['rwkv_g2'][0]),
        'rwkv_k_k': f(inputs['rwkv_k_k'][0]), 'rwkv_k_a': f(inputs['rwkv_k_a'][0]), 'rwkv_r_k': f(inputs['rwkv_r_k'][0].reshape(-1)),
        'rwkv_ln_g': f(inputs['rwkv_ln_g'][0]), 'rwkv_ln_b': f(inputs['rwkv_ln_b'][0]),
        'w_o_rwkv': f(inputs['w_o_rwkv'][0]), 'w_out': f(inputs['w_out'][0]),
        'w_ffn_in': f(inputs['w_ffn_in'][0]), 'w_ffn_out': f(inputs['w_ffn_out'][0]),
    }
    return m


def kernel(**inputs):
    if 'nc' not in _CACHE:
        _CACHE['nc'] = Builder().build()
    nc = _CACHE['nc']
    shared = _prep(inputs, 0)
    in_maps = []
    for b in range(8):
        m = dict(shared)
        m['xT'] = np.ascontiguousarray(np.asarray(inputs['x'][b], dtype=np.float32).T)
        m['ctxT'] = np.ascontiguousarray(np.asarray(inputs['ctx'][b], dtype=np.float32).T)
        m['cvec'] = np.ascontiguousarray(np.stack([inputs['c'][b], inputs['c_ctx']]).astype(np.float32))
        in_maps.append(m)
    res = run_bass_kernel_spmd(nc, in_maps, core_ids=list(range(8)))
    out = np.stack([np.ascontiguousarray(r['outT'].T) for r in res.results]).astype(np.float32)
    return out
```

```python
import contextlib
import numpy as np
import concourse.bass as bass
import concourse.mybir as mybir
from concourse.bass_utils import run_bass_kernel_spmd

F32 = mybir.dt.float32
BF16 = mybir.dt.bfloat16
AF = mybir.ActivationFunctionType
ALU = mybir.AluOpType

ENGS = ['pe', 'act', 'dve', 'pool', 'sp']
NCTX, NLAT, T = 256, 2048, 2304
D = 1024
LW, NBLK, BLK = 1280, 16, 80
RIN = 3488
DFF = 2816
RMS_EPS, GN_EPS = 1e-6, 64e-5


class _Rec:
    def __init__(self):
        self.call = None

    def __getattr__(self, name):
        def f(*a, **k):
            self.call = (name, a, k)
            return self
        return f


class Prog:
    def __init__(self, nc):
        self.nc = nc
        self.root = contextlib.ExitStack()
        self.stacks = [self.root]
        self.ops = {e: [] for e in ENGS}
        self.cnt = {e: 0 for e in ENGS}
        self.seen = {e: {} for e in ENGS}
        self.last_w = {}
        self.readers = {}
        self.esem = {e: self.root.enter_context(nc.semaphore('s_' + e)) for e in ENGS if e != 'sp'}
        self.dsem = {}
        self.fence = []
        self.ntile = 0

    def sb(self, shape, dt=F32, name=None):
        self.ntile += 1
        return self.stacks[-1].enter_context(self.nc.sbuf_tensor(f'{name or "t"}{self.ntile}', list(shape), dt))

    def sbm(self, shape, dt=F32, name=None):
        self.ntile += 1
        st = contextlib.ExitStack()
        t = st.enter_context(self.nc.sbuf_tensor(f'{name or "t"}{self.ntile}', list(shape), dt))
        return t, st

    def _set_fence(self):
        self.fence = [('E', e, self.cnt[e]) for e in self.esem if self.cnt[e] > 0]
        self.fence += [('D', s_, v) for s_, v in self.dsem.values() if v > 0]

    def free(self, stacks):
        for st in stacks:
            st.close()
        self._set_fence()

    def ps(self, shape, dt=F32, name=None):
        self.ntile += 1
        return self.root.enter_context(self.nc.psum_tensor(f'{name or "p"}{self.ntile}', list(shape), dt))

    @contextlib.contextmanager
    def scope(self):
        st = contextlib.ExitStack()
        self.stacks.append(st)
        try:
            yield
        finally:
            self.stacks.pop()
            st.close()
            self._set_fence()

    def _k(self, k):
        if isinstance(k, tuple):
            return tuple(self._k(x) for x in k)
        if isinstance(k, (str, int)):
            return k
        return id(k)

    @staticmethod
    def _tkey(tok):
        return ('E', tok[1]) if tok[0] == 'E' else ('D', id(tok[1]))

    def _deps(self, eng, r, w):
        deps = {}

        def add(tok):
            if tok is None:
                return
            if tok[0] == 'E' and tok[1] == eng == 'pe':
                return
            k = self._tkey(tok)
            if k not in deps or deps[k][2] < tok[2]:
                deps[k] = tok
        for tok in self.fence:
            add(tok)
        for k in r:
            add(self.last_w.get(k))
        for k in w:
            add(self.last_w.get(k))
            for t in self.readers.get(k, ()):
                add(t)
        out = []
        seen = self.seen[eng]
        for k, tok in deps.items():
            if seen.get(k, 0) >= tok[2]:
                continue
            seen[k] = tok[2]
            out.append(tok)
        return out

    def _commit(self, tok, r, w):
        for k in w:
            self.last_w[k] = tok
            self.readers[k] = []
        for k in r:
            if k in w:
                continue
            self.readers.setdefault(k, []).append(tok)

    def op(self, eng, fn, r=(), w=()):
        r = [self._k(k) for k in r]
        w = [self._k(k) for k in w]
        waits = self._deps(eng, r, w)
        self.cnt[eng] += 1
        tok = ('E', eng, self.cnt[eng])
        rec = _Rec()
        fn(rec)
        name, a, k = rec.call
        self.ops[eng].append((waits, lambda e: getattr(e, name)(*a, **k), tok))
        self._commit(tok, r, w)

    def dma(self, q, out, in_, r=(), w=(), group=None, **kw):
        r = [self._k(k) for k in r]
        w = [self._k(k) for k in w]
        waits = self._deps(q, r, w)
        g = group or ('dma_' + str(w[0] if w else 'x'))
        if g not in self.dsem:
            self.dsem[g] = [self.root.enter_context(self.nc.semaphore('d%d' % len(self.dsem))), 0]
        ent = self.dsem[g]
        ent[1] += 16
        tok = ('D', ent[0], ent[1])
        self.ops[q].append((waits, lambda e: e.dma_start(out=out, in_=in_, **kw), tok))
        self._commit(tok, r, w)

    def wait_group(self, eng, group):
        ent = self.dsem[group]
        self.ops[eng].append(([('D', ent[0], ent[1])], None, None))

    def emit(self):
        engobj = {'pe': 'tensor', 'act': 'scalar', 'dve': 'vector', 'pool': 'gpsimd', 'sp': 'sync'}
        waited = {e: set() for e in ENGS}
        for e in ENGS:
            for waits, fn, tok in self.ops[e]:
                for t in waits:
                    if t[0] == 'E':
                        waited[t[1]].add(t[2])
        rank = {e: {s_: i + 1 for i, s_ in enumerate(sorted(waited[e]))} for e in ENGS}
        with self.nc.Block() as block:
            for e in ENGS:
                ops = self.ops[e]

                def body(eng, ops=ops):
                    for waits, fn, tok in ops:
                        for t in waits:
                            if t[0] == 'E':
                                eng.wait_ge(self.esem[t[1]], rank[t[1]][t[2]])
                            else:
                                eng.wait_ge(t[1], t[2])
                        if fn is not None:
                            ins = fn(eng)
                            if tok[0] == 'D':
                                ins.then_inc(tok[1], 16)
                            elif tok[2] in rank[tok[1]]:
                                ins.then_inc(self.esem[tok[1]], 1)
                getattr(block, engobj[e])(body)


def rsl(start, n, step):
    if step > 0:
        return slice(start, start + n)
    stop = start - n
    return slice(start, stop if stop >= 0 else None, -1)


class Builder:
    def __init__(self, taps=(), stop_after=None):
        self.taps = set(taps)
        self.stop_after = stop_after
        nc = self.nc = bass.Bass("TRN2", target_bir_lowering=False)
        self.P = Prog(nc)
        self.din = {}
        self.tapout = {}
        self._castn = 0

    def inp(self, name, shape):
        self.din[name] = self.nc.dram_tensor(name, list(shape), F32, kind="ExternalInput").ap()
        return self.din[name]

    def mm(self, out, lhsT, rhs, start=True, stop=True, r=(), w=(), **kw):
        self.P.op('pe', lambda e: e.matmul(out, lhsT=lhsT, rhs=rhs, start=start, stop=stop, **kw), r=r, w=w)

    def act(self, out, in_, func, r=(), w=(), **kw):
        self.P.op('act', lambda e: e.activation(out=out, in_=in_, func=func, **kw), r=r, w=w)

    def tap(self, name, tile_ap, shape, r, dt=F32):
        if name not in self.taps:
            return
        o = self.nc.dram_tensor('tap_' + name, list(shape), dt, kind="ExternalOutput").ap()
        self.P.dma('pool', o, tile_ap, r=r, group='out_' + name)
        self.tapout[name] = o

    def cols(self, rows, n, chunk, name):
        P = self.P
        R = len(rows)
        nch = (n + chunk - 1) // chunk
        out = P.sb([chunk, nch, R], name=name)
        with P.scope():
            st = P.sb([R, n], name='colst')
            for i, rw in enumerate(rows):
                P.dma('sp', st[i:i + 1, :], rw.rearrange("(o n) -> o n", o=1), w=[(st, i)], group='colst')
            pp = self.pb[0]
            assert nch * R <= 512
            for c in range(nch):
                cs = min(chunk, n - c * chunk)
                P.op('pe', lambda e, c=c, cs=cs: e.transpose(out=pp[0:cs, c * R:(c + 1) * R], in_=st[0:R, c * chunk:c * chunk + cs],
                                                             identity=self.ident[0:R, 0:R]),
                     r=[(st, i) for i in range(R)] + [self.ident], w=[pp])
            P.op('dve', lambda e: e.tensor_copy(out=out[:].rearrange("p c r -> p (c r)"), in_=pp[0:chunk, 0:nch * R]), r=[pp], w=[out])
        return out

    def build(self):
        nc, P = self.nc, self.P
        inp = self.inp
        xT = inp('xT', [D, NLAT]); ctxT = inp('ctxT', [D, NCTX]); cvec = inp('cvec', [2, D])
        w_mod = inp('w_mod', [D, 6 * D]); b_mod = inp('b_mod', [6 * D])
        nmg = inp('norm_mix_g', [D]); nfg = inp('norm_ffn_g', [D]); nfin = inp('norm_final_g', [D])
        w_in = inp('w_in', [D, 8096])
        lru_conv_w = inp('lru_conv_w', [2, 4, LW]); lru_conv_b = inp('lru_conv_b', [2, LW])
        lru_wa = inp('lru_wa', [2, NBLK, BLK, BLK]); lru_ba = inp('lru_ba', [2, LW])
        lru_wx = inp('lru_wx', [2, NBLK, BLK, BLK]); lru_bx = inp('lru_bx', [2, LW])
        lru_lam = inp('lru_lambda', [2, LW]); w_o_lru = inp('w_o_lru', [LW, D])
        mu = inp('rwkv_mu', [2, RIN]); w0 = inp('rwkv_w0', [2, D]); w2 = inp('rwkv_w2', [2, 64, D])
        a0 = inp('rwkv_a0', [2, D]); a2 = inp('rwkv_a2', [2, 64, D]); g2 = inp('rwkv_g2', [160, D])
        k_k = inp('rwkv_k_k', [D]); k_a = inp('rwkv_k_a', [D]); r_k = inp('rwkv_r_k', [D])
        ln_g = inp('rwkv_ln_g', [D]); ln_b = inp('rwkv_ln_b', [D])
        w_o_rwkv = inp('w_o_rwkv', [D, D]); w_out = inp('w_out', [D, D])
        w_ffn_in = inp('w_ffn_in', [D, 2 * DFF]); w_ffn_out = inp('w_ffn_out', [DFF, D])
        outT = nc.dram_tensor('outT', [D, NLAT], F32, kind="ExternalOutput").ap()

        self.pb = [P.ps([128, 512], F32, name='pb') for _ in range(7)]
        self.pbh = P.ps([128, 1024], BF16, name='pbh')
        pb = self.pb

        ones = P.sb([128, 128], name='ones')
        P.op('dve', lambda e: e.memset(ones[:], 1.0), w=[ones])
        self.ident = ident = P.sb([128, 128], name='ident')
        P.op('pool', lambda e: e.affine_select(out=ident[:], in_=ones[:], pattern=[[-1, 128]], compare_op=ALU.is_equal, fill=0.0,
                                               base=0, channel_multiplier=1), r=[ones], w=[ident])
        identb = P.sb([128, 128], BF16, name='identb')
        P.op('dve', lambda e: e.tensor_copy(out=identb[:], in_=ident[:]), r=[ident], w=[identb])
        bones = P.sb([128, 128], name='bones')
        P.op('dve', lambda e: e.memset(bones[:], 0.0), w=[bones])
        P.op('dve', lambda e: e.memset(bones[0:64, 0:64], 1.0), w=[bones])
        P.op('dve', lambda e: e.memset(bones[64:128, 64:128], 1.0), w=[bones])
        self.ones, self.identb, self.bones = ones, identb, bones

        gains = self.cols([nmg, nfg, nfin], D, 128, 'gains')
        cT = self.cols([cvec[0], cvec[1]], D, 128, 'cT')
        bm = self.cols([b_mod], 6 * D, 128, 'bm')
        mod = P.sb([128, 48, 2], name='mod')
        with P.scope():
            sc = P.sb([128, 8, 2], name='sc')
            self.act(sc[:], cT[:], AF.Silu, r=[cT], w=[sc])
            wm = [P.sb([128, 8, 768], name='wm') for _ in range(2)]
            pm = pb[1]
            wv = w_mod.rearrange("(k p) n -> p k n", p=128)
            for jb in range(8):
                buf = wm[jb % 2]
                for k2 in range(2):
                    P.dma('sp' if k2 == 0 else 'act', buf[:, 4 * k2:4 * k2 + 4, :], wv[:, 4 * k2:4 * k2 + 4, jb * 768:(jb + 1) * 768],
                          w=[(buf, k2)], group='wm%d' % (jb % 2))
                for jj in range(6):
                    j = jb * 6 + jj
                    for k in range(8):
                        self.mm(pm[:, 2 * j:2 * j + 2], buf[:, k, jj * 128:(jj + 1) * 128], sc[:, k, :], start=(k == 0), stop=(k == 7),
                                r=[(buf, 0), (buf, 1), sc], w=[pm])
            for n in range(2):
                P.op('dve', lambda e, n=n: e.tensor_tensor(out=mod[:, :, n], in0=pm[:, 0:96].rearrange("p (j n) -> p j n", n=2)[:, :, n],
                                                           in1=bm[:, :, 0], op=ALU.add), r=[pm, bm], w=[mod])
        self.tap('mod', mod[:], [128, 48, 2], [mod])
        G1 = P.sb([128, 8, 2], name='G1'); G2 = P.sb([128, 8, 1], name='G2')
        for n in range(2):
            P.op('dve', lambda e, n=n: e.scalar_tensor_tensor(out=G1[:, :, n], in0=mod[:, 8:16, n], scalar=1.0, in1=gains[:, :, 0],
                                                              op0=ALU.add, op1=ALU.mult), r=[mod, gains], w=[G1])
        P.op('dve', lambda e: e.scalar_tensor_tensor(out=G2[:, :, 0], in0=mod[:, 32:40, 0], scalar=1.0, in1=gains[:, :, 1],
                                                     op0=ALU.add, op1=ALU.mult), r=[mod, gains], w=[G2])
        self.mod, self.gains = mod, gains

        arena = P.sb([128, 8 * T + 8 * NLAT], BF16, name='arena')
        hT = arena[:, 0:8 * T].rearrange("p (k t) -> p k t", t=T)
        xv = xT.rearrange("(k p) t -> p k t", p=128)
        cv = ctxT.rearrange("(k p) t -> p k t", p=128)
        self.modulate(hT, [(cv, 0, 256, 0, 1)] + [(xv, 512 * i, 512, 256 + 512 * i, 0) for i in range(4)], G1, mod, 0)
        self.tap('hT', hT[:], [128, 8, T], [(hT, i) for i in range(5)], dt=BF16)
        self.hT = hT
        if self.stop_after == 'B':
            return self.finish()
        self.rwT = arena[:, 8 * T:8 * T + 8 * NLAT].rearrange("p (k t) -> p k t", t=NLAT)
        if self.stop_after != 'C':
            with P.scope():
                self.rwkv()
        self.tap('rw', self.rwT[:], [128, 8, NLAT], [self.rwT], dt=BF16)
        if self.stop_after in ('D', 'D0'):
            return self.finish()
        self.lruT, lru_st = P.sbm([128, 10, NLAT], BF16, name='lruT')
        with P.scope():
            self.lru()
        self.tap('lru', self.lruT[:], [128, 10, NLAT], [self.lruT], dt=BF16)
        if self.stop_after == 'C':
            return self.finish()
        self.mT, m_st = P.sbm([128, 8, NLAT], BF16, name='mT')
        with P.scope():
            self.merge()
        P.free([])
        self.x1T = arena[:].bitcast(F32)[:, 0:8 * NLAT].rearrange("p (k t) -> p k t", t=NLAT)
        with P.scope():
            self.resid1()
        P.free([m_st, lru_st])
        self.tap('x1', self.x1T[:], [128, 8, NLAT], [self.x1T])
        if self.stop_after == 'E':
            return self.finish()
        self.h2T = P.sb([128, 8, NLAT], BF16, name='h2T')
        self.modulate(self.h2T, [(None, 512 * i, 512, 512 * i, 0) for i in range(4)], G2, mod, 24, src_sb=self.x1T)
        with P.scope():
            self.ffn()
        with P.scope():
            self.final(outT)
        return self.finish()

    def modulate(self, hT, blocks, G, mod, shift_j0, src_sb=None):
        P, pb = self.P, self.pb
        with P.scope():
            xb = [P.sb([128, 8, 512], name='xb') for _ in range(2)]
            sq = P.sb([128, 8, 512], name='sq')
            rs = P.sb([128, 512], name='rs')
            epst = P.sb([128, 1], name='epst')
            P.op('dve', lambda e: e.memset(epst[:], RMS_EPS), w=[epst])
            for bi, (src, so, n, do, mn) in enumerate(blocks):
                if src_sb is None:
                    x = xb[bi % 2]
                    for k2 in range(2):
                        P.dma('sp' if k2 == 0 else 'act', x[:, 4 * k2:4 * k2 + 4, 0:n], src[:, 4 * k2:4 * k2 + 4, so:so + n],
                              w=[(x, k2)], group='xb%d' % (bi % 2))
                    xr = [(x, 0), (x, 1)]
                    xa = lambda k, x=x, n=n: x[:, k, 0:n]
                    xall = x[:, :, 0:n]
                else:
                    xr = [src_sb]
                    xa = lambda k, so=so, n=n: src_sb[:, k, so:so + n]
                    xall = src_sb[:, :, so:so + n]
                self.act(sq[:, :, 0:n], xall, AF.Square, r=xr, w=[sq])
                pp = pb[bi % 2]
                for k in range(8):
                    self.mm(pp[:, 0:n], self.ones[:], sq[:, k, 0:n], start=(k == 0), stop=(k == 7), r=[sq, self.ones], w=[pp])
                self.act(rs[:, 0:n], pp[:, 0:n], AF.Sqrt, scale=1.0 / D, bias=epst[:], r=[pp, epst], w=[rs])
                P.op('dve', lambda e, n=n: e.reciprocal(out=rs[:, 0:n], in_=rs[:, 0:n]), r=[rs], w=[rs])
                for k in range(8):
                    P.op('dve', lambda e, k=k, n=n, xa=xa: e.tensor_tensor(out=sq[:, k, 0:n], in0=xa(k), in1=rs[:, 0:n], op=ALU.mult),
                         r=xr + [rs], w=[sq])
                    self.act(hT[:, k, do:do + n], sq[:, k, 0:n], AF.Identity, scale=G[:, k, mn:mn + 1], bias=mod[:, shift_j0 + k, mn:mn + 1],
                             r=[sq, G, mod], w=[(hT, bi)])

    def zshift(self, cq, ncol, dsts, zbuf, A, wz, wzb, mixw, segs=((0, 1, 256, 0), (1, 258, 2048, 256))):
        P, pb, hT = self.P, self.pb, self.hT
        w_in = self.din['w_in']
        i = self.zcount = getattr(self, 'zcount', 0) + 1
        wf, wb = wz[i % 2], wzb[i % 2]
        if isinstance(zbuf, list):
            zbuf = zbuf[i % len(zbuf)]
        if isinstance(A, list):
            A = A[i % len(A)]
        c0 = 2560 + 128 * cq
        P.dma('sp', wf[:, :, 0:ncol], w_in.rearrange("(k p) n -> p k n", p=128)[:, :, c0:c0 + ncol], w=[wf], group='wz%d' % (i % 2))
        P.op('pool', lambda e: e.tensor_copy(out=wb[:, :, 0:ncol], in_=wf[:, :, 0:ncol]), r=[wf], w=[wb])
        hk = [(hT, j) for j in range(5)]
        lat = lambda k: hT[:, k, 256:2304].rearrange("p (r c) -> p c r", c=64)
        nblk = 0
        for (seg, zc, n, _) in segs:
            nb = 1 if seg == 0 else 4
            for bi in range(nb):
                pp = pb[(nblk + 5 * i) % 6]; nblk += 1
                bn = 256 if seg == 0 else 512
                for k in range(8):
                    if seg == 0:
                        self.mm(pp[0:ncol, 0:256], wb[:, k, 0:ncol], hT[:, k, 0:256], start=(k == 0), stop=(k == 7), r=[wb] + hk, w=[pp])
                    else:
                        self.mm(pp[0:ncol, 0:512], wb[:, k, 0:ncol], hT[:, k, 256 + 512 * bi:256 + 512 * bi + 512],
                                start=(k == 0), stop=(k == 7), r=[wb] + hk, w=[pp])
                if seg == 0:
                    P.op('act', lambda e: e.copy(out=zbuf[0:ncol, zc:zc + 256], in_=pp[0:ncol, 0:256]), r=[pp], w=[zbuf])
                else:
                    zo = zbuf[0:ncol, zc:zc + 2048].rearrange("p (c r) -> p r c", r=32)[:, 8 * bi:8 * bi + 8, :]
                    P.op('act', lambda e: e.copy(out=zo, in_=pp[0:ncol, 0:512].rearrange("p (r c) -> p r c", c=64)), r=[pp], w=[zbuf])
        for (seg, zc, n, _), dst in zip(segs, dsts):
            if dst is None:
                continue
            dt, do, key = dst
            P.op('dve', lambda e, zc=zc, n=n: e.tensor_scalar(out=A[0:ncol, 0:n], in0=zbuf[0:ncol, zc:zc + n], scalar1=mixw[0:ncol, cq, 2:3],
                                                              scalar2=None, op0=ALU.mult), r=[zbuf, mixw], w=[A])
            P.op('dve', lambda e, zc=zc, n=n: e.scalar_tensor_tensor(out=A[0:ncol, 0:n], in0=zbuf[0:ncol, zc - 1:zc - 1 + n], scalar=mixw[0:ncol, cq, 0:1],
                                                                     in1=A[0:ncol, 0:n], op0=ALU.mult, op1=ALU.add), r=[zbuf, mixw, A], w=[A])
            P.op('dve', lambda e, zc=zc, n=n, dt=dt, do=do: e.scalar_tensor_tensor(out=dt[0:ncol, do:do + n], in0=zbuf[0:ncol, zc + 1:zc + 1 + n],
                                                                                   scalar=mixw[0:ncol, cq, 1:2], in1=A[0:ncol, 0:n],
                                                                                   op0=ALU.mult, op1=ALU.add), r=[zbuf, mixw, A], w=[key])

    def rwkv(self):
        P, pb, pbh, hT, din = self.P, self.pb, self.pbh, self.hT, self.din
        rwT = self.rwT
        CW = -0.5 * float(np.exp(-0.5))
        mixw = self.cols([din['rwkv_mu'][0], din['rwkv_mu'][1], din['rwkv_mu'][0]], RIN, 128, 'mixw')
        P.op('dve', lambda e: e.tensor_tensor(out=mixw[:, :, 2], in0=mixw[:, :, 0], in1=mixw[:, :, 1], op=ALU.add), r=[mixw], w=[mixw])
        P.op('dve', lambda e: e.tensor_scalar(out=mixw[:, :, 2], in0=mixw[:, :, 2], scalar1=-1.0, scalar2=1.0, op0=ALU.mult, op1=ALU.add), r=[mixw], w=[mixw])
        chp = self.cols([din['rwkv_w0'][0], din['rwkv_w0'][1], din['rwkv_a0'][0], din['rwkv_a0'][1], din['rwkv_k_k'], din['rwkv_k_a'],
                         din['rwkv_r_k'], din['rwkv_ln_g'], din['rwkv_ln_b']], D, 128, 'chp')
        hp2 = P.sb([128, 8, 6], name='hp2')
        P.op('dve', lambda e: e.tensor_scalar(out=hp2[:, :, 0:4], in0=chp[:, :, 0:4], scalar1=0.5, scalar2=None, op0=ALU.mult), r=[chp], w=[hp2])
        P.op('dve', lambda e: e.tensor_scalar(out=hp2[:, :, 4:5], in0=chp[:, :, 5:6], scalar1=0.5, scalar2=None, op0=ALU.mult), r=[chp], w=[hp2])
        P.op('dve', lambda e: e.tensor_scalar(out=hp2[:, :, 5:6], in0=chp[:, :, 5:6], scalar1=-0.5, scalar2=1.0, op0=ALU.mult, op1=ALU.add), r=[chp], w=[hp2])
        gneps = P.sb([128, 1], name='gneps')
        P.op('dve', lambda e: e.memset(gneps[:], GN_EPS), w=[gneps])
        w2s = P.sb([128, D], BF16, name='w2s'); a2s = P.sb([128, D], BF16, name='a2s')
        g2a = P.sb([128, D], BF16, name='g2a'); g2b = P.sb([32, D], BF16, name='g2b')
        with P.scope():
            st = P.sb([128, D], name='lst')
            for src, dst, npart in ((din['rwkv_w2'].rearrange("d r c -> (d r) c"), w2s, 128), (din['rwkv_a2'].rearrange("d r c -> (d r) c"), a2s, 128),
                                    (din['rwkv_g2'][0:128, :], g2a, 128), (din['rwkv_g2'][128:160, :], g2b, 32)):
                P.dma('sp', st[0:npart, :], src, w=[st], group='lst')
                P.op('dve', lambda e, dst=dst, npart=npart: e.tensor_copy(out=dst[0:npart, :], in_=st[0:npart, :]), r=[st], w=[dst])
        msk = {}
        onesb = P.sb([128, 4, 64], BF16, name='onesb')
        P.op('dve', lambda e: e.memset(onesb[:], 1.0), w=[onesb])
        for nm, op, sgn in (('su', ALU.is_gt, -1), ('sl', ALU.is_gt, 1), ('iu', ALU.is_ge, -1), ('id', ALU.is_equal, 1)):
            m = P.sb([128, 4, 64], BF16, name='m' + nm)
            for e_ in range(2):
                P.op('pool', lambda e, m=m, op=op, sgn=sgn, e_=e_: e.affine_select(out=m[64 * e_:64 * e_ + 64], in_=onesb[64 * e_:64 * e_ + 64],
                                                                                   pattern=[[0, 4], [-sgn, 64]], compare_op=op, fill=0.0,
                                                                                   base=0, channel_multiplier=sgn), r=[onesb], w=[m])
            msk[nm] = m
        cmask = P.sb([128, 256], name='cmask')
        P.op('dve', lambda e: e.memset(cmask[:], 1.0), w=[cmask])
        P.op('dve', lambda e: e.memset(cmask[:, 0:256:64], 0.0), w=[cmask])
        twd = P.sb([128, T], BF16, name='twd'); adb = P.sb([128, T], BF16, name='adb')
        sgd1 = P.sb([128, NLAT], BF16, name='sgd1'); sgd2 = P.sb([32, NLAT], BF16, name='sgd2')

        with P.scope():
            zbuf = [P.sb([128, 2307], name='zbuf') for _ in range(2)]; A = [P.sb([128, 2048], name='zA') for _ in range(2)]
            wz = [P.sb([128, 8, 128], name='wz') for _ in range(2)]; wzb = [P.sb([128, 8, 128], BF16, name='wzb') for _ in range(2)]
            for zb in zbuf:
                P.op('pool', lambda e, zb=zb: e.memset(zb[:], 0.0), w=[zb])
            tmp = P.sb([128, T], name='ltmp')
            self.zshift(24, 128, [(tmp, 0, tmp), (tmp, 256, tmp)], zbuf, A, wz, wzb, mixw)
            self.act(twd[:], tmp[:], AF.Tanh, r=[tmp], w=[twd])
            self.zshift(25, 128, [(adb, 0, adb), (adb, 256, adb)], zbuf, A, wz, wzb, mixw)
            for cq, ncol, dst in ((26, 128, sgd1), (27, 32, sgd2)):
                self.zshift(cq, ncol, [None, (tmp, 256, tmp)], zbuf, A, wz, wzb, mixw)
                self.act(tmp[0:ncol, 256:T], tmp[0:ncol, 256:T], AF.Tanh, scale=0.5, r=[tmp], w=[tmp])
                P.op('dve', lambda e, dst=dst, ncol=ncol: e.tensor_scalar(out=dst[0:ncol, :], in0=tmp[0:ncol, 256:T], scalar1=0.5, scalar2=0.5,
                                                                          op0=ALU.mult, op1=ALU.add), r=[tmp], w=[dst])

        self.tap('twd', twd[:], [128, T], [twd], dt=BF16)
        self.tap('adb', adb[:], [128, T], [adb], dt=BF16)
        self.tap('sgd1', sgd1[:], [128, NLAT], [sgd1], dt=BF16)
        for hp in range(8):
            if self.stop_after == 'D0' and hp > 0:
                break
            with P.scope():
                rb = P.sb([128, T], BF16, name='rb'); kb = P.sb([128, T], BF16, name='kb'); vb = P.sb([128, T], BF16, name='vb')
                kkb = P.sb([128, T], BF16, name='kkb')
                y0 = P.sb([128, NLAT], name='y0'); bacc = P.sb([128, NLAT], name='bacc')
                with P.scope():
                    zbufs = [P.sb([128, 2307], name='zbuf') for _ in range(3)]; As = [P.sb([128, 2048], name='zA') for _ in range(2)]
                    wz = [P.sb([128, 8, 128], name='wz') for _ in range(2)]; wzb = [P.sb([128, 8, 128], BF16, name='wzb') for _ in range(2)]
                    for zb in zbufs:
                        P.op('pool', lambda e, zb=zb: e.memset(zb[:], 0.0), w=[zb])
                    for cq, dst in ((8 + hp, kb), (hp, rb), (16 + hp, vb)):
                        self.zshift(cq, 128, [(dst, 0, dst), (dst, 256, dst)], zbufs, As, wz, wzb, mixw)
                    kq = P.sb([128, T], name='kq'); A = P.sb([128, 1024], name='kA')
                    self.act(kq[:, 0:T], kb[:], AF.Identity, scale=chp[:, hp, 4:5], r=[kb, chp], w=[kq])
                    for bi, (o, n) in enumerate([(0, 512), (512, 512), (1024, 512), (1536, 512), (2048, 256)]):
                        P.op('dve', lambda e, o=o, n=n: e.tensor_tensor(out=A[:, 0:n], in0=kq[:, o:o + n], in1=kq[:, o:o + n], op=ALU.mult), r=[kq], w=[A])
                        pp = pb[bi % 5]
                        self.mm(pp[:, 0:n], self.bones[:], A[:, 0:n], r=[A, self.bones], w=[pp])
                        self.act(A[:, 512:512 + n], pp[:, 0:n], AF.Sqrt, r=[pp], w=[A])
                        P.op('dve', lambda e, n=n: e.tensor_scalar(out=A[:, 512:512 + n], in0=A[:, 512:512 + n], scalar1=1e-12, scalar2=None, op0=ALU.max), r=[A], w=[A])
                        P.op('dve', lambda e, n=n: e.reciprocal(out=A[:, 512:512 + n], in_=A[:, 512:512 + n]), r=[A], w=[A])
                        P.op('dve', lambda e, o=o, n=n: e.tensor_tensor(out=kkb[:, o:o + n], in0=kq[:, o:o + n], in1=A[:, 512:512 + n], op=ALU.mult), r=[A, kq], w=[kkb])
                if hp == 0:
                    for nm, t_ in (('rb', rb), ('kb', kb), ('vb', vb), ('kkb', kkb)):
                        self.tap(nm, t_[:], [128, T], [t_], dt=BF16)
                for j in range(8):
                    P.op('pool', lambda e: e.memset(y0[:, 256 * j:256 * j + 256], 0.0), w=[(y0, 256 * j)])
                    P.op('pool', lambda e: e.memset(bacc[:, 256 * j:256 * j + 256], 0.0), w=[(bacc, 256 * j)])
                C = dict(rb=rb, kb=kb, vb=vb, kkb=kkb, y0=y0, bacc=bacc, twd=twd, adb=adb, w2s=w2s, a2s=a2s, chp=chp, hp2=hp2, msk=msk,
                         cmask=cmask, CW=CW, rot=[0])
                with P.scope():
                    self.rw_rounds(hp, C)
                if hp == 0:
                    self.tap('y0', y0[:], [128, NLAT], [y0])
                    self.tap('bacc', bacc[:], [128, NLAT], [bacc])
                with P.scope():
                    yc = P.sb([128, 512], name='yc'); sq = P.sb([128, 512], name='sq2'); rstd = P.sb([128, 512], name='rstd'); tt = P.sb([128, 512], name='tt')
                    for i in range(4):
                        c0 = 512 * i
                        pm, pv, pg = pb[0], pb[1], pb[2]
                        self.mm(pm[:], self.bones[:], y0[:, c0:c0 + 512], r=[y0, self.bones], w=[pm])
                        P.op('dve', lambda e, c0=c0: e.scalar_tensor_tensor(out=yc[:], in0=pm[:], scalar=-1.0 / 64, in1=y0[:, c0:c0 + 512], op0=ALU.mult, op1=ALU.add),
                             r=[pm, y0], w=[yc])
                        P.op('dve', lambda e: e.tensor_tensor(out=sq[:], in0=yc[:], in1=yc[:], op=ALU.mult), r=[yc], w=[sq])
                        self.mm(pv[:], self.bones[:], sq[:], r=[sq, self.bones], w=[pv])
                        self.act(rstd[:], pv[:], AF.Sqrt, scale=1.0 / 64, bias=gneps[:], r=[pv, gneps], w=[rstd])
                        P.op('dve', lambda e: e.reciprocal(out=rstd[:], in_=rstd[:]), r=[rstd], w=[rstd])
                        P.op('dve', lambda e: e.tensor_tensor(out=yc[:], in0=yc[:], in1=rstd[:], op=ALU.mult), r=[yc, rstd], w=[yc])
                        self.act(yc[:], yc[:], AF.Identity, scale=chp[:, hp, 7:8], bias=chp[:, hp, 8:9], r=[yc, chp], w=[yc])
                        P.op('dve', lambda e, c0=c0: e.tensor_tensor(out=tt[:], in0=vb[:, 256 + c0:256 + c0 + 512], in1=bacc[:, c0:c0 + 512], op=ALU.mult), r=[vb, bacc], w=[tt])
                        P.op('dve', lambda e: e.tensor_tensor(out=tt[:], in0=tt[:], in1=yc[:], op=ALU.add), r=[tt, yc], w=[tt])
                        self.mm(pg[:], g2a[:, hp * 128:(hp + 1) * 128], sgd1[:, c0:c0 + 512], start=True, stop=False, r=[g2a, sgd1], w=[pg])
                        self.mm(pg[:], g2b[0:32, hp * 128:(hp + 1) * 128], sgd2[0:32, c0:c0 + 512], start=False, stop=True, r=[g2b, sgd2], w=[pg])
                        P.op('dve', lambda e, i=i: e.tensor_tensor(out=rwT[:, hp, :].rearrange("p (r c) -> p c r", c=64)[:, 16 * i:16 * i + 16, :],
                                                                   in0=tt[:].rearrange("p (c r) -> p c r", r=32), in1=pg[:].rearrange("p (c r) -> p c r", r=32),
                                                                   op=ALU.mult), r=[tt, pg], w=[rwT])

    def rw_alloc_dir(self):
        P = self.P
        S = {}
        S['A1'] = P.sb([128, 256], name='A1'); S['B1'] = P.sb([128, 256], name='B1'); S['C1'] = P.sb([128, 256], name='C1')
        S['s1'] = [{nm: P.sb([128, 256], BF16, name=nm) for nm in ('at', 'bt', 'kt', 'vs')} for _ in range(2)]
        S['rw'] = [{'rt': P.sb([128, 256], BF16, name='rt'), 'wc': P.sb([128, 4], name='wc')} for _ in range(4)]
        S['QX'] = [[P.sb([128, 4, 2, 64], BF16, name='QX') for _ in range(2)] for _ in range(2)]
        S['QT'] = [[P.sb([128, 4, 64], BF16, name='QT') for _ in range(2)] for _ in range(2)]
        S['AakT'] = [P.sb([128, 4, 64], BF16, name='AakT') for _ in range(2)]
        S['slots'] = []
        for _ in range(3):
            sl = {'tokT': P.sb([128, 4, 4, 64], BF16, name='tokT'), 'MT': P.sb([128, 4, 64], BF16, name='MT'), 'Xak': P.sb([128, 4, 64], BF16, name='Xak'),
                  'ArbT': P.sb([128, 4, 64], BF16, name='ArbT'), 'ArkT': P.sb([128, 4, 64], BF16, name='ArkT'), 'AhT': P.sb([128, 4, 64], BF16, name='AhT')}
            S['slots'].append(sl)
        S['Tst'] = P.sb([128, 64], name='Tst'); S['Tw'] = P.sb([128, 64], name='Tw'); S['Tb'] = P.sb([128, 64], BF16, name='Tb')
        S['Ub'] = P.sb([128, 64], BF16, name='Ub')
        for nm in ('Tst', 'Tw', 'Tb'):
            P.op('dve', lambda e, t=S[nm]: e.memset(t[:], 0.0), w=[S[nm]])
        return S

    def gen_S1(self, hp, d, g, S, C):
        P, pb, pbh = self.P, self.pb, self.pbh
        rb, kb, vb, kkb, bacc = C['rb'], C['kb'], C['vb'], C['kkb'], C['bacc']
        twd, adb, w2s, a2s, chp, hp2, msk, cmask, CW = (C[k] for k in ('twd', 'adb', 'w2s', 'a2s', 'chp', 'hp2', 'msk', 'cmask', 'CW'))
        A1, B1, C1 = (S[k] for k in ('A1', 'B1', 'C1'))
        at, bt, kt, vs = (S['s1'][g % 2][k] for k in ('at', 'bt', 'kt', 'vs'))
        rt, wc = S['rw'][g % 4]['rt'], S['rw'][g % 4]['wc']
        if d == 0:
            s0, step = 256 * g, 1
        else:
            s0, step = (0 if g == 0 else 2304 - 256 * g), -1
        nat = slice(s0, s0 + 256)
        loc = lambda t: t[:, rsl(0 if step > 0 else 255, 256, step)]
        hc = slice(hp * 128, (hp + 1) * 128); ds = slice(64 * d, 64 * d + 64)
        rot = C['rot']
        H = lambda e_: slice(64 * e_, 64 * e_ + 64)
        TP = lambda e_: (64 * e_, 64 * e_)

        def bank():
            rot[0] = (rot[0] + 1) % 4
            return pb[(0, 1, 2, 5)[rot[0]]]
        pp = bank()
        self.mm(pp[:, 0:256], w2s[ds, hc], twd[ds, nat], r=[w2s, twd], w=[pp])
        self.act(loc(A1), pp[:, 0:256], AF.Tanh, scale=0.5, bias=hp2[:, hp, d:d + 1], r=[pp, hp2], w=[A1])
        yield
        P.op('dve', lambda e: e.tensor_scalar(out=A1[:], in0=A1[:], scalar1=1.0, scalar2=CW, op0=ALU.add, op1=ALU.mult), r=[A1], w=[A1])
        P.op('dve', lambda e: e.tensor_tensor_scan(out=B1[:], data0=cmask[:, 0:256], data1=A1[:], initial=0.0, op0=ALU.mult, op1=ALU.add), r=[A1, cmask], w=[B1])
        P.op('dve', lambda e: e.tensor_tensor(out=A1[:], in0=B1[:], in1=A1[:], op=ALU.subtract), r=[A1, B1], w=[A1])
        yield
        self.act(C1[:], A1[:], AF.Exp, r=[A1], w=[C1])
        P.op('dve', lambda e: e.scalar_tensor_tensor(out=loc(at), in0=kkb[:, nat], scalar=-1.0, in1=loc(C1), op0=ALU.mult, op1=ALU.mult), r=[kkb, C1], w=[at])
        yield
        self.act(C1[:], B1[:], AF.Exp, r=[B1], w=[C1])
        P.op('dve', lambda e: e.tensor_copy(out=wc[:], in_=C1[:, 63:256:64]), r=[C1], w=[wc])
        P.op('dve', lambda e: e.tensor_tensor(out=loc(rt), in0=rb[:, nat], in1=loc(C1), op=ALU.mult), r=[rb, C1], w=[rt])
        yield
        self.act(C1[:], B1[:], AF.Exp, scale=-1.0, r=[B1], w=[C1])
        pp = bank()
        self.mm(pp[:, 0:256], a2s[ds, hc], adb[ds, nat], r=[a2s, adb], w=[pp])
        self.act(loc(A1), pp[:, 0:256], AF.Tanh, scale=0.5, bias=hp2[:, hp, 2 + d:3 + d], r=[pp, hp2], w=[A1])
        yield
        P.op('dve', lambda e: e.tensor_scalar(out=B1[:], in0=A1[:], scalar1=0.5, scalar2=0.5, op0=ALU.mult, op1=ALU.add), r=[A1], w=[B1])
        P.op('dve', lambda e: e.tensor_tensor(out=loc(B1), in0=loc(B1), in1=kkb[:, nat], op=ALU.mult), r=[B1, kkb], w=[B1])
        P.op('dve', lambda e: e.tensor_tensor(out=bt[:], in0=B1[:], in1=C1[:], op=ALU.mult), r=[B1, C1], w=[bt])
        yield
        P.op('dve', lambda e: e.tensor_scalar(out=A1[:], in0=A1[:], scalar1=hp2[:, hp, 4:5], scalar2=hp2[:, hp, 5:6], op0=ALU.mult, op1=ALU.add), r=[A1, hp2], w=[A1])
        P.op('dve', lambda e: e.tensor_tensor(out=loc(A1), in0=loc(A1), in1=kb[:, nat], op=ALU.mult), r=[A1, kb], w=[A1])
        P.op('dve', lambda e: e.tensor_tensor(out=kt[:], in0=A1[:], in1=C1[:], op=ALU.mult), r=[A1, C1], w=[kt])
        P.op('pool', lambda e: e.tensor_copy(out=loc(vs), in_=vb[:, nat]), r=[vb], w=[vs])
        yield
        if s0 >= 256:
            P.op('dve', lambda e: e.scalar_tensor_tensor(out=B1[:], in0=loc(A1), scalar=chp[:, hp, 6:7], in1=rb[:, nat], op0=ALU.mult, op1=ALU.mult),
                 r=[A1, chp, rb, bt], w=[B1])
            pp = bank()
            self.mm(pp[:, 0:256], self.bones[:], B1[:], r=[B1, self.bones], w=[pp])
            bo = s0 - 256
            P.op('dve', lambda e: e.tensor_tensor(out=bacc[:, bo:bo + 256], in0=bacc[:, bo:bo + 256], in1=pp[:, 0:256], op=ALU.add), r=[pp, (bacc, bo)], w=[(bacc, bo)])
            yield

    def gen_S2a(self, hp, d, g, S, C):
        P, pb, pbh = self.P, self.pb, self.pbh
        msk = C['msk']
        at, bt, kt, vs = (S['s1'][g % 2][k] for k in ('at', 'bt', 'kt', 'vs'))
        rt = S['rw'][g % 4]['rt']
        sl = S['slots'][g % 3]
        tokT = sl['tokT']
        rot = C['rot']
        H = lambda e_: slice(64 * e_, 64 * e_ + 64)
        TP = lambda e_: (64 * e_, 64 * e_)

        def bank():
            rot[0] = (rot[0] + 1) % 4
            return pb[(0, 1, 2, 5)[rot[0]]]
        v3 = lambda p: p[:, 0:256].rearrange("p (c t) -> p c t", t=64)
        v4 = lambda p: p[:, :].rearrange("p (c x) -> p c x", x=128)
        QXs, QTs, AakT = S['QX'][g % 2], S['QT'][g % 2], S['AakT'][g % 2]

        def neumann_level(lvl, QX, QT):
            QXn, QTn = QXs[lvl % 2], QTs[lvl % 2]
            last = lvl == 6
            if lvl == 1:
                pq = bank()
                for c in range(4):
                    for e_ in range(2):
                        self.mm(v3(pq)[H(e_), c, :], QT[H(e_), c, :], QX[H(e_), c, 0, :], r=[QX, QT], w=[pq], tile_position=TP(e_))
                P.op('act', lambda e: e.copy(out=QXn[:, :, 0, :], in_=v3(pq)), r=[pq], w=[QXn])
                P.op('pool', lambda e: e.tensor_copy(out=QXn[:, :, 1, :], in_=QX[:, :, 1, :]), r=[QX], w=[QXn])
            else:
                ppx = bank()
                for c in range(4):
                    for e_ in range(2):
                        if last:
                            self.mm(v4(ppx)[H(e_), c, 64:128], QT[H(e_), c, :], QX[H(e_), c, 1, :], r=[QX, QT], w=[ppx], tile_position=TP(e_))
                        else:
                            self.mm(v4(ppx)[H(e_), c, :], QT[H(e_), c, :], QX[H(e_), c, :, :].rearrange("p a b -> p (a b)"), r=[QX, QT], w=[ppx],
                                    tile_position=TP(e_))
                dstP = sl['MT'][:] if last else QXn[:, :, 1, :]
                P.op('dve', lambda e: e.tensor_tensor(out=dstP, in0=v4(ppx)[:, :, 64:128], in1=QX[:, :, 1, :], op=ALU.add), r=[ppx, QX], w=[sl['MT'] if last else QXn])
                if not last:
                    P.op('act', lambda e: e.copy(out=QXn[:, :, 0, :], in_=v4(ppx)[:, :, 0:64]), r=[ppx], w=[QXn])
            if not last:
                pqt = bank()
                for c in range(4):
                    for e_ in range(2):
                        self.mm(v3(pqt)[H(e_), c, :], QX[H(e_), c, 0, :], QT[H(e_), c, :], r=[QX, QT], w=[pqt], tile_position=TP(e_))
                P.op('act', lambda e: e.copy(out=QTn[:], in_=v3(pqt)), r=[pqt], w=[QTn])
            return QXn, QTn
        for qi, q in enumerate((at, bt, kt, vs)):
            for c in range(4):
                for e_ in range(2):
                    o = (qi % 2) * 256 + c * 64
                    P.op('pe', lambda e: e.transpose(out=pbh[H(e_), o:o + 64], in_=q[H(e_), 64 * c:64 * c + 64], identity=self.identb[H(e_), H(e_)],
                                                     tile_position=TP(e_)), r=[q, self.identb], w=[pbh])
            if qi % 2 == 1:
                P.op('act', lambda e: e.copy(out=tokT[:, qi - 1:qi + 1, :, :].rearrange("p q c j -> p (q c j)"), in_=pbh[:, 0:512]), r=[pbh], w=[tokT])
                yield
        cs = lambda q, c, e_: q[H(e_), 64 * c:64 * c + 64]

        def score(L, Rr, mk, dst, dkey):
            pp = bank()
            for c in range(4):
                for e_ in range(2):
                    self.mm(v3(pp)[H(e_), c, :], cs(L, c, e_), cs(Rr, c, e_), r=[L, Rr], w=[pp], tile_position=TP(e_))
            P.op('dve', lambda e: e.tensor_tensor(out=dst, in0=v3(pp), in1=msk[mk][:], op=ALU.mult), r=[pp, msk[mk]], w=[dkey])
        QX, QT = QXs[0], QTs[0]
        score(bt, at, 'su', QX[:, :, 0, :], QX)
        yield
        score(at, bt, 'sl', QT[:], QT)
        yield
        P.op('pool', lambda e: e.tensor_tensor(out=QX[:, :, 1, :], in0=QX[:, :, 0, :], in1=msk['id'][:], op=ALU.add), r=[QX, msk['id']], w=[QX])
        score(kt, at, 'su', AakT[:], AakT)
        yield
        score(bt, rt, 'iu', sl['ArbT'][:], sl['ArbT'])
        yield
        score(kt, rt, 'iu', sl['ArkT'][:], sl['ArkT'])
        yield
        for lvl in (1, 2, 3):
            QX, QT = neumann_level(lvl, QX, QT)
            yield

    def gen_S2b(self, hp, d, g, S, C):
        P, pb, pbh = self.P, self.pb, self.pbh
        msk = C['msk']
        at, bt, kt, vs = (S['s1'][g % 2][k] for k in ('at', 'bt', 'kt', 'vs'))
        rt = S['rw'][g % 4]['rt']
        sl = S['slots'][g % 3]
        tokT = sl['tokT']
        rot = C['rot']
        H = lambda e_: slice(64 * e_, 64 * e_ + 64)
        TP = lambda e_: (64 * e_, 64 * e_)

        def bank():
            rot[0] = (rot[0] + 1) % 4
            return pb[(0, 1, 2, 5)[rot[0]]]
        v3 = lambda p: p[:, 0:256].rearrange("p (c t) -> p c t", t=64)
        v4 = lambda p: p[:, :].rearrange("p (c x) -> p c x", x=128)
        QXs, QTs, AakT = S['QX'][g % 2], S['QT'][g % 2], S['AakT'][g % 2]

        def neumann_level(lvl, QX, QT):
            QXn, QTn = QXs[lvl % 2], QTs[lvl % 2]
            last = lvl == 6
            if lvl == 1:
                pq = bank()
                for c in range(4):
                    for e_ in range(2):
                        self.mm(v3(pq)[H(e_), c, :], QT[H(e_), c, :], QX[H(e_), c, 0, :], r=[QX, QT], w=[pq], tile_position=TP(e_))
                P.op('act', lambda e: e.copy(out=QXn[:, :, 0, :], in_=v3(pq)), r=[pq], w=[QXn])
                P.op('pool', lambda e: e.tensor_copy(out=QXn[:, :, 1, :], in_=QX[:, :, 1, :]), r=[QX], w=[QXn])
            else:
                ppx = bank()
                for c in range(4):
                    for e_ in range(2):
                        if last:
                            self.mm(v4(ppx)[H(e_), c, 64:128], QT[H(e_), c, :], QX[H(e_), c, 1, :], r=[QX, QT], w=[ppx], tile_position=TP(e_))
                        else:
                            self.mm(v4(ppx)[H(e_), c, :], QT[H(e_), c, :], QX[H(e_), c, :, :].rearrange("p a b -> p (a b)"), r=[QX, QT], w=[ppx],
                                    tile_position=TP(e_))
                dstP = sl['MT'][:] if last else QXn[:, :, 1, :]
                P.op('dve', lambda e: e.tensor_tensor(out=dstP, in0=v4(ppx)[:, :, 64:128], in1=QX[:, :, 1, :], op=ALU.add), r=[ppx, QX], w=[sl['MT'] if last else QXn])
                if not last:
                    P.op('act', lambda e: e.copy(out=QXn[:, :, 0, :], in_=v4(ppx)[:, :, 0:64]), r=[ppx], w=[QXn])
            if not last:
                pqt = bank()
                for c in range(4):
                    for e_ in range(2):
                        self.mm(v3(pqt)[H(e_), c, :], QX[H(e_), c, 0, :], QT[H(e_), c, :], r=[QX, QT], w=[pqt], tile_position=TP(e_))
                P.op('act', lambda e: e.copy(out=QTn[:], in_=v3(pqt)), r=[pqt], w=[QTn])
            return QXn, QTn
        QX, QT = QXs[1], QTs[1]
        for lvl in (4, 5, 6):
            QX, QT = neumann_level(lvl, QX, QT)
            yield
        MT = sl['MT']
        pxa = bank()
        for c in range(4):
            for e_ in range(2):
                self.mm(v3(pxa)[H(e_), c, :], AakT[H(e_), c, :], tokT[H(e_), 3, c, :], r=[AakT, tokT], w=[pxa], tile_position=TP(e_))
        P.op('act', lambda e: e.copy(out=sl['Xak'][:], in_=v3(pxa)), r=[pxa], w=[sl['Xak']])
        yield
        pA = bank()
        for c in range(4):
            for e_ in range(2):
                self.mm(v3(pA)[H(e_), c, :], tokT[H(e_), 0, c, :], MT[H(e_), c, :], r=[tokT, MT], w=[pA], tile_position=TP(e_))
        P.op('act', lambda e: e.copy(out=sl['AhT'][:], in_=v3(pA)), r=[pA], w=[sl['AhT']])
        yield

    def gen_Q(self, hp, d, g, S, C):
        P, pb = self.P, self.pb
        y0 = C['y0']
        sl = S['slots'][g % 3]
        tokT, MT, Xak, ArbT, ArkT, AhT = (sl[k] for k in ('tokT', 'MT', 'Xak', 'ArbT', 'ArkT', 'AhT'))
        rt, wc = S['rw'][g % 4]['rt'], S['rw'][g % 4]['wc']
        Tst, Tw, Tb, Ub = S['Tst'], S['Tw'], S['Tb'], S['Ub']
        pU, pT, pY = pb[3], pb[4], pb[6]
        pUv = pU[:, 0:64]
        pTv = pT[:, 0:64]
        pYv = pY[:, 256 * d:256 * d + 256].rearrange("p (c t) -> p c t", t=64)
        H = lambda e_: slice(64 * e_, 64 * e_ + 64)
        TP = lambda e_: (64 * e_, 64 * e_)
        latent = g >= 1
        for c in range(4):
            for e_ in range(2):
                self.mm(pUv[H(e_), :], MT[H(e_), c, :], Xak[H(e_), c, :], start=True, stop=False, r=[MT, Xak], w=[pU], tile_position=TP(e_))
            for e_ in range(2):
                self.mm(pUv[H(e_), :], AhT[H(e_), c, :], Tb[H(e_), :], start=False, stop=True, r=[AhT, Tb], w=[pU], tile_position=TP(e_))
            P.op('act', lambda e: e.copy(out=Ub[:], in_=pUv), r=[pU], w=[Ub])
            yield
            for e_ in range(2):
                self.mm(pTv[H(e_), :], tokT[H(e_), 1, c, :], Ub[H(e_), :], start=True, stop=False, r=[tokT, Ub], w=[pT], tile_position=TP(e_))
            for e_ in range(2):
                self.mm(pTv[H(e_), :], tokT[H(e_), 2, c, :], tokT[H(e_), 3, c, :], start=False, stop=True, r=[tokT], w=[pT], tile_position=TP(e_))
            if latent:
                for e_ in range(2):
                    self.mm(pYv[H(e_), c, :], Tb[H(e_), :], rt[H(e_), 64 * c:64 * c + 64], start=True, stop=False, r=[Tb, rt], w=[pY], tile_position=TP(e_))
                for e_ in range(2):
                    self.mm(pYv[H(e_), c, :], Ub[H(e_), :], ArbT[H(e_), c, :], start=False, stop=False, r=[Ub, ArbT], w=[pY], tile_position=TP(e_))
                for e_ in range(2):
                    self.mm(pYv[H(e_), c, :], tokT[H(e_), 3, c, :], ArkT[H(e_), c, :], start=False, stop=True, r=[tokT, ArkT], w=[pY], tile_position=TP(e_))
            wcc = wc[:, c:c + 1]
            P.op('dve', lambda e: e.scalar_tensor_tensor(out=Tb[:], in0=pTv, scalar=wcc, in1=Tw[:], op0=ALU.mult, op1=ALU.add), r=[pT, wc, Tw], w=[Tb])
            P.op('dve', lambda e: e.scalar_tensor_tensor(out=Tst[:], in0=pTv, scalar=wcc, in1=Tw[:], op0=ALU.mult, op1=ALU.add), r=[pT, wc, Tw], w=[Tst])
            yield
            if c < 3:
                P.op('dve', lambda e: e.tensor_scalar(out=Tw[:], in0=Tst[:], scalar1=wc[:, c + 1:c + 2], scalar2=None, op0=ALU.mult), r=[Tst, wc], w=[Tw])
        if latent:
            g0 = 256 * g
            if d == 0:
                ysl = slice(g0 - 256, g0); yk = (y0, g0 - 256)
            else:
                ysl = rsl(2303 - g0, 256, -1); yk = (y0, 2048 - g0)
            P.op('dve', lambda e: e.tensor_tensor(out=y0[:, ysl], in0=y0[:, ysl], in1=pY[:, 256 * d:256 * d + 256], op=ALU.add), r=[pY, yk], w=[yk])
        yield

    def rw_rounds(self, hp, C):
        P = self.P
        dirs = [self.rw_alloc_dir() for _ in range(2)]
        for R in range(12):
            gens = []
            if 3 <= R:
                for d in range(2):
                    S = dirs[d]
                    wc = S['rw'][(R - 3) % 4]['wc']
                    P.op('dve', lambda e, S=S, wc=wc: e.tensor_scalar(out=S['Tw'][:], in0=S['Tst'][:], scalar1=wc[:, 0:1], scalar2=None, op0=ALU.mult),
                         r=[S['Tst'], wc], w=[S['Tw']])
                    gens.append(self.gen_Q(hp, d, R - 3, S, C))
            if 2 <= R <= 10:
                for d in range(2):
                    gens.append(self.gen_S2b(hp, d, R - 2, dirs[d], C))
            if 1 <= R <= 9:
                for d in range(2):
                    gens.append(self.gen_S2a(hp, d, R - 1, dirs[d], C))
            if R <= 8:
                for d in range(2):
                    gens.append(self.gen_S1(hp, d, R, dirs[d], C))
            while gens:
                for gn in list(gens):
                    try:
                        next(gn)
                    except StopIteration:
                        gens.remove(gn)

    def rwkv_half(self, hp, d, half, rb, kb, vb, kkb, y0, bacc, Tst, Tb, twd, adb, w2s, a2s, chp, hp2, msk, cmask, CW):
        P, pb, pbh = self.P, self.pb, self.pbh
        h0, W = (0, 1280) if half == 0 else (1280, 1024)
        if d == 0:
            pieces = [(0, 256, 0, 1), (256, 1024, 256, 1)] if half == 0 else [(1280, 1024, 1280, 1)]
        else:
            pieces = [(0, 256, 255, -1), (1280, 1024, 1279, -1)] if half == 0 else [(256, 1024, 2303, -1)]
        sg = lambda t, s0, n, sig0, step, off=0, nn=None: t[:, rsl(sig0 - h0 + step * off, nn if nn is not None else n, step)]
        A1 = P.sb([128, 1280], name='A1'); B1 = P.sb([128, 1280], name='B1'); C1 = P.sb([128, 1280], name='C1')
        rt = P.sb([128, 1280], BF16, name='rt'); at = P.sb([128, 1280], BF16, name='at'); bt = P.sb([128, 1280], BF16, name='bt')
        kt = P.sb([128, 1280], BF16, name='kt'); vs = P.sb([128, 1280], BF16, name='vs')
        wcs = P.sb([128, 20], name='wcs')
        hc = slice(hp * 128, (hp + 1) * 128)
        ds = slice(64 * d, 64 * d + 64)

        def lora(wts, src, bias_col, dst):
            nb = 0
            for (s0, n, sig0, step) in pieces:
                for o in range(0, n, 512):
                    nn = min(512, n - o)
                    pp = pb[nb % 2]; nb += 1
                    self.mm(pp[:, 0:nn], wts[ds, hc], src[ds, s0 + o:s0 + o + nn], r=[wts, src], w=[pp])
                    self.act(sg(dst, s0, n, sig0, step, o, nn), pp[:, 0:nn], AF.Tanh, scale=0.5, bias=hp2[:, hp, bias_col:bias_col + 1], r=[pp, hp2], w=[dst])
        lora(w2s, twd, d, A1)
        P.op('dve', lambda e: e.tensor_scalar(out=A1[:, 0:W], in0=A1[:, 0:W], scalar1=1.0, scalar2=CW, op0=ALU.add, op1=ALU.mult), r=[A1], w=[A1])
        P.op('dve', lambda e: e.tensor_tensor_scan(out=B1[:, 0:W], data0=cmask[:, 0:W], data1=A1[:, 0:W], initial=0.0, op0=ALU.mult, op1=ALU.add),
             r=[A1, cmask], w=[B1])
        P.op('dve', lambda e: e.tensor_tensor(out=A1[:, 0:W], in0=B1[:, 0:W], in1=A1[:, 0:W], op=ALU.subtract), r=[A1, B1], w=[A1])
        self.act(C1[:, 0:W], A1[:, 0:W], AF.Exp, r=[A1], w=[C1])
        for pc in pieces:
            s0, n = pc[0], pc[1]
            P.op('dve', lambda e, pc=pc, s0=s0, n=n: e.scalar_tensor_tensor(out=sg(at, *pc), in0=kkb[:, s0:s0 + n], scalar=-1.0, in1=sg(C1, *pc),
                                                                            op0=ALU.mult, op1=ALU.mult), r=[kkb, C1], w=[at])
        self.act(C1[:, 0:W], B1[:, 0:W], AF.Exp, r=[B1, at], w=[C1])
        P.op('dve', lambda e: e.tensor_copy(out=wcs[:, 0:W // 64], in_=C1[:, 63:W:64]), r=[C1], w=[wcs])
        for pc in pieces:
            s0, n = pc[0], pc[1]
            P.op('dve', lambda e, pc=pc, s0=s0, n=n: e.tensor_tensor(out=sg(rt, *pc), in0=rb[:, s0:s0 + n], in1=sg(C1, *pc), op=ALU.mult), r=[rb, C1], w=[rt])
        self.act(C1[:, 0:W], B1[:, 0:W], AF.Exp, scale=-1.0, r=[B1, rt, wcs], w=[C1])
        lora(a2s, adb, 2 + d, A1)
        P.op('dve', lambda e: e.tensor_scalar(out=B1[:, 0:W], in0=A1[:, 0:W], scalar1=0.5, scalar2=0.5, op0=ALU.mult, op1=ALU.add), r=[A1], w=[B1])
        for pc in pieces:
            s0, n = pc[0], pc[1]
            P.op('dve', lambda e, pc=pc, s0=s0, n=n: e.tensor_tensor(out=sg(B1, *pc), in0=sg(B1, *pc), in1=kkb[:, s0:s0 + n], op=ALU.mult), r=[B1, kkb], w=[B1])
        P.op('dve', lambda e: e.tensor_tensor(out=bt[:, 0:W], in0=B1[:, 0:W], in1=C1[:, 0:W], op=ALU.mult), r=[B1, C1], w=[bt])
        P.op('dve', lambda e: e.tensor_scalar(out=A1[:, 0:W], in0=A1[:, 0:W], scalar1=hp2[:, hp, 4:5], scalar2=hp2[:, hp, 5:6], op0=ALU.mult, op1=ALU.add),
             r=[A1, hp2], w=[A1])
        for pc in pieces:
            s0, n = pc[0], pc[1]
            P.op('dve', lambda e, pc=pc, s0=s0, n=n: e.tensor_tensor(out=sg(A1, *pc), in0=sg(A1, *pc), in1=kb[:, s0:s0 + n], op=ALU.mult), r=[A1, kb], w=[A1])
        P.op('dve', lambda e: e.tensor_tensor(out=kt[:, 0:W], in0=A1[:, 0:W], in1=C1[:, 0:W], op=ALU.mult), r=[A1, C1], w=[kt])
        nb = 0
        for pc in pieces:
            s0, n, sig0, step = pc
            if s0 < 256:
                continue
            P.op('dve', lambda e, pc=pc, s0=s0, n=n: e.scalar_tensor_tensor(out=B1[:, 0:n], in0=sg(A1, *pc), scalar=chp[:, hp, 6:7], in1=rb[:, s0:s0 + n],
                                                                            op0=ALU.mult, op1=ALU.mult), r=[A1, chp, rb, bt], w=[B1])
            for o in range(0, n, 512):
                pp = pb[nb % 2]; nb += 1
                self.mm(pp[:], self.bones[:], B1[:, o:o + 512], r=[B1, self.bones], w=[pp])
                bo = s0 - 256 + o
                if d == 0:
                    P.op('act', lambda e, pp=pp, bo=bo: e.copy(out=bacc[:, bo:bo + 512], in_=pp[:]), r=[pp], w=[bacc])
                else:
                    P.op('dve', lambda e, pp=pp, bo=bo: e.tensor_tensor(out=bacc[:, bo:bo + 512], in0=bacc[:, bo:bo + 512], in1=pp[:], op=ALU.add), r=[pp, bacc], w=[bacc])
        for pc in pieces:
            s0, n = pc[0], pc[1]
            P.op('pool', lambda e, pc=pc, s0=s0, n=n: e.tensor_copy(out=sg(vs, *pc), in_=vb[:, s0:s0 + n]), r=[vb], w=[vs])

        if hp == 0 and half == 0:
            for nm, t_ in (('rt', rt), ('at', at), ('bt', bt), ('kt', kt), ('vs', vs)):
                self.tap(nm + str(d), t_[:], [128, 1280], [t_], dt=BF16)
            self.tap('wcs' + str(d), wcs[:], [128, 20], [wcs])
            self.tap('kd' + str(d), A1[:], [128, 1280], [A1])
        tokT = P.sb([64, 4, 4, 128], BF16, name='tokT')
        Qs = [P.sb([64, 8, 64], BF16, name='Q') for _ in range(2)]; QTs = [P.sb([64, 8, 64], BF16, name='QT') for _ in range(2)]
        Xs = [P.sb([64, 8, 64], BF16, name='X') for _ in range(2)]
        AakT = P.sb([64, 8, 64], BF16, name='AakT'); ArbT = P.sb([64, 8, 64], BF16, name='ArbT'); ArkT = P.sb([64, 8, 64], BF16, name='ArkT')
        Xak = P.sb([64, 8, 64], BF16, name='Xak'); AhT = P.sb([128, 4, 64], BF16, name='AhT')
        Ub = P.sb([64, 2, 64], BF16, name='Ub'); Ts = P.sb([128, 64], name='Ts')
        pU, pT, pY, pA = pb[4], pb[5], pb[6], pb[0]
        pYv = pb[6][:, 0:256].rearrange("p (c t) -> p c t", t=64)
        pAv = pb[0][:, 0:256].rearrange("p (c t) -> p c t", t=64)
        pUv = pb[4][0:64, 0:128].rearrange("p (e v) -> p e v", v=64)
        pTv = pb[5][:, 0:64]
        bank = [0]

        def nextbank():
            bank[0] = (bank[0] + 1) % 2
            return pb[2 + bank[0]]
        v3 = lambda p: p[0:64, :].rearrange("p (i t) -> p i t", t=64)
        for gi in range(W // 256):
            loc = 256 * gi
            g0 = h0 + loc
            latent = g0 >= 256
            for qi, q in enumerate((at, bt, kt, vs)):
                for c in range(4):
                    P.op('pe', lambda e, qi=qi, q=q, c=c: e.transpose(out=pbh[0:64, (qi % 2) * 512 + c * 128:(qi % 2) * 512 + (c + 1) * 128],
                                                                      in_=q[:, loc + 64 * c:loc + 64 * c + 64], identity=self.identb[:]),
                         r=[q, self.identb], w=[pbh])
                if qi % 2 == 1:
                    P.op('act', lambda e, qi=qi: e.copy(out=tokT[:, qi - 1:qi + 1, :, :].rearrange("p q c j -> p (q c j)"), in_=pbh[0:64, :]), r=[pbh], w=[tokT])
            cs = lambda q, c, e_: q[64 * e_:64 * e_ + 64, loc + 64 * c:loc + 64 * c + 64]

            def score(L, Rr, mk, dst):
                pp = nextbank()
                for c in range(4):
                    for e_ in range(2):
                        self.mm(v3(pp)[:, 2 * c + e_, :], cs(L, c, e_), cs(Rr, c, e_), r=[L, Rr], w=[pp])
                P.op('dve', lambda e: e.tensor_tensor(out=dst[:], in0=v3(pp), in1=msk[mk][:], op=ALU.mult), r=[pp, msk[mk]], w=[dst])
            Q, QT, X = Qs[0], QTs[0], Xs[0]
            score(bt, at, 'su', Q)
            score(at, bt, 'sl', QT)
            score(kt, at, 'su', AakT)
            score(bt, rt, 'iu', ArbT)
            score(kt, rt, 'iu', ArkT)
            P.op('dve', lambda e, Q=Q, X=X: e.tensor_tensor(out=X[:], in0=Q[:], in1=msk['id'][:], op=ALU.add), r=[Q, msk['id']], w=[X])
            for lvl in range(2, 7):
                Qn, QTn, Xn = Qs[(lvl + 1) % 2], QTs[(lvl + 1) % 2], Xs[(lvl + 1) % 2]
                if lvl < 6:
                    pq = nextbank()
                    for i in range(8):
                        self.mm(v3(pq)[:, i, :], QT[:, i, :], Q[:, i, :], r=[Q, QT], w=[pq])
                    P.op('act', lambda e, pq=pq, Qn=Qn: e.copy(out=Qn[:], in_=v3(pq)), r=[pq], w=[Qn])
                pqt = nextbank()
                for i in range(8):
                    self.mm(v3(pqt)[:, i, :], Q[:, i, :], QT[:, i, :], r=[Q, QT], w=[pqt])
                P.op('act', lambda e, pqt=pqt, QTn=QTn: e.copy(out=QTn[:], in_=v3(pqt)), r=[pqt], w=[QTn])
                px = nextbank()
                for i in range(8):
                    self.mm(v3(px)[:, i, :], QTn[:, i, :], X[:, i, :], r=[QTn, X], w=[px])
                P.op('dve', lambda e, px=px, X=X, Xn=Xn: e.tensor_tensor(out=Xn[:], in0=v3(px), in1=X[:], op=ALU.add), r=[px, X], w=[Xn])
                Q, QT, X = Qn, QTn, Xn
            MT = X
            pxa = nextbank()
            for c in range(4):
                for e_ in range(2):
                    self.mm(v3(pxa)[:, 2 * c + e_, :], AakT[:, 2 * c + e_, :], tokT[:, 3, c, 64 * e_:64 * e_ + 64], r=[AakT, tokT], w=[pxa])
            P.op('act', lambda e, pxa=pxa: e.copy(out=Xak[:], in_=v3(pxa)), r=[pxa], w=[Xak])
            for c in range(4):
                for e_ in range(2):
                    self.mm(pAv[64 * e_:64 * e_ + 64, c, :], tokT[:, 0, c, 64 * e_:64 * e_ + 64], MT[:, 2 * c + e_, :], r=[tokT, MT], w=[pA],
                            tile_position=(0, 64 * e_))
            P.op('act', lambda e: e.copy(out=AhT[:], in_=pAv), r=[pA], w=[AhT])
            if hp == 0 and half == 0 and d == 0 and gi == 0:
                self.tap('MT', MT[:], [64, 8, 64], [MT], dt=BF16)
                self.tap('AhT', AhT[:], [128, 4, 64], [AhT], dt=BF16)
                self.tap('Xak', Xak[:], [64, 8, 64], [Xak], dt=BF16)
                self.tap('tokT', tokT[:], [64, 4, 4, 128], [tokT], dt=BF16)
                self.tap('ArbT', ArbT[:], [64, 8, 64], [ArbT], dt=BF16)
            for c in range(4):
                for e_ in range(2):
                    i = 2 * c + e_
                    es = slice(64 * e_, 64 * e_ + 64)
                    self.mm(pUv[:, e_, :], MT[:, i, :], Xak[:, i, :], start=True, stop=False, r=[MT, Xak], w=[pU])
                    self.mm(pUv[:, e_, :], AhT[es, c, :], Tb[es, :], start=False, stop=True, r=[AhT, Tb], w=[pU], tile_position=(64 * e_, 0))
                P.op('act', lambda e: e.copy(out=Ub[:], in_=pUv), r=[pU], w=[Ub])
                for e_ in range(2):
                    i = 2 * c + e_
                    es = slice(64 * e_, 64 * e_ + 64)
                    if latent:
                        self.mm(pYv[es, c, :], Tb[es, :], rt[es, loc + 64 * c:loc + 64 * c + 64], start=True, stop=False, r=[Tb, rt], w=[pY],
                                tile_position=(64 * e_, 64 * e_))
                        self.mm(pYv[es, c, :], Ub[:, e_, :], ArbT[:, i, :], start=False, stop=False, r=[Ub, ArbT], w=[pY], tile_position=(0, 64 * e_))
                        self.mm(pYv[es, c, :], tokT[:, 3, c, es], ArkT[:, i, :], start=False, stop=True, r=[tokT, ArkT], w=[pY], tile_position=(0, 64 * e_))
                    self.mm(pTv[es, :], tokT[:, 1, c, es], Ub[:, e_, :], start=True, stop=False, r=[tokT, Ub], w=[pT], tile_position=(0, 64 * e_))
                    self.mm(pTv[es, :], tokT[:, 2, c, es], tokT[:, 3, c, es], start=False, stop=True, r=[tokT], w=[pT], tile_position=(0, 64 * e_))
                P.op('dve', lambda e: e.tensor_tensor(out=Ts[:], in0=pTv, in1=Tst[:], op=ALU.add), r=[pT, Tst], w=[Ts])
                wc = wcs[:, 4 * gi + c:4 * gi + c + 1]
                P.op('dve', lambda e, wc=wc: e.tensor_scalar(out=Tst[:], in0=Ts[:], scalar1=wc, scalar2=None, op0=ALU.mult), r=[Ts, wcs], w=[Tst])
                self.act(Tb[:], Ts[:], AF.Identity, scale=wc, r=[Ts, wcs], w=[Tb])
            if latent:
                if d == 0:
                    P.op('act', lambda e, g0=g0: e.copy(out=y0[:, g0 - 256:g0], in_=pb[6][:, 0:256]), r=[pY], w=[y0])
                else:
                    ysl = rsl(2303 - g0, 256, -1)
                    P.op('dve', lambda e, ysl=ysl: e.tensor_tensor(out=y0[:, ysl], in0=y0[:, ysl], in1=pb[6][:, 0:256], op=ALU.add), r=[pY, y0], w=[y0])

    def wload(self, pool, src, npart, K, ncol, q='sp'):
        P = self.P
        i = pool['i'] = pool['i'] + 1
        wf, wb = pool['f'][i % len(pool['f'])], pool['b'][i % len(pool['b'])]
        P.dma(q, wf[0:npart, 0:K, 0:ncol], src, w=[wf], group=pool['name'] + str(i % len(pool['f'])))
        ceng = 'pool' if (self._castn % 2 == 0) else 'dve'
        self._castn += 1
        P.op(ceng, lambda e: e.tensor_copy(out=wb[0:npart, 0:K, 0:ncol], in_=wf[0:npart, 0:K, 0:ncol]), r=[wf], w=[wb])
        return wb

    def mkpool(self, name, npart, K, ncol, n=2):
        P = self.P
        return {'name': name, 'i': 0, 'f': [P.sb([npart, K, ncol], name=name + 'f') for _ in range(n)],
                'b': [P.sb([npart, K, ncol], BF16, name=name + 'b') for _ in range(n)]}

    def lru(self):
        P, pb, hT, din = self.P, self.pb, self.hT, self.din
        lruT = self.lruT
        cw_, cb_, ba_, bx_, lam_ = din['lru_conv_w'], din['lru_conv_b'], din['lru_ba'], din['lru_bx'], din['lru_lambda']
        rows = [cw_[d, j] for d in range(2) for j in range(4)] + [cb_[0], cb_[1], ba_[0], ba_[1], bx_[0], bx_[1], lam_[0], lam_[1]]
        lp = self.cols(rows, LW, 80, 'lp')
        hb = P.sb([80, 16, 4], name='hb')
        P.op('dve', lambda e: e.tensor_scalar(out=hb[:], in0=lp[:, :, 10:14], scalar1=0.5, scalar2=None, op0=ALU.mult), r=[lp], w=[hb])
        cs = P.sb([80, 16, 4], name='cs')
        one1 = P.sb([80, 1], name='one1')
        P.op('dve', lambda e: e.memset(one1[:], 1.0), w=[one1])
        self.act(cs[:, :, 0:2], lp[:, :, 14:16], AF.Exp, scale=-1.0, r=[lp], w=[cs])
        self.act(cs[:, :, 0:2], cs[:, :, 0:2], AF.Ln, bias=one1[:], r=[cs, one1], w=[cs])
        P.op('dve', lambda e: e.tensor_scalar(out=cs[:, :, 2:4], in0=cs[:, :, 0:2], scalar1=-4.0, scalar2=None, op0=ALU.mult), r=[cs], w=[cs])
        P.op('dve', lambda e: e.tensor_scalar(out=cs[:, :, 0:2], in0=cs[:, :, 0:2], scalar1=-8.0, scalar2=None, op0=ALU.mult), r=[cs], w=[cs])
        q25 = P.sb([80, 1], name='q25')
        P.op('dve', lambda e: e.memset(q25[:], 0.25), w=[q25])
        gwa = P.sb([80, 32, 80], BF16, name='gwa'); gwx = P.sb([80, 32, 80], BF16, name='gwx')
        with P.scope():
            st = P.sb([80, 32, 80], name='gst')
            for src, dst in ((din['lru_wa'], gwa), (din['lru_wx'], gwx)):
                P.dma('sp', st[:], src.rearrange("d n c e -> c (d n) e"), w=[st], group='gst')
                P.op('dve', lambda e, dst=dst: e.tensor_copy(out=dst[:], in_=st[:]), r=[st], w=[dst])
        U = P.sb([80, 2313], name='U'); guy = P.sb([80, NLAT], BF16, name='guy')
        xcs = [P.sb([80, 2313], name='xc') for _ in range(2)]
        xcb = P.sb([80, 2313], BF16, name='xcb')
        thr = P.sb([80, 2313], name='thr'); thi = P.sb([80, 2313], name='thi'); aa = P.sb([80, 2313], name='aa')
        lrub = P.sb([80, NLAT], BF16, name='lrub')
        wp = self.mkpool('lw', 128, 8, 80, n=2)
        P.op('dve', lambda e: e.memset(U[:], 0.0), w=[U])
        for xc in xcs:
            P.op('dve', lambda e, xc=xc: e.memset(xc[:], 0.0), w=[xc])
        P.op('dve', lambda e: e.memset(thr[:], 0.0), w=[thr])
        P.op('dve', lambda e: e.memset(thi[:], 0.0), w=[thi])
        wv = din['w_in'].rearrange("(k p) n -> p k n", p=128)
        hk = [(hT, j) for j in range(5)]
        tbs = [(0, 256, 3)] + [(256 + 512 * i, 512, 262 + 512 * i) for i in range(4)]
        nb = 0
        for n in range(NBLK):
            wx = self.wload(wp, wv[:, :, 80 * n:80 * n + 80], 128, 8, 80)
            for (hc, nn, uc) in tbs:
                pp = pb[nb % 6]; nb += 1
                for k in range(8):
                    self.mm(pp[0:80, 0:nn], wx[:, k, :], hT[:, k, hc:hc + nn], start=(k == 0), stop=(k == 7), r=[wx] + hk, w=[pp])
                P.op('act', lambda e: e.copy(out=U[:, uc:uc + nn], in_=pp[0:80, 0:nn]), r=[pp], w=[U])
            wy = self.wload(wp, wv[:, :, 1280 + 80 * n:1280 + 80 * n + 80], 128, 8, 80, q='act')
            for (hc, nn, uc) in tbs[1:]:
                pp = pb[nb % 6]; nb += 1
                for k in range(8):
                    self.mm(pp[0:80, 0:nn], wy[:, k, :], hT[:, k, hc:hc + nn], start=(k == 0), stop=(k == 7), r=[wy] + hk, w=[pp])
                self.act(guy[:, hc - 256:hc - 256 + nn], pp[0:80, 0:nn], AF.Gelu_apprx_tanh, r=[pp], w=[guy])
            for d in range(2):
                xc = xcs[d]
                sgn = -1 if d == 0 else 1
                cwj = lambda j: lp[:, n, 4 * d + j:4 * d + j + 1]

                def chunk(ci, uc, nn, blocks):
                    nonlocal nb
                    cs_ = slice(uc, uc + nn)
                    P.op('dve', lambda e: e.tensor_scalar(out=xc[:, cs_], in0=U[:, cs_], scalar1=cwj(3), scalar2=lp[:, n, 8 + d:9 + d],
                                                          op0=ALU.mult, op1=ALU.add), r=[U, lp], w=[(xc, ci)])
                    for j in range(3):
                        o = uc + sgn * (3 - j)
                        P.op('dve', lambda e: e.scalar_tensor_tensor(out=xc[:, cs_], in0=U[:, o:o + nn], scalar=cwj(j), in1=xc[:, cs_],
                                                                     op0=ALU.mult, op1=ALU.add), r=[U, lp, (xc, ci)], w=[(xc, ci)])
                    yield
                    P.op('act', lambda e: e.copy(out=xcb[:, cs_], in_=xc[:, cs_]), r=[(xc, ci)], w=[(xcb, ci)])
                    yield
                    for (hc, bn, bc) in blocks:
                        for gw, dst, bcol in ((gwa, thr, d), (gwx, thi, 2 + d)):
                            pp = pb[nb % 6]; nb += 1
                            self.mm(pp[0:80, 0:bn], gw[:, 16 * d + n, :], xcb[:, bc:bc + bn], r=[gw, (xcb, ci)], w=[pp])
                            self.act(dst[:, bc:bc + bn], pp[0:80, 0:bn], AF.Tanh, scale=0.5, bias=hb[:, n, bcol:bcol + 1], r=[pp, hb], w=[(dst, ci)])
                    yield
                    self.act(aa[:, cs_], thr[:, cs_], AF.Exp, scale=cs[:, n, 2 + d:3 + d], bias=cs[:, n, 2 + d:3 + d], r=[(thr, ci), cs], w=[(aa, ci)])
                    self.act(thr[:, cs_], thr[:, cs_], AF.Exp, scale=cs[:, n, d:d + 1], bias=cs[:, n, d:d + 1], r=[(thr, ci), cs], w=[(thr, ci)])
                    P.op('dve', lambda e: e.scalar_tensor_tensor(out=thi[:, cs_], in0=thi[:, cs_], scalar=1.0, in1=xc[:, cs_], op0=ALU.add, op1=ALU.mult),
                         r=[(thi, ci), (xc, ci)], w=[(thi, ci)])
                    yield
                    self.act(thr[:, cs_], thr[:, cs_], AF.Sqrt, scale=-0.25, bias=q25[:], r=[(thr, ci), q25], w=[(thr, ci)])
                    yield
                    P.op('dve', lambda e: e.tensor_tensor(out=thi[:, cs_], in0=thi[:, cs_], in1=thr[:, cs_], op=ALU.mult), r=[(thi, ci), (thr, ci)], w=[(thi, ci)])
                    yield
                gens = [chunk(0, 3, 1283, tbs[0:3]), chunk(1, 1286, 1024, tbs[3:5])]
                while gens:
                    for gn in list(gens):
                        try:
                            next(gn)
                        except StopIteration:
                            gens.remove(gn)
                allk = lambda t_: [(t_, ci) for ci in range(2)]
                if d == 0:
                    P.op('dve', lambda e: e.tensor_tensor_scan(out=xc[:, 3:259], data0=aa[:, 3:259], data1=thi[:, 3:259], initial=0.0, op0=ALU.mult, op1=ALU.add),
                         r=allk(aa) + allk(thi), w=allk(xc))
                    P.op('dve', lambda e: e.tensor_tensor_scan(out=xc[:, 262:2310], data0=aa[:, 262:2310], data1=thi[:, 262:2310], initial=xc[:, 258:259],
                                                               op0=ALU.mult, op1=ALU.add), r=allk(aa) + allk(thi) + allk(xc), w=allk(xc))
                else:
                    rv = lambda t_, a_, b_: t_[:, rsl(b_ - 1, b_ - a_, -1)]
                    P.op('dve', lambda e: e.tensor_tensor_scan(out=rv(xc, 3, 259), data0=rv(aa, 3, 259), data1=rv(thi, 3, 259), initial=0.0, op0=ALU.mult, op1=ALU.add),
                         r=allk(aa) + allk(thi), w=allk(xc))
                    P.op('dve', lambda e: e.tensor_tensor_scan(out=rv(xc, 262, 2310), data0=rv(aa, 262, 2310), data1=rv(thi, 262, 2310), initial=xc[:, 3:4],
                                                               op0=ALU.mult, op1=ALU.add), r=allk(aa) + allk(thi) + allk(xc), w=allk(xc))
            allk = lambda t_: [(t_, ci) for ci in range(2)]
            P.op('dve', lambda e: e.tensor_tensor(out=aa[:, 0:NLAT], in0=xcs[0][:, 262:2310], in1=xcs[1][:, 262:2310], op=ALU.add), r=allk(xcs[0]) + allk(xcs[1]) + allk(aa), w=allk(aa))
            P.op('dve', lambda e: e.tensor_tensor(out=lrub[:], in0=aa[:, 0:NLAT], in1=guy[:], op=ALU.mult), r=allk(aa) + [guy], w=[lrub])
            p0, c0 = (80 * n) % 128, (80 * n) // 128
            n1 = min(80, 128 - p0)
            P.dma('sp', lruT[p0:p0 + n1, c0, :], lrub[0:n1, :], r=[lrub], w=[lruT], group='lruT')
            if n1 < 80:
                P.dma('sp', lruT[0:80 - n1, c0 + 1, :], lrub[n1:80, :], r=[lrub], w=[lruT], group='lruT')

    def merge(self):
        P, pb, hT, din = self.P, self.pb, self.hT, self.din
        lruT, rwT, mT = self.lruT, self.rwT, self.mT
        wv = din['w_in'].rearrange("(k p) n -> p k n", p=128)
        wol = din['w_o_lru'].rearrange("(k p) n -> p k n", p=128)
        wor = din['w_o_rwkv'].rearrange("(k p) n -> p k n", p=128)
        pl = self.mkpool('wl', 128, 10, 128); pr = self.mkpool('wr', 128, 8, 128); pg = self.mkpool('wg', 128, 8, 128, n=3)
        thl = P.sb([128, 512], name='thl'); thr = P.sb([128, 512], name='thr2'); t1 = P.sb([128, 512], name='t1'); t2 = P.sb([128, 512], name='t2')
        hk = [(hT, j) for j in range(5)]
        for dc in range(8):
            cs_ = slice(dc * 128, dc * 128 + 128)
            wl = self.wload(pl, wol[:, :, cs_], 128, 10, 128)
            wr = self.wload(pr, wor[:, :, cs_], 128, 8, 128, q='act')
            wgl = self.wload(pg, wv[:, :, 6048 + dc * 128:6048 + dc * 128 + 128], 128, 8, 128)
            wgr = self.wload(pg, wv[:, :, 7072 + dc * 128:7072 + dc * 128 + 128], 128, 8, 128, q='act')
            for tb in range(4):
                ts_ = slice(512 * tb, 512 * tb + 512); hs_ = slice(256 + 512 * tb, 256 + 512 * tb + 512)
                p1, p2, p3, p4 = pb[0], pb[1], pb[2], pb[3]
                for c in range(10):
                    self.mm(p1[:], wl[:, c, :], lruT[:, c, ts_], start=(c == 0), stop=(c == 9), r=[wl, lruT], w=[p1])
                for c in range(8):
                    self.mm(p2[:], wr[:, c, :], rwT[:, c, ts_], start=(c == 0), stop=(c == 7), r=[wr, rwT], w=[p2])
                for c in range(8):
                    self.mm(p3[:], wgl[:, c, :], hT[:, c, hs_], start=(c == 0), stop=(c == 7), r=[wgl] + hk, w=[p3])
                for c in range(8):
                    self.mm(p4[:], wgr[:, c, :], hT[:, c, hs_], start=(c == 0), stop=(c == 7), r=[wgr] + hk, w=[p4])
                self.act(thl[:], p3[:], AF.Tanh, scale=0.5, r=[p3], w=[thl])
                self.act(thr[:], p4[:], AF.Tanh, scale=0.5, r=[p4], w=[thr])
                P.op('dve', lambda e: e.scalar_tensor_tensor(out=t1[:], in0=thl[:], scalar=1.0, in1=p1[:], op0=ALU.add, op1=ALU.mult), r=[thl, p1], w=[t1])
                P.op('dve', lambda e: e.scalar_tensor_tensor(out=t2[:], in0=thr[:], scalar=1.0, in1=p2[:], op0=ALU.add, op1=ALU.mult), r=[thr, p2], w=[t2])
                P.op('dve', lambda e: e.tensor_tensor(out=mT[:, dc, ts_], in0=t1[:], in1=t2[:], op=ALU.add), r=[t1, t2], w=[mT])

    def resid1(self):
        P, pb, din, mod = self.P, self.pb, self.din, self.mod
        mT, x1T = self.mT, self.x1T
        hg = P.sb([128, 8], name='hg')
        P.op('dve', lambda e: e.tensor_scalar(out=hg[:], in0=mod[:, 16:24, 0], scalar1=0.5, scalar2=None, op0=ALU.mult), r=[mod], w=[hg])
        wo = P.sb([128, 8, D], BF16, name='wo')
        with P.scope():
            st = P.sb([128, 8, 256], name='wost')
            for j in range(4):
                P.dma('sp', st[:], din['w_out'].rearrange("(k p) n -> p k n", p=128)[:, :, 256 * j:256 * j + 256], w=[st], group='wost')
                P.op('pool', lambda e: e.tensor_copy(out=wo[:, :, 256 * j:256 * j + 256], in_=st[:]), r=[st], w=[wo])
        xv = din['xT'].rearrange("(k p) t -> p k t", p=128)
        for tb in range(4):
            ts_ = slice(512 * tb, 512 * tb + 512)
            P.dma('sp', x1T[:, :, ts_], xv[:, :, ts_], w=[x1T], group='x1ld')
            for dc in range(8):
                pp = pb[(tb * 8 + dc) % 2]
                for c in range(8):
                    self.mm(pp[:], wo[:, c, dc * 128:dc * 128 + 128], mT[:, c, ts_], start=(c == 0), stop=(c == 7), r=[wo, mT], w=[pp])
                P.op('dve', lambda e: e.scalar_tensor_tensor(out=x1T[:, dc, ts_], in0=pp[:], scalar=hg[:, dc:dc + 1], in1=x1T[:, dc, ts_], op0=ALU.mult, op1=ALU.add),
                     r=[pp, hg, x1T], w=[x1T])

    def ffn(self):
        P, pb, din, mod = self.P, self.pb, self.din, self.mod
        x1T, h2T = self.x1T, self.h2T
        wi = din['w_ffn_in'].rearrange("(k p) n -> p k n", p=128)
        wo_ = din['w_ffn_out'].rearrange("(f p) n -> p f n", p=128)
        actT = P.sb([128, 22, 1024], BF16, name='actT')
        pin = self.mkpool('fi', 128, 8, 128, n=3); pout = self.mkpool('fo', 128, 22, 128, n=2)
        sl = P.sb([128, 512], name='sl')
        hk = [(h2T, j) for j in range(4)]
        nb = 0
        for half in range(2):
            for f in range(22):
                wg = self.wload(pin, wi[:, :, f * 128:f * 128 + 128], 128, 8, 128)
                wu = self.wload(pin, wi[:, :, DFF + f * 128:DFF + f * 128 + 128], 128, 8, 128, q='act')
                for t2 in range(2):
                    tok = slice(1024 * half + 512 * t2, 1024 * half + 512 * t2 + 512)
                    pg_, pu_ = pb[nb % 4], pb[(nb + 1) % 4]; nb += 2
                    for k in range(8):
                        self.mm(pg_[:], wg[:, k, :], h2T[:, k, tok], start=(k == 0), stop=(k == 7), r=[wg] + hk, w=[pg_])
                    for k in range(8):
                        self.mm(pu_[:], wu[:, k, :], h2T[:, k, tok], start=(k == 0), stop=(k == 7), r=[wu] + hk, w=[pu_])
                    self.act(sl[:], pg_[:], AF.Silu, r=[pg_], w=[sl])
                    P.op('dve', lambda e: e.tensor_tensor(out=actT[:, f, 512 * t2:512 * t2 + 512], in0=sl[:], in1=pu_[:], op=ALU.mult), r=[sl, pu_], w=[(actT, f)])
            for dc in range(8):
                wo = self.wload(pout, wo_[:, :, dc * 128:dc * 128 + 128], 128, 22, 128)
                for t2 in range(2):
                    tok = slice(1024 * half + 512 * t2, 1024 * half + 512 * t2 + 512)
                    pp = pb[4 + (nb % 2)]; nb += 1
                    for f in range(22):
                        self.mm(pp[:], wo[:, f, :], actT[:, f, 512 * t2:512 * t2 + 512], start=(f == 0), stop=(f == 21), r=[wo, (actT, f)], w=[pp])
                    P.op('dve', lambda e: e.scalar_tensor_tensor(out=x1T[:, dc, tok], in0=pp[:], scalar=mod[:, 40 + dc, 0:1], in1=x1T[:, dc, tok], op0=ALU.mult, op1=ALU.add),
                         r=[pp, mod, x1T], w=[x1T])

    def final(self, outT):
        P, pb, x = self.P, self.pb, self.x1T
        gains = self.gains
        sq = P.sb([128, 8, 512], name='fsq'); rs = P.sb([128, 512], name='frs')
        epst = P.sb([128, 1], name='fepst')
        P.op('dve', lambda e: e.memset(epst[:], RMS_EPS), w=[epst])
        ov = outT.rearrange("(k p) t -> p k t", p=128)
        for tb in range(4):
            ts_ = slice(512 * tb, 512 * tb + 512)
            self.act(sq[:], x[:, :, ts_], AF.Square, r=[x], w=[sq])
            pp = pb[tb % 2]
            for k in range(8):
                self.mm(pp[:], self.ones[:], sq[:, k, :], start=(k == 0), stop=(k == 7), r=[sq, self.ones], w=[pp])
            self.act(rs[:], pp[:], AF.Sqrt, scale=1.0 / D, bias=epst[:], r=[pp, epst], w=[rs])
            P.op('dve', lambda e: e.reciprocal(out=rs[:], in_=rs[:]), r=[rs], w=[rs])
            for k in range(8):
                P.op('dve', lambda e: e.scalar_tensor_tensor(out=sq[:, k, :], in0=x[:, k, ts_], scalar=gains[:, k, 2:3], in1=rs[:], op0=ALU.mult, op1=ALU.mult),
                     r=[x, gains, rs], w=[sq])
            P.op('dve', lambda e: e.tensor_copy(out=epst[:], in_=epst[:]), r=[sq, epst], w=[sq, epst])
            P.dma('sp', ov[:, :, ts_], sq[:], r=[sq], group='out')

    def finish(self):
        P = self.P
        for gname in list(P.dsem):
            if gname.startswith('out'):
                P.wait_group('pool', gname)
        P.emit()
        return self.nc


_CACHE = {}


def _prep(inputs, b):
    f = lambda a: np.ascontiguousarray(a, dtype=np.float32)
    m = {
        'xT': f(inputs['x'][b].T), 'ctxT': f(inputs['ctx'][b].T),
        'cvec': f(np.stack([inputs['c'][b], inputs['c_ctx']])),
        'w_mod': f(inputs['w_mod'][0]), 'b_mod': f(inputs['b_mod'][0]),
        'norm_mix_g': f(inputs['norm_mix_g'][0]), 'norm_ffn_g': f(inputs['norm_ffn_g'][0]), 'norm_final_g': f(inputs['norm_final_g']),
        'w_in': f(inputs['w_in'][0]),
        'lru_conv_w': f(inputs['lru_conv_w'][0]), 'lru_conv_b': f(inputs['lru_conv_b'][0]),
        'lru_wa': f(inputs['lru_wa'][0]), 'lru_ba': f(inputs['lru_ba'][0]), 'lru_wx': f(inputs['lru_wx'][0]), 'lru_bx': f(inputs['lru_bx'][0]),
        'lru_lambda': f(inputs['lru_lambda'][0]), 'w_o_lru': f(inputs['w_o_lru'][0]),
        'rwkv_mu': f(inputs['rwkv_mu'][0]), 'rwkv_w0': f(inputs['rwkv_w0'][0]), 'rwkv_w2': f(inputs['rwkv_w2'][0]),
        'rwkv_a0': f(inputs['rwkv_a0'][0]), 'rwkv_a2': f(inputs['rwkv_a2'][0]), 'rwkv_g2': f(inputs['rwkv_g2'][0]),
        'rwkv_k_k': f(inputs['rwkv_k_k'][0]), 'rwkv_k_a': f(inputs['rwkv_k_a'][0]), 'rwkv_r_k': f(inputs['rwkv_r_k'][0].reshape(-1)),
        'rwkv_ln_g': f(inputs['rwkv_ln_g'][0]), 'rwkv_ln_b': f(inputs['rwkv_ln_b'][0]),
        'w_o_rwkv': f(inputs['w_o_rwkv'][0]), 'w_out': f(inputs['w_out'][0]),
        'w_ffn_in': f(inputs['w_ffn_in'][0]), 'w_ffn_out': f(inputs['w_ffn_out'][0]),
    }
    return m


def kernel(**inputs):
    if 'nc' not in _CACHE:
        _CACHE['nc'] = Builder().build()
    nc = _CACHE['nc']
    shared = _prep(inputs, 0)
    in_maps = []
    for b in range(8):
        m = dict(shared)
        m['xT'] = np.ascontiguousarray(np.asarray(inputs['x'][b], dtype=np.float32).T)
        m['ctxT'] = np.ascontiguousarray(np.asarray(inputs['ctx'][b], dtype=np.float32).T)
        m['cvec'] = np.ascontiguousarray(np.stack([inputs['c'][b], inputs['c_ctx']]).astype(np.float32))
        in_maps.append(m)
    res = run_bass_kernel_spmd(nc, in_maps, core_ids=list(range(8)))
    out = np.stack([np.ascontiguousarray(r['outT'].T) for r in res.results]).astype(np.float32)
    return out
```

```python
import contextlib
import numpy as np
import concourse.bass as bass
import concourse.mybir as mybir
from concourse.bass_utils import run_bass_kernel_spmd

F32 = mybir.dt.float32
BF16 = mybir.dt.bfloat16
AF = mybir.ActivationFunctionType
ALU = mybir.AluOpType

ENGS = ['pe', 'act', 'dve', 'pool', 'sp']
NCTX, NLAT, T = 256, 2048, 2304
D = 1024
LW, NBLK, BLK = 1280, 16, 80
RIN = 3488
DFF = 2816
RMS_EPS, GN_EPS = 1e-6, 64e-5


class _Rec:
    def __init__(self):
        self.call = None

    def __getattr__(self, name):
        def f(*a, **k):
            self.call = (name, a, k)
            return self
        return f


class Prog:
    def __init__(self, nc):
        self.nc = nc
        self.root = contextlib.ExitStack()
        self.stacks = [self.root]
        self.ops = {e: [] for e in ENGS}
        self.cnt = {e: 0 for e in ENGS}
        self.seen = {e: {} for e in ENGS}
        self.last_w = {}
        self.readers = {}
        self.esem = {e: self.root.enter_context(nc.semaphore('s_' + e)) for e in ENGS if e != 'sp'}
        self.dsem = {}
        self.fence = []
        self.ntile = 0

    def sb(self, shape, dt=F32, name=None):
        self.ntile += 1
        return self.stacks[-1].enter_context(self.nc.sbuf_tensor(f'{name or "t"}{self.ntile}', list(shape), dt))

    def sbm(self, shape, dt=F32, name=None):
        self.ntile += 1
        st = contextlib.ExitStack()
        t = st.enter_context(self.nc.sbuf_tensor(f'{name or "t"}{self.ntile}', list(shape), dt))
        return t, st

    def _set_fence(self):
        self.fence = [('E', e, self.cnt[e]) for e in self.esem if self.cnt[e] > 0]
        self.fence += [('D', s_, v) for s_, v in self.dsem.values() if v > 0]

    def free(self, stacks):
        for st in stacks:
            st.close()
        self._set_fence()

    def ps(self, shape, dt=F32, name=None):
        self.ntile += 1
        return self.root.enter_context(self.nc.psum_tensor(f'{name or "p"}{self.ntile}', list(shape), dt))

    @contextlib.contextmanager
    def scope(self):
        st = contextlib.ExitStack()
        self.stacks.append(st)
        try:
            yield
        finally:
            self.stacks.pop()
            st.close()
            self._set_fence()

    def _k(self, k):
        if isinstance(k, tuple):
            return tuple(self._k(x) for x in k)
        if isinstance(k, (str, int)):
            return k
        return id(k)

    @staticmethod
    def _tkey(tok):
        return ('E', tok[1]) if tok[0] == 'E' else ('D', id(tok[1]))

    def _deps(self, eng, r, w):
        deps = {}

        def add(tok):
            if tok is None:
                return
            if tok[0] == 'E' and tok[1] == eng == 'pe':
                return
            k = self._tkey(tok)
            if k not in deps or deps[k][2] < tok[2]:
                deps[k] = tok
        for tok in self.fence:
            add(tok)
        for k in r:
            add(self.last_w.get(k))
        for k in w:
            add(self.last_w.get(k))
            for t in self.readers.get(k, ()):
                add(t)
        out = []
        seen = self.seen[eng]
        for k, tok in deps.items():
            if seen.get(k, 0) >= tok[2]:
                continue
            seen[k] = tok[2]
            out.append(tok)
        return out

    def _commit(self, tok, r, w):
        for k in w:
            self.last_w[k] = tok
            self.readers[k] = []
        for k in r:
            if k in w:
                continue
            self.readers.setdefault(k, []).append(tok)

    def op(self, eng, fn, r=(), w=()):
        r = [self._k(k) for k in r]
        w = [self._k(k) for k in w]
        waits = self._deps(eng, r, w)
        self.cnt[eng] += 1
        tok = ('E', eng, self.cnt[eng])
        rec = _Rec()
        fn(rec)
        name, a, k = rec.call
        self.ops[eng].append((waits, lambda e: getattr(e, name)(*a, **k), tok))
        self._commit(tok, r, w)

    def dma(self, q, out, in_, r=(), w=(), group=None, **kw):
        r = [self._k(k) for k in r]
        w = [self._k(k) for k in w]
        waits = self._deps(q, r, w)
        g = group or ('dma_' + str(w[0] if w else 'x'))
        if g not in self.dsem:
            self.dsem[g] = [self.root.enter_context(self.nc.semaphore('d%d' % len(self.dsem))), 0]
        ent = self.dsem[g]
        ent[1] += 16
        tok = ('D', ent[0], ent[1])
        self.ops[q].append((waits, lambda e: e.dma_start(out=out, in_=in_, **kw), tok))
        self._commit(tok, r, w)

    def wait_group(self, eng, group):
        ent = self.dsem[group]
        self.ops[eng].append(([('D', ent[0], ent[1])], None, None))

    def emit(self):
        engobj = {'pe': 'tensor', 'act': 'scalar', 'dve': 'vector', 'pool': 'gpsimd', 'sp': 'sync'}
        waited = {e: set() for e in ENGS}
        for e in ENGS:
            for waits, fn, tok in self.ops[e]:
                for t in waits:
                    if t[0] == 'E':
                        waited[t[1]].add(t[2])
        rank = {e: {s_: i + 1 for i, s_ in enumerate(sorted(waited[e]))} for e in ENGS}
        with self.nc.Block() as block:
            for e in ENGS:
                ops = self.ops[e]

                def body(eng, ops=ops):
                    for waits, fn, tok in ops:
                        for t in waits:
                            if t[0] == 'E':
                                eng.wait_ge(self.esem[t[1]], rank[t[1]][t[2]])
                            else:
                                eng.wait_ge(t[1], t[2])
                        if fn is not None:
                            ins = fn(eng)
                            if tok[0] == 'D':
                                ins.then_inc(tok[1], 16)
                            elif tok[2] in rank[tok[1]]:
                                ins.then_inc(self.esem[tok[1]], 1)
                getattr(block, engobj[e])(body)


def rsl(start, n, step):
    if step > 0:
        return slice(start, start + n)
    stop = start - n
    return slice(start, stop if stop >= 0 else None, -1)


class Builder:
    def __init__(self, taps=(), stop_after=None):
        self.taps = set(taps)
        self.stop_after = stop_after
        nc = self.nc = bass.Bass("TRN2", target_bir_lowering=False)
        self.P = Prog(nc)
        self.din = {}
        self.tapout = {}
        self._castn = 0

    def inp(self, name, shape):
        self.din[name] = self.nc.dram_tensor(name, list(shape), F32, kind="ExternalInput").ap()
        return self.din[name]

    def mm(self, out, lhsT, rhs, start=True, stop=True, r=(), w=(), **kw):
        self.P.op('pe', lambda e: e.matmul(out, lhsT=lhsT, rhs=rhs, start=start, stop=stop, **kw), r=r, w=w)

    def act(self, out, in_, func, r=(), w=(), **kw):
        self.P.op('act', lambda e: e.activation(out=out, in_=in_, func=func, **kw), r=r, w=w)

    def tap(self, name, tile_ap, shape, r, dt=F32):
        if name not in self.taps:
            return
        o = self.nc.dram_tensor('tap_' + name, list(shape), dt, kind="ExternalOutput").ap()
        self.P.dma('pool', o, tile_ap, r=r, group='out_' + name)
        self.tapout[name] = o

    def cols(self, rows, n, chunk, name):
        P = self.P
        R = len(rows)
        nch = (n + chunk - 1) // chunk
        out = P.sb([chunk, nch, R], name=name)
        with P.scope():
            st = P.sb([R, n], name='colst')
            for i, rw in enumerate(rows):
                P.dma('sp', st[i:i + 1, :], rw.rearrange("(o n) -> o n", o=1), w=[(st, i)], group='colst')
            pp = self.pb[0]
            assert nch * R <= 512
            for c in range(nch):
                cs = min(chunk, n - c * chunk)
                P.op('pe', lambda e, c=c, cs=cs: e.transpose(out=pp[0:cs, c * R:(c + 1) * R], in_=st[0:R, c * chunk:c * chunk + cs],
                                                             identity=self.ident[0:R, 0:R]),
                     r=[(st, i) for i in range(R)] + [self.ident], w=[pp])
            P.op('dve', lambda e: e.tensor_copy(out=out[:].rearrange("p c r -> p (c r)"), in_=pp[0:chunk, 0:nch * R]), r=[pp], w=[out])
        return out

    def build(self):
        nc, P = self.nc, self.P
        inp = self.inp
        xT = inp('xT', [D, NLAT]); ctxT = inp('ctxT', [D, NCTX]); cvec = inp('cvec', [2, D])
        w_mod = inp('w_mod', [D, 6 * D]); b_mod = inp('b_mod', [6 * D])
        nmg = inp('norm_mix_g', [D]); nfg = inp('norm_ffn_g', [D]); nfin = inp('norm_final_g', [D])
        w_in = inp('w_in', [D, 8096])
        lru_conv_w = inp('lru_conv_w', [2, 4, LW]); lru_conv_b = inp('lru_conv_b', [2, LW])
        lru_wa = inp('lru_wa', [2, NBLK, BLK, BLK]); lru_ba = inp('lru_ba', [2, LW])
        lru_wx = inp('lru_wx', [2, NBLK, BLK, BLK]); lru_bx = inp('lru_bx', [2, LW])
        lru_lam = inp('lru_lambda', [2, LW]); w_o_lru = inp('w_o_lru', [LW, D])
        mu = inp('rwkv_mu', [2, RIN]); w0 = inp('rwkv_w0', [2, D]); w2 = inp('rwkv_w2', [2, 64, D])
        a0 = inp('rwkv_a0', [2, D]); a2 = inp('rwkv_a2', [2, 64, D]); g2 = inp('rwkv_g2', [160, D])
        k_k = inp('rwkv_k_k', [D]); k_a = inp('rwkv_k_a', [D]); r_k = inp('rwkv_r_k', [D])
        ln_g = inp('rwkv_ln_g', [D]); ln_b = inp('rwkv_ln_b', [D])
        w_o_rwkv = inp('w_o_rwkv', [D, D]); w_out = inp('w_out', [D, D])
        w_ffn_in = inp('w_ffn_in', [D, 2 * DFF]); w_ffn_out = inp('w_ffn_out', [DFF, D])
        outT = nc.dram_tensor('outT', [D, NLAT], F32, kind="ExternalOutput").ap()

        self.pb = [P.ps([128, 512], F32, name='pb') for _ in range(7)]
        self.pbh = P.ps([128, 1024], BF16, name='pbh')
        pb = self.pb

        ones = P.sb([128, 128], name='ones')
        P.op('dve', lambda e: e.memset(ones[:], 1.0), w=[ones])
        self.ident = ident = P.sb([128, 128], name='ident')
        P.op('pool', lambda e: e.affine_select(out=ident[:], in_=ones[:], pattern=[[-1, 128]], compare_op=ALU.is_equal, fill=0.0,
                                               base=0, channel_multiplier=1), r=[ones], w=[ident])
        identb = P.sb([128, 128], BF16, name='identb')
        P.op('dve', lambda e: e.tensor_copy(out=identb[:], in_=ident[:]), r=[ident], w=[identb])
        bones = P.sb([128, 128], name='bones')
        P.op('dve', lambda e: e.memset(bones[:], 0.0), w=[bones])
        P.op('dve', lambda e: e.memset(bones[0:64, 0:64], 1.0), w=[bones])
        P.op('dve', lambda e: e.memset(bones[64:128, 64:128], 1.0), w=[bones])
        self.ones, self.identb, self.bones = ones, identb, bones

        gains = self.cols([nmg, nfg, nfin], D, 128, 'gains')
        cT = self.cols([cvec[0], cvec[1]], D, 128, 'cT')
        bm = self.cols([b_mod], 6 * D, 128, 'bm')
        mod = P.sb([128, 48, 2], name='mod')
        with P.scope():
            sc = P.sb([128, 8, 2], name='sc')
            self.act(sc[:], cT[:], AF.Silu, r=[cT], w=[sc])
            wm = [P.sb([128, 8, 768], name='wm') for _ in range(2)]
            pm = pb[1]
            wv = w_mod.rearrange("(k p) n -> p k n", p=128)
            for jb in range(8):
                buf = wm[jb % 2]
                for k2 in range(2):
                    P.dma('sp' if k2 == 0 else 'act', buf[:, 4 * k2:4 * k2 + 4, :], wv[:, 4 * k2:4 * k2 + 4, jb * 768:(jb + 1) * 768],
                          w=[(buf, k2)], group='wm%d' % (jb % 2))
                for jj in range(6):
                    j = jb * 6 + jj
                    for k in range(8):
                        self.mm(pm[:, 2 * j:2 * j + 2], buf[:, k, jj * 128:(jj + 1) * 128], sc[:, k, :], start=(k == 0), stop=(k == 7),
                                r=[(buf, 0), (buf, 1), sc], w=[pm])
            for n in range(2):
                P.op('dve', lambda e, n=n: e.tensor_tensor(out=mod[:, :, n], in0=pm[:, 0:96].rearrange("p (j n) -> p j n", n=2)[:, :, n],
                                                           in1=bm[:, :, 0], op=ALU.add), r=[pm, bm], w=[mod])
        self.tap('mod', mod[:], [128, 48, 2], [mod])
        G1 = P.sb([128, 8, 2], name='G1'); G2 = P.sb([128, 8, 1], name='G2')
        for n in range(2):
            P.op('dve', lambda e, n=n: e.scalar_tensor_tensor(out=G1[:, :, n], in0=mod[:, 8:16, n], scalar=1.0, in1=gains[:, :, 0],
                                                              op0=ALU.add, op1=ALU.mult), r=[mod, gains], w=[G1])
        P.op('dve', lambda e: e.scalar_tensor_tensor(out=G2[:, :, 0], in0=mod[:, 32:40, 0], scalar=1.0, in1=gains[:, :, 1],
                                                     op0=ALU.add, op1=ALU.mult), r=[mod, gains], w=[G2])
        self.mod, self.gains = mod, gains

        arena = P.sb([128, 8 * T + 8 * NLAT], BF16, name='arena')
        hT = arena[:, 0:8 * T].rearrange("p (k t) -> p k t", t=T)
        xv = xT.rearrange("(k p) t -> p k t", p=128)
        cv = ctxT.rearrange("(k p) t -> p k t", p=128)
        self.modulate(hT, [(cv, 0, 256, 0, 1)] + [(xv, 512 * i, 512, 256 + 512 * i, 0) for i in range(4)], G1, mod, 0)
        self.tap('hT', hT[:], [128, 8, T], [(hT, i) for i in range(5)], dt=BF16)
        self.hT = hT
        if self.stop_after == 'B':
            return self.finish()
        self.rwT = arena[:, 8 * T:8 * T + 8 * NLAT].rearrange("p (k t) -> p k t", t=NLAT)
        if self.stop_after != 'C':
            with P.scope():
                self.rwkv()
        self.tap('rw', self.rwT[:], [128, 8, NLAT], [self.rwT], dt=BF16)
        if self.stop_after in ('D', 'D0'):
            return self.finish()
        self.lruT, lru_st = P.sbm([128, 10, NLAT], BF16, name='lruT')
        with P.scope():
            self.lru()
        self.tap('lru', self.lruT[:], [128, 10, NLAT], [self.lruT], dt=BF16)
        if self.stop_after == 'C':
            return self.finish()
        self.mT, m_st = P.sbm([128, 8, NLAT], BF16, name='mT')
        with P.scope():
            self.merge()
        P.free([])
        self.x1T = arena[:].bitcast(F32)[:, 0:8 * NLAT].rearrange("p (k t) -> p k t", t=NLAT)
        with P.scope():
            self.resid1()
        P.free([m_st, lru_st])
        self.tap('x1', self.x1T[:], [128, 8, NLAT], [self.x1T])
        if self.stop_after == 'E':
            return self.finish()
        self.h2T = P.sb([128, 8, NLAT], BF16, name='h2T')
        self.modulate(self.h2T, [(None, 512 * i, 512, 512 * i, 0) for i in range(4)], G2, mod, 24, src_sb=self.x1T)
        with P.scope():
            self.ffn()
        with P.scope():
            self.final(outT)
        return self.finish()

    def modulate(self, hT, blocks, G, mod, shift_j0, src_sb=None):
        P, pb = self.P, self.pb
        with P.scope():
            xb = [P.sb([128, 8, 512], name='xb') for _ in range(2)]
            sq = P.sb([128, 8, 512], name='sq')
            rs = P.sb([128, 512], name='rs')
            epst = P.sb([128, 1], name='epst')
            P.op('dve', lambda e: e.memset(epst[:], RMS_EPS), w=[epst])
            for bi, (src, so, n, do, mn) in enumerate(blocks):
                if src_sb is None:
                    x = xb[bi % 2]
                    for k2 in range(2):
                        P.dma('sp' if k2 == 0 else 'act', x[:, 4 * k2:4 * k2 + 4, 0:n], src[:, 4 * k2:4 * k2 + 4, so:so + n],
                              w=[(x, k2)], group='xb%d' % (bi % 2))
                    xr = [(x, 0), (x, 1)]
                    xa = lambda k, x=x, n=n: x[:, k, 0:n]
                    xall = x[:, :, 0:n]
                else:
                    xr = [src_sb]
                    xa = lambda k, so=so, n=n: src_sb[:, k, so:so + n]
                    xall = src_sb[:, :, so:so + n]
                self.act(sq[:, :, 0:n], xall, AF.Square, r=xr, w=[sq])
                pp = pb[bi % 2]
                for k in range(8):
                    self.mm(pp[:, 0:n], self.ones[:], sq[:, k, 0:n], start=(k == 0), stop=(k == 7), r=[sq, self.ones], w=[pp])
                self.act(rs[:, 0:n], pp[:, 0:n], AF.Sqrt, scale=1.0 / D, bias=epst[:], r=[pp, epst], w=[rs])
                P.op('dve', lambda e, n=n: e.reciprocal(out=rs[:, 0:n], in_=rs[:, 0:n]), r=[rs], w=[rs])
                for k in range(8):
                    P.op('dve', lambda e, k=k, n=n, xa=xa: e.tensor_tensor(out=sq[:, k, 0:n], in0=xa(k), in1=rs[:, 0:n], op=ALU.mult),
                         r=xr + [rs], w=[sq])
                    self.act(hT[:, k, do:do + n], sq[:, k, 0:n], AF.Identity, scale=G[:, k, mn:mn + 1], bias=mod[:, shift_j0 + k, mn:mn + 1],
                             r=[sq, G, mod], w=[(hT, bi)])

    def zshift(self, cq, ncol, dsts, zbuf, A, wz, wzb, mixw, segs=((0, 1, 256, 0), (1, 258, 2048, 256))):
        P, pb, hT = self.P, self.pb, self.hT
        w_in = self.din['w_in']
        i = self.zcount = getattr(self, 'zcount', 0) + 1
        wf, wb = wz[i % 2], wzb[i % 2]
        if isinstance(zbuf, list):
            zbuf = zbuf[i % len(zbuf)]
        if isinstance(A, list):
            A = A[i % len(A)]
        pre = getattr(self, 'zpre', None)
        if pre is not None and pre[0] == cq:
            wb = pre[1]
            self.zpre = None
        else:
            self.zload(cq, ncol, wf, wb, 'wz%d' % (i % 2))
        hk = [(hT, j) for j in range(5)]
        lat = lambda k: hT[:, k, 256:2304].rearrange("p (r c) -> p c r", c=64)
        nblk = 0
        for (seg, zc, n, _) in segs:
            nb = 1 if seg == 0 else 4
            for bi in range(nb):
                pp = pb[(nblk + 5 * i) % 6]; nblk += 1
                bn = 256 if seg == 0 else 512
                for k in range(8):
                    if seg == 0:
                        self.mm(pp[0:ncol, 0:256], wb[:, k, 0:ncol], hT[:, k, 0:256], start=(k == 0), stop=(k == 7), r=[wb] + hk, w=[pp])
                    else:
                        self.mm(pp[0:ncol, 0:512], wb[:, k, 0:ncol], hT[:, k, 256 + 512 * bi:256 + 512 * bi + 512],
                                start=(k == 0), stop=(k == 7), r=[wb] + hk, w=[pp])
                if seg == 0:
                    P.op('act', lambda e: e.copy(out=zbuf[0:ncol, zc:zc + 256], in_=pp[0:ncol, 0:256]), r=[pp], w=[zbuf])
                else:
                    zo = zbuf[0:ncol, zc:zc + 2048].rearrange("p (c r) -> p r c", r=32)[:, 8 * bi:8 * bi + 8, :]
                    P.op('act', lambda e: e.copy(out=zo, in_=pp[0:ncol, 0:512].rearrange("p (r c) -> p r c", c=64)), r=[pp], w=[zbuf])
        for (seg, zc, n, _), dst in zip(segs, dsts):
            if dst is None:
                continue
            dt, do, key = dst
            P.op('dve', lambda e, zc=zc, n=n: e.tensor_scalar(out=A[0:ncol, 0:n], in0=zbuf[0:ncol, zc:zc + n], scalar1=mixw[0:ncol, cq, 2:3],
                                                              scalar2=None, op0=ALU.mult), r=[zbuf, mixw], w=[A])
            P.op('dve', lambda e, zc=zc, n=n: e.scalar_tensor_tensor(out=A[0:ncol, 0:n], in0=zbuf[0:ncol, zc - 1:zc - 1 + n], scalar=mixw[0:ncol, cq, 0:1],
                                                                     in1=A[0:ncol, 0:n], op0=ALU.mult, op1=ALU.add), r=[zbuf, mixw, A], w=[A])
            P.op('dve', lambda e, zc=zc, n=n, dt=dt, do=do: e.scalar_tensor_tensor(out=dt[0:ncol, do:do + n], in0=zbuf[0:ncol, zc + 1:zc + 1 + n],
                                                                                   scalar=mixw[0:ncol, cq, 1:2], in1=A[0:ncol, 0:n],
                                                                                   op0=ALU.mult, op1=ALU.add), r=[zbuf, mixw, A], w=[key])

    def zload(self, cq, ncol, wf, wb, group):
        P = self.P
        c0 = 2560 + 128 * cq
        P.dma('sp', wf[:, :, 0:ncol], self.din['w_in'].rearrange("(k p) n -> p k n", p=128)[:, :, c0:c0 + ncol], w=[wf], group=group)
        P.op('pool', lambda e: e.tensor_copy(out=wb[:, :, 0:ncol], in_=wf[:, :, 0:ncol]), r=[wf], w=[wb])

    def rwkv(self):
        P, pb, pbh, hT, din = self.P, self.pb, self.pbh, self.hT, self.din
        rwT = self.rwT
        CW = -0.5 * float(np.exp(-0.5))
        mixw = self.cols([din['rwkv_mu'][0], din['rwkv_mu'][1], din['rwkv_mu'][0]], RIN, 128, 'mixw')
        P.op('dve', lambda e: e.tensor_tensor(out=mixw[:, :, 2], in0=mixw[:, :, 0], in1=mixw[:, :, 1], op=ALU.add), r=[mixw], w=[mixw])
        P.op('dve', lambda e: e.tensor_scalar(out=mixw[:, :, 2], in0=mixw[:, :, 2], scalar1=-1.0, scalar2=1.0, op0=ALU.mult, op1=ALU.add), r=[mixw], w=[mixw])
        chp = self.cols([din['rwkv_w0'][0], din['rwkv_w0'][1], din['rwkv_a0'][0], din['rwkv_a0'][1], din['rwkv_k_k'], din['rwkv_k_a'],
                         din['rwkv_r_k'], din['rwkv_ln_g'], din['rwkv_ln_b']], D, 128, 'chp')
        hp2 = P.sb([128, 8, 6], name='hp2')
        P.op('dve', lambda e: e.tensor_scalar(out=hp2[:, :, 0:4], in0=chp[:, :, 0:4], scalar1=0.5, scalar2=None, op0=ALU.mult), r=[chp], w=[hp2])
        P.op('dve', lambda e: e.tensor_scalar(out=hp2[:, :, 4:5], in0=chp[:, :, 5:6], scalar1=0.5, scalar2=None, op0=ALU.mult), r=[chp], w=[hp2])
        P.op('dve', lambda e: e.tensor_scalar(out=hp2[:, :, 5:6], in0=chp[:, :, 5:6], scalar1=-0.5, scalar2=1.0, op0=ALU.mult, op1=ALU.add), r=[chp], w=[hp2])
        gneps = P.sb([128, 1], name='gneps')
        P.op('dve', lambda e: e.memset(gneps[:], GN_EPS), w=[gneps])
        w2s = P.sb([128, D], BF16, name='w2s'); a2s = P.sb([128, D], BF16, name='a2s')
        g2a = P.sb([128, D], BF16, name='g2a'); g2b = P.sb([32, D], BF16, name='g2b')
        with P.scope():
            st = P.sb([128, D], name='lst')
            for src, dst, npart in ((din['rwkv_w2'].rearrange("d r c -> (d r) c"), w2s, 128), (din['rwkv_a2'].rearrange("d r c -> (d r) c"), a2s, 128),
                                    (din['rwkv_g2'][0:128, :], g2a, 128), (din['rwkv_g2'][128:160, :], g2b, 32)):
                P.dma('sp', st[0:npart, :], src, w=[st], group='lst')
                P.op('dve', lambda e, dst=dst, npart=npart: e.tensor_copy(out=dst[0:npart, :], in_=st[0:npart, :]), r=[st], w=[dst])
        msk = {}
        onesb = P.sb([128, 4, 64], BF16, name='onesb')
        P.op('dve', lambda e: e.memset(onesb[:], 1.0), w=[onesb])
        for nm, op, sgn in (('su', ALU.is_gt, -1), ('sl', ALU.is_gt, 1), ('iu', ALU.is_ge, -1), ('id', ALU.is_equal, 1)):
            m = P.sb([128, 4, 64], BF16, name='m' + nm)
            for e_ in range(2):
                P.op('pool', lambda e, m=m, op=op, sgn=sgn, e_=e_: e.affine_select(out=m[64 * e_:64 * e_ + 64], in_=onesb[64 * e_:64 * e_ + 64],
                                                                                   pattern=[[0, 4], [-sgn, 64]], compare_op=op, fill=0.0,
                                                                                   base=0, channel_multiplier=sgn), r=[onesb], w=[m])
            msk[nm] = m
        cmask = P.sb([128, 256], name='cmask')
        P.op('dve', lambda e: e.memset(cmask[:], 1.0), w=[cmask])
        P.op('dve', lambda e: e.memset(cmask[:, 0:256:64], 0.0), w=[cmask])
        twd = P.sb([128, T], BF16, name='twd'); adb = P.sb([128, T], BF16, name='adb')
        sgd1 = P.sb([128, NLAT], BF16, name='sgd1'); sgd2 = P.sb([32, NLAT], BF16, name='sgd2')

        with P.scope():
            zbuf = [P.sb([128, 2307], name='zbuf') for _ in range(2)]; A = [P.sb([128, 2048], name='zA') for _ in range(2)]
            wz = [P.sb([128, 8, 128], name='wz') for _ in range(2)]; wzb = [P.sb([128, 8, 128], BF16, name='wzb') for _ in range(2)]
            for zb in zbuf:
                P.op('pool', lambda e, zb=zb: e.memset(zb[:], 0.0), w=[zb])
            tmp = P.sb([128, T], name='ltmp')
            self.zshift(24, 128, [(tmp, 0, tmp), (tmp, 256, tmp)], zbuf, A, wz, wzb, mixw)
            self.act(twd[:], tmp[:], AF.Tanh, r=[tmp], w=[twd])
            self.zshift(25, 128, [(adb, 0, adb), (adb, 256, adb)], zbuf, A, wz, wzb, mixw)
            for cq, ncol, dst in ((26, 128, sgd1), (27, 32, sgd2)):
                self.zshift(cq, ncol, [None, (tmp, 256, tmp)], zbuf, A, wz, wzb, mixw)
                self.act(tmp[0:ncol, 256:T], tmp[0:ncol, 256:T], AF.Tanh, scale=0.5, r=[tmp], w=[tmp])
                P.op('dve', lambda e, dst=dst, ncol=ncol: e.tensor_scalar(out=dst[0:ncol, :], in0=tmp[0:ncol, 256:T], scalar1=0.5, scalar2=0.5,
                                                                          op0=ALU.mult, op1=ALU.add), r=[tmp], w=[dst])

        pre_f = P.sb([128, 8, 128], name='pre_f'); pre_b = P.sb([128, 8, 128], BF16, name='pre_b')
        self.zload(8, 128, pre_f, pre_b, 'wzpre')
        self.zpre = (8, pre_b)
        self.tap('twd', twd[:], [128, T], [twd], dt=BF16)
        self.tap('adb', adb[:], [128, T], [adb], dt=BF16)
        self.tap('sgd1', sgd1[:], [128, NLAT], [sgd1], dt=BF16)
        for hp in range(8):
            if self.stop_after == 'D0' and hp > 0:
                break
            with P.scope():
                rb = P.sb([128, T], BF16, name='rb'); kb = P.sb([128, T], BF16, name='kb'); vb = P.sb([128, T], BF16, name='vb')
                kkb = P.sb([128, T], BF16, name='kkb')
                y0 = P.sb([128, NLAT], name='y0'); bacc = P.sb([128, NLAT], name='bacc')
                with P.scope():
                    zbufs = [P.sb([128, 2307], name='zbuf') for _ in range(3)]; As = [P.sb([128, 2048], name='zA') for _ in range(2)]
                    wz = [P.sb([128, 8, 128], name='wz') for _ in range(2)]; wzb = [P.sb([128, 8, 128], BF16, name='wzb') for _ in range(2)]
                    for zb in zbufs:
                        for pc in (0, 257, 2306):
                            P.op('pool', lambda e, zb=zb, pc=pc: e.memset(zb[:, pc:pc + 1], 0.0), w=[zb])
                    for cq, dst in ((8 + hp, kb), (hp, rb), (16 + hp, vb)):
                        self.zshift(cq, 128, [(dst, 0, dst), (dst, 256, dst)], zbufs, As, wz, wzb, mixw)
                    if hp < 7:
                        self.zload(8 + hp + 1, 128, pre_f, pre_b, 'wzpre')
                        self.zpre = (8 + hp + 1, pre_b)
                    kq = zbufs[0]; A = As[0]
                    self.act(kq[:, 0:T], kb[:], AF.Identity, scale=chp[:, hp, 4:5], r=[kb, chp], w=[kq])
                    for bi, (o, n) in enumerate([(0, 512), (512, 512), (1024, 512), (1536, 512), (2048, 256)]):
                        P.op('dve', lambda e, o=o, n=n: e.tensor_tensor(out=A[:, 0:n], in0=kq[:, o:o + n], in1=kq[:, o:o + n], op=ALU.mult), r=[kq], w=[A])
                        pp = pb[bi % 5]
                        self.mm(pp[:, 0:n], self.bones[:], A[:, 0:n], r=[A, self.bones], w=[pp])
                        self.act(A[:, 512:512 + n], pp[:, 0:n], AF.Sqrt, r=[pp], w=[A])
                        P.op('dve', lambda e, n=n: e.tensor_scalar(out=A[:, 512:512 + n], in0=A[:, 512:512 + n], scalar1=1e-12, scalar2=None, op0=ALU.max), r=[A], w=[A])
                        P.op('dve', lambda e, n=n: e.reciprocal(out=A[:, 512:512 + n], in_=A[:, 512:512 + n]), r=[A], w=[A])
                        P.op('dve', lambda e, o=o, n=n: e.tensor_tensor(out=kkb[:, o:o + n], in0=kq[:, o:o + n], in1=A[:, 512:512 + n], op=ALU.mult), r=[A, kq], w=[kkb])
                if hp == 0:
                    for nm, t_ in (('rb', rb), ('kb', kb), ('vb', vb), ('kkb', kkb)):
                        self.tap(nm, t_[:], [128, T], [t_], dt=BF16)
                for j in range(8):
                    P.op('pool', lambda e: e.memset(y0[:, 256 * j:256 * j + 256], 0.0), w=[(y0, 256 * j)])
                    P.op('pool', lambda e: e.memset(bacc[:, 256 * j:256 * j + 256], 0.0), w=[(bacc, 256 * j)])
                C = dict(rb=rb, kb=kb, vb=vb, kkb=kkb, y0=y0, bacc=bacc, twd=twd, adb=adb, w2s=w2s, a2s=a2s, chp=chp, hp2=hp2, msk=msk,
                         cmask=cmask, CW=CW, rot=[0])
                with P.scope():
                    self.rw_rounds(hp, C)
                if hp == 0:
                    self.tap('y0', y0[:], [128, NLAT], [y0])
                    self.tap('bacc', bacc[:], [128, NLAT], [bacc])
                with P.scope():
                    yc = P.sb([128, 512], name='yc'); sq = P.sb([128, 512], name='sq2'); rstd = P.sb([128, 512], name='rstd'); tt = P.sb([128, 512], name='tt')
                    for i in range(4):
                        c0 = 512 * i
                        pm, pv, pg = pb[0], pb[1], pb[2]
                        self.mm(pm[:], self.bones[:], y0[:, c0:c0 + 512], r=[y0, self.bones], w=[pm])
                        P.op('dve', lambda e, c0=c0: e.scalar_tensor_tensor(out=yc[:], in0=pm[:], scalar=-1.0 / 64, in1=y0[:, c0:c0 + 512], op0=ALU.mult, op1=ALU.add),
                             r=[pm, y0], w=[yc])
                        P.op('dve', lambda e: e.tensor_tensor(out=sq[:], in0=yc[:], in1=yc[:], op=ALU.mult), r=[yc], w=[sq])
                        self.mm(pv[:], self.bones[:], sq[:], r=[sq, self.bones], w=[pv])
                        self.act(rstd[:], pv[:], AF.Sqrt, scale=1.0 / 64, bias=gneps[:], r=[pv, gneps], w=[rstd])
                        P.op('dve', lambda e: e.reciprocal(out=rstd[:], in_=rstd[:]), r=[rstd], w=[rstd])
                        P.op('dve', lambda e: e.tensor_tensor(out=yc[:], in0=yc[:], in1=rstd[:], op=ALU.mult), r=[yc, rstd], w=[yc])
                        self.act(yc[:], yc[:], AF.Identity, scale=chp[:, hp, 7:8], bias=chp[:, hp, 8:9], r=[yc, chp], w=[yc])
                        P.op('dve', lambda e, c0=c0: e.tensor_tensor(out=tt[:], in0=vb[:, 256 + c0:256 + c0 + 512], in1=bacc[:, c0:c0 + 512], op=ALU.mult), r=[vb, bacc], w=[tt])
                        P.op('dve', lambda e: e.tensor_tensor(out=tt[:], in0=tt[:], in1=yc[:], op=ALU.add), r=[tt, yc], w=[tt])
                        self.mm(pg[:], g2a[:, hp * 128:(hp + 1) * 128], sgd1[:, c0:c0 + 512], start=True, stop=False, r=[g2a, sgd1], w=[pg])
                        self.mm(pg[:], g2b[0:32, hp * 128:(hp + 1) * 128], sgd2[0:32, c0:c0 + 512], start=False, stop=True, r=[g2b, sgd2], w=[pg])
                        P.op('dve', lambda e, i=i: e.tensor_tensor(out=rwT[:, hp, :].rearrange("p (r c) -> p c r", c=64)[:, 16 * i:16 * i + 16, :],
                                                                   in0=tt[:].rearrange("p (c r) -> p c r", r=32), in1=pg[:].rearrange("p (c r) -> p c r", r=32),
                                                                   op=ALU.mult), r=[tt, pg], w=[rwT])

    def rw_alloc_dir(self):
        P = self.P
        S = {}
        S['A1'] = P.sb([128, 256], name='A1'); S['B1'] = P.sb([128, 256], name='B1'); S['C1'] = P.sb([128, 256], name='C1')
        S['s1'] = [{nm: P.sb([128, 256], BF16, name=nm) for nm in ('at', 'bt', 'kt', 'vs')} for _ in range(2)]
        S['rw'] = [{'rt': P.sb([128, 256], BF16, name='rt'), 'wc': P.sb([128, 4], name='wc')} for _ in range(4)]
        S['QX'] = [[P.sb([128, 4, 2, 64], BF16, name='QX') for _ in range(2)] for _ in range(2)]
        S['QT'] = [[P.sb([128, 4, 64], BF16, name='QT') for _ in range(2)] for _ in range(2)]
        S['AakT'] = [P.sb([128, 4, 64], BF16, name='AakT') for _ in range(2)]
        S['slots'] = []
        for _ in range(3):
            sl = {'tokT': P.sb([128, 4, 4, 64], BF16, name='tokT'), 'MT': P.sb([128, 4, 64], BF16, name='MT'), 'Xak': P.sb([128, 4, 64], BF16, name='Xak'),
                  'ArbT': P.sb([128, 4, 64], BF16, name='ArbT'), 'ArkT': P.sb([128, 4, 64], BF16, name='ArkT'), 'AhT': P.sb([128, 4, 64], BF16, name='AhT')}
            S['slots'].append(sl)
        S['Tst'] = P.sb([128, 64], name='Tst'); S['Tw'] = P.sb([128, 64], name='Tw'); S['Tb'] = P.sb([128, 64], BF16, name='Tb')
        S['Ub'] = P.sb([128, 64], BF16, name='Ub')
        for nm in ('Tst', 'Tw', 'Tb'):
            P.op('dve', lambda e, t=S[nm]: e.memset(t[:], 0.0), w=[S[nm]])
        return S

    def gen_S1(self, hp, d, g, S, C):
        P, pb, pbh = self.P, self.pb, self.pbh
        rb, kb, vb, kkb, bacc = C['rb'], C['kb'], C['vb'], C['kkb'], C['bacc']
        twd, adb, w2s, a2s, chp, hp2, msk, cmask, CW = (C[k] for k in ('twd', 'adb', 'w2s', 'a2s', 'chp', 'hp2', 'msk', 'cmask', 'CW'))
        A1, B1, C1 = (S[k] for k in ('A1', 'B1', 'C1'))
        at, bt, kt, vs = (S['s1'][g % 2][k] for k in ('at', 'bt', 'kt', 'vs'))
        rt, wc = S['rw'][g % 4]['rt'], S['rw'][g % 4]['wc']
        if d == 0:
            s0, step = 256 * g, 1
        else:
            s0, step = (0 if g == 0 else 2304 - 256 * g), -1
        nat = slice(s0, s0 + 256)
        loc = lambda t: t[:, rsl(0 if step > 0 else 255, 256, step)]
        hc = slice(hp * 128, (hp + 1) * 128); ds = slice(64 * d, 64 * d + 64)
        rot = C['rot']
        H = lambda e_: slice(64 * e_, 64 * e_ + 64)
        TP = lambda e_: (64 * e_, 64 * e_)

        def bank():
            rot[0] = (rot[0] + 1) % 4
            return pb[(0, 1, 2, 5)[rot[0]]]
        pp = bank()
        self.mm(pp[:, 0:256], w2s[ds, hc], twd[ds, nat], r=[w2s, twd], w=[pp])
        self.act(loc(A1), pp[:, 0:256], AF.Tanh, scale=0.5, bias=hp2[:, hp, d:d + 1], r=[pp, hp2], w=[A1])
        yield
        P.op('dve', lambda e: e.tensor_scalar(out=A1[:], in0=A1[:], scalar1=1.0, scalar2=CW, op0=ALU.add, op1=ALU.mult), r=[A1], w=[A1])
        P.op('dve', lambda e: e.tensor_tensor_scan(out=B1[:], data0=cmask[:, 0:256], data1=A1[:], initial=0.0, op0=ALU.mult, op1=ALU.add), r=[A1, cmask], w=[B1])
        P.op('dve', lambda e: e.tensor_tensor(out=A1[:], in0=B1[:], in1=A1[:], op=ALU.subtract), r=[A1, B1], w=[A1])
        yield
        self.act(C1[:], A1[:], AF.Exp, r=[A1], w=[C1])
        P.op('dve', lambda e: e.scalar_tensor_tensor(out=loc(at), in0=kkb[:, nat], scalar=-1.0, in1=loc(C1), op0=ALU.mult, op1=ALU.mult), r=[kkb, C1], w=[at])
        yield
        self.act(C1[:], B1[:], AF.Exp, r=[B1], w=[C1])
        P.op('dve', lambda e: e.tensor_copy(out=wc[:], in_=C1[:, 63:256:64]), r=[C1], w=[wc])
        P.op('dve', lambda e: e.tensor_tensor(out=loc(rt), in0=rb[:, nat], in1=loc(C1), op=ALU.mult), r=[rb, C1], w=[rt])
        yield
        self.act(C1[:], B1[:], AF.Exp, scale=-1.0, r=[B1], w=[C1])
        pp = bank()
        self.mm(pp[:, 0:256], a2s[ds, hc], adb[ds, nat], r=[a2s, adb], w=[pp])
        self.act(loc(A1), pp[:, 0:256], AF.Tanh, scale=0.5, bias=hp2[:, hp, 2 + d:3 + d], r=[pp, hp2], w=[A1])
        yield
        P.op('dve', lambda e: e.tensor_scalar(out=B1[:], in0=A1[:], scalar1=0.5, scalar2=0.5, op0=ALU.mult, op1=ALU.add), r=[A1], w=[B1])
        P.op('dve', lambda e: e.tensor_tensor(out=loc(B1), in0=loc(B1), in1=kkb[:, nat], op=ALU.mult), r=[B1, kkb], w=[B1])
        P.op('dve', lambda e: e.tensor_tensor(out=bt[:], in0=B1[:], in1=C1[:], op=ALU.mult), r=[B1, C1], w=[bt])
        yield
        P.op('dve', lambda e: e.tensor_scalar(out=A1[:], in0=A1[:], scalar1=hp2[:, hp, 4:5], scalar2=hp2[:, hp, 5:6], op0=ALU.mult, op1=ALU.add), r=[A1, hp2], w=[A1])
        P.op('dve', lambda e: e.tensor_tensor(out=loc(A1), in0=loc(A1), in1=kb[:, nat], op=ALU.mult), r=[A1, kb], w=[A1])
        P.op('dve', lambda e: e.tensor_tensor(out=kt[:], in0=A1[:], in1=C1[:], op=ALU.mult), r=[A1, C1], w=[kt])
        P.op('pool', lambda e: e.tensor_copy(out=loc(vs), in_=vb[:, nat]), r=[vb], w=[vs])
        yield
        if s0 >= 256:
            P.op('dve', lambda e: e.scalar_tensor_tensor(out=B1[:], in0=loc(A1), scalar=chp[:, hp, 6:7], in1=rb[:, nat], op0=ALU.mult, op1=ALU.mult),
                 r=[A1, chp, rb, bt], w=[B1])
            pp = bank()
            self.mm(pp[:, 0:256], self.bones[:], B1[:], r=[B1, self.bones], w=[pp])
            bo = s0 - 256
            P.op('dve', lambda e: e.tensor_tensor(out=bacc[:, bo:bo + 256], in0=bacc[:, bo:bo + 256], in1=pp[:, 0:256], op=ALU.add), r=[pp, (bacc, bo)], w=[(bacc, bo)])
            yield

    def gen_S2a(self, hp, d, g, S, C):
        P, pb, pbh = self.P, self.pb, self.pbh
        msk = C['msk']
        at, bt, kt, vs = (S['s1'][g % 2][k] for k in ('at', 'bt', 'kt', 'vs'))
        rt = S['rw'][g % 4]['rt']
        sl = S['slots'][g % 3]
        tokT = sl['tokT']
        rot = C['rot']
        H = lambda e_: slice(64 * e_, 64 * e_ + 64)
        TP = lambda e_: (64 * e_, 64 * e_)

        def bank():
            rot[0] = (rot[0] + 1) % 4
            return pb[(0, 1, 2, 5)[rot[0]]]
        v3 = lambda p: p[:, 0:256].rearrange("p (c t) -> p c t", t=64)
        v4 = lambda p: p[:, :].rearrange("p (c x) -> p c x", x=128)
        QXs, QTs, AakT = S['QX'][g % 2], S['QT'][g % 2], S['AakT'][g % 2]

        def neumann_level(lvl, QX, QT):
            QXn, QTn = QXs[lvl % 2], QTs[lvl % 2]
            last = lvl == 6
            if lvl == 1:
                pq = bank()
                for c in range(4):
                    for e_ in range(2):
                        self.mm(v3(pq)[H(e_), c, :], QT[H(e_), c, :], QX[H(e_), c, 0, :], r=[QX, QT], w=[pq], tile_position=TP(e_))
                P.op('act', lambda e: e.copy(out=QXn[:, :, 0, :], in_=v3(pq)), r=[pq], w=[QXn])
                P.op('pool', lambda e: e.tensor_copy(out=QXn[:, :, 1, :], in_=QX[:, :, 1, :]), r=[QX], w=[QXn])
            else:
                ppx = bank()
                for c in range(4):
                    for e_ in range(2):
                        if last:
                            self.mm(v4(ppx)[H(e_), c, 64:128], QT[H(e_), c, :], QX[H(e_), c, 1, :], r=[QX, QT], w=[ppx], tile_position=TP(e_))
                        else:
                            self.mm(v4(ppx)[H(e_), c, :], QT[H(e_), c, :], QX[H(e_), c, :, :].rearrange("p a b -> p (a b)"), r=[QX, QT], w=[ppx],
                                    tile_position=TP(e_))
                dstP = sl['MT'][:] if last else QXn[:, :, 1, :]
                P.op('dve', lambda e: e.tensor_tensor(out=dstP, in0=v4(ppx)[:, :, 64:128], in1=QX[:, :, 1, :], op=ALU.add), r=[ppx, QX], w=[sl['MT'] if last else QXn])
                if not last:
                    P.op('act', lambda e: e.copy(out=QXn[:, :, 0, :], in_=v4(ppx)[:, :, 0:64]), r=[ppx], w=[QXn])
            if not last:
                pqt = bank()
                for c in range(4):
                    for e_ in range(2):
                        self.mm(v3(pqt)[H(e_), c, :], QX[H(e_), c, 0, :], QT[H(e_), c, :], r=[QX, QT], w=[pqt], tile_position=TP(e_))
                P.op('act', lambda e: e.copy(out=QTn[:], in_=v3(pqt)), r=[pqt], w=[QTn])
            return QXn, QTn
        for qi, q in enumerate((at, bt, kt, vs)):
            for c in range(4):
                for e_ in range(2):
                    o = (qi % 2) * 256 + c * 64
                    P.op('pe', lambda e: e.transpose(out=pbh[H(e_), o:o + 64], in_=q[H(e_), 64 * c:64 * c + 64], identity=self.identb[H(e_), H(e_)],
                                                     tile_position=TP(e_)), r=[q, self.identb], w=[pbh])
            if qi % 2 == 1:
                P.op('act', lambda e: e.copy(out=tokT[:, qi - 1:qi + 1, :, :].rearrange("p q c j -> p (q c j)"), in_=pbh[:, 0:512]), r=[pbh], w=[tokT])
                yield
        cs = lambda q, c, e_: q[H(e_), 64 * c:64 * c + 64]

        def score(L, Rr, mk, dst, dkey):
            pp = bank()
            for c in range(4):
                for e_ in range(2):
                    self.mm(v3(pp)[H(e_), c, :], cs(L, c, e_), cs(Rr, c, e_), r=[L, Rr], w=[pp], tile_position=TP(e_))
            P.op('dve', lambda e: e.tensor_tensor(out=dst, in0=v3(pp), in1=msk[mk][:], op=ALU.mult), r=[pp, msk[mk]], w=[dkey])
        QX, QT = QXs[0], QTs[0]
        score(bt, at, 'su', QX[:, :, 0, :], QX)
        yield
        score(at, bt, 'sl', QT[:], QT)
        yield
        P.op('pool', lambda e: e.tensor_tensor(out=QX[:, :, 1, :], in0=QX[:, :, 0, :], in1=msk['id'][:], op=ALU.add), r=[QX, msk['id']], w=[QX])
        score(kt, at, 'su', AakT[:], AakT)
        yield
        score(bt, rt, 'iu', sl['ArbT'][:], sl['ArbT'])
        yield
        score(kt, rt, 'iu', sl['ArkT'][:], sl['ArkT'])
        yield
        for lvl in (1, 2, 3):
            QX, QT = neumann_level(lvl, QX, QT)
            yield

    def gen_S2b(self, hp, d, g, S, C):
        P, pb, pbh = self.P, self.pb, self.pbh
        msk = C['msk']
        at, bt, kt, vs = (S['s1'][g % 2][k] for k in ('at', 'bt', 'kt', 'vs'))
        rt = S['rw'][g % 4]['rt']
        sl = S['slots'][g % 3]
        tokT = sl['tokT']
        rot = C['rot']
        H = lambda e_: slice(64 * e_, 64 * e_ + 64)
        TP = lambda e_: (64 * e_, 64 * e_)

        def bank():
            rot[0] = (rot[0] + 1) % 4
            return pb[(0, 1, 2, 5)[rot[0]]]
        v3 = lambda p: p[:, 0:256].rearrange("p (c t) -> p c t", t=64)
        v4 = lambda p: p[:, :].rearrange("p (c x) -> p c x", x=128)
        QXs, QTs, AakT = S['QX'][g % 2], S['QT'][g % 2], S['AakT'][g % 2]

        def neumann_level(lvl, QX, QT):
            QXn, QTn = QXs[lvl % 2], QTs[lvl % 2]
            last = lvl == 6
            if lvl == 1:
                pq = bank()
                for c in range(4):
                    for e_ in range(2):
                        self.mm(v3(pq)[H(e_), c, :], QT[H(e_), c, :], QX[H(e_), c, 0, :], r=[QX, QT], w=[pq], tile_position=TP(e_))
                P.op('act', lambda e: e.copy(out=QXn[:, :, 0, :], in_=v3(pq)), r=[pq], w=[QXn])
                P.op('pool', lambda e: e.tensor_copy(out=QXn[:, :, 1, :], in_=QX[:, :, 1, :]), r=[QX], w=[QXn])
            else:
                ppx = bank()
                for c in range(4):
                    for e_ in range(2):
                        if last:
                            self.mm(v4(ppx)[H(e_), c, 64:128], QT[H(e_), c, :], QX[H(e_), c, 1, :], r=[QX, QT], w=[ppx], tile_position=TP(e_))
                        else:
                            self.mm(v4(ppx)[H(e_), c, :], QT[H(e_), c, :], QX[H(e_), c, :, :].rearrange("p a b -> p (a b)"), r=[QX, QT], w=[ppx],
                                    tile_position=TP(e_))
                dstP = sl['MT'][:] if last else QXn[:, :, 1, :]
                P.op('dve', lambda e: e.tensor_tensor(out=dstP, in0=v4(ppx)[:, :, 64:128], in1=QX[:, :, 1, :], op=ALU.add), r=[ppx, QX], w=[sl['MT'] if last else QXn])
                if not last:
                    P.op('act', lambda e: e.copy(out=QXn[:, :, 0, :], in_=v4(ppx)[:, :, 0:64]), r=[ppx], w=[QXn])
            if not last:
                pqt = bank()
                for c in range(4):
                    for e_ in range(2):
                        self.mm(v3(pqt)[H(e_), c, :], QX[H(e_), c, 0, :], QT[H(e_), c, :], r=[QX, QT], w=[pqt], tile_position=TP(e_))
                P.op('act', lambda e: e.copy(out=QTn[:], in_=v3(pqt)), r=[pqt], w=[QTn])
            return QXn, QTn
        QX, QT = QXs[1], QTs[1]
        for lvl in (4, 5, 6):
            QX, QT = neumann_level(lvl, QX, QT)
            yield
        MT = sl['MT']
        pxa = bank()
        for c in range(4):
            for e_ in range(2):
                self.mm(v3(pxa)[H(e_), c, :], AakT[H(e_), c, :], tokT[H(e_), 3, c, :], r=[AakT, tokT], w=[pxa], tile_position=TP(e_))
        P.op('act', lambda e: e.copy(out=sl['Xak'][:], in_=v3(pxa)), r=[pxa], w=[sl['Xak']])
        yield
        pA = bank()
        for c in range(4):
            for e_ in range(2):
                self.mm(v3(pA)[H(e_), c, :], tokT[H(e_), 0, c, :], MT[H(e_), c, :], r=[tokT, MT], w=[pA], tile_position=TP(e_))
        P.op('act', lambda e: e.copy(out=sl['AhT'][:], in_=v3(pA)), r=[pA], w=[sl['AhT']])
        yield

    def gen_Q(self, hp, d, g, S, C):
        P, pb = self.P, self.pb
        y0 = C['y0']
        sl = S['slots'][g % 3]
        tokT, MT, Xak, ArbT, ArkT, AhT = (sl[k] for k in ('tokT', 'MT', 'Xak', 'ArbT', 'ArkT', 'AhT'))
        rt, wc = S['rw'][g % 4]['rt'], S['rw'][g % 4]['wc']
        Tst, Tw, Tb, Ub = S['Tst'], S['Tw'], S['Tb'], S['Ub']
        pU, pT, pY = pb[3], pb[4], pb[6]
        pUv = pU[:, 0:64]
        pTv = pT[:, 0:64]
        pYv = pY[:, 256 * d:256 * d + 256].rearrange("p (c t) -> p c t", t=64)
        H = lambda e_: slice(64 * e_, 64 * e_ + 64)
        TP = lambda e_: (64 * e_, 64 * e_)
        latent = g >= 1
        for c in range(4):
            for e_ in range(2):
                self.mm(pUv[H(e_), :], MT[H(e_), c, :], Xak[H(e_), c, :], start=True, stop=False, r=[MT, Xak], w=[pU], tile_position=TP(e_))
            for e_ in range(2):
                self.mm(pUv[H(e_), :], AhT[H(e_), c, :], Tb[H(e_), :], start=False, stop=True, r=[AhT, Tb], w=[pU], tile_position=TP(e_))
            P.op('act', lambda e: e.copy(out=Ub[:], in_=pUv), r=[pU], w=[Ub])
            yield
            for e_ in range(2):
                self.mm(pTv[H(e_), :], tokT[H(e_), 1, c, :], Ub[H(e_), :], start=True, stop=False, r=[tokT, Ub], w=[pT], tile_position=TP(e_))
            for e_ in range(2):
                self.mm(pTv[H(e_), :], tokT[H(e_), 2, c, :], tokT[H(e_), 3, c, :], start=False, stop=True, r=[tokT], w=[pT], tile_position=TP(e_))
            if latent:
                for e_ in range(2):
                    self.mm(pYv[H(e_), c, :], Tb[H(e_), :], rt[H(e_), 64 * c:64 * c + 64], start=True, stop=False, r=[Tb, rt], w=[pY], tile_position=TP(e_))
                for e_ in range(2):
                    self.mm(pYv[H(e_), c, :], Ub[H(e_), :], ArbT[H(e_), c, :], start=False, stop=False, r=[Ub, ArbT], w=[pY], tile_position=TP(e_))
                for e_ in range(2):
                    self.mm(pYv[H(e_), c, :], tokT[H(e_), 3, c, :], ArkT[H(e_), c, :], start=False, stop=True, r=[tokT, ArkT], w=[pY], tile_position=TP(e_))
            wcc = wc[:, c:c + 1]
            P.op('dve', lambda e: e.scalar_tensor_tensor(out=Tb[:], in0=pTv, scalar=wcc, in1=Tw[:], op0=ALU.mult, op1=ALU.add), r=[pT, wc, Tw], w=[Tb])
            P.op('dve', lambda e: e.scalar_tensor_tensor(out=Tst[:], in0=pTv, scalar=wcc, in1=Tw[:], op0=ALU.mult, op1=ALU.add), r=[pT, wc, Tw], w=[Tst])
            yield
            if c < 3:
                P.op('dve', lambda e: e.tensor_scalar(out=Tw[:], in0=Tst[:], scalar1=wc[:, c + 1:c + 2], scalar2=None, op0=ALU.mult), r=[Tst, wc], w=[Tw])
        if latent:
            g0 = 256 * g
            if d == 0:
                ysl = slice(g0 - 256, g0); yk = (y0, g0 - 256)
            else:
                ysl = rsl(2303 - g0, 256, -1); yk = (y0, 2048 - g0)
            P.op('dve', lambda e: e.tensor_tensor(out=y0[:, ysl], in0=y0[:, ysl], in1=pY[:, 256 * d:256 * d + 256], op=ALU.add), r=[pY, yk], w=[yk])
        yield

    def rw_rounds(self, hp, C):
        P = self.P
        dirs = [self.rw_alloc_dir() for _ in range(2)]
        for R in range(12):
            gens = []
            if 3 <= R:
                for d in range(2):
                    S = dirs[d]
                    wc = S['rw'][(R - 3) % 4]['wc']
                    P.op('dve', lambda e, S=S, wc=wc: e.tensor_scalar(out=S['Tw'][:], in0=S['Tst'][:], scalar1=wc[:, 0:1], scalar2=None, op0=ALU.mult),
                         r=[S['Tst'], wc], w=[S['Tw']])
                    gens.append(self.gen_Q(hp, d, R - 3, S, C))
            if 2 <= R <= 10:
                for d in range(2):
                    gens.append(self.gen_S2b(hp, d, R - 2, dirs[d], C))
            if 1 <= R <= 9:
                for d in range(2):
                    gens.append(self.gen_S2a(hp, d, R - 1, dirs[d], C))
            if R <= 8:
                for d in range(2):
                    gens.append(self.gen_S1(hp, d, R, dirs[d], C))
            while gens:
                for gn in list(gens):
                    try:
                        next(gn)
                    except StopIteration:
                        gens.remove(gn)

    def rwkv_half(self, hp, d, half, rb, kb, vb, kkb, y0, bacc, Tst, Tb, twd, adb, w2s, a2s, chp, hp2, msk, cmask, CW):
        P, pb, pbh = self.P, self.pb, self.pbh
        h0, W = (0, 1280) if half == 0 else (1280, 1024)
        if d == 0:
            pieces = [(0, 256, 0, 1), (256, 1024, 256, 1)] if half == 0 else [(1280, 1024, 1280, 1)]
        else:
            pieces = [(0, 256, 255, -1), (1280, 1024, 1279, -1)] if half == 0 else [(256, 1024, 2303, -1)]
        sg = lambda t, s0, n, sig0, step, off=0, nn=None: t[:, rsl(sig0 - h0 + step * off, nn if nn is not None else n, step)]
        A1 = P.sb([128, 1280], name='A1'); B1 = P.sb([128, 1280], name='B1'); C1 = P.sb([128, 1280], name='C1')
        rt = P.sb([128, 1280], BF16, name='rt'); at = P.sb([128, 1280], BF16, name='at'); bt = P.sb([128, 1280], BF16, name='bt')
        kt = P.sb([128, 1280], BF16, name='kt'); vs = P.sb([128, 1280], BF16, name='vs')
        wcs = P.sb([128, 20], name='wcs')
        hc = slice(hp * 128, (hp + 1) * 128)
        ds = slice(64 * d, 64 * d + 64)

        def lora(wts, src, bias_col, dst):
            nb = 0
            for (s0, n, sig0, step) in pieces:
                for o in range(0, n, 512):
                    nn = min(512, n - o)
                    pp = pb[nb % 2]; nb += 1
                    self.mm(pp[:, 0:nn], wts[ds, hc], src[ds, s0 + o:s0 + o + nn], r=[wts, src], w=[pp])
                    self.act(sg(dst, s0, n, sig0, step, o, nn), pp[:, 0:nn], AF.Tanh, scale=0.5, bias=hp2[:, hp, bias_col:bias_col + 1], r=[pp, hp2], w=[dst])
        lora(w2s, twd, d, A1)
        P.op('dve', lambda e: e.tensor_scalar(out=A1[:, 0:W], in0=A1[:, 0:W], scalar1=1.0, scalar2=CW, op0=ALU.add, op1=ALU.mult), r=[A1], w=[A1])
        P.op('dve', lambda e: e.tensor_tensor_scan(out=B1[:, 0:W], data0=cmask[:, 0:W], data1=A1[:, 0:W], initial=0.0, op0=ALU.mult, op1=ALU.add),
             r=[A1, cmask], w=[B1])
        P.op('dve', lambda e: e.tensor_tensor(out=A1[:, 0:W], in0=B1[:, 0:W], in1=A1[:, 0:W], op=ALU.subtract), r=[A1, B1], w=[A1])
        self.act(C1[:, 0:W], A1[:, 0:W], AF.Exp, r=[A1], w=[C1])
        for pc in pieces:
            s0, n = pc[0], pc[1]
            P.op('dve', lambda e, pc=pc, s0=s0, n=n: e.scalar_tensor_tensor(out=sg(at, *pc), in0=kkb[:, s0:s0 + n], scalar=-1.0, in1=sg(C1, *pc),
                                                                            op0=ALU.mult, op1=ALU.mult), r=[kkb, C1], w=[at])
        self.act(C1[:, 0:W], B1[:, 0:W], AF.Exp, r=[B1, at], w=[C1])
        P.op('dve', lambda e: e.tensor_copy(out=wcs[:, 0:W // 64], in_=C1[:, 63:W:64]), r=[C1], w=[wcs])
        for pc in pieces:
            s0, n = pc[0], pc[1]
            P.op('dve', lambda e, pc=pc, s0=s0, n=n: e.tensor_tensor(out=sg(rt, *pc), in0=rb[:, s0:s0 + n], in1=sg(C1, *pc), op=ALU.mult), r=[rb, C1], w=[rt])
        self.act(C1[:, 0:W], B1[:, 0:W], AF.Exp, scale=-1.0, r=[B1, rt, wcs], w=[C1])
        lora(a2s, adb, 2 + d, A1)
        P.op('dve', lambda e: e.tensor_scalar(out=B1[:, 0:W], in0=A1[:, 0:W], scalar1=0.5, scalar2=0.5, op0=ALU.mult, op1=ALU.add), r=[A1], w=[B1])
        for pc in pieces:
            s0, n = pc[0], pc[1]
            P.op('dve', lambda e, pc=pc, s0=s0, n=n: e.tensor_tensor(out=sg(B1, *pc), in0=sg(B1, *pc), in1=kkb[:, s0:s0 + n], op=ALU.mult), r=[B1, kkb], w=[B1])
        P.op('dve', lambda e: e.tensor_tensor(out=bt[:, 0:W], in0=B1[:, 0:W], in1=C1[:, 0:W], op=ALU.mult), r=[B1, C1], w=[bt])
        P.op('dve', lambda e: e.tensor_scalar(out=A1[:, 0:W], in0=A1[:, 0:W], scalar1=hp2[:, hp, 4:5], scalar2=hp2[:, hp, 5:6], op0=ALU.mult, op1=ALU.add),
             r=[A1, hp2], w=[A1])
        for pc in pieces:
            s0, n = pc[0], pc[1]
            P.op('dve', lambda e, pc=pc, s0=s0, n=n: e.tensor_tensor(out=sg(A1, *pc), in0=sg(A1, *pc), in1=kb[:, s0:s0 + n], op=ALU.mult), r=[A1, kb], w=[A1])
        P.op('dve', lambda e: e.tensor_tensor(out=kt[:, 0:W], in0=A1[:, 0:W], in1=C1[:, 0:W], op=ALU.mult), r=[A1, C1], w=[kt])
        nb = 0
        for pc in pieces:
            s0, n, sig0, step = pc
            if s0 < 256:
                continue
            P.op('dve', lambda e, pc=pc, s0=s0, n=n: e.scalar_tensor_tensor(out=B1[:, 0:n], in0=sg(A1, *pc), scalar=chp[:, hp, 6:7], in1=rb[:, s0:s0 + n],
                                                                            op0=ALU.mult, op1=ALU.mult), r=[A1, chp, rb, bt], w=[B1])
            for o in range(0, n, 512):
                pp = pb[nb % 2]; nb += 1
                self.mm(pp[:], self.bones[:], B1[:, o:o + 512], r=[B1, self.bones], w=[pp])
                bo = s0 - 256 + o
                if d == 0:
                    P.op('act', lambda e, pp=pp, bo=bo: e.copy(out=bacc[:, bo:bo + 512], in_=pp[:]), r=[pp], w=[bacc])
                else:
                    P.op('dve', lambda e, pp=pp, bo=bo: e.tensor_tensor(out=bacc[:, bo:bo + 512], in0=bacc[:, bo:bo + 512], in1=pp[:], op=ALU.add), r=[pp, bacc], w=[bacc])
        for pc in pieces:
            s0, n = pc[0], pc[1]
            P.op('pool', lambda e, pc=pc, s0=s0, n=n: e.tensor_copy(out=sg(vs, *pc), in_=vb[:, s0:s0 + n]), r=[vb], w=[vs])

        if hp == 0 and half == 0:
            for nm, t_ in (('rt', rt), ('at', at), ('bt', bt), ('kt', kt), ('vs', vs)):
                self.tap(nm + str(d), t_[:], [128, 1280], [t_], dt=BF16)
            self.tap('wcs' + str(d), wcs[:], [128, 20], [wcs])
            self.tap('kd' + str(d), A1[:], [128, 1280], [A1])
        tokT = P.sb([64, 4, 4, 128], BF16, name='tokT')
        Qs = [P.sb([64, 8, 64], BF16, name='Q') for _ in range(2)]; QTs = [P.sb([64, 8, 64], BF16, name='QT') for _ in range(2)]
        Xs = [P.sb([64, 8, 64], BF16, name='X') for _ in range(2)]
        AakT = P.sb([64, 8, 64], BF16, name='AakT'); ArbT = P.sb([64, 8, 64], BF16, name='ArbT'); ArkT = P.sb([64, 8, 64], BF16, name='ArkT')
        Xak = P.sb([64, 8, 64], BF16, name='Xak'); AhT = P.sb([128, 4, 64], BF16, name='AhT')
        Ub = P.sb([64, 2, 64], BF16, name='Ub'); Ts = P.sb([128, 64], name='Ts')
        pU, pT, pY, pA = pb[4], pb[5], pb[6], pb[0]
        pYv = pb[6][:, 0:256].rearrange("p (c t) -> p c t", t=64)
        pAv = pb[0][:, 0:256].rearrange("p (c t) -> p c t", t=64)
        pUv = pb[4][0:64, 0:128].rearrange("p (e v) -> p e v", v=64)
        pTv = pb[5][:, 0:64]
        bank = [0]

        def nextbank():
            bank[0] = (bank[0] + 1) % 2
            return pb[2 + bank[0]]
        v3 = lambda p: p[0:64, :].rearrange("p (i t) -> p i t", t=64)
        for gi in range(W // 256):
            loc = 256 * gi
            g0 = h0 + loc
            latent = g0 >= 256
            for qi, q in enumerate((at, bt, kt, vs)):
                for c in range(4):
                    P.op('pe', lambda e, qi=qi, q=q, c=c: e.transpose(out=pbh[0:64, (qi % 2) * 512 + c * 128:(qi % 2) * 512 + (c + 1) * 128],
                                                                      in_=q[:, loc + 64 * c:loc + 64 * c + 64], identity=self.identb[:]),
                         r=[q, self.identb], w=[pbh])
                if qi % 2 == 1:
                    P.op('act', lambda e, qi=qi: e.copy(out=tokT[:, qi - 1:qi + 1, :, :].rearrange("p q c j -> p (q c j)"), in_=pbh[0:64, :]), r=[pbh], w=[tokT])
            cs = lambda q, c, e_: q[64 * e_:64 * e_ + 64, loc + 64 * c:loc + 64 * c + 64]

            def score(L, Rr, mk, dst):
                pp = nextbank()
                for c in range(4):
                    for e_ in range(2):
                        self.mm(v3(pp)[:, 2 * c + e_, :], cs(L, c, e_), cs(Rr, c, e_), r=[L, Rr], w=[pp])
                P.op('dve', lambda e: e.tensor_tensor(out=dst[:], in0=v3(pp), in1=msk[mk][:], op=ALU.mult), r=[pp, msk[mk]], w=[dst])
            Q, QT, X = Qs[0], QTs[0], Xs[0]
            score(bt, at, 'su', Q)
            score(at, bt, 'sl', QT)
            score(kt, at, 'su', AakT)
            score(bt, rt, 'iu', ArbT)
            score(kt, rt, 'iu', ArkT)
            P.op('dve', lambda e, Q=Q, X=X: e.tensor_tensor(out=X[:], in0=Q[:], in1=msk['id'][:], op=ALU.add), r=[Q, msk['id']], w=[X])
            for lvl in range(2, 7):
                Qn, QTn, Xn = Qs[(lvl + 1) % 2], QTs[(lvl + 1) % 2], Xs[(lvl + 1) % 2]
                if lvl < 6:
                    pq = nextbank()
                    for i in range(8):
                        self.mm(v3(pq)[:, i, :], QT[:, i, :], Q[:, i, :], r=[Q, QT], w=[pq])
                    P.op('act', lambda e, pq=pq, Qn=Qn: e.copy(out=Qn[:], in_=v3(pq)), r=[pq], w=[Qn])
                pqt = nextbank()
                for i in range(8):
                    self.mm(v3(pqt)[:, i, :], Q[:, i, :], QT[:, i, :], r=[Q, QT], w=[pqt])
                P.op('act', lambda e, pqt=pqt, QTn=QTn: e.copy(out=QTn[:], in_=v3(pqt)), r=[pqt], w=[QTn])
                px = nextbank()
                for i in range(8):
                    self.mm(v3(px)[:, i, :], QTn[:, i, :], X[:, i, :], r=[QTn, X], w=[px])
                P.op('dve', lambda e, px=px, X=X, Xn=Xn: e.tensor_tensor(out=Xn[:], in0=v3(px), in1=X[:], op=ALU.add), r=[px, X], w=[Xn])
                Q, QT, X = Qn, QTn, Xn
            MT = X
            pxa = nextbank()
            for c in range(4):
                for e_ in range(2):
                    self.mm(v3(pxa)[:, 2 * c + e_, :], AakT[:, 2 * c + e_, :], tokT[:, 3, c, 64 * e_:64 * e_ + 64], r=[AakT, tokT], w=[pxa])
            P.op('act', lambda e, pxa=pxa: e.copy(out=Xak[:], in_=v3(pxa)), r=[pxa], w=[Xak])
            for c in range(4):
                for e_ in range(2):
                    self.mm(pAv[64 * e_:64 * e_ + 64, c, :], tokT[:, 0, c, 64 * e_:64 * e_ + 64], MT[:, 2 * c + e_, :], r=[tokT, MT], w=[pA],
                            tile_position=(0, 64 * e_))
            P.op('act', lambda e: e.copy(out=AhT[:], in_=pAv), r=[pA], w=[AhT])
            if hp == 0 and half == 0 and d == 0 and gi == 0:
                self.tap('MT', MT[:], [64, 8, 64], [MT], dt=BF16)
                self.tap('AhT', AhT[:], [128, 4, 64], [AhT], dt=BF16)
                self.tap('Xak', Xak[:], [64, 8, 64], [Xak], dt=BF16)
                self.tap('tokT', tokT[:], [64, 4, 4, 128], [tokT], dt=BF16)
                self.tap('ArbT', ArbT[:], [64, 8, 64], [ArbT], dt=BF16)
            for c in range(4):
                for e_ in range(2):
                    i = 2 * c + e_
                    es = slice(64 * e_, 64 * e_ + 64)
                    self.mm(pUv[:, e_, :], MT[:, i, :], Xak[:, i, :], start=True, stop=False, r=[MT, Xak], w=[pU])
                    self.mm(pUv[:, e_, :], AhT[es, c, :], Tb[es, :], start=False, stop=True, r=[AhT, Tb], w=[pU], tile_position=(64 * e_, 0))
                P.op('act', lambda e: e.copy(out=Ub[:], in_=pUv), r=[pU], w=[Ub])
                for e_ in range(2):
                    i = 2 * c + e_
                    es = slice(64 * e_, 64 * e_ + 64)
                    if latent:
                        self.mm(pYv[es, c, :], Tb[es, :], rt[es, loc + 64 * c:loc + 64 * c + 64], start=True, stop=False, r=[Tb, rt], w=[pY],
                                tile_position=(64 * e_, 64 * e_))
                        self.mm(pYv[es, c, :], Ub[:, e_, :], ArbT[:, i, :], start=False, stop=False, r=[Ub, ArbT], w=[pY], tile_position=(0, 64 * e_))
                        self.mm(pYv[es, c, :], tokT[:, 3, c, es], ArkT[:, i, :], start=False, stop=True, r=[tokT, ArkT], w=[pY], tile_position=(0, 64 * e_))
                    self.mm(pTv[es, :], tokT[:, 1, c, es], Ub[:, e_, :], start=True, stop=False, r=[tokT, Ub], w=[pT], tile_position=(0, 64 * e_))
                    self.mm(pTv[es, :], tokT[:, 2, c, es], tokT[:, 3, c, es], start=False, stop=True, r=[tokT], w=[pT], tile_position=(0, 64 * e_))
                P.op('dve', lambda e: e.tensor_tensor(out=Ts[:], in0=pTv, in1=Tst[:], op=ALU.add), r=[pT, Tst], w=[Ts])
                wc = wcs[:, 4 * gi + c:4 * gi + c + 1]
                P.op('dve', lambda e, wc=wc: e.tensor_scalar(out=Tst[:], in0=Ts[:], scalar1=wc, scalar2=None, op0=ALU.mult), r=[Ts, wcs], w=[Tst])
                self.act(Tb[:], Ts[:], AF.Identity, scale=wc, r=[Ts, wcs], w=[Tb])
            if latent:
                if d == 0:
                    P.op('act', lambda e, g0=g0: e.copy(out=y0[:, g0 - 256:g0], in_=pb[6][:, 0:256]), r=[pY], w=[y0])
                else:
                    ysl = rsl(2303 - g0, 256, -1)
                    P.op('dve', lambda e, ysl=ysl: e.tensor_tensor(out=y0[:, ysl], in0=y0[:, ysl], in1=pb[6][:, 0:256], op=ALU.add), r=[pY, y0], w=[y0])

    def wload(self, pool, src, npart, K, ncol, q='sp'):
        P = self.P
        i = pool['i'] = pool['i'] + 1
        wf, wb = pool['f'][i % len(pool['f'])], pool['b'][i % len(pool['b'])]
        P.dma(q, wf[0:npart, 0:K, 0:ncol], src, w=[wf], group=pool['name'] + str(i % len(pool['f'])))
        ceng = 'pool' if (self._castn % 2 == 0) else 'dve'
        self._castn += 1
        P.op(ceng, lambda e: e.tensor_copy(out=wb[0:npart, 0:K, 0:ncol], in_=wf[0:npart, 0:K, 0:ncol]), r=[wf], w=[wb])
        return wb

    def mkpool(self, name, npart, K, ncol, n=2):
        P = self.P
        return {'name': name, 'i': 0, 'f': [P.sb([npart, K, ncol], name=name + 'f') for _ in range(n)],
                'b': [P.sb([npart, K, ncol], BF16, name=name + 'b') for _ in range(n)]}

    def lru(self):
        P, pb, hT, din = self.P, self.pb, self.hT, self.din
        lruT = self.lruT
        cw_, cb_, ba_, bx_, lam_ = din['lru_conv_w'], din['lru_conv_b'], din['lru_ba'], din['lru_bx'], din['lru_lambda']
        rows = [cw_[d, j] for d in range(2) for j in range(4)] + [cb_[0], cb_[1], ba_[0], ba_[1], bx_[0], bx_[1], lam_[0], lam_[1]]
        lp = self.cols(rows, LW, 80, 'lp')
        hb = P.sb([80, 16, 4], name='hb')
        P.op('dve', lambda e: e.tensor_scalar(out=hb[:], in0=lp[:, :, 10:14], scalar1=0.5, scalar2=None, op0=ALU.mult), r=[lp], w=[hb])
        cs = P.sb([80, 16, 4], name='cs')
        one1 = P.sb([80, 1], name='one1')
        P.op('dve', lambda e: e.memset(one1[:], 1.0), w=[one1])
        self.act(cs[:, :, 0:2], lp[:, :, 14:16], AF.Exp, scale=-1.0, r=[lp], w=[cs])
        self.act(cs[:, :, 0:2], cs[:, :, 0:2], AF.Ln, bias=one1[:], r=[cs, one1], w=[cs])
        P.op('dve', lambda e: e.tensor_scalar(out=cs[:, :, 2:4], in0=cs[:, :, 0:2], scalar1=-4.0, scalar2=None, op0=ALU.mult), r=[cs], w=[cs])
        P.op('dve', lambda e: e.tensor_scalar(out=cs[:, :, 0:2], in0=cs[:, :, 0:2], scalar1=-8.0, scalar2=None, op0=ALU.mult), r=[cs], w=[cs])
        q25 = P.sb([80, 1], name='q25')
        P.op('dve', lambda e: e.memset(q25[:], 0.25), w=[q25])
        gwa = P.sb([80, 32, 80], BF16, name='gwa'); gwx = P.sb([80, 32, 80], BF16, name='gwx')
        with P.scope():
            st = P.sb([80, 32, 80], name='gst')
            for src, dst in ((din['lru_wa'], gwa), (din['lru_wx'], gwx)):
                P.dma('sp', st[:], src.rearrange("d n c e -> c (d n) e"), w=[st], group='gst')
                P.op('dve', lambda e, dst=dst: e.tensor_copy(out=dst[:], in_=st[:]), r=[st], w=[dst])
        U = P.sb([80, 2313], name='U'); guy = P.sb([80, NLAT], BF16, name='guy')
        xcs = [P.sb([80, 2313], name='xc') for _ in range(2)]
        xcb = P.sb([80, 2313], BF16, name='xcb')
        thr = P.sb([80, 2313], name='thr'); thi = P.sb([80, 2313], name='thi'); aa = P.sb([80, 2313], name='aa')
        lrub = P.sb([80, NLAT], BF16, name='lrub')
        wp = self.mkpool('lw', 128, 8, 80, n=2)
        P.op('dve', lambda e: e.memset(U[:], 0.0), w=[U])
        for xc in xcs:
            P.op('dve', lambda e, xc=xc: e.memset(xc[:], 0.0), w=[xc])
        P.op('dve', lambda e: e.memset(thr[:], 0.0), w=[thr])
        P.op('dve', lambda e: e.memset(thi[:], 0.0), w=[thi])
        wv = din['w_in'].rearrange("(k p) n -> p k n", p=128)
        hk = [(hT, j) for j in range(5)]
        tbs = [(0, 256, 3)] + [(256 + 512 * i, 512, 262 + 512 * i) for i in range(4)]
        nb = 0
        for n in range(NBLK):
            wx = self.wload(wp, wv[:, :, 80 * n:80 * n + 80], 128, 8, 80)
            for (hc, nn, uc) in tbs:
                pp = pb[nb % 6]; nb += 1
                for k in range(8):
                    self.mm(pp[0:80, 0:nn], wx[:, k, :], hT[:, k, hc:hc + nn], start=(k == 0), stop=(k == 7), r=[wx] + hk, w=[pp])
                P.op('act', lambda e: e.copy(out=U[:, uc:uc + nn], in_=pp[0:80, 0:nn]), r=[pp], w=[U])
            wy = self.wload(wp, wv[:, :, 1280 + 80 * n:1280 + 80 * n + 80], 128, 8, 80, q='act')
            for (hc, nn, uc) in tbs[1:]:
                pp = pb[nb % 6]; nb += 1
                for k in range(8):
                    self.mm(pp[0:80, 0:nn], wy[:, k, :], hT[:, k, hc:hc + nn], start=(k == 0), stop=(k == 7), r=[wy] + hk, w=[pp])
                self.act(guy[:, hc - 256:hc - 256 + nn], pp[0:80, 0:nn], AF.Gelu_apprx_tanh, r=[pp], w=[guy])
            for d in range(2):
                xc = xcs[d]
                sgn = -1 if d == 0 else 1
                cwj = lambda j: lp[:, n, 4 * d + j:4 * d + j + 1]

                def chunk(ci, uc, nn, blocks):
                    nonlocal nb
                    cs_ = slice(uc, uc + nn)
                    P.op('dve', lambda e: e.tensor_scalar(out=xc[:, cs_], in0=U[:, cs_], scalar1=cwj(3), scalar2=lp[:, n, 8 + d:9 + d],
                                                          op0=ALU.mult, op1=ALU.add), r=[U, lp], w=[(xc, ci)])
                    for j in range(3):
                        o = uc + sgn * (3 - j)
                        P.op('dve', lambda e: e.scalar_tensor_tensor(out=xc[:, cs_], in0=U[:, o:o + nn], scalar=cwj(j), in1=xc[:, cs_],
                                                                     op0=ALU.mult, op1=ALU.add), r=[U, lp, (xc, ci)], w=[(xc, ci)])
                    yield
                    P.op('act', lambda e: e.copy(out=xcb[:, cs_], in_=xc[:, cs_]), r=[(xc, ci)], w=[(xcb, ci)])
                    yield
                    for (hc, bn, bc) in blocks:
                        for gw, dst, bcol in ((gwa, thr, d), (gwx, thi, 2 + d)):
                            pp = pb[nb % 6]; nb += 1
                            self.mm(pp[0:80, 0:bn], gw[:, 16 * d + n, :], xcb[:, bc:bc + bn], r=[gw, (xcb, ci)], w=[pp])
                            self.act(dst[:, bc:bc + bn], pp[0:80, 0:bn], AF.Tanh, scale=0.5, bias=hb[:, n, bcol:bcol + 1], r=[pp, hb], w=[(dst, ci)])
                    yield
                    self.act(aa[:, cs_], thr[:, cs_], AF.Exp, scale=cs[:, n, 2 + d:3 + d], bias=cs[:, n, 2 + d:3 + d], r=[(thr, ci), cs], w=[(aa, ci)])
                    self.act(thr[:, cs_], thr[:, cs_], AF.Exp, scale=cs[:, n, d:d + 1], bias=cs[:, n, d:d + 1], r=[(thr, ci), cs], w=[(thr, ci)])
                    P.op('dve', lambda e: e.scalar_tensor_tensor(out=thi[:, cs_], in0=thi[:, cs_], scalar=1.0, in1=xc[:, cs_], op0=ALU.add, op1=ALU.mult),
                         r=[(thi, ci), (xc, ci)], w=[(thi, ci)])
                    yield
                    self.act(thr[:, cs_], thr[:, cs_], AF.Sqrt, scale=-0.25, bias=q25[:], r=[(thr, ci), q25], w=[(thr, ci)])
                    yield
                    P.op('dve', lambda e: e.tensor_tensor(out=thi[:, cs_], in0=thi[:, cs_], in1=thr[:, cs_], op=ALU.mult), r=[(thi, ci), (thr, ci)], w=[(thi, ci)])
                    yield
                gens = [chunk(0, 3, 1283, tbs[0:3]), chunk(1, 1286, 1024, tbs[3:5])]
                while gens:
                    for gn in list(gens):
                        try:
                            next(gn)
                        except StopIteration:
                            gens.remove(gn)
                allk = lambda t_: [(t_, ci) for ci in range(2)]
                if d == 0:
                    P.op('dve', lambda e: e.tensor_tensor_scan(out=xc[:, 3:259], data0=aa[:, 3:259], data1=thi[:, 3:259], initial=0.0, op0=ALU.mult, op1=ALU.add),
                         r=allk(aa) + allk(thi), w=allk(xc))
                    P.op('dve', lambda e: e.tensor_tensor_scan(out=xc[:, 262:2310], data0=aa[:, 262:2310], data1=thi[:, 262:2310], initial=xc[:, 258:259],
                                                               op0=ALU.mult, op1=ALU.add), r=allk(aa) + allk(thi) + allk(xc), w=allk(xc))
                else:
                    rv = lambda t_, a_, b_: t_[:, rsl(b_ - 1, b_ - a_, -1)]
                    P.op('dve', lambda e: e.tensor_tensor_scan(out=rv(xc, 3, 259), data0=rv(aa, 3, 259), data1=rv(thi, 3, 259), initial=0.0, op0=ALU.mult, op1=ALU.add),
                         r=allk(aa) + allk(thi), w=allk(xc))
                    P.op('dve', lambda e: e.tensor_tensor_scan(out=rv(xc, 262, 2310), data0=rv(aa, 262, 2310), data1=rv(thi, 262, 2310), initial=xc[:, 3:4],
                                                               op0=ALU.mult, op1=ALU.add), r=allk(aa) + allk(thi) + allk(xc), w=allk(xc))
            allk = lambda t_: [(t_, ci) for ci in range(2)]
            P.op('dve', lambda e: e.tensor_tensor(out=aa[:, 0:NLAT], in0=xcs[0][:, 262:2310], in1=xcs[1][:, 262:2310], op=ALU.add), r=allk(xcs[0]) + allk(xcs[1]) + allk(aa), w=allk(aa))
            P.op('dve', lambda e: e.tensor_tensor(out=lrub[:], in0=aa[:, 0:NLAT], in1=guy[:], op=ALU.mult), r=allk(aa) + [guy], w=[lrub])
            p0, c0 = (80 * n) % 128, (80 * n) // 128
            n1 = min(80, 128 - p0)
            P.dma('sp', lruT[p0:p0 + n1, c0, :], lrub[0:n1, :], r=[lrub], w=[lruT], group='lruT')
            if n1 < 80:
                P.dma('sp', lruT[0:80 - n1, c0 + 1, :], lrub[n1:80, :], r=[lrub], w=[lruT], group='lruT')

    def merge(self):
        P, pb, hT, din = self.P, self.pb, self.hT, self.din
        lruT, rwT, mT = self.lruT, self.rwT, self.mT
        wv = din['w_in'].rearrange("(k p) n -> p k n", p=128)
        wol = din['w_o_lru'].rearrange("(k p) n -> p k n", p=128)
        wor = din['w_o_rwkv'].rearrange("(k p) n -> p k n", p=128)
        pl = self.mkpool('wl', 128, 10, 128); pr = self.mkpool('wr', 128, 8, 128); pg = self.mkpool('wg', 128, 8, 128, n=3)
        thl = P.sb([128, 512], name='thl'); thr = P.sb([128, 512], name='thr2'); t1 = P.sb([128, 512], name='t1'); t2 = P.sb([128, 512], name='t2')
        hk = [(hT, j) for j in range(5)]
        for dc in range(8):
            cs_ = slice(dc * 128, dc * 128 + 128)
            wl = self.wload(pl, wol[:, :, cs_], 128, 10, 128)
            wr = self.wload(pr, wor[:, :, cs_], 128, 8, 128, q='act')
            wgl = self.wload(pg, wv[:, :, 6048 + dc * 128:6048 + dc * 128 + 128], 128, 8, 128)
            wgr = self.wload(pg, wv[:, :, 7072 + dc * 128:7072 + dc * 128 + 128], 128, 8, 128, q='act')
            for tb in range(4):
                ts_ = slice(512 * tb, 512 * tb + 512); hs_ = slice(256 + 512 * tb, 256 + 512 * tb + 512)
                p1, p2, p3, p4 = pb[0], pb[1], pb[2], pb[3]
                for c in range(10):
                    self.mm(p1[:], wl[:, c, :], lruT[:, c, ts_], start=(c == 0), stop=(c == 9), r=[wl, lruT], w=[p1])
                for c in range(8):
                    self.mm(p2[:], wr[:, c, :], rwT[:, c, ts_], start=(c == 0), stop=(c == 7), r=[wr, rwT], w=[p2])
                for c in range(8):
                    self.mm(p3[:], wgl[:, c, :], hT[:, c, hs_], start=(c == 0), stop=(c == 7), r=[wgl] + hk, w=[p3])
                for c in range(8):
                    self.mm(p4[:], wgr[:, c, :], hT[:, c, hs_], start=(c == 0), stop=(c == 7), r=[wgr] + hk, w=[p4])
                self.act(thl[:], p3[:], AF.Tanh, scale=0.5, r=[p3], w=[thl])
                self.act(thr[:], p4[:], AF.Tanh, scale=0.5, r=[p4], w=[thr])
                P.op('dve', lambda e: e.scalar_tensor_tensor(out=t1[:], in0=thl[:], scalar=1.0, in1=p1[:], op0=ALU.add, op1=ALU.mult), r=[thl, p1], w=[t1])
                P.op('dve', lambda e: e.scalar_tensor_tensor(out=t2[:], in0=thr[:], scalar=1.0, in1=p2[:], op0=ALU.add, op1=ALU.mult), r=[thr, p2], w=[t2])
                P.op('dve', lambda e: e.tensor_tensor(out=mT[:, dc, ts_], in0=t1[:], in1=t2[:], op=ALU.add), r=[t1, t2], w=[mT])

    def resid1(self):
        P, pb, din, mod = self.P, self.pb, self.din, self.mod
        mT, x1T = self.mT, self.x1T
        hg = P.sb([128, 8], name='hg')
        P.op('dve', lambda e: e.tensor_scalar(out=hg[:], in0=mod[:, 16:24, 0], scalar1=0.5, scalar2=None, op0=ALU.mult), r=[mod], w=[hg])
        wo = P.sb([128, 8, D], BF16, name='wo')
        with P.scope():
            st = P.sb([128, 8, 256], name='wost')
            for j in range(4):
                P.dma('sp', st[:], din['w_out'].rearrange("(k p) n -> p k n", p=128)[:, :, 256 * j:256 * j + 256], w=[st], group='wost')
                P.op('pool', lambda e: e.tensor_copy(out=wo[:, :, 256 * j:256 * j + 256], in_=st[:]), r=[st], w=[wo])
        xv = din['xT'].rearrange("(k p) t -> p k t", p=128)
        for tb in range(4):
            ts_ = slice(512 * tb, 512 * tb + 512)
            P.dma('sp', x1T[:, :, ts_], xv[:, :, ts_], w=[x1T], group='x1ld')
            for dc in range(8):
                pp = pb[(tb * 8 + dc) % 2]
                for c in range(8):
                    self.mm(pp[:], wo[:, c, dc * 128:dc * 128 + 128], mT[:, c, ts_], start=(c == 0), stop=(c == 7), r=[wo, mT], w=[pp])
                P.op('dve', lambda e: e.scalar_tensor_tensor(out=x1T[:, dc, ts_], in0=pp[:], scalar=hg[:, dc:dc + 1], in1=x1T[:, dc, ts_], op0=ALU.mult, op1=ALU.add),
                     r=[pp, hg, x1T], w=[x1T])

    def ffn(self):
        P, pb, din, mod = self.P, self.pb, self.din, self.mod
        x1T, h2T = self.x1T, self.h2T
        wi = din['w_ffn_in'].rearrange("(k p) n -> p k n", p=128)
        wo_ = din['w_ffn_out'].rearrange("(f p) n -> p f n", p=128)
        actT = P.sb([128, 22, 1024], BF16, name='actT')
        pin = self.mkpool('fi', 128, 8, 128, n=3); pout = self.mkpool('fo', 128, 22, 128, n=2)
        sl = P.sb([128, 512], name='sl')
        hk = [(h2T, j) for j in range(4)]
        nb = 0
        for half in range(2):
            for f in range(22):
                wg = self.wload(pin, wi[:, :, f * 128:f * 128 + 128], 128, 8, 128)
                wu = self.wload(pin, wi[:, :, DFF + f * 128:DFF + f * 128 + 128], 128, 8, 128, q='act')
                for t2 in range(2):
                    tok = slice(1024 * half + 512 * t2, 1024 * half + 512 * t2 + 512)
                    pg_, pu_ = pb[nb % 4], pb[(nb + 1) % 4]; nb += 2
                    for k in range(8):
                        self.mm(pg_[:], wg[:, k, :], h2T[:, k, tok], start=(k == 0), stop=(k == 7), r=[wg] + hk, w=[pg_])
                    for k in range(8):
                        self.mm(pu_[:], wu[:, k, :], h2T[:, k, tok], start=(k == 0), stop=(k == 7), r=[wu] + hk, w=[pu_])
                    self.act(sl[:], pg_[:], AF.Silu, r=[pg_], w=[sl])
                    P.op('dve', lambda e: e.tensor_tensor(out=actT[:, f, 512 * t2:512 * t2 + 512], in0=sl[:], in1=pu_[:], op=ALU.mult), r=[sl, pu_], w=[(actT, f)])
            for dc in range(8):
                wo = self.wload(pout, wo_[:, :, dc * 128:dc * 128 + 128], 128, 22, 128)
                for t2 in range(2):
                    tok = slice(1024 * half + 512 * t2, 1024 * half + 512 * t2 + 512)
                    pp = pb[4 + (nb % 2)]; nb += 1
                    for f in range(22):
                        self.mm(pp[:], wo[:, f, :], actT[:, f, 512 * t2:512 * t2 + 512], start=(f == 0), stop=(f == 21), r=[wo, (actT, f)], w=[pp])
                    P.op('dve', lambda e: e.scalar_tensor_tensor(out=x1T[:, dc, tok], in0=pp[:], scalar=mod[:, 40 + dc, 0:1], in1=x1T[:, dc, tok], op0=ALU.mult, op1=ALU.add),
                         r=[pp, mod, x1T], w=[x1T])

    def final(self, outT):
        P, pb, x = self.P, self.pb, self.x1T
        gains = self.gains
        sq = P.sb([128, 8, 512], name='fsq'); rs = P.sb([128, 512], name='frs')
        epst = P.sb([128, 1], name='fepst')
        P.op('dve', lambda e: e.memset(epst[:], RMS_EPS), w=[epst])
        ov = outT.rearrange("(k p) t -> p k t", p=128)
        for tb in range(4):
            ts_ = slice(512 * tb, 512 * tb + 512)
            self.act(sq[:], x[:, :, ts_], AF.Square, r=[x], w=[sq])
            pp = pb[tb % 2]
            for k in range(8):
                self.mm(pp[:], self.ones[:], sq[:, k, :], start=(k == 0), stop=(k == 7), r=[sq, self.ones], w=[pp])
            self.act(rs[:], pp[:], AF.Sqrt, scale=1.0 / D, bias=epst[:], r=[pp, epst], w=[rs])
            P.op('dve', lambda e: e.reciprocal(out=rs[:], in_=rs[:]), r=[rs], w=[rs])
            for k in range(8):
                P.op('dve', lambda e: e.scalar_tensor_tensor(out=sq[:, k, :], in0=x[:, k, ts_], scalar=gains[:, k, 2:3], in1=rs[:], op0=ALU.mult, op1=ALU.mult),
                     r=[x, gains, rs], w=[sq])
            P.op('dve', lambda e: e.tensor_copy(out=epst[:], in_=epst[:]), r=[sq, epst], w=[sq, epst])
            P.dma('sp', ov[:, :, ts_], sq[:], r=[sq], group='out')

    def finish(self):
        P = self.P
        for gname in list(P.dsem):
            if gname.startswith('out'):
                P.wait_group('pool', gname)
        P.emit()
        return self.nc


_CACHE = {}


def _prep(inputs, b):
    f = lambda a: np.ascontiguousarray(a, dtype=np.float32)
    m = {
        'xT': f(inputs['x'][b].T), 'ctxT': f(inputs['ctx'][b].T),
        'cvec': f(np.stack([inputs['c'][b], inputs['c_ctx']])),
        'w_mod': f(inputs['w_mod'][0]), 'b_mod': f(inputs['b_mod'][0]),
        'norm_mix_g': f(inputs['norm_mix_g'][0]), 'norm_ffn_g': f(inputs['norm_ffn_g'][0]), 'norm_final_g': f(inputs['norm_final_g']),
        'w_in': f(inputs['w_in'][0]),
        'lru_conv_w': f(inputs['lru_conv_w'][0]), 'lru_conv_b': f(inputs['lru_conv_b'][0]),
        'lru_wa': f(inputs['lru_wa'][0]), 'lru_ba': f(inputs['lru_ba'][0]), 'lru_wx': f(inputs['lru_wx'][0]), 'lru_bx': f(inputs['lru_bx'][0]),
        'lru_lambda': f(inputs['lru_lambda'][0]), 'w_o_lru': f(inputs['w_o_lru'][0]),
        'rwkv_mu': f(inputs['rwkv_mu'][0]), 'rwkv_w0': f(inputs['rwkv_w0'][0]), 'rwkv_w2': f(inputs['rwkv_w2'][0]),
        'rwkv_a0': f(inputs['rwkv_a0'][0]), 'rwkv_a2': f(inputs['rwkv_a2'][0]), 'rwkv_g2': f(inputs['rwkv_g2'][0]),
        'rwkv_k_k': f(inputs['rwkv_k_k'][0]), 'rwkv_k_a': f(inputs['rwkv_k_a'][0]), 'rwkv_r_k': f(inputs['rwkv_r_k'][0].reshape(-1)),
        'rwkv_ln_g': f(inputs['rwkv_ln_g'][0]), 'rwkv_ln_b': f(inputs['rwkv_ln_b'][0]),
        'w_o_rwkv': f(inputs['w_o_rwkv'][0]), 'w_out': f(inputs['w_out'][0]),
        'w_ffn_in': f(inputs['w_ffn_in'][0]), 'w_ffn_out': f(inputs['w_ffn_out'][0]),
    }
    return m


def kernel(**inputs):
    if 'nc' not in _CACHE:
        _CACHE['nc'] = Builder().build()
    nc = _CACHE['nc']
    shared = _prep(inputs, 0)
    in_maps = []
    for b in range(8):
        m = dict(shared)
        m['xT'] = np.ascontiguousarray(np.asarray(inputs['x'][b], dtype=np.float32).T)
        m['ctxT'] = np.ascontiguousarray(np.asarray(inputs['ctx'][b], dtype=np.float32).T)
        m['cvec'] = np.ascontiguousarray(np.stack([inputs['c'][b], inputs['c_ctx']]).astype(np.float32))
        in_maps.append(m)
    res = run_bass_kernel_spmd(nc, in_maps, core_ids=list(range(8)))
    out = np.stack([np.ascontiguousarray(r['outT'].T) for r in res.results]).astype(np.float32)
    return out
```

```python
import contextlib
import numpy as np
import concourse.bass as bass
import concourse.mybir as mybir
from concourse.bass_utils import run_bass_kernel_spmd

F32 = mybir.dt.float32
BF16 = mybir.dt.bfloat16
AF = mybir.ActivationFunctionType
ALU = mybir.AluOpType

ENGS = ['pe', 'act', 'dve', 'pool', 'sp']
NCTX, NLAT, T = 256, 2048, 2304
D = 1024
LW, NBLK, BLK = 1280, 16, 80
RIN = 3488
DFF = 2816
RMS_EPS, GN_EPS = 1e-6, 64e-5


class _Rec:
    def __init__(self):
        self.call = None

    def __getattr__(self, name):
        def f(*a, **k):
            self.call = (name, a, k)
            return self
        return f


class Prog:
    def __init__(self, nc):
        self.nc = nc
        self.root = contextlib.ExitStack()
        self.stacks = [self.root]
        self.ops = {e: [] for e in ENGS}
        self.cnt = {e: 0 for e in ENGS}
        self.seen = {e: {} for e in ENGS}
        self.last_w = {}
        self.readers = {}
        self.esem = {e: self.root.enter_context(nc.semaphore('s_' + e)) for e in ENGS if e != 'sp'}
        self.dsem = {}
        self.fence = []
        self.ntile = 0

    def sb(self, shape, dt=F32, name=None):
        self.ntile += 1
        return self.stacks[-1].enter_context(self.nc.sbuf_tensor(f'{name or "t"}{self.ntile}', list(shape), dt))

    def sbm(self, shape, dt=F32, name=None):
        self.ntile += 1
        st = contextlib.ExitStack()
        t = st.enter_context(self.nc.sbuf_tensor(f'{name or "t"}{self.ntile}', list(shape), dt))
        return t, st

    def _set_fence(self):
        self.fence = [('E', e, self.cnt[e]) for e in self.esem if self.cnt[e] > 0]
        self.fence += [('D', s_, v) for s_, v in self.dsem.values() if v > 0]

    def free(self, stacks):
        for st in stacks:
            st.close()
        self._set_fence()

    def ps(self, shape, dt=F32, name=None):
        self.ntile += 1
        return self.root.enter_context(self.nc.psum_tensor(f'{name or "p"}{self.ntile}', list(shape), dt))

    @contextlib.contextmanager
    def scope(self):
        st = contextlib.ExitStack()
        self.stacks.append(st)
        try:
            yield
        finally:
            self.stacks.pop()
            st.close()
            self._set_fence()

    def _k(self, k):
        if isinstance(k, tuple):
            return tuple(self._k(x) for x in k)
        if isinstance(k, (str, int)):
            return k
        return id(k)

    @staticmethod
    def _tkey(tok):
        return ('E', tok[1]) if tok[0] == 'E' else ('D', id(tok[1]))

    def _deps(self, eng, r, w):
        deps = {}

        def add(tok):
            if tok is None:
                return
            if tok[0] == 'E' and tok[1] == eng == 'pe':
                return
            k = self._tkey(tok)
            if k not in deps or deps[k][2] < tok[2]:
                deps[k] = tok
        for tok in self.fence:
            add(tok)
        for k in r:
            add(self.last_w.get(k))
        for k in w:
            add(self.last_w.get(k))
            for t in self.readers.get(k, ()):
                add(t)
        out = []
        seen = self.seen[eng]
        for k, tok in deps.items():
            if seen.get(k, 0) >= tok[2]:
                continue
            seen[k] = tok[2]
            out.append(tok)
        return out

    def _commit(self, tok, r, w):
        for k in w:
            self.last_w[k] = tok
            self.readers[k] = []
        for k in r:
            if k in w:
                continue
            self.readers.setdefault(k, []).append(tok)

    def op(self, eng, fn, r=(), w=()):
        r = [self._k(k) for k in r]
        w = [self._k(k) for k in w]
        waits = self._deps(eng, r, w)
        self.cnt[eng] += 1
        tok = ('E', eng, self.cnt[eng])
        rec = _Rec()
        fn(rec)
        name, a, k = rec.call
        self.ops[eng].append((waits, lambda e: getattr(e, name)(*a, **k), tok))
        self._commit(tok, r, w)

    def dma(self, q, out, in_, r=(), w=(), group=None, **kw):
        r = [self._k(k) for k in r]
        w = [self._k(k) for k in w]
        waits = self._deps(q, r, w)
        g = group or ('dma_' + str(w[0] if w else 'x'))
        if g not in self.dsem:
            self.dsem[g] = [self.root.enter_context(self.nc.semaphore('d%d' % len(self.dsem))), 0]
        ent = self.dsem[g]
        ent[1] += 16
        tok = ('D', ent[0], ent[1])
        self.ops[q].append((waits, lambda e: e.dma_start(out=out, in_=in_, **kw), tok))
        self._commit(tok, r, w)

    def wait_group(self, eng, group):
        ent = self.dsem[group]
        self.ops[eng].append(([('D', ent[0], ent[1])], None, None))

    def emit(self):
        engobj = {'pe': 'tensor', 'act': 'scalar', 'dve': 'vector', 'pool': 'gpsimd', 'sp': 'sync'}
        waited = {e: set() for e in ENGS}
        for e in ENGS:
            for waits, fn, tok in self.ops[e]:
                for t in waits:
                    if t[0] == 'E':
                        waited[t[1]].add(t[2])
        rank = {e: {s_: i + 1 for i, s_ in enumerate(sorted(waited[e]))} for e in ENGS}
        with self.nc.Block() as block:
            for e in ENGS:
                ops = self.ops[e]

                def body(eng, ops=ops):
                    for waits, fn, tok in ops:
                        for t in waits:
                            if t[0] == 'E':
                                eng.wait_ge(self.esem[t[1]], rank[t[1]][t[2]])
                            else:
                                eng.wait_ge(t[1], t[2])
                        if fn is not None:
                            ins = fn(eng)
                            if tok[0] == 'D':
                                ins.then_inc(tok[1], 16)
                            elif tok[2] in rank[tok[1]]:
                                ins.then_inc(self.esem[tok[1]], 1)
                getattr(block, engobj[e])(body)


def rsl(start, n, step):
    if step > 0:
        return slice(start, start + n)
    stop = start - n
    return slice(start, stop if stop >= 0 else None, -1)


class Builder:
    def __init__(self, taps=(), stop_after=None):
        self.taps = set(taps)
        self.stop_after = stop_after
        nc = self.nc = bass.Bass("TRN2", target_bir_lowering=False)
        self.P = Prog(nc)
        self.din = {}
        self.tapout = {}
        self._castn = 0

    def inp(self, name, shape):
        self.din[name] = self.nc.dram_tensor(name, list(shape), F32, kind="ExternalInput").ap()
        return self.din[name]

    def mm(self, out, lhsT, rhs, start=True, stop=True, r=(), w=(), **kw):
        self.P.op('pe', lambda e: e.matmul(out, lhsT=lhsT, rhs=rhs, start=start, stop=stop, **kw), r=r, w=w)

    def act(self, out, in_, func, r=(), w=(), **kw):
        self.P.op('act', lambda e: e.activation(out=out, in_=in_, func=func, **kw), r=r, w=w)

    def tap(self, name, tile_ap, shape, r, dt=F32):
        if name not in self.taps:
            return
        o = self.nc.dram_tensor('tap_' + name, list(shape), dt, kind="ExternalOutput").ap()
        self.P.dma('pool', o, tile_ap, r=r, group='out_' + name)
        self.tapout[name] = o

    def cols(self, rows, n, chunk, name):
        P = self.P
        R = len(rows)
        nch = (n + chunk - 1) // chunk
        out = P.sb([chunk, nch, R], name=name)
        with P.scope():
            st = P.sb([R, n], name='colst')
            for i, rw in enumerate(rows):
                P.dma('sp', st[i:i + 1, :], rw.rearrange("(o n) -> o n", o=1), w=[(st, i)], group='colst')
            pp = self.pb[0]
            assert nch * R <= 512
            for c in range(nch):
                cs = min(chunk, n - c * chunk)
                P.op('pe', lambda e, c=c, cs=cs: e.transpose(out=pp[0:cs, c * R:(c + 1) * R], in_=st[0:R, c * chunk:c * chunk + cs],
                                                             identity=self.ident[0:R, 0:R]),
                     r=[(st, i) for i in range(R)] + [self.ident], w=[pp])
            P.op('dve', lambda e: e.tensor_copy(out=out[:].rearrange("p c r -> p (c r)"), in_=pp[0:chunk, 0:nch * R]), r=[pp], w=[out])
        return out

    def build(self):
        nc, P = self.nc, self.P
        inp = self.inp
        xT = inp('xT', [D, NLAT]); ctxT = inp('ctxT', [D, NCTX]); cvec = inp('cvec', [2, D])
        w_mod = inp('w_mod', [D, 6 * D]); b_mod = inp('b_mod', [6 * D])
        nmg = inp('norm_mix_g', [D]); nfg = inp('norm_ffn_g', [D]); nfin = inp('norm_final_g', [D])
        w_in = inp('w_in', [D, 8096])
        lru_conv_w = inp('lru_conv_w', [2, 4, LW]); lru_conv_b = inp('lru_conv_b', [2, LW])
        lru_wa = inp('lru_wa', [2, NBLK, BLK, BLK]); lru_ba = inp('lru_ba', [2, LW])
        lru_wx = inp('lru_wx', [2, NBLK, BLK, BLK]); lru_bx = inp('lru_bx', [2, LW])
        lru_lam = inp('lru_lambda', [2, LW]); w_o_lru = inp('w_o_lru', [LW, D])
        mu = inp('rwkv_mu', [2, RIN]); w0 = inp('rwkv_w0', [2, D]); w2 = inp('rwkv_w2', [2, 64, D])
        a0 = inp('rwkv_a0', [2, D]); a2 = inp('rwkv_a2', [2, 64, D]); g2 = inp('rwkv_g2', [160, D])
        k_k = inp('rwkv_k_k', [D]); k_a = inp('rwkv_k_a', [D]); r_k = inp('rwkv_r_k', [D])
        ln_g = inp('rwkv_ln_g', [D]); ln_b = inp('rwkv_ln_b', [D])
        w_o_rwkv = inp('w_o_rwkv', [D, D]); w_out = inp('w_out', [D, D])
        w_ffn_in = inp('w_ffn_in', [D, 2 * DFF]); w_ffn_out = inp('w_ffn_out', [DFF, D])
        outT = nc.dram_tensor('outT', [D, NLAT], F32, kind="ExternalOutput").ap()

        self.pb = [P.ps([128, 512], F32, name='pb') for _ in range(7)]
        self.pbh = P.ps([128, 1024], BF16, name='pbh')
        pb = self.pb

        ones = P.sb([128, 128], name='ones')
        P.op('dve', lambda e: e.memset(ones[:], 1.0), w=[ones])
        self.ident = ident = P.sb([128, 128], name='ident')
        P.op('pool', lambda e: e.affine_select(out=ident[:], in_=ones[:], pattern=[[-1, 128]], compare_op=ALU.is_equal, fill=0.0,
                                               base=0, channel_multiplier=1), r=[ones], w=[ident])
        identb = P.sb([128, 128], BF16, name='identb')
        P.op('dve', lambda e: e.tensor_copy(out=identb[:], in_=ident[:]), r=[ident], w=[identb])
        bones = P.sb([128, 128], name='bones')
        P.op('dve', lambda e: e.memset(bones[:], 0.0), w=[bones])
        P.op('dve', lambda e: e.memset(bones[0:64, 0:64], 1.0), w=[bones])
        P.op('dve', lambda e: e.memset(bones[64:128, 64:128], 1.0), w=[bones])
        self.ones, self.identb, self.bones = ones, identb, bones

        gains = self.cols([nmg, nfg, nfin], D, 128, 'gains')
        cT = self.cols([cvec[0], cvec[1]], D, 128, 'cT')
        bm = self.cols([b_mod], 6 * D, 128, 'bm')
        mod = P.sb([128, 48, 2], name='mod')
        with P.scope():
            sc = P.sb([128, 8, 2], name='sc')
            self.act(sc[:], cT[:], AF.Silu, r=[cT], w=[sc])
            wm = [P.sb([128, 8, 768], name='wm') for _ in range(2)]
            pm = pb[1]
            wv = w_mod.rearrange("(k p) n -> p k n", p=128)
            for jb in range(8):
                buf = wm[jb % 2]
                for k2 in range(2):
                    P.dma('sp' if k2 == 0 else 'act', buf[:, 4 * k2:4 * k2 + 4, :], wv[:, 4 * k2:4 * k2 + 4, jb * 768:(jb + 1) * 768],
                          w=[(buf, k2)], group='wm%d' % (jb % 2))
                for jj in range(6):
                    j = jb * 6 + jj
                    for k in range(8):
                        self.mm(pm[:, 2 * j:2 * j + 2], buf[:, k, jj * 128:(jj + 1) * 128], sc[:, k, :], start=(k == 0), stop=(k == 7),
                                r=[(buf, 0), (buf, 1), sc], w=[pm])
            for n in range(2):
                P.op('dve', lambda e, n=n: e.tensor_tensor(out=mod[:, :, n], in0=pm[:, 0:96].rearrange("p (j n) -> p j n", n=2)[:, :, n],
                                                           in1=bm[:, :, 0], op=ALU.add), r=[pm, bm], w=[mod])
        self.tap('mod', mod[:], [128, 48, 2], [mod])
        G1 = P.sb([128, 8, 2], name='G1'); G2 = P.sb([128, 8, 1], name='G2')
        for n in range(2):
            P.op('dve', lambda e, n=n: e.scalar_tensor_tensor(out=G1[:, :, n], in0=mod[:, 8:16, n], scalar=1.0, in1=gains[:, :, 0],
                                                              op0=ALU.add, op1=ALU.mult), r=[mod, gains], w=[G1])
        P.op('dve', lambda e: e.scalar_tensor_tensor(out=G2[:, :, 0], in0=mod[:, 32:40, 0], scalar=1.0, in1=gains[:, :, 1],
                                                     op0=ALU.add, op1=ALU.mult), r=[mod, gains], w=[G2])
        self.mod, self.gains = mod, gains

        arena = P.sb([128, 8 * T + 8 * NLAT], BF16, name='arena')
        hT = arena[:, 0:8 * T].rearrange("p (k t) -> p k t", t=T)
        xv = xT.rearrange("(k p) t -> p k t", p=128)
        cv = ctxT.rearrange("(k p) t -> p k t", p=128)
        self.modulate(hT, [(cv, 0, 256, 0, 1)] + [(xv, 512 * i, 512, 256 + 512 * i, 0) for i in range(4)], G1, mod, 0)
        self.tap('hT', hT[:], [128, 8, T], [(hT, i) for i in range(5)], dt=BF16)
        self.hT = hT
        if self.stop_after == 'B':
            return self.finish()
        self.rwT = arena[:, 8 * T:8 * T + 8 * NLAT].rearrange("p (k t) -> p k t", t=NLAT)
        if self.stop_after != 'C':
            with P.scope():
                self.rwkv()
        self.tap('rw', self.rwT[:], [128, 8, NLAT], [self.rwT], dt=BF16)
        if self.stop_after in ('D', 'D0'):
            return self.finish()
        self.lruT, lru_st = P.sbm([128, 10, NLAT], BF16, name='lruT')
        with P.scope():
            self.lru()
        self.tap('lru', self.lruT[:], [128, 10, NLAT], [self.lruT], dt=BF16)
        if self.stop_after == 'C':
            return self.finish()
        self.mT, m_st = P.sbm([128, 8, NLAT], BF16, name='mT')
        with P.scope():
            self.merge()
        P.free([])
        self.x1T = arena[:].bitcast(F32)[:, 0:8 * NLAT].rearrange("p (k t) -> p k t", t=NLAT)
        with P.scope():
            self.resid1()
        P.free([m_st, lru_st])
        self.tap('x1', self.x1T[:], [128, 8, NLAT], [self.x1T])
        if self.stop_after == 'E':
            return self.finish()
        self.h2T = P.sb([128, 8, NLAT], BF16, name='h2T')
        self.modulate(self.h2T, [(None, 512 * i, 512, 512 * i, 0) for i in range(4)], G2, mod, 24, src_sb=self.x1T)
        with P.scope():
            self.ffn()
        with P.scope():
            self.final(outT)
        return self.finish()

    def modulate(self, hT, blocks, G, mod, shift_j0, src_sb=None):
        P, pb = self.P, self.pb
        with P.scope():
            xb = [P.sb([128, 8, 512], name='xb') for _ in range(2)]
            sq = P.sb([128, 8, 512], name='sq')
            rs = P.sb([128, 512], name='rs')
            epst = P.sb([128, 1], name='epst')
            P.op('dve', lambda e: e.memset(epst[:], RMS_EPS), w=[epst])
            for bi, (src, so, n, do, mn) in enumerate(blocks):
                if src_sb is None:
                    x = xb[bi % 2]
                    for k2 in range(2):
                        P.dma('sp' if k2 == 0 else 'act', x[:, 4 * k2:4 * k2 + 4, 0:n], src[:, 4 * k2:4 * k2 + 4, so:so + n],
                              w=[(x, k2)], group='xb%d' % (bi % 2))
                    xr = [(x, 0), (x, 1)]
                    xa = lambda k, x=x, n=n: x[:, k, 0:n]
                    xall = x[:, :, 0:n]
                else:
                    xr = [src_sb]
                    xa = lambda k, so=so, n=n: src_sb[:, k, so:so + n]
                    xall = src_sb[:, :, so:so + n]
                self.act(sq[:, :, 0:n], xall, AF.Square, r=xr, w=[sq])
                pp = pb[bi % 2]
                for k in range(8):
                    self.mm(pp[:, 0:n], self.ones[:], sq[:, k, 0:n], start=(k == 0), stop=(k == 7), r=[sq, self.ones], w=[pp])
                self.act(rs[:, 0:n], pp[:, 0:n], AF.Sqrt, scale=1.0 / D, bias=epst[:], r=[pp, epst], w=[rs])
                P.op('dve', lambda e, n=n: e.reciprocal(out=rs[:, 0:n], in_=rs[:, 0:n]), r=[rs], w=[rs])
                for k in range(8):
                    P.op('dve', lambda e, k=k, n=n, xa=xa: e.tensor_tensor(out=sq[:, k, 0:n], in0=xa(k), in1=rs[:, 0:n], op=ALU.mult),
                         r=xr + [rs], w=[sq])
                    self.act(hT[:, k, do:do + n], sq[:, k, 0:n], AF.Identity, scale=G[:, k, mn:mn + 1], bias=mod[:, shift_j0 + k, mn:mn + 1],
                             r=[sq, G, mod], w=[(hT, bi)])

    def zshift(self, cq, ncol, dsts, zbuf, A, wz, wzb, mixw, segs=((0, 1, 256, 0), (1, 258, 2048, 256))):
        P, pb, hT = self.P, self.pb, self.hT
        w_in = self.din['w_in']
        i = self.zcount = getattr(self, 'zcount', 0) + 1
        wf, wb = wz[i % 2], wzb[i % 2]
        if isinstance(zbuf, list):
            zbuf = zbuf[i % len(zbuf)]
        if isinstance(A, list):
            A = A[i % len(A)]
        pre = getattr(self, 'zpre', None)
        if pre is not None and pre[0] == cq:
            wb = pre[1]
            self.zpre = None
        else:
            self.zload(cq, ncol, wf, wb, 'wz%d' % (i % 2))
        hk = [(hT, j) for j in range(5)]
        lat = lambda k: hT[:, k, 256:2304].rearrange("p (r c) -> p c r", c=64)
        nblk = 0
        for (seg, zc, n, _) in segs:
            nb = 1 if seg == 0 else 4
            for bi in range(nb):
                pp = pb[(nblk + 5 * i) % 6]; nblk += 1
                bn = 256 if seg == 0 else 512
                for k in range(8):
                    if seg == 0:
                        self.mm(pp[0:ncol, 0:256], wb[:, k, 0:ncol], hT[:, k, 0:256], start=(k == 0), stop=(k == 7), r=[wb] + hk, w=[pp])
                    else:
                        self.mm(pp[0:ncol, 0:512], wb[:, k, 0:ncol], hT[:, k, 256 + 512 * bi:256 + 512 * bi + 512],
                                start=(k == 0), stop=(k == 7), r=[wb] + hk, w=[pp])
                if seg == 0:
                    P.op('act', lambda e: e.copy(out=zbuf[0:ncol, zc:zc + 256], in_=pp[0:ncol, 0:256]), r=[pp], w=[zbuf])
                else:
                    zo = zbuf[0:ncol, zc:zc + 2048].rearrange("p (c r) -> p r c", r=32)[:, 8 * bi:8 * bi + 8, :]
                    P.op('act', lambda e: e.copy(out=zo, in_=pp[0:ncol, 0:512].rearrange("p (r c) -> p r c", c=64)), r=[pp], w=[zbuf])
        for (seg, zc, n, _), dst in zip(segs, dsts):
            if dst is None:
                continue
            dt, do, key = dst
            P.op('dve', lambda e, zc=zc, n=n: e.tensor_scalar(out=A[0:ncol, 0:n], in0=zbuf[0:ncol, zc:zc + n], scalar1=mixw[0:ncol, cq, 2:3],
                                                              scalar2=None, op0=ALU.mult), r=[zbuf, mixw], w=[A])
            P.op('dve', lambda e, zc=zc, n=n: e.scalar_tensor_tensor(out=A[0:ncol, 0:n], in0=zbuf[0:ncol, zc - 1:zc - 1 + n], scalar=mixw[0:ncol, cq, 0:1],
                                                                     in1=A[0:ncol, 0:n], op0=ALU.mult, op1=ALU.add), r=[zbuf, mixw, A], w=[A])
            P.op('dve', lambda e, zc=zc, n=n, dt=dt, do=do: e.scalar_tensor_tensor(out=dt[0:ncol, do:do + n], in0=zbuf[0:ncol, zc + 1:zc + 1 + n],
                                                                                   scalar=mixw[0:ncol, cq, 1:2], in1=A[0:ncol, 0:n],
                                                                                   op0=ALU.mult, op1=ALU.add), r=[zbuf, mixw, A], w=[key])

    def zload(self, cq, ncol, wf, wb, group):
        P = self.P
        c0 = 2560 + 128 * cq
        P.dma('sp', wf[:, :, 0:ncol], self.din['w_in'].rearrange("(k p) n -> p k n", p=128)[:, :, c0:c0 + ncol], w=[wf], group=group)
        P.op('pool', lambda e: e.tensor_copy(out=wb[:, :, 0:ncol], in_=wf[:, :, 0:ncol]), r=[wf], w=[wb])

    def rwkv(self):
        P, pb, pbh, hT, din = self.P, self.pb, self.pbh, self.hT, self.din
        rwT = self.rwT
        CW = -0.5 * float(np.exp(-0.5))
        mixw = self.cols([din['rwkv_mu'][0], din['rwkv_mu'][1], din['rwkv_mu'][0]], RIN, 128, 'mixw')
        P.op('dve', lambda e: e.tensor_tensor(out=mixw[:, :, 2], in0=mixw[:, :, 0], in1=mixw[:, :, 1], op=ALU.add), r=[mixw], w=[mixw])
        P.op('dve', lambda e: e.tensor_scalar(out=mixw[:, :, 2], in0=mixw[:, :, 2], scalar1=-1.0, scalar2=1.0, op0=ALU.mult, op1=ALU.add), r=[mixw], w=[mixw])
        chp = self.cols([din['rwkv_w0'][0], din['rwkv_w0'][1], din['rwkv_a0'][0], din['rwkv_a0'][1], din['rwkv_k_k'], din['rwkv_k_a'],
                         din['rwkv_r_k'], din['rwkv_ln_g'], din['rwkv_ln_b']], D, 128, 'chp')
        hp2 = P.sb([128, 8, 6], name='hp2')
        P.op('dve', lambda e: e.tensor_scalar(out=hp2[:, :, 0:4], in0=chp[:, :, 0:4], scalar1=0.5, scalar2=None, op0=ALU.mult), r=[chp], w=[hp2])
        P.op('dve', lambda e: e.tensor_scalar(out=hp2[:, :, 4:5], in0=chp[:, :, 5:6], scalar1=0.5, scalar2=None, op0=ALU.mult), r=[chp], w=[hp2])
        P.op('dve', lambda e: e.tensor_scalar(out=hp2[:, :, 5:6], in0=chp[:, :, 5:6], scalar1=-0.5, scalar2=1.0, op0=ALU.mult, op1=ALU.add), r=[chp], w=[hp2])
        gneps = P.sb([128, 1], name='gneps')
        P.op('dve', lambda e: e.memset(gneps[:], GN_EPS), w=[gneps])
        w2s = P.sb([128, D], BF16, name='w2s'); a2s = P.sb([128, D], BF16, name='a2s')
        g2a = P.sb([128, D], BF16, name='g2a'); g2b = P.sb([32, D], BF16, name='g2b')
        with P.scope():
            st = P.sb([128, D], name='lst')
            for src, dst, npart in ((din['rwkv_w2'].rearrange("d r c -> (d r) c"), w2s, 128), (din['rwkv_a2'].rearrange("d r c -> (d r) c"), a2s, 128),
                                    (din['rwkv_g2'][0:128, :], g2a, 128), (din['rwkv_g2'][128:160, :], g2b, 32)):
                P.dma('sp', st[0:npart, :], src, w=[st], group='lst')
                P.op('dve', lambda e, dst=dst, npart=npart: e.tensor_copy(out=dst[0:npart, :], in_=st[0:npart, :]), r=[st], w=[dst])
        msk = {}
        onesb = P.sb([128, 4, 64], BF16, name='onesb')
        P.op('dve', lambda e: e.memset(onesb[:], 1.0), w=[onesb])
        for nm, op, sgn in (('su', ALU.is_gt, -1), ('sl', ALU.is_gt, 1), ('iu', ALU.is_ge, -1), ('id', ALU.is_equal, 1)):
            m = P.sb([128, 4, 64], BF16, name='m' + nm)
            for e_ in range(2):
                P.op('pool', lambda e, m=m, op=op, sgn=sgn, e_=e_: e.affine_select(out=m[64 * e_:64 * e_ + 64], in_=onesb[64 * e_:64 * e_ + 64],
                                                                                   pattern=[[0, 4], [-sgn, 64]], compare_op=op, fill=0.0,
                                                                                   base=0, channel_multiplier=sgn), r=[onesb], w=[m])
            msk[nm] = m
        cmask = P.sb([128, 256], name='cmask')
        P.op('dve', lambda e: e.memset(cmask[:], 1.0), w=[cmask])
        P.op('dve', lambda e: e.memset(cmask[:, 0:256:64], 0.0), w=[cmask])
        twd = P.sb([128, T], BF16, name='twd'); adb = P.sb([128, T], BF16, name='adb')
        sgd1 = P.sb([128, NLAT], BF16, name='sgd1'); sgd2 = P.sb([32, NLAT], BF16, name='sgd2')

        with P.scope():
            zbuf = [P.sb([128, 2307], name='zbuf') for _ in range(2)]; A = [P.sb([128, 2048], name='zA') for _ in range(2)]
            wz = [P.sb([128, 8, 128], name='wz') for _ in range(2)]; wzb = [P.sb([128, 8, 128], BF16, name='wzb') for _ in range(2)]
            for zb in zbuf:
                P.op('pool', lambda e, zb=zb: e.memset(zb[:], 0.0), w=[zb])
            tmp = P.sb([128, T], name='ltmp')
            self.zshift(24, 128, [(tmp, 0, tmp), (tmp, 256, tmp)], zbuf, A, wz, wzb, mixw)
            self.act(twd[:], tmp[:], AF.Tanh, r=[tmp], w=[twd])
            self.zshift(25, 128, [(adb, 0, adb), (adb, 256, adb)], zbuf, A, wz, wzb, mixw)
            for cq, ncol, dst in ((26, 128, sgd1), (27, 32, sgd2)):
                self.zshift(cq, ncol, [None, (tmp, 256, tmp)], zbuf, A, wz, wzb, mixw)
                self.act(tmp[0:ncol, 256:T], tmp[0:ncol, 256:T], AF.Tanh, scale=0.5, r=[tmp], w=[tmp])
                P.op('dve', lambda e, dst=dst, ncol=ncol: e.tensor_scalar(out=dst[0:ncol, :], in0=tmp[0:ncol, 256:T], scalar1=0.5, scalar2=0.5,
                                                                          op0=ALU.mult, op1=ALU.add), r=[tmp], w=[dst])

        pre_f = P.sb([128, 8, 128], name='pre_f'); pre_b = P.sb([128, 8, 128], BF16, name='pre_b')
        self.zload(8, 128, pre_f, pre_b, 'wzpre')
        self.zpre = (8, pre_b)
        self.tap('twd', twd[:], [128, T], [twd], dt=BF16)
        self.tap('adb', adb[:], [128, T], [adb], dt=BF16)
        self.tap('sgd1', sgd1[:], [128, NLAT], [sgd1], dt=BF16)
        for hp in range(8):
            if self.stop_after == 'D0' and hp > 0:
                break
            with P.scope():
                rb = P.sb([128, T], BF16, name='rb'); kb = P.sb([128, T], BF16, name='kb'); vb = P.sb([128, T], BF16, name='vb')
                kkb = P.sb([128, T], BF16, name='kkb')
                y0 = P.sb([128, NLAT], name='y0'); bacc = P.sb([128, NLAT], name='bacc')
                with P.scope():
                    zbufs = [P.sb([128, 2307], name='zbuf') for _ in range(3)]; As = [P.sb([128, 2048], name='zA') for _ in range(2)]
                    wz = [P.sb([128, 8, 128], name='wz') for _ in range(2)]; wzb = [P.sb([128, 8, 128], BF16, name='wzb') for _ in range(2)]
                    for zb in zbufs:
                        for pc in (0, 257, 2306):
                            P.op('pool', lambda e, zb=zb, pc=pc: e.memset(zb[:, pc:pc + 1], 0.0), w=[zb])
                    for cq, dst in ((8 + hp, kb), (hp, rb), (16 + hp, vb)):
                        self.zshift(cq, 128, [(dst, 0, dst), (dst, 256, dst)], zbufs, As, wz, wzb, mixw)
                    if hp < 7:
                        self.zload(8 + hp + 1, 128, pre_f, pre_b, 'wzpre')
                        self.zpre = (8 + hp + 1, pre_b)
                    kq = zbufs[0]; A = As[0]
                    self.act(kq[:, 0:T], kb[:], AF.Identity, scale=chp[:, hp, 4:5], r=[kb, chp], w=[kq])
                    for bi, (o, n) in enumerate([(0, 512), (512, 512), (1024, 512), (1536, 512), (2048, 256)]):
                        P.op('dve', lambda e, o=o, n=n: e.tensor_tensor(out=A[:, 0:n], in0=kq[:, o:o + n], in1=kq[:, o:o + n], op=ALU.mult), r=[kq], w=[A])
                        pp = pb[bi % 5]
                        self.mm(pp[:, 0:n], self.bones[:], A[:, 0:n], r=[A, self.bones], w=[pp])
                        self.act(A[:, 512:512 + n], pp[:, 0:n], AF.Sqrt, r=[pp], w=[A])
                        P.op('dve', lambda e, n=n: e.tensor_scalar(out=A[:, 512:512 + n], in0=A[:, 512:512 + n], scalar1=1e-12, scalar2=None, op0=ALU.max), r=[A], w=[A])
                        P.op('dve', lambda e, n=n: e.reciprocal(out=A[:, 512:512 + n], in_=A[:, 512:512 + n]), r=[A], w=[A])
                        P.op('dve', lambda e, o=o, n=n: e.tensor_tensor(out=kkb[:, o:o + n], in0=kq[:, o:o + n], in1=A[:, 512:512 + n], op=ALU.mult), r=[A, kq], w=[kkb])
                if hp == 0:
                    for nm, t_ in (('rb', rb), ('kb', kb), ('vb', vb), ('kkb', kkb)):
                        self.tap(nm, t_[:], [128, T], [t_], dt=BF16)
                for j in range(8):
                    P.op('pool', lambda e: e.memset(y0[:, 256 * j:256 * j + 256], 0.0), w=[(y0, 256 * j)])
                    P.op('pool', lambda e: e.memset(bacc[:, 256 * j:256 * j + 256], 0.0), w=[(bacc, 256 * j)])
                C = dict(rb=rb, kb=kb, vb=vb, kkb=kkb, y0=y0, bacc=bacc, twd=twd, adb=adb, w2s=w2s, a2s=a2s, chp=chp, hp2=hp2, msk=msk,
                         cmask=cmask, CW=CW, rot=[0])
                with P.scope():
                    self.rw_rounds(hp, C)
                if hp == 0:
                    self.tap('y0', y0[:], [128, NLAT], [y0])
                    self.tap('bacc', bacc[:], [128, NLAT], [bacc])
                with P.scope():
                    yc = P.sb([128, 512], name='yc'); sq = P.sb([128, 512], name='sq2'); rstd = P.sb([128, 512], name='rstd'); tt = P.sb([128, 512], name='tt')
                    for i in range(4):
                        c0 = 512 * i
                        pm, pv, pg = pb[0], pb[1], pb[2]
                        self.mm(pm[:], self.bones[:], y0[:, c0:c0 + 512], r=[y0, self.bones], w=[pm])
                        P.op('dve', lambda e, c0=c0: e.scalar_tensor_tensor(out=yc[:], in0=pm[:], scalar=-1.0 / 64, in1=y0[:, c0:c0 + 512], op0=ALU.mult, op1=ALU.add),
                             r=[pm, y0], w=[yc])
                        P.op('dve', lambda e: e.tensor_tensor(out=sq[:], in0=yc[:], in1=yc[:], op=ALU.mult), r=[yc], w=[sq])
                        self.mm(pv[:], self.bones[:], sq[:], r=[sq, self.bones], w=[pv])
                        self.act(rstd[:], pv[:], AF.Sqrt, scale=1.0 / 64, bias=gneps[:], r=[pv, gneps], w=[rstd])
                        P.op('dve', lambda e: e.reciprocal(out=rstd[:], in_=rstd[:]), r=[rstd], w=[rstd])
                        P.op('dve', lambda e: e.tensor_tensor(out=yc[:], in0=yc[:], in1=rstd[:], op=ALU.mult), r=[yc, rstd], w=[yc])
                        self.act(yc[:], yc[:], AF.Identity, scale=chp[:, hp, 7:8], bias=chp[:, hp, 8:9], r=[yc, chp], w=[yc])
                        P.op('dve', lambda e, c0=c0: e.tensor_tensor(out=tt[:], in0=vb[:, 256 + c0:256 + c0 + 512], in1=bacc[:, c0:c0 + 512], op=ALU.mult), r=[vb, bacc], w=[tt])
                        P.op('dve', lambda e: e.tensor_tensor(out=tt[:], in0=tt[:], in1=yc[:], op=ALU.add), r=[tt, yc], w=[tt])
                        self.mm(pg[:], g2a[:, hp * 128:(hp + 1) * 128], sgd1[:, c0:c0 + 512], start=True, stop=False, r=[g2a, sgd1], w=[pg])
                        self.mm(pg[:], g2b[0:32, hp * 128:(hp + 1) * 128], sgd2[0:32, c0:c0 + 512], start=False, stop=True, r=[g2b, sgd2], w=[pg])
                        P.op('dve', lambda e, i=i: e.tensor_tensor(out=rwT[:, hp, :].rearrange("p (r c) -> p c r", c=64)[:, 16 * i:16 * i + 16, :],
                                                                   in0=tt[:].rearrange("p (c r) -> p c r", r=32), in1=pg[:].rearrange("p (c r) -> p c r", r=32),
                                                                   op=ALU.mult), r=[tt, pg], w=[rwT])

    def rw_alloc_dir(self):
        P = self.P
        S = {}
        S['A1'] = P.sb([128, 256], name='A1'); S['B1'] = P.sb([128, 256], name='B1'); S['C1'] = P.sb([128, 256], name='C1')
        S['s1'] = [{nm: P.sb([128, 256], BF16, name=nm) for nm in ('at', 'bt', 'kt', 'vs')} for _ in range(2)]
        S['rw'] = [{'rt': P.sb([128, 256], BF16, name='rt'), 'wc': P.sb([128, 4], name='wc')} for _ in range(4)]
        S['QX'] = [[P.sb([128, 4, 2, 64], BF16, name='QX') for _ in range(2)] for _ in range(2)]
        S['QT'] = [[P.sb([128, 4, 64], BF16, name='QT') for _ in range(2)] for _ in range(2)]
        S['AakT'] = [P.sb([128, 4, 64], BF16, name='AakT') for _ in range(2)]
        S['slots'] = []
        for _ in range(3):
            sl = {'tokT': P.sb([128, 4, 4, 64], BF16, name='tokT'), 'MT': P.sb([128, 4, 64], BF16, name='MT'), 'Xak': P.sb([128, 4, 64], BF16, name='Xak'),
                  'ArbT': P.sb([128, 4, 64], BF16, name='ArbT'), 'ArkT': P.sb([128, 4, 64], BF16, name='ArkT'), 'AhT': P.sb([128, 4, 64], BF16, name='AhT')}
            S['slots'].append(sl)
        S['Tst'] = P.sb([128, 64], name='Tst'); S['Tw'] = P.sb([128, 64], name='Tw'); S['Tb'] = P.sb([128, 64], BF16, name='Tb')
        S['Ub'] = P.sb([128, 64], BF16, name='Ub')
        for nm in ('Tst', 'Tw', 'Tb'):
            P.op('dve', lambda e, t=S[nm]: e.memset(t[:], 0.0), w=[S[nm]])
        return S

    def gen_S1(self, hp, d, g, S, C):
        P, pb, pbh = self.P, self.pb, self.pbh
        rb, kb, vb, kkb, bacc = C['rb'], C['kb'], C['vb'], C['kkb'], C['bacc']
        twd, adb, w2s, a2s, chp, hp2, msk, cmask, CW = (C[k] for k in ('twd', 'adb', 'w2s', 'a2s', 'chp', 'hp2', 'msk', 'cmask', 'CW'))
        A1, B1, C1 = (S[k] for k in ('A1', 'B1', 'C1'))
        at, bt, kt, vs = (S['s1'][g % 2][k] for k in ('at', 'bt', 'kt', 'vs'))
        rt, wc = S['rw'][g % 4]['rt'], S['rw'][g % 4]['wc']
        if d == 0:
            s0, step = 256 * g, 1
        else:
            s0, step = (0 if g == 0 else 2304 - 256 * g), -1
        nat = slice(s0, s0 + 256)
        loc = lambda t: t[:, rsl(0 if step > 0 else 255, 256, step)]
        hc = slice(hp * 128, (hp + 1) * 128); ds = slice(64 * d, 64 * d + 64)
        rot = C['rot']
        H = lambda e_: slice(64 * e_, 64 * e_ + 64)
        TP = lambda e_: (64 * e_, 64 * e_)

        def bank():
            rot[0] = (rot[0] + 1) % 4
            return pb[(0, 1, 2, 5)[rot[0]]]
        pp = bank()
        self.mm(pp[:, 0:256], w2s[ds, hc], twd[ds, nat], r=[w2s, twd], w=[pp])
        self.act(loc(A1), pp[:, 0:256], AF.Tanh, scale=0.5, bias=hp2[:, hp, d:d + 1], r=[pp, hp2], w=[A1])
        yield
        P.op('dve', lambda e: e.tensor_scalar(out=A1[:], in0=A1[:], scalar1=1.0, scalar2=CW, op0=ALU.add, op1=ALU.mult), r=[A1], w=[A1])
        P.op('dve', lambda e: e.tensor_tensor_scan(out=B1[:], data0=cmask[:, 0:256], data1=A1[:], initial=0.0, op0=ALU.mult, op1=ALU.add), r=[A1, cmask], w=[B1])
        P.op('dve', lambda e: e.tensor_tensor(out=A1[:], in0=B1[:], in1=A1[:], op=ALU.subtract), r=[A1, B1], w=[A1])
        yield
        self.act(C1[:], A1[:], AF.Exp, r=[A1], w=[C1])
        P.op('dve', lambda e: e.scalar_tensor_tensor(out=loc(at), in0=kkb[:, nat], scalar=-1.0, in1=loc(C1), op0=ALU.mult, op1=ALU.mult), r=[kkb, C1], w=[at])
        yield
        self.act(C1[:], B1[:], AF.Exp, r=[B1], w=[C1])
        P.op('dve', lambda e: e.tensor_copy(out=wc[:], in_=C1[:, 63:256:64]), r=[C1], w=[wc])
        P.op('dve', lambda e: e.tensor_tensor(out=loc(rt), in0=rb[:, nat], in1=loc(C1), op=ALU.mult), r=[rb, C1], w=[rt])
        yield
        self.act(C1[:], B1[:], AF.Exp, scale=-1.0, r=[B1], w=[C1])
        pp = bank()
        self.mm(pp[:, 0:256], a2s[ds, hc], adb[ds, nat], r=[a2s, adb], w=[pp])
        self.act(loc(A1), pp[:, 0:256], AF.Tanh, scale=0.5, bias=hp2[:, hp, 2 + d:3 + d], r=[pp, hp2], w=[A1])
        yield
        P.op('dve', lambda e: e.tensor_scalar(out=B1[:], in0=A1[:], scalar1=0.5, scalar2=0.5, op0=ALU.mult, op1=ALU.add), r=[A1], w=[B1])
        P.op('dve', lambda e: e.tensor_tensor(out=loc(B1), in0=loc(B1), in1=kkb[:, nat], op=ALU.mult), r=[B1, kkb], w=[B1])
        P.op('dve', lambda e: e.tensor_tensor(out=bt[:], in0=B1[:], in1=C1[:], op=ALU.mult), r=[B1, C1], w=[bt])
        yield
        P.op('dve', lambda e: e.tensor_scalar(out=A1[:], in0=A1[:], scalar1=hp2[:, hp, 4:5], scalar2=hp2[:, hp, 5:6], op0=ALU.mult, op1=ALU.add), r=[A1, hp2], w=[A1])
        P.op('dve', lambda e: e.tensor_tensor(out=loc(A1), in0=loc(A1), in1=kb[:, nat], op=ALU.mult), r=[A1, kb], w=[A1])
        P.op('dve', lambda e: e.tensor_tensor(out=kt[:], in0=A1[:], in1=C1[:], op=ALU.mult), r=[A1, C1], w=[kt])
        P.op('pool', lambda e: e.tensor_copy(out=loc(vs), in_=vb[:, nat]), r=[vb], w=[vs])
        yield
        if s0 >= 256:
            P.op('dve', lambda e: e.scalar_tensor_tensor(out=B1[:], in0=loc(A1), scalar=chp[:, hp, 6:7], in1=rb[:, nat], op0=ALU.mult, op1=ALU.mult),
                 r=[A1, chp, rb, bt], w=[B1])
            pp = bank()
            self.mm(pp[:, 0:256], self.bones[:], B1[:], r=[B1, self.bones], w=[pp])
            bo = s0 - 256
            P.op('dve', lambda e: e.tensor_tensor(out=bacc[:, bo:bo + 256], in0=bacc[:, bo:bo + 256], in1=pp[:, 0:256], op=ALU.add), r=[pp, (bacc, bo)], w=[(bacc, bo)])
            yield

    def gen_S2a(self, hp, d, g, S, C):
        P, pb, pbh = self.P, self.pb, self.pbh
        msk = C['msk']
        at, bt, kt, vs = (S['s1'][g % 2][k] for k in ('at', 'bt', 'kt', 'vs'))
        rt = S['rw'][g % 4]['rt']
        sl = S['slots'][g % 3]
        tokT = sl['tokT']
        rot = C['rot']
        H = lambda e_: slice(64 * e_, 64 * e_ + 64)
        TP = lambda e_: (64 * e_, 64 * e_)

        def bank():
            rot[0] = (rot[0] + 1) % 4
            return pb[(0, 1, 2, 5)[rot[0]]]
        v3 = lambda p: p[:, 0:256].rearrange("p (c t) -> p c t", t=64)
        v4 = lambda p: p[:, :].rearrange("p (c x) -> p c x", x=128)
        QXs, QTs, AakT = S['QX'][g % 2], S['QT'][g % 2], S['AakT'][g % 2]

        def neumann_level(lvl, QX, QT):
            QXn, QTn = QXs[lvl % 2], QTs[lvl % 2]
            last = lvl == 6
            if lvl == 1:
                pq = bank()
                for c in range(4):
                    for e_ in range(2):
                        self.mm(v3(pq)[H(e_), c, :], QT[H(e_), c, :], QX[H(e_), c, 0, :], r=[QX, QT], w=[pq], tile_position=TP(e_))
                P.op('act', lambda e: e.copy(out=QXn[:, :, 0, :], in_=v3(pq)), r=[pq], w=[QXn])
                P.op('pool', lambda e: e.tensor_copy(out=QXn[:, :, 1, :], in_=QX[:, :, 1, :]), r=[QX], w=[QXn])
            else:
                ppx = bank()
                for c in range(4):
                    for e_ in range(2):
                        if last:
                            self.mm(v4(ppx)[H(e_), c, 64:128], QT[H(e_), c, :], QX[H(e_), c, 1, :], r=[QX, QT], w=[ppx], tile_position=TP(e_))
                        else:
                            self.mm(v4(ppx)[H(e_), c, :], QT[H(e_), c, :], QX[H(e_), c, :, :].rearrange("p a b -> p (a b)"), r=[QX, QT], w=[ppx],
                                    tile_position=TP(e_))
                dstP = sl['MT'][:] if last else QXn[:, :, 1, :]
                P.op('dve', lambda e: e.tensor_tensor(out=dstP, in0=v4(ppx)[:, :, 64:128], in1=QX[:, :, 1, :], op=ALU.add), r=[ppx, QX], w=[sl['MT'] if last else QXn])
                if not last:
                    P.op('act', lambda e: e.copy(out=QXn[:, :, 0, :], in_=v4(ppx)[:, :, 0:64]), r=[ppx], w=[QXn])
            if not last:
                pqt = bank()
                for c in range(4):
                    for e_ in range(2):
                        self.mm(v3(pqt)[H(e_), c, :], QX[H(e_), c, 0, :], QT[H(e_), c, :], r=[QX, QT], w=[pqt], tile_position=TP(e_))
                P.op('act', lambda e: e.copy(out=QTn[:], in_=v3(pqt)), r=[pqt], w=[QTn])
            return QXn, QTn
        for qi, q in enumerate((at, bt, kt, vs)):
            for c in range(4):
                for e_ in range(2):
                    o = (qi % 2) * 256 + c * 64
                    P.op('pe', lambda e: e.transpose(out=pbh[H(e_), o:o + 64], in_=q[H(e_), 64 * c:64 * c + 64], identity=self.identb[H(e_), H(e_)],
                                                     tile_position=TP(e_)), r=[q, self.identb], w=[pbh])
            if qi % 2 == 1:
                P.op('act', lambda e: e.copy(out=tokT[:, qi - 1:qi + 1, :, :].rearrange("p q c j -> p (q c j)"), in_=pbh[:, 0:512]), r=[pbh], w=[tokT])
                yield
        cs = lambda q, c, e_: q[H(e_), 64 * c:64 * c + 64]

        def score(L, Rr, mk, dst, dkey):
            pp = bank()
            for c in range(4):
                for e_ in range(2):
                    self.mm(v3(pp)[H(e_), c, :], cs(L, c, e_), cs(Rr, c, e_), r=[L, Rr], w=[pp], tile_position=TP(e_))
            P.op('dve', lambda e: e.tensor_tensor(out=dst, in0=v3(pp), in1=msk[mk][:], op=ALU.mult), r=[pp, msk[mk]], w=[dkey])
        QX, QT = QXs[0], QTs[0]
        score(bt, at, 'su', QX[:, :, 0, :], QX)
        yield
        score(at, bt, 'sl', QT[:], QT)
        yield
        P.op('pool', lambda e: e.tensor_tensor(out=QX[:, :, 1, :], in0=QX[:, :, 0, :], in1=msk['id'][:], op=ALU.add), r=[QX, msk['id']], w=[QX])
        score(kt, at, 'su', AakT[:], AakT)
        yield
        score(bt, rt, 'iu', sl['ArbT'][:], sl['ArbT'])
        yield
        score(kt, rt, 'iu', sl['ArkT'][:], sl['ArkT'])
        yield
        for lvl in (1, 2, 3):
            QX, QT = neumann_level(lvl, QX, QT)
            yield

    def gen_S2b(self, hp, d, g, S, C):
        P, pb, pbh = self.P, self.pb, self.pbh
        msk = C['msk']
        at, bt, kt, vs = (S['s1'][g % 2][k] for k in ('at', 'bt', 'kt', 'vs'))
        rt = S['rw'][g % 4]['rt']
        sl = S['slots'][g % 3]
        tokT = sl['tokT']
        rot = C['rot']
        H = lambda e_: slice(64 * e_, 64 * e_ + 64)
        TP = lambda e_: (64 * e_, 64 * e_)

        def bank():
            rot[0] = (rot[0] + 1) % 4
            return pb[(0, 1, 2, 5)[rot[0]]]
        v3 = lambda p: p[:, 0:256].rearrange("p (c t) -> p c t", t=64)
        v4 = lambda p: p[:, :].rearrange("p (c x) -> p c x", x=128)
        QXs, QTs, AakT = S['QX'][g % 2], S['QT'][g % 2], S['AakT'][g % 2]

        def neumann_level(lvl, QX, QT):
            QXn, QTn = QXs[lvl % 2], QTs[lvl % 2]
            last = lvl == 6
            if lvl == 1:
                pq = bank()
                for c in range(4):
                    for e_ in range(2):
                        self.mm(v3(pq)[H(e_), c, :], QT[H(e_), c, :], QX[H(e_), c, 0, :], r=[QX, QT], w=[pq], tile_position=TP(e_))
                P.op('act', lambda e: e.copy(out=QXn[:, :, 0, :], in_=v3(pq)), r=[pq], w=[QXn])
                P.op('pool', lambda e: e.tensor_copy(out=QXn[:, :, 1, :], in_=QX[:, :, 1, :]), r=[QX], w=[QXn])
            else:
                ppx = bank()
                for c in range(4):
                    for e_ in range(2):
                        if last:
                            self.mm(v4(ppx)[H(e_), c, 64:128], QT[H(e_), c, :], QX[H(e_), c, 1, :], r=[QX, QT], w=[ppx], tile_position=TP(e_))
                        else:
                            self.mm(v4(ppx)[H(e_), c, :], QT[H(e_), c, :], QX[H(e_), c, :, :].rearrange("p a b -> p (a b)"), r=[QX, QT], w=[ppx],
                                    tile_position=TP(e_))
                dstP = sl['MT'][:] if last else QXn[:, :, 1, :]
                P.op('dve', lambda e: e.tensor_tensor(out=dstP, in0=v4(ppx)[:, :, 64:128], in1=QX[:, :, 1, :], op=ALU.add), r=[ppx, QX], w=[sl['MT'] if last else QXn])
                if not last:
                    P.op('act', lambda e: e.copy(out=QXn[:, :, 0, :], in_=v4(ppx)[:, :, 0:64]), r=[ppx], w=[QXn])
            if not last:
                pqt = bank()
                for c in range(4):
                    for e_ in range(2):
                        self.mm(v3(pqt)[H(e_), c, :], QX[H(e_), c, 0, :], QT[H(e_), c, :], r=[QX, QT], w=[pqt], tile_position=TP(e_))
                P.op('act', lambda e: e.copy(out=QTn[:], in_=v3(pqt)), r=[pqt], w=[QTn])
            return QXn, QTn
        QX, QT = QXs[1], QTs[1]
        for lvl in (4, 5, 6):
            QX, QT = neumann_level(lvl, QX, QT)
            yield
        MT = sl['MT']
        pxa = bank()
        for c in range(4):
            for e_ in range(2):
                self.mm(v3(pxa)[H(e_), c, :], AakT[H(e_), c, :], tokT[H(e_), 3, c, :], r=[AakT, tokT], w=[pxa], tile_position=TP(e_))
        P.op('act', lambda e: e.copy(out=sl['Xak'][:], in_=v3(pxa)), r=[pxa], w=[sl['Xak']])
        yield
        pA = bank()
        for c in range(4):
            for e_ in range(2):
                self.mm(v3(pA)[H(e_), c, :], tokT[H(e_), 0, c, :], MT[H(e_), c, :], r=[tokT, MT], w=[pA], tile_position=TP(e_))
        P.op('act', lambda e: e.copy(out=sl['AhT'][:], in_=v3(pA)), r=[pA], w=[sl['AhT']])
        yield

    def gen_Q(self, hp, d, g, S, C):
        P, pb = self.P, self.pb
        y0 = C['y0']
        sl = S['slots'][g % 3]
        tokT, MT, Xak, ArbT, ArkT, AhT = (sl[k] for k in ('tokT', 'MT', 'Xak', 'ArbT', 'ArkT', 'AhT'))
        rt, wc = S['rw'][g % 4]['rt'], S['rw'][g % 4]['wc']
        Tst, Tw, Tb, Ub = S['Tst'], S['Tw'], S['Tb'], S['Ub']
        pU, pT, pY = pb[3], pb[4], pb[6]
        pUv = pU[:, 0:64]
        pTv = pT[:, 0:64]
        pYv = pY[:, 256 * d:256 * d + 256].rearrange("p (c t) -> p c t", t=64)
        H = lambda e_: slice(64 * e_, 64 * e_ + 64)
        TP = lambda e_: (64 * e_, 64 * e_)
        latent = g >= 1
        for c in range(4):
            for e_ in range(2):
                self.mm(pUv[H(e_), :], MT[H(e_), c, :], Xak[H(e_), c, :], start=True, stop=False, r=[MT, Xak], w=[pU], tile_position=TP(e_))
            for e_ in range(2):
                self.mm(pUv[H(e_), :], AhT[H(e_), c, :], Tb[H(e_), :], start=False, stop=True, r=[AhT, Tb], w=[pU], tile_position=TP(e_))
            P.op('act', lambda e: e.copy(out=Ub[:], in_=pUv), r=[pU], w=[Ub])
            yield
            for e_ in range(2):
                self.mm(pTv[H(e_), :], tokT[H(e_), 1, c, :], Ub[H(e_), :], start=True, stop=False, r=[tokT, Ub], w=[pT], tile_position=TP(e_))
            for e_ in range(2):
                self.mm(pTv[H(e_), :], tokT[H(e_), 2, c, :], tokT[H(e_), 3, c, :], start=False, stop=True, r=[tokT], w=[pT], tile_position=TP(e_))
            if latent:
                for e_ in range(2):
                    self.mm(pYv[H(e_), c, :], Tb[H(e_), :], rt[H(e_), 64 * c:64 * c + 64], start=True, stop=False, r=[Tb, rt], w=[pY], tile_position=TP(e_))
                for e_ in range(2):
                    self.mm(pYv[H(e_), c, :], Ub[H(e_), :], ArbT[H(e_), c, :], start=False, stop=False, r=[Ub, ArbT], w=[pY], tile_position=TP(e_))
                for e_ in range(2):
                    self.mm(pYv[H(e_), c, :], tokT[H(e_), 3, c, :], ArkT[H(e_), c, :], start=False, stop=True, r=[tokT, ArkT], w=[pY], tile_position=TP(e_))
            wcc = wc[:, c:c + 1]
            P.op('dve', lambda e: e.scalar_tensor_tensor(out=Tb[:], in0=pTv, scalar=wcc, in1=Tw[:], op0=ALU.mult, op1=ALU.add), r=[pT, wc, Tw], w=[Tb])
            P.op('dve', lambda e: e.scalar_tensor_tensor(out=Tst[:], in0=pTv, scalar=wcc, in1=Tw[:], op0=ALU.mult, op1=ALU.add), r=[pT, wc, Tw], w=[Tst])
            yield
            if c < 3:
                P.op('dve', lambda e: e.tensor_scalar(out=Tw[:], in0=Tst[:], scalar1=wc[:, c + 1:c + 2], scalar2=None, op0=ALU.mult), r=[Tst, wc], w=[Tw])
        if latent:
            g0 = 256 * g
            if d == 0:
                ysl = slice(g0 - 256, g0); yk = (y0, g0 - 256)
            else:
                ysl = rsl(2303 - g0, 256, -1); yk = (y0, 2048 - g0)
            P.op('dve', lambda e: e.tensor_tensor(out=y0[:, ysl], in0=y0[:, ysl], in1=pY[:, 256 * d:256 * d + 256], op=ALU.add), r=[pY, yk], w=[yk])
        yield

    def rw_rounds(self, hp, C):
        P = self.P
        dirs = [self.rw_alloc_dir() for _ in range(2)]
        for R in range(12):
            gens = []
            if 3 <= R:
                for d in range(2):
                    S = dirs[d]
                    wc = S['rw'][(R - 3) % 4]['wc']
                    P.op('dve', lambda e, S=S, wc=wc: e.tensor_scalar(out=S['Tw'][:], in0=S['Tst'][:], scalar1=wc[:, 0:1], scalar2=None, op0=ALU.mult),
                         r=[S['Tst'], wc], w=[S['Tw']])
                    gens.append(self.gen_Q(hp, d, R - 3, S, C))
            if 2 <= R <= 10:
                for d in range(2):
                    gens.append(self.gen_S2b(hp, d, R - 2, dirs[d], C))
            if 1 <= R <= 9:
                for d in range(2):
                    gens.append(self.gen_S2a(hp, d, R - 1, dirs[d], C))
            if R <= 8:
                for d in range(2):
                    gens.append(self.gen_S1(hp, d, R, dirs[d], C))
            while gens:
                for gn in list(gens):
                    try:
                        next(gn)
                    except StopIteration:
                        gens.remove(gn)

    def rwkv_half(self, hp, d, half, rb, kb, vb, kkb, y0, bacc, Tst, Tb, twd, adb, w2s, a2s, chp, hp2, msk, cmask, CW):
        P, pb, pbh = self.P, self.pb, self.pbh
        h0, W = (0, 1280) if half == 0 else (1280, 1024)
        if d == 0:
            pieces = [(0, 256, 0, 1), (256, 1024, 256, 1)] if half == 0 else [(1280, 1024, 1280, 1)]
        else:
            pieces = [(0, 256, 255, -1), (1280, 1024, 1279, -1)] if half == 0 else [(256, 1024, 2303, -1)]
        sg = lambda t, s0, n, sig0, step, off=0, nn=None: t[:, rsl(sig0 - h0 + step * off, nn if nn is not None else n, step)]
        A1 = P.sb([128, 1280], name='A1'); B1 = P.sb([128, 1280], name='B1'); C1 = P.sb([128, 1280], name='C1')
        rt = P.sb([128, 1280], BF16, name='rt'); at = P.sb([128, 1280], BF16, name='at'); bt = P.sb([128, 1280], BF16, name='bt')
        kt = P.sb([128, 1280], BF16, name='kt'); vs = P.sb([128, 1280], BF16, name='vs')
        wcs = P.sb([128, 20], name='wcs')
        hc = slice(hp * 128, (hp + 1) * 128)
        ds = slice(64 * d, 64 * d + 64)

        def lora(wts, src, bias_col, dst):
            nb = 0
            for (s0, n, sig0, step) in pieces:
                for o in range(0, n, 512):
                    nn = min(512, n - o)
                    pp = pb[nb % 2]; nb += 1
                    self.mm(pp[:, 0:nn], wts[ds, hc], src[ds, s0 + o:s0 + o + nn], r=[wts, src], w=[pp])
                    self.act(sg(dst, s0, n, sig0, step, o, nn), pp[:, 0:nn], AF.Tanh, scale=0.5, bias=hp2[:, hp, bias_col:bias_col + 1], r=[pp, hp2], w=[dst])
        lora(w2s, twd, d, A1)
        P.op('dve', lambda e: e.tensor_scalar(out=A1[:, 0:W], in0=A1[:, 0:W], scalar1=1.0, scalar2=CW, op0=ALU.add, op1=ALU.mult), r=[A1], w=[A1])
        P.op('dve', lambda e: e.tensor_tensor_scan(out=B1[:, 0:W], data0=cmask[:, 0:W], data1=A1[:, 0:W], initial=0.0, op0=ALU.mult, op1=ALU.add),
             r=[A1, cmask], w=[B1])
        P.op('dve', lambda e: e.tensor_tensor(out=A1[:, 0:W], in0=B1[:, 0:W], in1=A1[:, 0:W], op=ALU.subtract), r=[A1, B1], w=[A1])
        self.act(C1[:, 0:W], A1[:, 0:W], AF.Exp, r=[A1], w=[C1])
        for pc in pieces:
            s0, n = pc[0], pc[1]
            P.op('dve', lambda e, pc=pc, s0=s0, n=n: e.scalar_tensor_tensor(out=sg(at, *pc), in0=kkb[:, s0:s0 + n], scalar=-1.0, in1=sg(C1, *pc),
                                                                            op0=ALU.mult, op1=ALU.mult), r=[kkb, C1], w=[at])
        self.act(C1[:, 0:W], B1[:, 0:W], AF.Exp, r=[B1, at], w=[C1])
        P.op('dve', lambda e: e.tensor_copy(out=wcs[:, 0:W // 64], in_=C1[:, 63:W:64]), r=[C1], w=[wcs])
        for pc in pieces:
            s0, n = pc[0], pc[1]
            P.op('dve', lambda e, pc=pc, s0=s0, n=n: e.tensor_tensor(out=sg(rt, *pc), in0=rb[:, s0:s0 + n], in1=sg(C1, *pc), op=ALU.mult), r=[rb, C1], w=[rt])
        self.act(C1[:, 0:W], B1[:, 0:W], AF.Exp, scale=-1.0, r=[B1, rt, wcs], w=[C1])
        lora(a2s, adb, 2 + d, A1)
        P.op('dve', lambda e: e.tensor_scalar(out=B1[:, 0:W], in0=A1[:, 0:W], scalar1=0.5, scalar2=0.5, op0=ALU.mult, op1=ALU.add), r=[A1], w=[B1])
        for pc in pieces:
            s0, n = pc[0], pc[1]
            P.op('dve', lambda e, pc=pc, s0=s0, n=n: e.tensor_tensor(out=sg(B1, *pc), in0=sg(B1, *pc), in1=kkb[:, s0:s0 + n], op=ALU.mult), r=[B1, kkb], w=[B1])
        P.op('dve', lambda e: e.tensor_tensor(out=bt[:, 0:W], in0=B1[:, 0:W], in1=C1[:, 0:W], op=ALU.mult), r=[B1, C1], w=[bt])
        P.op('dve', lambda e: e.tensor_scalar(out=A1[:, 0:W], in0=A1[:, 0:W], scalar1=hp2[:, hp, 4:5], scalar2=hp2[:, hp, 5:6], op0=ALU.mult, op1=ALU.add),
             r=[A1, hp2], w=[A1])
        for pc in pieces:
            s0, n = pc[0], pc[1]
            P.op('dve', lambda e, pc=pc, s0=s0, n=n: e.tensor_tensor(out=sg(A1, *pc), in0=sg(A1, *pc), in1=kb[:, s0:s0 + n], op=ALU.mult), r=[A1, kb], w=[A1])
        P.op('dve', lambda e: e.tensor_tensor(out=kt[:, 0:W], in0=A1[:, 0:W], in1=C1[:, 0:W], op=ALU.mult), r=[A1, C1], w=[kt])
        nb = 0
        for pc in pieces:
            s0, n, sig0, step = pc
            if s0 < 256:
                continue
            P.op('dve', lambda e, pc=pc, s0=s0, n=n: e.scalar_tensor_tensor(out=B1[:, 0:n], in0=sg(A1, *pc), scalar=chp[:, hp, 6:7], in1=rb[:, s0:s0 + n],
                                                                            op0=ALU.mult, op1=ALU.mult), r=[A1, chp, rb, bt], w=[B1])
            for o in range(0, n, 512):
                pp = pb[nb % 2]; nb += 1
                self.mm(pp[:], self.bones[:], B1[:, o:o + 512], r=[B1, self.bones], w=[pp])
                bo = s0 - 256 + o
                if d == 0:
                    P.op('act', lambda e, pp=pp, bo=bo: e.copy(out=bacc[:, bo:bo + 512], in_=pp[:]), r=[pp], w=[bacc])
                else:
                    P.op('dve', lambda e, pp=pp, bo=bo: e.tensor_tensor(out=bacc[:, bo:bo + 512], in0=bacc[:, bo:bo + 512], in1=pp[:], op=ALU.add), r=[pp, bacc], w=[bacc])
        for pc in pieces:
            s0, n = pc[0], pc[1]
            P.op('pool', lambda e, pc=pc, s0=s0, n=n: e.tensor_copy(out=sg(vs, *pc), in_=vb[:, s0:s0 + n]), r=[vb], w=[vs])

        if hp == 0 and half == 0:
            for nm, t_ in (('rt', rt), ('at', at), ('bt', bt), ('kt', kt), ('vs', vs)):
                self.tap(nm + str(d), t_[:], [128, 1280], [t_], dt=BF16)
            self.tap('wcs' + str(d), wcs[:], [128, 20], [wcs])
            self.tap('kd' + str(d), A1[:], [128, 1280], [A1])
        tokT = P.sb([64, 4, 4, 128], BF16, name='tokT')
        Qs = [P.sb([64, 8, 64], BF16, name='Q') for _ in range(2)]; QTs = [P.sb([64, 8, 64], BF16, name='QT') for _ in range(2)]
        Xs = [P.sb([64, 8, 64], BF16, name='X') for _ in range(2)]
        AakT = P.sb([64, 8, 64], BF16, name='AakT'); ArbT = P.sb([64, 8, 64], BF16, name='ArbT'); ArkT = P.sb([64, 8, 64], BF16, name='ArkT')
        Xak = P.sb([64, 8, 64], BF16, name='Xak'); AhT = P.sb([128, 4, 64], BF16, name='AhT')
        Ub = P.sb([64, 2, 64], BF16, name='Ub'); Ts = P.sb([128, 64], name='Ts')
        pU, pT, pY, pA = pb[4], pb[5], pb[6], pb[0]
        pYv = pb[6][:, 0:256].rearrange("p (c t) -> p c t", t=64)
        pAv = pb[0][:, 0:256].rearrange("p (c t) -> p c t", t=64)
        pUv = pb[4][0:64, 0:128].rearrange("p (e v) -> p e v", v=64)
        pTv = pb[5][:, 0:64]
        bank = [0]

        def nextbank():
            bank[0] = (bank[0] + 1) % 2
            return pb[2 + bank[0]]
        v3 = lambda p: p[0:64, :].rearrange("p (i t) -> p i t", t=64)
        for gi in range(W // 256):
            loc = 256 * gi
            g0 = h0 + loc
            latent = g0 >= 256
            for qi, q in enumerate((at, bt, kt, vs)):
                for c in range(4):
                    P.op('pe', lambda e, qi=qi, q=q, c=c: e.transpose(out=pbh[0:64, (qi % 2) * 512 + c * 128:(qi % 2) * 512 + (c + 1) * 128],
                                                                      in_=q[:, loc + 64 * c:loc + 64 * c + 64], identity=self.identb[:]),
                         r=[q, self.identb], w=[pbh])
                if qi % 2 == 1:
                    P.op('act', lambda e, qi=qi: e.copy(out=tokT[:, qi - 1:qi + 1, :, :].rearrange("p q c j -> p (q c j)"), in_=pbh[0:64, :]), r=[pbh], w=[tokT])
            cs = lambda q, c, e_: q[64 * e_:64 * e_ + 64, loc + 64 * c:loc + 64 * c + 64]

            def score(L, Rr, mk, dst):
                pp = nextbank()
                for c in range(4):
                    for e_ in range(2):
                        self.mm(v3(pp)[:, 2 * c + e_, :], cs(L, c, e_), cs(Rr, c, e_), r=[L, Rr], w=[pp])
                P.op('dve', lambda e: e.tensor_tensor(out=dst[:], in0=v3(pp), in1=msk[mk][:], op=ALU.mult), r=[pp, msk[mk]], w=[dst])
            Q, QT, X = Qs[0], QTs[0], Xs[0]
            score(bt, at, 'su', Q)
            score(at, bt, 'sl', QT)
            score(kt, at, 'su', AakT)
            score(bt, rt, 'iu', ArbT)
            score(kt, rt, 'iu', ArkT)
            P.op('dve', lambda e, Q=Q, X=X: e.tensor_tensor(out=X[:], in0=Q[:], in1=msk['id'][:], op=ALU.add), r=[Q, msk['id']], w=[X])
            for lvl in range(2, 7):
                Qn, QTn, Xn = Qs[(lvl + 1) % 2], QTs[(lvl + 1) % 2], Xs[(lvl + 1) % 2]
                if lvl < 6:
                    pq = nextbank()
                    for i in range(8):
                        self.mm(v3(pq)[:, i, :], QT[:, i, :], Q[:, i, :], r=[Q, QT], w=[pq])
                    P.op('act', lambda e, pq=pq, Qn=Qn: e.copy(out=Qn[:], in_=v3(pq)), r=[pq], w=[Qn])
                pqt = nextbank()
                for i in range(8):
                    self.mm(v3(pqt)[:, i, :], Q[:, i, :], QT[:, i, :], r=[Q, QT], w=[pqt])
                P.op('act', lambda e, pqt=pqt, QTn=QTn: e.copy(out=QTn[:], in_=v3(pqt)), r=[pqt], w=[QTn])
                px = nextbank()
                for i in range(8):
                    self.mm(v3(px)[:, i, :], QTn[:, i, :], X[:, i, :], r=[QTn, X], w=[px])
                P.op('dve', lambda e, px=px, X=X, Xn=Xn: e.tensor_tensor(out=Xn[:], in0=v3(px), in1=X[:], op=ALU.add), r=[px, X], w=[Xn])
                Q, QT, X = Qn, QTn, Xn
            MT = X
            pxa = nextbank()
            for c in range(4):
                for e_ in range(2):
                    self.mm(v3(pxa)[:, 2 * c + e_, :], AakT[:, 2 * c + e_, :], tokT[:, 3, c, 64 * e_:64 * e_ + 64], r=[AakT, tokT], w=[pxa])
            P.op('act', lambda e, pxa=pxa: e.copy(out=Xak[:], in_=v3(pxa)), r=[pxa], w=[Xak])
            for c in range(4):
                for e_ in range(2):
                    self.mm(pAv[64 * e_:64 * e_ + 64, c, :], tokT[:, 0, c, 64 * e_:64 * e_ + 64], MT[:, 2 * c + e_, :], r=[tokT, MT], w=[pA],
                            tile_position=(0, 64 * e_))
            P.op('act', lambda e: e.copy(out=AhT[:], in_=pAv), r=[pA], w=[AhT])
            if hp == 0 and half == 0 and d == 0 and gi == 0:
                self.tap('MT', MT[:], [64, 8, 64], [MT], dt=BF16)
                self.tap('AhT', AhT[:], [128, 4, 64], [AhT], dt=BF16)
                self.tap('Xak', Xak[:], [64, 8, 64], [Xak], dt=BF16)
                self.tap('tokT', tokT[:], [64, 4, 4, 128], [tokT], dt=BF16)
                self.tap('ArbT', ArbT[:], [64, 8, 64], [ArbT], dt=BF16)
            for c in range(4):
                for e_ in range(2):
                    i = 2 * c + e_
                    es = slice(64 * e_, 64 * e_ + 64)
                    self.mm(pUv[:, e_, :], MT[:, i, :], Xak[:, i, :], start=True, stop=False, r=[MT, Xak], w=[pU])
                    self.mm(pUv[:, e_, :], AhT[es, c, :], Tb[es, :], start=False, stop=True, r=[AhT, Tb], w=[pU], tile_position=(64 * e_, 0))
                P.op('act', lambda e: e.copy(out=Ub[:], in_=pUv), r=[pU], w=[Ub])
                for e_ in range(2):
                    i = 2 * c + e_
                    es = slice(64 * e_, 64 * e_ + 64)
                    if latent:
                        self.mm(pYv[es, c, :], Tb[es, :], rt[es, loc + 64 * c:loc + 64 * c + 64], start=True, stop=False, r=[Tb, rt], w=[pY],
                                tile_position=(64 * e_, 64 * e_))
                        self.mm(pYv[es, c, :], Ub[:, e_, :], ArbT[:, i, :], start=False, stop=False, r=[Ub, ArbT], w=[pY], tile_position=(0, 64 * e_))
                        self.mm(pYv[es, c, :], tokT[:, 3, c, es], ArkT[:, i, :], start=False, stop=True, r=[tokT, ArkT], w=[pY], tile_position=(0, 64 * e_))
                    self.mm(pTv[es, :], tokT[:, 1, c, es], Ub[:, e_, :], start=True, stop=False, r=[tokT, Ub], w=[pT], tile_position=(0, 64 * e_))
                    self.mm(pTv[es, :], tokT[:, 2, c, es], tokT[:, 3, c, es], start=False, stop=True, r=[tokT], w=[pT], tile_position=(0, 64 * e_))
                P.op('dve', lambda e: e.tensor_tensor(out=Ts[:], in0=pTv, in1=Tst[:], op=ALU.add), r=[pT, Tst], w=[Ts])
                wc = wcs[:, 4 * gi + c:4 * gi + c + 1]
                P.op('dve', lambda e, wc=wc: e.tensor_scalar(out=Tst[:], in0=Ts[:], scalar1=wc, scalar2=None, op0=ALU.mult), r=[Ts, wcs], w=[Tst])
                self.act(Tb[:], Ts[:], AF.Identity, scale=wc, r=[Ts, wcs], w=[Tb])
            if latent:
                if d == 0:
                    P.op('act', lambda e, g0=g0: e.copy(out=y0[:, g0 - 256:g0], in_=pb[6][:, 0:256]), r=[pY], w=[y0])
                else:
                    ysl = rsl(2303 - g0, 256, -1)
                    P.op('dve', lambda e, ysl=ysl: e.tensor_tensor(out=y0[:, ysl], in0=y0[:, ysl], in1=pb[6][:, 0:256], op=ALU.add), r=[pY, y0], w=[y0])

    def wload(self, pool, src, npart, K, ncol, q='sp'):
        P = self.P
        i = pool['i'] = pool['i'] + 1
        wf, wb = pool['f'][i % len(pool['f'])], pool['b'][i % len(pool['b'])]
        P.dma(q, wf[0:npart, 0:K, 0:ncol], src, w=[wf], group=pool['name'] + str(i % len(pool['f'])))
        ceng = 'pool' if (self._castn % 2 == 0) else 'dve'
        self._castn += 1
        P.op(ceng, lambda e: e.tensor_copy(out=wb[0:npart, 0:K, 0:ncol], in_=wf[0:npart, 0:K, 0:ncol]), r=[wf], w=[wb])
        return wb

    def mkpool(self, name, npart, K, ncol, n=2):
        P = self.P
        return {'name': name, 'i': 0, 'f': [P.sb([npart, K, ncol], name=name + 'f') for _ in range(n)],
                'b': [P.sb([npart, K, ncol], BF16, name=name + 'b') for _ in range(n)]}

    def lru(self):
        P, pb, hT, din = self.P, self.pb, self.hT, self.din
        lruT = self.lruT
        cw_, cb_, ba_, bx_, lam_ = din['lru_conv_w'], din['lru_conv_b'], din['lru_ba'], din['lru_bx'], din['lru_lambda']
        rows = [cw_[d, j] for d in range(2) for j in range(4)] + [cb_[0], cb_[1], ba_[0], ba_[1], bx_[0], bx_[1], lam_[0], lam_[1]]
        lp = self.cols(rows, LW, 80, 'lp')
        hb = P.sb([80, 16, 4], name='hb')
        P.op('dve', lambda e: e.tensor_scalar(out=hb[:], in0=lp[:, :, 10:14], scalar1=0.5, scalar2=None, op0=ALU.mult), r=[lp], w=[hb])
        cs = P.sb([80, 16, 4], name='cs')
        one1 = P.sb([80, 1], name='one1')
        P.op('dve', lambda e: e.memset(one1[:], 1.0), w=[one1])
        self.act(cs[:, :, 0:2], lp[:, :, 14:16], AF.Exp, scale=-1.0, r=[lp], w=[cs])
        self.act(cs[:, :, 0:2], cs[:, :, 0:2], AF.Ln, bias=one1[:], r=[cs, one1], w=[cs])
        P.op('dve', lambda e: e.tensor_scalar(out=cs[:, :, 2:4], in0=cs[:, :, 0:2], scalar1=-4.0, scalar2=None, op0=ALU.mult), r=[cs], w=[cs])
        P.op('dve', lambda e: e.tensor_scalar(out=cs[:, :, 0:2], in0=cs[:, :, 0:2], scalar1=-8.0, scalar2=None, op0=ALU.mult), r=[cs], w=[cs])
        q25 = P.sb([80, 1], name='q25')
        P.op('dve', lambda e: e.memset(q25[:], 0.25), w=[q25])
        gwa = P.sb([80, 32, 80], BF16, name='gwa'); gwx = P.sb([80, 32, 80], BF16, name='gwx')
        with P.scope():
            st = P.sb([80, 32, 80], name='gst')
            for src, dst in ((din['lru_wa'], gwa), (din['lru_wx'], gwx)):
                P.dma('sp', st[:], src.rearrange("d n c e -> c (d n) e"), w=[st], group='gst')
                P.op('dve', lambda e, dst=dst: e.tensor_copy(out=dst[:], in_=st[:]), r=[st], w=[dst])
        U = P.sb([80, 2313], name='U'); guy = P.sb([80, NLAT], BF16, name='guy')
        xcs = [P.sb([80, 2313], name='xc') for _ in range(2)]
        xcb = P.sb([80, 2313], BF16, name='xcb')
        thr = P.sb([80, 2313], name='thr'); thi = P.sb([80, 2313], name='thi'); aa = P.sb([80, 2313], name='aa')
        lrub = P.sb([80, NLAT], BF16, name='lrub')
        wp = self.mkpool('lw', 128, 8, 80, n=2)
        P.op('dve', lambda e: e.memset(U[:], 0.0), w=[U])
        for xc in xcs:
            P.op('dve', lambda e, xc=xc: e.memset(xc[:], 0.0), w=[xc])
        P.op('dve', lambda e: e.memset(thr[:], 0.0), w=[thr])
        P.op('dve', lambda e: e.memset(thi[:], 0.0), w=[thi])
        wv = din['w_in'].rearrange("(k p) n -> p k n", p=128)
        hk = [(hT, j) for j in range(5)]
        tbs = [(0, 256, 3)] + [(256 + 512 * i, 512, 262 + 512 * i) for i in range(4)]
        nb = 0
        for n in range(NBLK):
            wx = self.wload(wp, wv[:, :, 80 * n:80 * n + 80], 128, 8, 80)
            for (hc, nn, uc) in tbs:
                pp = pb[nb % 6]; nb += 1
                for k in range(8):
                    self.mm(pp[0:80, 0:nn], wx[:, k, :], hT[:, k, hc:hc + nn], start=(k == 0), stop=(k == 7), r=[wx] + hk, w=[pp])
                P.op('act', lambda e: e.copy(out=U[:, uc:uc + nn], in_=pp[0:80, 0:nn]), r=[pp], w=[U])
            wy = self.wload(wp, wv[:, :, 1280 + 80 * n:1280 + 80 * n + 80], 128, 8, 80, q='act')
            for (hc, nn, uc) in tbs[1:]:
                pp = pb[nb % 6]; nb += 1
                for k in range(8):
                    self.mm(pp[0:80, 0:nn], wy[:, k, :], hT[:, k, hc:hc + nn], start=(k == 0), stop=(k == 7), r=[wy] + hk, w=[pp])
                self.act(guy[:, hc - 256:hc - 256 + nn], pp[0:80, 0:nn], AF.Gelu_apprx_tanh, r=[pp], w=[guy])
            for d in range(2):
                xc = xcs[d]
                sgn = -1 if d == 0 else 1
                cwj = lambda j: lp[:, n, 4 * d + j:4 * d + j + 1]

                def chunk(ci, uc, nn, blocks):
                    nonlocal nb
                    cs_ = slice(uc, uc + nn)
                    P.op('dve', lambda e: e.tensor_scalar(out=xc[:, cs_], in0=U[:, cs_], scalar1=cwj(3), scalar2=lp[:, n, 8 + d:9 + d],
                                                          op0=ALU.mult, op1=ALU.add), r=[U, lp], w=[(xc, ci)])
                    for j in range(3):
                        o = uc + sgn * (3 - j)
                        P.op('dve', lambda e: e.scalar_tensor_tensor(out=xc[:, cs_], in0=U[:, o:o + nn], scalar=cwj(j), in1=xc[:, cs_],
                                                                     op0=ALU.mult, op1=ALU.add), r=[U, lp, (xc, ci)], w=[(xc, ci)])
                    yield
                    P.op('act', lambda e: e.copy(out=xcb[:, cs_], in_=xc[:, cs_]), r=[(xc, ci)], w=[(xcb, ci)])
                    yield
                    for (hc, bn, bc) in blocks:
                        for gw, dst, bcol in ((gwa, thr, d), (gwx, thi, 2 + d)):
                            pp = pb[nb % 6]; nb += 1
                            self.mm(pp[0:80, 0:bn], gw[:, 16 * d + n, :], xcb[:, bc:bc + bn], r=[gw, (xcb, ci)], w=[pp])
                            self.act(dst[:, bc:bc + bn], pp[0:80, 0:bn], AF.Tanh, scale=0.5, bias=hb[:, n, bcol:bcol + 1], r=[pp, hb], w=[(dst, ci)])
                    yield
                    self.act(aa[:, cs_], thr[:, cs_], AF.Exp, scale=cs[:, n, 2 + d:3 + d], bias=cs[:, n, 2 + d:3 + d], r=[(thr, ci), cs], w=[(aa, ci)])
                    self.act(thr[:, cs_], thr[:, cs_], AF.Exp, scale=cs[:, n, d:d + 1], bias=cs[:, n, d:d + 1], r=[(thr, ci), cs], w=[(thr, ci)])
                    P.op('dve', lambda e: e.scalar_tensor_tensor(out=thi[:, cs_], in0=thi[:, cs_], scalar=1.0, in1=xc[:, cs_], op0=ALU.add, op1=ALU.mult),
                         r=[(thi, ci), (xc, ci)], w=[(thi, ci)])
                    yield
                    self.act(thr[:, cs_], thr[:, cs_], AF.Sqrt, scale=-0.25, bias=q25[:], r=[(thr, ci), q25], w=[(thr, ci)])
                    yield
                    P.op('dve', lambda e: e.tensor_tensor(out=thi[:, cs_], in0=thi[:, cs_], in1=thr[:, cs_], op=ALU.mult), r=[(thi, ci), (thr, ci)], w=[(thi, ci)])
                    yield
                gens = [chunk(0, 3, 1283, tbs[0:3]), chunk(1, 1286, 1024, tbs[3:5])]
                while gens:
                    for gn in list(gens):
                        try:
                            next(gn)
                        except StopIteration:
                            gens.remove(gn)
                allk = lambda t_: [(t_, ci) for ci in range(2)]
                if d == 0:
                    P.op('dve', lambda e: e.tensor_tensor_scan(out=xc[:, 3:259], data0=aa[:, 3:259], data1=thi[:, 3:259], initial=0.0, op0=ALU.mult, op1=ALU.add),
                         r=allk(aa) + allk(thi), w=allk(xc))
                    P.op('dve', lambda e: e.tensor_tensor_scan(out=xc[:, 262:2310], data0=aa[:, 262:2310], data1=thi[:, 262:2310], initial=xc[:, 258:259],
                                                               op0=ALU.mult, op1=ALU.add), r=allk(aa) + allk(thi) + allk(xc), w=allk(xc))
                else:
                    rv = lambda t_, a_, b_: t_[:, rsl(b_ - 1, b_ - a_, -1)]
                    P.op('dve', lambda e: e.tensor_tensor_scan(out=rv(xc, 3, 259), data0=rv(aa, 3, 259), data1=rv(thi, 3, 259), initial=0.0, op0=ALU.mult, op1=ALU.add),
                         r=allk(aa) + allk(thi), w=allk(xc))
                    P.op('dve', lambda e: e.tensor_tensor_scan(out=rv(xc, 262, 2310), data0=rv(aa, 262, 2310), data1=rv(thi, 262, 2310), initial=xc[:, 3:4],
                                                               op0=ALU.mult, op1=ALU.add), r=allk(aa) + allk(thi) + allk(xc), w=allk(xc))
            allk = lambda t_: [(t_, ci) for ci in range(2)]
            P.op('dve', lambda e: e.tensor_tensor(out=aa[:, 0:NLAT], in0=xcs[0][:, 262:2310], in1=xcs[1][:, 262:2310], op=ALU.add), r=allk(xcs[0]) + allk(xcs[1]) + allk(aa), w=allk(aa))
            P.op('dve', lambda e: e.tensor_tensor(out=lrub[:], in0=aa[:, 0:NLAT], in1=guy[:], op=ALU.mult), r=allk(aa) + [guy], w=[lrub])
            p0, c0 = (80 * n) % 128, (80 * n) // 128
            n1 = min(80, 128 - p0)
            P.dma('sp', lruT[p0:p0 + n1, c0, :], lrub[0:n1, :], r=[lrub], w=[lruT], group='lruT')
            if n1 < 80:
                P.dma('sp', lruT[0:80 - n1, c0 + 1, :], lrub[n1:80, :], r=[lrub], w=[lruT], group='lruT')

    def merge(self):
        P, pb, hT, din = self.P, self.pb, self.hT, self.din
        lruT, rwT, mT = self.lruT, self.rwT, self.mT
        wv = din['w_in'].rearrange("(k p) n -> p k n", p=128)
        wol = din['w_o_lru'].rearrange("(k p) n -> p k n", p=128)
        wor = din['w_o_rwkv'].rearrange("(k p) n -> p k n", p=128)
        pl = self.mkpool('wl', 128, 10, 128); pr = self.mkpool('wr', 128, 8, 128); pg = self.mkpool('wg', 128, 8, 128, n=3)
        thl = P.sb([128, 512], name='thl'); thr = P.sb([128, 512], name='thr2'); t1 = P.sb([128, 512], name='t1'); t2 = P.sb([128, 512], name='t2')
        hk = [(hT, j) for j in range(5)]
        for dc in range(8):
            cs_ = slice(dc * 128, dc * 128 + 128)
            wl = self.wload(pl, wol[:, :, cs_], 128, 10, 128)
            wr = self.wload(pr, wor[:, :, cs_], 128, 8, 128, q='act')
            wgl = self.wload(pg, wv[:, :, 6048 + dc * 128:6048 + dc * 128 + 128], 128, 8, 128)
            wgr = self.wload(pg, wv[:, :, 7072 + dc * 128:7072 + dc * 128 + 128], 128, 8, 128, q='act')
            for tb in range(4):
                ts_ = slice(512 * tb, 512 * tb + 512); hs_ = slice(256 + 512 * tb, 256 + 512 * tb + 512)
                it = dc * 4 + tb
                p1, p2, p3, p4 = (pb[(4 * it + j_) % 7] for j_ in range(4))
                for c in range(10):
                    self.mm(p1[:], wl[:, c, :], lruT[:, c, ts_], start=(c == 0), stop=(c == 9), r=[wl, lruT], w=[p1])
                for c in range(8):
                    self.mm(p2[:], wr[:, c, :], rwT[:, c, ts_], start=(c == 0), stop=(c == 7), r=[wr, rwT], w=[p2])
                for c in range(8):
                    self.mm(p3[:], wgl[:, c, :], hT[:, c, hs_], start=(c == 0), stop=(c == 7), r=[wgl] + hk, w=[p3])
                for c in range(8):
                    self.mm(p4[:], wgr[:, c, :], hT[:, c, hs_], start=(c == 0), stop=(c == 7), r=[wgr] + hk, w=[p4])
                self.act(thl[:], p3[:], AF.Tanh, scale=0.5, r=[p3], w=[thl])
                self.act(thr[:], p4[:], AF.Tanh, scale=0.5, r=[p4], w=[thr])
                P.op('dve', lambda e: e.scalar_tensor_tensor(out=t1[:], in0=thl[:], scalar=1.0, in1=p1[:], op0=ALU.add, op1=ALU.mult), r=[thl, p1], w=[t1])
                P.op('dve', lambda e: e.scalar_tensor_tensor(out=t2[:], in0=thr[:], scalar=1.0, in1=p2[:], op0=ALU.add, op1=ALU.mult), r=[thr, p2], w=[t2])
                P.op('dve', lambda e: e.tensor_tensor(out=mT[:, dc, ts_], in0=t1[:], in1=t2[:], op=ALU.add), r=[t1, t2], w=[mT])

    def resid1(self):
        P, pb, din, mod = self.P, self.pb, self.din, self.mod
        mT, x1T = self.mT, self.x1T
        hg = P.sb([128, 8], name='hg')
        P.op('dve', lambda e: e.tensor_scalar(out=hg[:], in0=mod[:, 16:24, 0], scalar1=0.5, scalar2=None, op0=ALU.mult), r=[mod], w=[hg])
        wo = P.sb([128, 8, D], BF16, name='wo')
        with P.scope():
            st = P.sb([128, 8, 256], name='wost')
            for j in range(4):
                P.dma('sp', st[:], din['w_out'].rearrange("(k p) n -> p k n", p=128)[:, :, 256 * j:256 * j + 256], w=[st], group='wost')
                P.op('pool', lambda e: e.tensor_copy(out=wo[:, :, 256 * j:256 * j + 256], in_=st[:]), r=[st], w=[wo])
        xv = din['xT'].rearrange("(k p) t -> p k t", p=128)
        for tb in range(4):
            ts_ = slice(512 * tb, 512 * tb + 512)
            P.dma('sp', x1T[:, :, ts_], xv[:, :, ts_], w=[x1T], group='x1ld')
            for dc in range(8):
                pp = pb[(tb * 8 + dc) % 6]
                for c in range(8):
                    self.mm(pp[:], wo[:, c, dc * 128:dc * 128 + 128], mT[:, c, ts_], start=(c == 0), stop=(c == 7), r=[wo, mT], w=[pp])
                P.op('dve', lambda e: e.scalar_tensor_tensor(out=x1T[:, dc, ts_], in0=pp[:], scalar=hg[:, dc:dc + 1], in1=x1T[:, dc, ts_], op0=ALU.mult, op1=ALU.add),
                     r=[pp, hg, x1T], w=[x1T])

    def ffn(self):
        P, pb, din, mod = self.P, self.pb, self.din, self.mod
        x1T, h2T = self.x1T, self.h2T
        wi = din['w_ffn_in'].rearrange("(k p) n -> p k n", p=128)
        wo_ = din['w_ffn_out'].rearrange("(f p) n -> p f n", p=128)
        actT = P.sb([128, 22, 1024], BF16, name='actT')
        pin = self.mkpool('fi', 128, 8, 128, n=3); pout = self.mkpool('fo', 128, 22, 128, n=2)
        sl = P.sb([128, 512], name='sl')
        hk = [(h2T, j) for j in range(4)]
        nb = 0
        for half in range(2):
            for f in range(22):
                wg = self.wload(pin, wi[:, :, f * 128:f * 128 + 128], 128, 8, 128)
                wu = self.wload(pin, wi[:, :, DFF + f * 128:DFF + f * 128 + 128], 128, 8, 128, q='act')
                for t2 in range(2):
                    tok = slice(1024 * half + 512 * t2, 1024 * half + 512 * t2 + 512)
                    pg_, pu_ = pb[nb % 4], pb[(nb + 1) % 4]; nb += 2
                    for k in range(8):
                        self.mm(pg_[:], wg[:, k, :], h2T[:, k, tok], start=(k == 0), stop=(k == 7), r=[wg] + hk, w=[pg_])
                    for k in range(8):
                        self.mm(pu_[:], wu[:, k, :], h2T[:, k, tok], start=(k == 0), stop=(k == 7), r=[wu] + hk, w=[pu_])
                    self.act(sl[:], pg_[:], AF.Silu, r=[pg_], w=[sl])
                    P.op('dve', lambda e: e.tensor_tensor(out=actT[:, f, 512 * t2:512 * t2 + 512], in0=sl[:], in1=pu_[:], op=ALU.mult), r=[sl, pu_], w=[(actT, f)])
            for dc in range(8):
                wo = self.wload(pout, wo_[:, :, dc * 128:dc * 128 + 128], 128, 22, 128)
                for t2 in range(2):
                    tok = slice(1024 * half + 512 * t2, 1024 * half + 512 * t2 + 512)
                    pp = pb[4 + (nb % 2)]; nb += 1
                    for f in range(22):
                        self.mm(pp[:], wo[:, f, :], actT[:, f, 512 * t2:512 * t2 + 512], start=(f == 0), stop=(f == 21), r=[wo, (actT, f)], w=[pp])
                    P.op('dve', lambda e: e.scalar_tensor_tensor(out=x1T[:, dc, tok], in0=pp[:], scalar=mod[:, 40 + dc, 0:1], in1=x1T[:, dc, tok], op0=ALU.mult, op1=ALU.add),
                         r=[pp, mod, x1T], w=[x1T])

    def final(self, outT):
        P, pb, x = self.P, self.pb, self.x1T
        gains = self.gains
        sq = P.sb([128, 8, 512], name='fsq'); rs = P.sb([128, 512], name='frs')
        epst = P.sb([128, 1], name='fepst')
        P.op('dve', lambda e: e.memset(epst[:], RMS_EPS), w=[epst])
        ov = outT.rearrange("(k p) t -> p k t", p=128)
        for tb in range(4):
            ts_ = slice(512 * tb, 512 * tb + 512)
            self.act(sq[:], x[:, :, ts_], AF.Square, r=[x], w=[sq])
            pp = pb[tb % 2]
            for k in range(8):
                self.mm(pp[:], self.ones[:], sq[:, k, :], start=(k == 0), stop=(k == 7), r=[sq, self.ones], w=[pp])
            self.act(rs[:], pp[:], AF.Sqrt, scale=1.0 / D, bias=epst[:], r=[pp, epst], w=[rs])
            P.op('dve', lambda e: e.reciprocal(out=rs[:], in_=rs[:]), r=[rs], w=[rs])
            for k in range(8):
                P.op('dve', lambda e: e.scalar_tensor_tensor(out=sq[:, k, :], in0=x[:, k, ts_], scalar=gains[:, k, 2:3], in1=rs[:], op0=ALU.mult, op1=ALU.mult),
                     r=[x, gains, rs], w=[sq])
            P.op('dve', lambda e: e.tensor_copy(out=epst[:], in_=epst[:]), r=[sq, epst], w=[sq, epst])
            P.dma('sp', ov[:, :, ts_], sq[:], r=[sq], group='out')

    def finish(self):
        P = self.P
        for gname in list(P.dsem):
            if gname.startswith('out'):
                P.wait_group('pool', gname)
        P.emit()
        return self.nc


_CACHE = {}


def _prep(inputs, b):
    f = lambda a: np.ascontiguousarray(a, dtype=np.float32)
    m = {
        'xT': f(inputs['x'][b].T), 'ctxT': f(inputs['ctx'][b].T),
        'cvec': f(np.stack([inputs['c'][b], inputs['c_ctx']])),
        'w_mod': f(inputs['w_mod'][0]), 'b_mod': f(inputs['b_mod'][0]),
        'norm_mix_g': f(inputs['norm_mix_g'][0]), 'norm_ffn_g': f(inputs['norm_ffn_g'][0]), 'norm_final_g': f(inputs['norm_final_g']),
        'w_in': f(inputs['w_in'][0]),
        'lru_conv_w': f(inputs['lru_conv_w'][0]), 'lru_conv_b': f(inputs['lru_conv_b'][0]),
        'lru_wa': f(inputs['lru_wa'][0]), 'lru_ba': f(inputs['lru_ba'][0]), 'lru_wx': f(inputs['lru_wx'][0]), 'lru_bx': f(inputs['lru_bx'][0]),
        'lru_lambda': f(inputs['lru_lambda'][0]), 'w_o_lru': f(inputs['w_o_lru'][0]),
        'rwkv_mu': f(inputs['rwkv_mu'][0]), 'rwkv_w0': f(inputs['rwkv_w0'][0]), 'rwkv_w2': f(inputs['rwkv_w2'][0]),
        'rwkv_a0': f(inputs['rwkv_a0'][0]), 'rwkv_a2': f(inputs['rwkv_a2'][0]), 'rwkv_g2': f(inputs['rwkv_g2'][0]),
        'rwkv_k_k': f(inputs['rwkv_k_k'][0]), 'rwkv_k_a': f(inputs['rwkv_k_a'][0]), 'rwkv_r_k': f(inputs['rwkv_r_k'][0].reshape(-1)),
        'rwkv_ln_g': f(inputs['rwkv_ln_g'][0]), 'rwkv_ln_b': f(inputs['rwkv_ln_b'][0]),
        'w_o_rwkv': f(inputs['w_o_rwkv'][0]), 'w_out': f(inputs['w_out'][0]),
        'w_ffn_in': f(inputs['w_ffn_in'][0]), 'w_ffn_out': f(inputs['w_ffn_out'][0]),
    }
    return m


def kernel(**inputs):
    if 'nc' not in _CACHE:
        _CACHE['nc'] = Builder().build()
    nc = _CACHE['nc']
    shared = _prep(inputs, 0)
    in_maps = []
    for b in range(8):
        m = dict(shared)
        m['xT'] = np.ascontiguousarray(np.asarray(inputs['x'][b], dtype=np.float32).T)
        m['ctxT'] = np.ascontiguousarray(np.asarray(inputs['ctx'][b], dtype=np.float32).T)
        m['cvec'] = np.ascontiguousarray(np.stack([inputs['c'][b], inputs['c_ctx']]).astype(np.float32))
        in_maps.append(m)
    res = run_bass_kernel_spmd(nc, in_maps, core_ids=list(range(8)))
    out = np.stack([np.ascontiguousarray(r['outT'].T) for r in res.results]).astype(np.float32)
    return out
```

```python
import contextlib
import numpy as np
import concourse.bass as bass
import concourse.mybir as mybir
from concourse.bass_utils import run_bass_kernel_spmd

F32 = mybir.dt.float32
BF16 = mybir.dt.bfloat16
AF = mybir.ActivationFunctionType
ALU = mybir.AluOpType

ENGS = ['pe', 'act', 'dve', 'pool', 'sp']
NCTX, NLAT, T = 256, 2048, 2304
D = 1024
LW, NBLK, BLK = 1280, 16, 80
RIN = 3488
DFF = 2816
RMS_EPS, GN_EPS = 1e-6, 64e-5


class _Rec:
    def __init__(self):
        self.call = None

    def __getattr__(self, name):
        def f(*a, **k):
            self.call = (name, a, k)
            return self
        return f


class Prog:
    def __init__(self, nc):
        self.nc = nc
        self.root = contextlib.ExitStack()
        self.stacks = [self.root]
        self.ops = {e: [] for e in ENGS}
        self.cnt = {e: 0 for e in ENGS}
        self.seen = {e: {} for e in ENGS}
        self.last_w = {}
        self.readers = {}
        self.esem = {e: self.root.enter_context(nc.semaphore('s_' + e)) for e in ENGS if e != 'sp'}
        self.dsem = {}
        self.fence = []
        self.ntile = 0

    def sb(self, shape, dt=F32, name=None):
        self.ntile += 1
        return self.stacks[-1].enter_context(self.nc.sbuf_tensor(f'{name or "t"}{self.ntile}', list(shape), dt))

    def sbm(self, shape, dt=F32, name=None):
        self.ntile += 1
        st = contextlib.ExitStack()
        t = st.enter_context(self.nc.sbuf_tensor(f'{name or "t"}{self.ntile}', list(shape), dt))
        return t, st

    def _set_fence(self):
        self.fence = [('E', e, self.cnt[e]) for e in self.esem if self.cnt[e] > 0]
        self.fence += [('D', s_, v) for s_, v in self.dsem.values() if v > 0]

    def free(self, stacks):
        for st in stacks:
            st.close()
        self._set_fence()

    def ps(self, shape, dt=F32, name=None):
        self.ntile += 1
        return self.root.enter_context(self.nc.psum_tensor(f'{name or "p"}{self.ntile}', list(shape), dt))

    @contextlib.contextmanager
    def scope(self):
        st = contextlib.ExitStack()
        self.stacks.append(st)
        try:
            yield
        finally:
            self.stacks.pop()
            st.close()
            self._set_fence()

    def _k(self, k):
        if isinstance(k, tuple):
            return tuple(self._k(x) for x in k)
        if isinstance(k, (str, int)):
            return k
        return id(k)

    @staticmethod
    def _tkey(tok):
        return ('E', tok[1]) if tok[0] == 'E' else ('D', id(tok[1]))

    def _deps(self, eng, r, w):
        deps = {}

        def add(tok):
            if tok is None:
                return
            if tok[0] == 'E' and tok[1] == eng == 'pe':
                return
            k = self._tkey(tok)
            if k not in deps or deps[k][2] < tok[2]:
                deps[k] = tok
        for tok in self.fence:
            add(tok)
        for k in r:
            add(self.last_w.get(k))
        for k in w:
            add(self.last_w.get(k))
            for t in self.readers.get(k, ()):
                add(t)
        out = []
        seen = self.seen[eng]
        for k, tok in deps.items():
            if seen.get(k, 0) >= tok[2]:
                continue
            seen[k] = tok[2]
            out.append(tok)
        return out

    def _commit(self, tok, r, w):
        for k in w:
            self.last_w[k] = tok
            self.readers[k] = []
        for k in r:
            if k in w:
                continue
            self.readers.setdefault(k, []).append(tok)

    def op(self, eng, fn, r=(), w=()):
        r = [self._k(k) for k in r]
        w = [self._k(k) for k in w]
        waits = self._deps(eng, r, w)
        self.cnt[eng] += 1
        tok = ('E', eng, self.cnt[eng])
        rec = _Rec()
        fn(rec)
        name, a, k = rec.call
        self.ops[eng].append((waits, lambda e: getattr(e, name)(*a, **k), tok))
        self._commit(tok, r, w)

    def dma(self, q, out, in_, r=(), w=(), group=None, **kw):
        r = [self._k(k) for k in r]
        w = [self._k(k) for k in w]
        waits = self._deps(q, r, w)
        g = group or ('dma_' + str(w[0] if w else 'x'))
        if g not in self.dsem:
            self.dsem[g] = [self.root.enter_context(self.nc.semaphore('d%d' % len(self.dsem))), 0]
        ent = self.dsem[g]
        ent[1] += 16
        tok = ('D', ent[0], ent[1])
        self.ops[q].append((waits, lambda e: e.dma_start(out=out, in_=in_, **kw), tok))
        self._commit(tok, r, w)

    def wait_group(self, eng, group):
        ent = self.dsem[group]
        self.ops[eng].append(([('D', ent[0], ent[1])], None, None))

    def emit(self):
        engobj = {'pe': 'tensor', 'act': 'scalar', 'dve': 'vector', 'pool': 'gpsimd', 'sp': 'sync'}
        waited = {e: set() for e in ENGS}
        for e in ENGS:
            for waits, fn, tok in self.ops[e]:
                for t in waits:
                    if t[0] == 'E':
                        waited[t[1]].add(t[2])
        rank = {e: {s_: i + 1 for i, s_ in enumerate(sorted(waited[e]))} for e in ENGS}
        with self.nc.Block() as block:
            for e in ENGS:
                ops = self.ops[e]

                def body(eng, ops=ops):
                    for waits, fn, tok in ops:
                        for t in waits:
                            if t[0] == 'E':
                                eng.wait_ge(self.esem[t[1]], rank[t[1]][t[2]])
                            else:
                                eng.wait_ge(t[1], t[2])
                        if fn is not None:
                            ins = fn(eng)
                            if tok[0] == 'D':
                                ins.then_inc(tok[1], 16)
                            elif tok[2] in rank[tok[1]]:
                                ins.then_inc(self.esem[tok[1]], 1)
                getattr(block, engobj[e])(body)


def rsl(start, n, step):
    if step > 0:
        return slice(start, start + n)
    stop = start - n
    return slice(start, stop if stop >= 0 else None, -1)


class Builder:
    def __init__(self, taps=(), stop_after=None):
        self.taps = set(taps)
        self.stop_after = stop_after
        nc = self.nc = bass.Bass("TRN2", target_bir_lowering=False)
        self.P = Prog(nc)
        self.din = {}
        self.tapout = {}
        self._castn = 0

    def inp(self, name, shape):
        self.din[name] = self.nc.dram_tensor(name, list(shape), F32, kind="ExternalInput").ap()
        return self.din[name]

    def mm(self, out, lhsT, rhs, start=True, stop=True, r=(), w=(), **kw):
        self.P.op('pe', lambda e: e.matmul(out, lhsT=lhsT, rhs=rhs, start=start, stop=stop, **kw), r=r, w=w)

    def act(self, out, in_, func, r=(), w=(), **kw):
        self.P.op('act', lambda e: e.activation(out=out, in_=in_, func=func, **kw), r=r, w=w)

    def tap(self, name, tile_ap, shape, r, dt=F32):
        if name not in self.taps:
            return
        o = self.nc.dram_tensor('tap_' + name, list(shape), dt, kind="ExternalOutput").ap()
        self.P.dma('pool', o, tile_ap, r=r, group='out_' + name)
        self.tapout[name] = o

    def cols(self, rows, n, chunk, name):
        P = self.P
        R = len(rows)
        nch = (n + chunk - 1) // chunk
        out = P.sb([chunk, nch, R], name=name)
        with P.scope():
            st = P.sb([R, n], name='colst')
            for i, rw in enumerate(rows):
                P.dma('sp', st[i:i + 1, :], rw.rearrange("(o n) -> o n", o=1), w=[(st, i)], group='colst')
            pp = self.pb[0]
            assert nch * R <= 512
            for c in range(nch):
                cs = min(chunk, n - c * chunk)
                P.op('pe', lambda e, c=c, cs=cs: e.transpose(out=pp[0:cs, c * R:(c + 1) * R], in_=st[0:R, c * chunk:c * chunk + cs],
                                                             identity=self.ident[0:R, 0:R]),
                     r=[(st, i) for i in range(R)] + [self.ident], w=[pp])
            P.op('dve', lambda e: e.tensor_copy(out=out[:].rearrange("p c r -> p (c r)"), in_=pp[0:chunk, 0:nch * R]), r=[pp], w=[out])
        return out

    def build(self):
        nc, P = self.nc, self.P
        inp = self.inp
        xT = inp('xT', [D, NLAT]); ctxT = inp('ctxT', [D, NCTX]); cvec = inp('cvec', [2, D])
        w_mod = inp('w_mod', [D, 6 * D]); b_mod = inp('b_mod', [6 * D])
        nmg = inp('norm_mix_g', [D]); nfg = inp('norm_ffn_g', [D]); nfin = inp('norm_final_g', [D])
        w_in = inp('w_in', [D, 8096])
        lru_conv_w = inp('lru_conv_w', [2, 4, LW]); lru_conv_b = inp('lru_conv_b', [2, LW])
        lru_wa = inp('lru_wa', [2, NBLK, BLK, BLK]); lru_ba = inp('lru_ba', [2, LW])
        lru_wx = inp('lru_wx', [2, NBLK, BLK, BLK]); lru_bx = inp('lru_bx', [2, LW])
        lru_lam = inp('lru_lambda', [2, LW]); w_o_lru = inp('w_o_lru', [LW, D])
        mu = inp('rwkv_mu', [2, RIN]); w0 = inp('rwkv_w0', [2, D]); w2 = inp('rwkv_w2', [2, 64, D])
        a0 = inp('rwkv_a0', [2, D]); a2 = inp('rwkv_a2', [2, 64, D]); g2 = inp('rwkv_g2', [160, D])
        k_k = inp('rwkv_k_k', [D]); k_a = inp('rwkv_k_a', [D]); r_k = inp('rwkv_r_k', [D])
        ln_g = inp('rwkv_ln_g', [D]); ln_b = inp('rwkv_ln_b', [D])
        w_o_rwkv = inp('w_o_rwkv', [D, D]); w_out = inp('w_out', [D, D])
        w_ffn_in = inp('w_ffn_in', [D, 2 * DFF]); w_ffn_out = inp('w_ffn_out', [DFF, D])
        outT = nc.dram_tensor('outT', [D, NLAT], F32, kind="ExternalOutput").ap()

        self.pb = [P.ps([128, 512], F32, name='pb') for _ in range(7)]
        self.pbh = P.ps([128, 1024], BF16, name='pbh')
        pb = self.pb

        ones = P.sb([128, 128], name='ones')
        P.op('dve', lambda e: e.memset(ones[:], 1.0), w=[ones])
        self.ident = ident = P.sb([128, 128], name='ident')
        P.op('pool', lambda e: e.affine_select(out=ident[:], in_=ones[:], pattern=[[-1, 128]], compare_op=ALU.is_equal, fill=0.0,
                                               base=0, channel_multiplier=1), r=[ones], w=[ident])
        identb = P.sb([128, 128], BF16, name='identb')
        P.op('dve', lambda e: e.tensor_copy(out=identb[:], in_=ident[:]), r=[ident], w=[identb])
        bones = P.sb([128, 128], name='bones')
        P.op('dve', lambda e: e.memset(bones[:], 0.0), w=[bones])
        P.op('dve', lambda e: e.memset(bones[0:64, 0:64], 1.0), w=[bones])
        P.op('dve', lambda e: e.memset(bones[64:128, 64:128], 1.0), w=[bones])
        self.ones, self.identb, self.bones = ones, identb, bones

        gains = self.cols([nmg, nfg, nfin], D, 128, 'gains')
        cT = self.cols([cvec[0], cvec[1]], D, 128, 'cT')
        bm = self.cols([b_mod], 6 * D, 128, 'bm')
        mod = P.sb([128, 48, 2], name='mod')
        with P.scope():
            sc = P.sb([128, 8, 2], name='sc')
            self.act(sc[:], cT[:], AF.Silu, r=[cT], w=[sc])
            wm = [P.sb([128, 8, 768], name='wm') for _ in range(2)]
            pm = pb[1]
            wv = w_mod.rearrange("(k p) n -> p k n", p=128)
            for jb in range(8):
                buf = wm[jb % 2]
                for k2 in range(2):
                    P.dma('sp' if k2 == 0 else 'act', buf[:, 4 * k2:4 * k2 + 4, :], wv[:, 4 * k2:4 * k2 + 4, jb * 768:(jb + 1) * 768],
                          w=[(buf, k2)], group='wm%d' % (jb % 2))
                for jj in range(6):
                    j = jb * 6 + jj
                    for k in range(8):
                        self.mm(pm[:, 2 * j:2 * j + 2], buf[:, k, jj * 128:(jj + 1) * 128], sc[:, k, :], start=(k == 0), stop=(k == 7),
                                r=[(buf, 0), (buf, 1), sc], w=[pm])
            for n in range(2):
                P.op('dve', lambda e, n=n: e.tensor_tensor(out=mod[:, :, n], in0=pm[:, 0:96].rearrange("p (j n) -> p j n", n=2)[:, :, n],
                                                           in1=bm[:, :, 0], op=ALU.add), r=[pm, bm], w=[mod])
        self.tap('mod', mod[:], [128, 48, 2], [mod])
        G1 = P.sb([128, 8, 2], name='G1'); G2 = P.sb([128, 8, 1], name='G2')
        for n in range(2):
            P.op('dve', lambda e, n=n: e.scalar_tensor_tensor(out=G1[:, :, n], in0=mod[:, 8:16, n], scalar=1.0, in1=gains[:, :, 0],
                                                              op0=ALU.add, op1=ALU.mult), r=[mod, gains], w=[G1])
        P.op('dve', lambda e: e.scalar_tensor_tensor(out=G2[:, :, 0], in0=mod[:, 32:40, 0], scalar=1.0, in1=gains[:, :, 1],
                                                     op0=ALU.add, op1=ALU.mult), r=[mod, gains], w=[G2])
        self.mod, self.gains = mod, gains

        arena = P.sb([128, 8 * T + 8 * NLAT], BF16, name='arena')
        hT = arena[:, 0:8 * T].rearrange("p (k t) -> p k t", t=T)
        xv = xT.rearrange("(k p) t -> p k t", p=128)
        cv = ctxT.rearrange("(k p) t -> p k t", p=128)
        self.modulate(hT, [(cv, 0, 256, 0, 1)] + [(xv, 512 * i, 512, 256 + 512 * i, 0) for i in range(4)], G1, mod, 0)
        self.tap('hT', hT[:], [128, 8, T], [(hT, i) for i in range(5)], dt=BF16)
        self.hT = hT
        if self.stop_after == 'B':
            return self.finish()
        self.rwT = arena[:, 8 * T:8 * T + 8 * NLAT].rearrange("p (k t) -> p k t", t=NLAT)
        if self.stop_after != 'C':
            with P.scope():
                self.rwkv()
        self.tap('rw', self.rwT[:], [128, 8, NLAT], [self.rwT], dt=BF16)
        if self.stop_after in ('D', 'D0'):
            return self.finish()
        self.lruT, lru_st = P.sbm([128, 10, NLAT], BF16, name='lruT')
        with P.scope():
            self.lru()
        self.tap('lru', self.lruT[:], [128, 10, NLAT], [self.lruT], dt=BF16)
        if self.stop_after == 'C':
            return self.finish()
        self.mT, m_st = P.sbm([128, 8, NLAT], BF16, name='mT')
        with P.scope():
            self.merge()
        P.free([])
        self.x1T = arena[:].bitcast(F32)[:, 0:8 * NLAT].rearrange("p (k t) -> p k t", t=NLAT)
        with P.scope():
            self.resid1()
        P.free([m_st, lru_st])
        self.tap('x1', self.x1T[:], [128, 8, NLAT], [self.x1T])
        if self.stop_after == 'E':
            return self.finish()
        self.h2T = P.sb([128, 8, NLAT], BF16, name='h2T')
        self.modulate(self.h2T, [(None, 512 * i, 512, 512 * i, 0) for i in range(4)], G2, mod, 24, src_sb=self.x1T)
        with P.scope():
            self.ffn()
        with P.scope():
            self.final(outT)
        return self.finish()

    def modulate(self, hT, blocks, G, mod, shift_j0, src_sb=None):
        P, pb = self.P, self.pb
        with P.scope():
            xb = [P.sb([128, 8, 512], name='xb') for _ in range(2)]
            sq = P.sb([128, 8, 512], name='sq')
            rs = P.sb([128, 512], name='rs')
            epst = P.sb([128, 1], name='epst')
            P.op('dve', lambda e: e.memset(epst[:], RMS_EPS), w=[epst])
            for bi, (src, so, n, do, mn) in enumerate(blocks):
                if src_sb is None:
                    x = xb[bi % 2]
                    for k2 in range(2):
                        P.dma('sp' if k2 == 0 else 'act', x[:, 4 * k2:4 * k2 + 4, 0:n], src[:, 4 * k2:4 * k2 + 4, so:so + n],
                              w=[(x, k2)], group='xb%d' % (bi % 2))
                    xr = [(x, 0), (x, 1)]
                    xa = lambda k, x=x, n=n: x[:, k, 0:n]
                    xall = x[:, :, 0:n]
                else:
                    xr = [src_sb]
                    xa = lambda k, so=so, n=n: src_sb[:, k, so:so + n]
                    xall = src_sb[:, :, so:so + n]
                self.act(sq[:, :, 0:n], xall, AF.Square, r=xr, w=[sq])
                pp = pb[bi % 2]
                for k in range(8):
                    self.mm(pp[:, 0:n], self.ones[:], sq[:, k, 0:n], start=(k == 0), stop=(k == 7), r=[sq, self.ones], w=[pp])
                self.act(rs[:, 0:n], pp[:, 0:n], AF.Ln, scale=1.0 / D, bias=epst[:], r=[pp, epst], w=[rs])
                self.act(rs[:, 0:n], rs[:, 0:n], AF.Exp, scale=-0.5, r=[rs], w=[rs])
                for k in range(8):
                    P.op('dve', lambda e, k=k, n=n, xa=xa: e.tensor_tensor(out=sq[:, k, 0:n], in0=xa(k), in1=rs[:, 0:n], op=ALU.mult),
                         r=xr + [rs], w=[sq])
                    self.act(hT[:, k, do:do + n], sq[:, k, 0:n], AF.Identity, scale=G[:, k, mn:mn + 1], bias=mod[:, shift_j0 + k, mn:mn + 1],
                             r=[sq, G, mod], w=[(hT, bi)])

    def zshift(self, cq, ncol, dsts, zbuf, A, wz, wzb, mixw, segs=((0, 1, 256, 0), (1, 258, 2048, 256))):
        P, pb, hT = self.P, self.pb, self.hT
        w_in = self.din['w_in']
        i = self.zcount = getattr(self, 'zcount', 0) + 1
        wf, wb = wz[i % 2], wzb[i % 2]
        if isinstance(zbuf, list):
            zbuf = zbuf[i % len(zbuf)]
        if isinstance(A, list):
            A = A[i % len(A)]
        pre = getattr(self, 'zpre', None)
        if pre is not None and pre[0] == cq:
            wb = pre[1]
            self.zpre = None
        else:
            self.zload(cq, ncol, wf, wb, 'wz%d' % (i % 2))
        hk = [(hT, j) for j in range(5)]
        lat = lambda k: hT[:, k, 256:2304].rearrange("p (r c) -> p c r", c=64)
        nblk = 0
        for (seg, zc, n, _) in segs:
            nb = 1 if seg == 0 else 4
            for bi in range(nb):
                pp = pb[(nblk + 5 * i) % 6]; nblk += 1
                bn = 256 if seg == 0 else 512
                for k in range(8):
                    if seg == 0:
                        self.mm(pp[0:ncol, 0:256], wb[:, k, 0:ncol], hT[:, k, 0:256], start=(k == 0), stop=(k == 7), r=[wb] + hk, w=[pp])
                    else:
                        self.mm(pp[0:ncol, 0:512], wb[:, k, 0:ncol], hT[:, k, 256 + 512 * bi:256 + 512 * bi + 512],
                                start=(k == 0), stop=(k == 7), r=[wb] + hk, w=[pp])
                if seg == 0:
                    P.op('act', lambda e: e.copy(out=zbuf[0:ncol, zc:zc + 256], in_=pp[0:ncol, 0:256]), r=[pp], w=[zbuf])
                else:
                    zo = zbuf[0:ncol, zc:zc + 2048].rearrange("p (c r) -> p r c", r=32)[:, 8 * bi:8 * bi + 8, :]
                    P.op('act', lambda e: e.copy(out=zo, in_=pp[0:ncol, 0:512].rearrange("p (r c) -> p r c", c=64)), r=[pp], w=[zbuf])
        for (seg, zc, n, _), dst in zip(segs, dsts):
            if dst is None:
                continue
            dt, do, key = dst
            P.op('dve', lambda e, zc=zc, n=n: e.tensor_scalar(out=A[0:ncol, 0:n], in0=zbuf[0:ncol, zc:zc + n], scalar1=mixw[0:ncol, cq, 2:3],
                                                              scalar2=None, op0=ALU.mult), r=[zbuf, mixw], w=[A])
            P.op('dve', lambda e, zc=zc, n=n: e.scalar_tensor_tensor(out=A[0:ncol, 0:n], in0=zbuf[0:ncol, zc - 1:zc - 1 + n], scalar=mixw[0:ncol, cq, 0:1],
                                                                     in1=A[0:ncol, 0:n], op0=ALU.mult, op1=ALU.add), r=[zbuf, mixw, A], w=[A])
            P.op('dve', lambda e, zc=zc, n=n, dt=dt, do=do: e.scalar_tensor_tensor(out=dt[0:ncol, do:do + n], in0=zbuf[0:ncol, zc + 1:zc + 1 + n],
                                                                                   scalar=mixw[0:ncol, cq, 1:2], in1=A[0:ncol, 0:n],
                                                                                   op0=ALU.mult, op1=ALU.add), r=[zbuf, mixw, A], w=[key])

    def zload(self, cq, ncol, wf, wb, group):
        P = self.P
        c0 = 2560 + 128 * cq
        P.dma('sp', wf[:, :, 0:ncol], self.din['w_in'].rearrange("(k p) n -> p k n", p=128)[:, :, c0:c0 + ncol], w=[wf], group=group)
        P.op('pool', lambda e: e.tensor_copy(out=wb[:, :, 0:ncol], in_=wf[:, :, 0:ncol]), r=[wf], w=[wb])

    def rwkv(self):
        P, pb, pbh, hT, din = self.P, self.pb, self.pbh, self.hT, self.din
        rwT = self.rwT
        CW = -0.5 * float(np.exp(-0.5))
        mixw = self.cols([din['rwkv_mu'][0], din['rwkv_mu'][1], din['rwkv_mu'][0]], RIN, 128, 'mixw')
        P.op('dve', lambda e: e.tensor_tensor(out=mixw[:, :, 2], in0=mixw[:, :, 0], in1=mixw[:, :, 1], op=ALU.add), r=[mixw], w=[mixw])
        P.op('dve', lambda e: e.tensor_scalar(out=mixw[:, :, 2], in0=mixw[:, :, 2], scalar1=-1.0, scalar2=1.0, op0=ALU.mult, op1=ALU.add), r=[mixw], w=[mixw])
        chp = self.cols([din['rwkv_w0'][0], din['rwkv_w0'][1], din['rwkv_a0'][0], din['rwkv_a0'][1], din['rwkv_k_k'], din['rwkv_k_a'],
                         din['rwkv_r_k'], din['rwkv_ln_g'], din['rwkv_ln_b']], D, 128, 'chp')
        hp2 = P.sb([128, 8, 6], name='hp2')
        P.op('dve', lambda e: e.tensor_scalar(out=hp2[:, :, 0:4], in0=chp[:, :, 0:4], scalar1=0.5, scalar2=None, op0=ALU.mult), r=[chp], w=[hp2])
        P.op('dve', lambda e: e.tensor_scalar(out=hp2[:, :, 4:5], in0=chp[:, :, 5:6], scalar1=0.5, scalar2=None, op0=ALU.mult), r=[chp], w=[hp2])
        P.op('dve', lambda e: e.tensor_scalar(out=hp2[:, :, 5:6], in0=chp[:, :, 5:6], scalar1=-0.5, scalar2=1.0, op0=ALU.mult, op1=ALU.add), r=[chp], w=[hp2])
        gneps = P.sb([128, 1], name='gneps')
        P.op('dve', lambda e: e.memset(gneps[:], GN_EPS), w=[gneps])
        w2s = P.sb([128, D], BF16, name='w2s'); a2s = P.sb([128, D], BF16, name='a2s')
        g2a = P.sb([128, D], BF16, name='g2a'); g2b = P.sb([32, D], BF16, name='g2b')
        with P.scope():
            st = P.sb([128, D], name='lst')
            for src, dst, npart in ((din['rwkv_w2'].rearrange("d r c -> (d r) c"), w2s, 128), (din['rwkv_a2'].rearrange("d r c -> (d r) c"), a2s, 128),
                                    (din['rwkv_g2'][0:128, :], g2a, 128), (din['rwkv_g2'][128:160, :], g2b, 32)):
                P.dma('sp', st[0:npart, :], src, w=[st], group='lst')
                P.op('dve', lambda e, dst=dst, npart=npart: e.tensor_copy(out=dst[0:npart, :], in_=st[0:npart, :]), r=[st], w=[dst])
        msk = {}
        onesb = P.sb([128, 4, 64], BF16, name='onesb')
        P.op('dve', lambda e: e.memset(onesb[:], 1.0), w=[onesb])
        for nm, op, sgn in (('su', ALU.is_gt, -1), ('sl', ALU.is_gt, 1), ('iu', ALU.is_ge, -1), ('id', ALU.is_equal, 1)):
            m = P.sb([128, 4, 64], BF16, name='m' + nm)
            for e_ in range(2):
                P.op('pool', lambda e, m=m, op=op, sgn=sgn, e_=e_: e.affine_select(out=m[64 * e_:64 * e_ + 64], in_=onesb[64 * e_:64 * e_ + 64],
                                                                                   pattern=[[0, 4], [-sgn, 64]], compare_op=op, fill=0.0,
                                                                                   base=0, channel_multiplier=sgn), r=[onesb], w=[m])
            msk[nm] = m
        cmask = P.sb([128, 256], name='cmask')
        P.op('dve', lambda e: e.memset(cmask[:], 1.0), w=[cmask])
        P.op('dve', lambda e: e.memset(cmask[:, 0:256:64], 0.0), w=[cmask])
        twd = P.sb([128, T], BF16, name='twd'); adb = P.sb([128, T], BF16, name='adb')
        sgd1 = P.sb([128, NLAT], BF16, name='sgd1'); sgd2 = P.sb([32, NLAT], BF16, name='sgd2')

        with P.scope():
            zbuf = [P.sb([128, 2307], name='zbuf') for _ in range(2)]; A = [P.sb([128, 2048], name='zA') for _ in range(2)]
            wz = [P.sb([128, 8, 128], name='wz') for _ in range(2)]; wzb = [P.sb([128, 8, 128], BF16, name='wzb') for _ in range(2)]
            for zb in zbuf:
                P.op('pool', lambda e, zb=zb: e.memset(zb[:], 0.0), w=[zb])
            tmp = P.sb([128, T], name='ltmp')
            self.zshift(24, 128, [(tmp, 0, tmp), (tmp, 256, tmp)], zbuf, A, wz, wzb, mixw)
            self.act(twd[:], tmp[:], AF.Tanh, r=[tmp], w=[twd])
            self.zshift(25, 128, [(adb, 0, adb), (adb, 256, adb)], zbuf, A, wz, wzb, mixw)
            for cq, ncol, dst in ((26, 128, sgd1), (27, 32, sgd2)):
                self.zshift(cq, ncol, [None, (tmp, 256, tmp)], zbuf, A, wz, wzb, mixw)
                self.act(tmp[0:ncol, 256:T], tmp[0:ncol, 256:T], AF.Tanh, scale=0.5, r=[tmp], w=[tmp])
                P.op('dve', lambda e, dst=dst, ncol=ncol: e.tensor_scalar(out=dst[0:ncol, :], in0=tmp[0:ncol, 256:T], scalar1=0.5, scalar2=0.5,
                                                                          op0=ALU.mult, op1=ALU.add), r=[tmp], w=[dst])

        pre_f = P.sb([128, 8, 128], name='pre_f'); pre_b = P.sb([128, 8, 128], BF16, name='pre_b')
        self.zload(8, 128, pre_f, pre_b, 'wzpre')
        self.zpre = (8, pre_b)
        self.tap('twd', twd[:], [128, T], [twd], dt=BF16)
        self.tap('adb', adb[:], [128, T], [adb], dt=BF16)
        self.tap('sgd1', sgd1[:], [128, NLAT], [sgd1], dt=BF16)
        for hp in range(8):
            if self.stop_after == 'D0' and hp > 0:
                break
            with P.scope():
                rb = P.sb([128, T], BF16, name='rb'); kb = P.sb([128, T], BF16, name='kb'); vb = P.sb([128, T], BF16, name='vb')
                kkb = P.sb([128, T], BF16, name='kkb')
                y0 = P.sb([128, NLAT], name='y0'); bacc = P.sb([128, NLAT], name='bacc')
                with P.scope():
                    zbufs = [P.sb([128, 2307], name='zbuf') for _ in range(3)]; As = [P.sb([128, 2048], name='zA') for _ in range(2)]
                    wz = [P.sb([128, 8, 128], name='wz') for _ in range(2)]; wzb = [P.sb([128, 8, 128], BF16, name='wzb') for _ in range(2)]
                    for zb in zbufs:
                        for pc in (0, 257, 2306):
                            P.op('pool', lambda e, zb=zb, pc=pc: e.memset(zb[:, pc:pc + 1], 0.0), w=[zb])
                    for cq, dst in ((8 + hp, kb), (hp, rb), (16 + hp, vb)):
                        self.zshift(cq, 128, [(dst, 0, dst), (dst, 256, dst)], zbufs, As, wz, wzb, mixw)
                    if hp < 7:
                        self.zload(8 + hp + 1, 128, pre_f, pre_b, 'wzpre')
                        self.zpre = (8 + hp + 1, pre_b)
                    kq = zbufs[0]; A = As[0]
                    self.act(kq[:, 0:T], kb[:], AF.Identity, scale=chp[:, hp, 4:5], r=[kb, chp], w=[kq])
                    ssb = zbufs[1]
                    for bi, (o, n) in enumerate([(0, 512), (512, 512), (1024, 512), (1536, 512), (2048, 256)]):
                        ao = 512 * (bi % 4)
                        P.op('dve', lambda e: e.tensor_tensor(out=A[:, ao:ao + n], in0=kq[:, o:o + n], in1=kq[:, o:o + n], op=ALU.mult), r=[kq], w=[(A, bi % 4)])
                        pp = pb[bi % 5]
                        self.mm(pp[:, 0:n], self.bones[:], A[:, ao:ao + n], r=[(A, bi % 4), self.bones], w=[pp])
                        P.op('dve', lambda e: e.tensor_scalar(out=ssb[:, o:o + n], in0=pp[:, 0:n], scalar1=1e-24, scalar2=None, op0=ALU.max), r=[pp], w=[ssb])
                    self.act(ssb[:, 0:T], ssb[:, 0:T], AF.Ln, r=[ssb], w=[ssb])
                    self.act(ssb[:, 0:T], ssb[:, 0:T], AF.Exp, scale=-0.5, r=[ssb], w=[ssb])
                    P.op('dve', lambda e: e.tensor_tensor(out=kkb[:], in0=kq[:, 0:T], in1=ssb[:, 0:T], op=ALU.mult), r=[ssb, kq], w=[kkb])
                if hp == 0:
                    for nm, t_ in (('rb', rb), ('kb', kb), ('vb', vb), ('kkb', kkb)):
                        self.tap(nm, t_[:], [128, T], [t_], dt=BF16)
                for j in range(8):
                    P.op('pool', lambda e: e.memset(y0[:, 256 * j:256 * j + 256], 0.0), w=[(y0, 256 * j)])
                    P.op('pool', lambda e: e.memset(bacc[:, 256 * j:256 * j + 256], 0.0), w=[(bacc, 256 * j)])
                C = dict(rb=rb, kb=kb, vb=vb, kkb=kkb, y0=y0, bacc=bacc, twd=twd, adb=adb, w2s=w2s, a2s=a2s, chp=chp, hp2=hp2, msk=msk,
                         cmask=cmask, CW=CW, rot=[0])
                with P.scope():
                    self.rw_rounds(hp, C)
                if hp == 0:
                    self.tap('y0', y0[:], [128, NLAT], [y0])
                    self.tap('bacc', bacc[:], [128, NLAT], [bacc])
                with P.scope():
                    yc = P.sb([128, 512], name='yc'); sq = P.sb([128, 512], name='sq2'); rstd = P.sb([128, 512], name='rstd'); tt = P.sb([128, 512], name='tt')
                    for i in range(4):
                        c0 = 512 * i
                        pm, pv, pg = pb[0], pb[1], pb[2]
                        self.mm(pm[:], self.bones[:], y0[:, c0:c0 + 512], r=[y0, self.bones], w=[pm])
                        P.op('dve', lambda e, c0=c0: e.scalar_tensor_tensor(out=yc[:], in0=pm[:], scalar=-1.0 / 64, in1=y0[:, c0:c0 + 512], op0=ALU.mult, op1=ALU.add),
                             r=[pm, y0], w=[yc])
                        P.op('dve', lambda e: e.tensor_tensor(out=sq[:], in0=yc[:], in1=yc[:], op=ALU.mult), r=[yc], w=[sq])
                        self.mm(pv[:], self.bones[:], sq[:], r=[sq, self.bones], w=[pv])
                        self.act(rstd[:], pv[:], AF.Ln, scale=1.0 / 64, bias=gneps[:], r=[pv, gneps], w=[rstd])
                        self.act(rstd[:], rstd[:], AF.Exp, scale=-0.5, r=[rstd], w=[rstd])
                        P.op('dve', lambda e: e.tensor_tensor(out=yc[:], in0=yc[:], in1=rstd[:], op=ALU.mult), r=[yc, rstd], w=[yc])
                        self.act(yc[:], yc[:], AF.Identity, scale=chp[:, hp, 7:8], bias=chp[:, hp, 8:9], r=[yc, chp], w=[yc])
                        P.op('dve', lambda e, c0=c0: e.tensor_tensor(out=tt[:], in0=vb[:, 256 + c0:256 + c0 + 512], in1=bacc[:, c0:c0 + 512], op=ALU.mult), r=[vb, bacc], w=[tt])
                        P.op('dve', lambda e: e.tensor_tensor(out=tt[:], in0=tt[:], in1=yc[:], op=ALU.add), r=[tt, yc], w=[tt])
                        self.mm(pg[:], g2a[:, hp * 128:(hp + 1) * 128], sgd1[:, c0:c0 + 512], start=True, stop=False, r=[g2a, sgd1], w=[pg])
                        self.mm(pg[:], g2b[0:32, hp * 128:(hp + 1) * 128], sgd2[0:32, c0:c0 + 512], start=False, stop=True, r=[g2b, sgd2], w=[pg])
                        P.op('dve', lambda e, i=i: e.tensor_tensor(out=rwT[:, hp, :].rearrange("p (r c) -> p c r", c=64)[:, 16 * i:16 * i + 16, :],
                                                                   in0=tt[:].rearrange("p (c r) -> p c r", r=32), in1=pg[:].rearrange("p (c r) -> p c r", r=32),
                                                                   op=ALU.mult), r=[tt, pg], w=[rwT])

    def rw_alloc_dir(self):
        P = self.P
        S = {}
        S['A1'] = P.sb([128, 256], name='A1'); S['B1'] = P.sb([128, 256], name='B1'); S['C1'] = P.sb([128, 256], name='C1')
        S['s1'] = [{nm: P.sb([128, 256], BF16, name=nm) for nm in ('at', 'bt', 'kt', 'vs')} for _ in range(2)]
        S['rw'] = [{'rt': P.sb([128, 256], BF16, name='rt'), 'wc': P.sb([128, 4], name='wc')} for _ in range(4)]
        S['QX'] = [[P.sb([128, 4, 2, 64], BF16, name='QX') for _ in range(2)] for _ in range(2)]
        S['QT'] = [[P.sb([128, 4, 64], BF16, name='QT') for _ in range(2)] for _ in range(2)]
        S['AakT'] = [P.sb([128, 4, 64], BF16, name='AakT') for _ in range(2)]
        S['slots'] = []
        for _ in range(3):
            sl = {'tokT': P.sb([128, 4, 4, 64], BF16, name='tokT'), 'MT': P.sb([128, 4, 64], BF16, name='MT'), 'Xak': P.sb([128, 4, 64], BF16, name='Xak'),
                  'ArbT': P.sb([128, 4, 64], BF16, name='ArbT'), 'ArkT': P.sb([128, 4, 64], BF16, name='ArkT'), 'AhT': P.sb([128, 4, 64], BF16, name='AhT')}
            S['slots'].append(sl)
        S['Tst'] = P.sb([128, 64], name='Tst'); S['Tw'] = P.sb([128, 64], name='Tw'); S['Tb'] = P.sb([128, 64], BF16, name='Tb')
        S['Ub'] = P.sb([128, 64], BF16, name='Ub')
        for nm in ('Tst', 'Tw', 'Tb'):
            P.op('dve', lambda e, t=S[nm]: e.memset(t[:], 0.0), w=[S[nm]])
        return S

    def gen_S1(self, hp, d, g, S, C):
        P, pb, pbh = self.P, self.pb, self.pbh
        rb, kb, vb, kkb, bacc = C['rb'], C['kb'], C['vb'], C['kkb'], C['bacc']
        twd, adb, w2s, a2s, chp, hp2, msk, cmask, CW = (C[k] for k in ('twd', 'adb', 'w2s', 'a2s', 'chp', 'hp2', 'msk', 'cmask', 'CW'))
        A1, B1, C1 = (S[k] for k in ('A1', 'B1', 'C1'))
        at, bt, kt, vs = (S['s1'][g % 2][k] for k in ('at', 'bt', 'kt', 'vs'))
        rt, wc = S['rw'][g % 4]['rt'], S['rw'][g % 4]['wc']
        if d == 0:
            s0, step = 256 * g, 1
        else:
            s0, step = (0 if g == 0 else 2304 - 256 * g), -1
        nat = slice(s0, s0 + 256)
        loc = lambda t: t[:, rsl(0 if step > 0 else 255, 256, step)]
        hc = slice(hp * 128, (hp + 1) * 128); ds = slice(64 * d, 64 * d + 64)
        rot = C['rot']
        H = lambda e_: slice(64 * e_, 64 * e_ + 64)
        TP = lambda e_: (64 * e_, 64 * e_)

        def bank():
            rot[0] = (rot[0] + 1) % 4
            return pb[(0, 1, 2, 5)[rot[0]]]
        pp = bank()
        self.mm(pp[:, 0:256], w2s[ds, hc], twd[ds, nat], r=[w2s, twd], w=[pp])
        self.act(loc(A1), pp[:, 0:256], AF.Tanh, scale=0.5, bias=hp2[:, hp, d:d + 1], r=[pp, hp2], w=[A1])
        yield
        P.op('dve', lambda e: e.tensor_scalar(out=A1[:], in0=A1[:], scalar1=1.0, scalar2=CW, op0=ALU.add, op1=ALU.mult), r=[A1], w=[A1])
        P.op('dve', lambda e: e.tensor_tensor_scan(out=B1[:], data0=cmask[:, 0:256], data1=A1[:], initial=0.0, op0=ALU.mult, op1=ALU.add), r=[A1, cmask], w=[B1])
        P.op('dve', lambda e: e.tensor_tensor(out=A1[:], in0=B1[:], in1=A1[:], op=ALU.subtract), r=[A1, B1], w=[A1])
        yield
        self.act(C1[:], A1[:], AF.Exp, r=[A1], w=[C1])
        P.op('dve', lambda e: e.scalar_tensor_tensor(out=loc(at), in0=kkb[:, nat], scalar=-1.0, in1=loc(C1), op0=ALU.mult, op1=ALU.mult), r=[kkb, C1], w=[at])
        yield
        self.act(C1[:], B1[:], AF.Exp, r=[B1], w=[C1])
        P.op('dve', lambda e: e.tensor_copy(out=wc[:], in_=C1[:, 63:256:64]), r=[C1], w=[wc])
        P.op('dve', lambda e: e.tensor_tensor(out=loc(rt), in0=rb[:, nat], in1=loc(C1), op=ALU.mult), r=[rb, C1], w=[rt])
        yield
        self.act(C1[:], B1[:], AF.Exp, scale=-1.0, r=[B1], w=[C1])
        pp = bank()
        self.mm(pp[:, 0:256], a2s[ds, hc], adb[ds, nat], r=[a2s, adb], w=[pp])
        self.act(loc(A1), pp[:, 0:256], AF.Tanh, scale=0.5, bias=hp2[:, hp, 2 + d:3 + d], r=[pp, hp2], w=[A1])
        yield
        P.op('dve', lambda e: e.tensor_scalar(out=B1[:], in0=A1[:], scalar1=0.5, scalar2=0.5, op0=ALU.mult, op1=ALU.add), r=[A1], w=[B1])
        P.op('dve', lambda e: e.tensor_tensor(out=loc(B1), in0=loc(B1), in1=kkb[:, nat], op=ALU.mult), r=[B1, kkb], w=[B1])
        P.op('dve', lambda e: e.tensor_tensor(out=bt[:], in0=B1[:], in1=C1[:], op=ALU.mult), r=[B1, C1], w=[bt])
        yield
        P.op('dve', lambda e: e.tensor_scalar(out=A1[:], in0=A1[:], scalar1=hp2[:, hp, 4:5], scalar2=hp2[:, hp, 5:6], op0=ALU.mult, op1=ALU.add), r=[A1, hp2], w=[A1])
        P.op('dve', lambda e: e.tensor_tensor(out=loc(A1), in0=loc(A1), in1=kb[:, nat], op=ALU.mult), r=[A1, kb], w=[A1])
        P.op('dve', lambda e: e.tensor_tensor(out=kt[:], in0=A1[:], in1=C1[:], op=ALU.mult), r=[A1, C1], w=[kt])
        P.op('pool', lambda e: e.tensor_copy(out=loc(vs), in_=vb[:, nat]), r=[vb], w=[vs])
        yield
        if s0 >= 256:
            P.op('dve', lambda e: e.scalar_tensor_tensor(out=B1[:], in0=loc(A1), scalar=chp[:, hp, 6:7], in1=rb[:, nat], op0=ALU.mult, op1=ALU.mult),
                 r=[A1, chp, rb, bt], w=[B1])
            pp = bank()
            self.mm(pp[:, 0:256], self.bones[:], B1[:], r=[B1, self.bones], w=[pp])
            bo = s0 - 256
            P.op('dve', lambda e: e.tensor_tensor(out=bacc[:, bo:bo + 256], in0=bacc[:, bo:bo + 256], in1=pp[:, 0:256], op=ALU.add), r=[pp, (bacc, bo)], w=[(bacc, bo)])
            yield

    def gen_S2a(self, hp, d, g, S, C):
        P, pb, pbh = self.P, self.pb, self.pbh
        msk = C['msk']
        at, bt, kt, vs = (S['s1'][g % 2][k] for k in ('at', 'bt', 'kt', 'vs'))
        rt = S['rw'][g % 4]['rt']
        sl = S['slots'][g % 3]
        tokT = sl['tokT']
        rot = C['rot']
        H = lambda e_: slice(64 * e_, 64 * e_ + 64)
        TP = lambda e_: (64 * e_, 64 * e_)

        def bank():
            rot[0] = (rot[0] + 1) % 4
            return pb[(0, 1, 2, 5)[rot[0]]]
        v3 = lambda p: p[:, 0:256].rearrange("p (c t) -> p c t", t=64)
        v4 = lambda p: p[:, :].rearrange("p (c x) -> p c x", x=128)
        QXs, QTs, AakT = S['QX'][g % 2], S['QT'][g % 2], S['AakT'][g % 2]

        def neumann_level(lvl, QX, QT):
            QXn, QTn = QXs[lvl % 2], QTs[lvl % 2]
            last = lvl == 6
            if lvl == 1:
                pq = bank()
                for c in range(4):
                    for e_ in range(2):
                        self.mm(v3(pq)[H(e_), c, :], QT[H(e_), c, :], QX[H(e_), c, 0, :], r=[QX, QT], w=[pq], tile_position=TP(e_))
                P.op('act', lambda e: e.copy(out=QXn[:, :, 0, :], in_=v3(pq)), r=[pq], w=[QXn])
                P.op('pool', lambda e: e.tensor_copy(out=QXn[:, :, 1, :], in_=QX[:, :, 1, :]), r=[QX], w=[QXn])
            else:
                ppx = bank()
                for c in range(4):
                    for e_ in range(2):
                        if last:
                            self.mm(v4(ppx)[H(e_), c, 64:128], QT[H(e_), c, :], QX[H(e_), c, 1, :], r=[QX, QT], w=[ppx], tile_position=TP(e_))
                        else:
                            self.mm(v4(ppx)[H(e_), c, :], QT[H(e_), c, :], QX[H(e_), c, :, :].rearrange("p a b -> p (a b)"), r=[QX, QT], w=[ppx],
                                    tile_position=TP(e_))
                dstP = sl['MT'][:] if last else QXn[:, :, 1, :]
                P.op('dve', lambda e: e.tensor_tensor(out=dstP, in0=v4(ppx)[:, :, 64:128], in1=QX[:, :, 1, :], op=ALU.add), r=[ppx, QX], w=[sl['MT'] if last else QXn])
                if not last:
                    P.op('act', lambda e: e.copy(out=QXn[:, :, 0, :], in_=v4(ppx)[:, :, 0:64]), r=[ppx], w=[QXn])
            if not last:
                pqt = bank()
                for c in range(4):
                    for e_ in range(2):
                        self.mm(v3(pqt)[H(e_), c, :], QX[H(e_), c, 0, :], QT[H(e_), c, :], r=[QX, QT], w=[pqt], tile_position=TP(e_))
                P.op('act', lambda e: e.copy(out=QTn[:], in_=v3(pqt)), r=[pqt], w=[QTn])
            return QXn, QTn
        for qi, q in enumerate((at, bt, kt, vs)):
            for c in range(4):
                for e_ in range(2):
                    o = (qi % 2) * 256 + c * 64
                    P.op('pe', lambda e: e.transpose(out=pbh[H(e_), o:o + 64], in_=q[H(e_), 64 * c:64 * c + 64], identity=self.identb[H(e_), H(e_)],
                                                     tile_position=TP(e_)), r=[q, self.identb], w=[pbh])
            if qi % 2 == 1:
                P.op('act', lambda e: e.copy(out=tokT[:, qi - 1:qi + 1, :, :].rearrange("p q c j -> p (q c j)"), in_=pbh[:, 0:512]), r=[pbh], w=[tokT])
                yield
        cs = lambda q, c, e_: q[H(e_), 64 * c:64 * c + 64]

        def score(L, Rr, mk, dst, dkey):
            pp = bank()
            for c in range(4):
                for e_ in range(2):
                    self.mm(v3(pp)[H(e_), c, :], cs(L, c, e_), cs(Rr, c, e_), r=[L, Rr], w=[pp], tile_position=TP(e_))
            P.op('dve', lambda e: e.tensor_tensor(out=dst, in0=v3(pp), in1=msk[mk][:], op=ALU.mult), r=[pp, msk[mk]], w=[dkey])
        QX, QT = QXs[0], QTs[0]
        score(bt, at, 'su', QX[:, :, 0, :], QX)
        yield
        score(at, bt, 'sl', QT[:], QT)
        yield
        P.op('pool', lambda e: e.tensor_tensor(out=QX[:, :, 1, :], in0=QX[:, :, 0, :], in1=msk['id'][:], op=ALU.add), r=[QX, msk['id']], w=[QX])
        score(kt, at, 'su', AakT[:], AakT)
        yield
        score(bt, rt, 'iu', sl['ArbT'][:], sl['ArbT'])
        yield
        score(kt, rt, 'iu', sl['ArkT'][:], sl['ArkT'])
        yield
        for lvl in (1, 2, 3):
            QX, QT = neumann_level(lvl, QX, QT)
            yield

    def gen_S2b(self, hp, d, g, S, C):
        P, pb, pbh = self.P, self.pb, self.pbh
        msk = C['msk']
        at, bt, kt, vs = (S['s1'][g % 2][k] for k in ('at', 'bt', 'kt', 'vs'))
        rt = S['rw'][g % 4]['rt']
        sl = S['slots'][g % 3]
        tokT = sl['tokT']
        rot = C['rot']
        H = lambda e_: slice(64 * e_, 64 * e_ + 64)
        TP = lambda e_: (64 * e_, 64 * e_)

        def bank():
            rot[0] = (rot[0] + 1) % 4
            return pb[(0, 1, 2, 5)[rot[0]]]
        v3 = lambda p: p[:, 0:256].rearrange("p (c t) -> p c t", t=64)
        v4 = lambda p: p[:, :].rearrange("p (c x) -> p c x", x=128)
        QXs, QTs, AakT = S['QX'][g % 2], S['QT'][g % 2], S['AakT'][g % 2]

        def neumann_level(lvl, QX, QT):
            QXn, QTn = QXs[lvl % 2], QTs[lvl % 2]
            last = lvl == 6
            if lvl == 1:
                pq = bank()
                for c in range(4):
                    for e_ in range(2):
                        self.mm(v3(pq)[H(e_), c, :], QT[H(e_), c, :], QX[H(e_), c, 0, :], r=[QX, QT], w=[pq], tile_position=TP(e_))
                P.op('act', lambda e: e.copy(out=QXn[:, :, 0, :], in_=v3(pq)), r=[pq], w=[QXn])
                P.op('pool', lambda e: e.tensor_copy(out=QXn[:, :, 1, :], in_=QX[:, :, 1, :]), r=[QX], w=[QXn])
            else:
                ppx = bank()
                for c in range(4):
                    for e_ in range(2):
                        if last:
                            self.mm(v4(ppx)[H(e_), c, 64:128], QT[H(e_), c, :], QX[H(e_), c, 1, :], r=[QX, QT], w=[ppx], tile_position=TP(e_))
                        else:
                            self.mm(v4(ppx)[H(e_), c, :], QT[H(e_), c, :], QX[H(e_), c, :, :].rearrange("p a b -> p (a b)"), r=[QX, QT], w=[ppx],
                                    tile_position=TP(e_))
                dstP = sl['MT'][:] if last else QXn[:, :, 1, :]
                P.op('dve', lambda e: e.tensor_tensor(out=dstP, in0=v4(ppx)[:, :, 64:128], in1=QX[:, :, 1, :], op=ALU.add), r=[ppx, QX], w=[sl['MT'] if last else QXn])
                if not last:
                    P.op('act', lambda e: e.copy(out=QXn[:, :, 0, :], in_=v4(ppx)[:, :, 0:64]), r=[ppx], w=[QXn])
            if not last:
                pqt = bank()
                for c in range(4):
                    for e_ in range(2):
                        self.mm(v3(pqt)[H(e_), c, :], QX[H(e_), c, 0, :], QT[H(e_), c, :], r=[QX, QT], w=[pqt], tile_position=TP(e_))
                P.op('act', lambda e: e.copy(out=QTn[:], in_=v3(pqt)), r=[pqt], w=[QTn])
            return QXn, QTn
        QX, QT = QXs[1], QTs[1]
        for lvl in (4, 5, 6):
            QX, QT = neumann_level(lvl, QX, QT)
            yield
        MT = sl['MT']
        pxa = bank()
        for c in range(4):
            for e_ in range(2):
                self.mm(v3(pxa)[H(e_), c, :], AakT[H(e_), c, :], tokT[H(e_), 3, c, :], r=[AakT, tokT], w=[pxa], tile_position=TP(e_))
        P.op('act', lambda e: e.copy(out=sl['Xak'][:], in_=v3(pxa)), r=[pxa], w=[sl['Xak']])
        yield
        pA = bank()
        for c in range(4):
            for e_ in range(2):
                self.mm(v3(pA)[H(e_), c, :], tokT[H(e_), 0, c, :], MT[H(e_), c, :], r=[tokT, MT], w=[pA], tile_position=TP(e_))
        P.op('act', lambda e: e.copy(out=sl['AhT'][:], in_=v3(pA)), r=[pA], w=[sl['AhT']])
        yield

    def gen_Q(self, hp, d, g, S, C):
        P, pb = self.P, self.pb
        y0 = C['y0']
        sl = S['slots'][g % 3]
        tokT, MT, Xak, ArbT, ArkT, AhT = (sl[k] for k in ('tokT', 'MT', 'Xak', 'ArbT', 'ArkT', 'AhT'))
        rt, wc = S['rw'][g % 4]['rt'], S['rw'][g % 4]['wc']
        Tst, Tw, Tb, Ub = S['Tst'], S['Tw'], S['Tb'], S['Ub']
        pU, pT, pY = pb[3], pb[4], pb[6]
        pUv = pU[:, 0:64]
        pTv = pT[:, 0:64]
        pYv = pY[:, 256 * d:256 * d + 256].rearrange("p (c t) -> p c t", t=64)
        H = lambda e_: slice(64 * e_, 64 * e_ + 64)
        TP = lambda e_: (64 * e_, 64 * e_)
        latent = g >= 1
        for c in range(4):
            for e_ in range(2):
                self.mm(pUv[H(e_), :], MT[H(e_), c, :], Xak[H(e_), c, :], start=True, stop=False, r=[MT, Xak], w=[pU], tile_position=TP(e_))
            for e_ in range(2):
                self.mm(pUv[H(e_), :], AhT[H(e_), c, :], Tb[H(e_), :], start=False, stop=True, r=[AhT, Tb], w=[pU], tile_position=TP(e_))
            P.op('act', lambda e: e.copy(out=Ub[:], in_=pUv), r=[pU], w=[Ub])
            yield
            for e_ in range(2):
                self.mm(pTv[H(e_), :], tokT[H(e_), 1, c, :], Ub[H(e_), :], start=True, stop=False, r=[tokT, Ub], w=[pT], tile_position=TP(e_))
            for e_ in range(2):
                self.mm(pTv[H(e_), :], tokT[H(e_), 2, c, :], tokT[H(e_), 3, c, :], start=False, stop=True, r=[tokT], w=[pT], tile_position=TP(e_))
            if latent:
                for e_ in range(2):
                    self.mm(pYv[H(e_), c, :], Tb[H(e_), :], rt[H(e_), 64 * c:64 * c + 64], start=True, stop=False, r=[Tb, rt], w=[pY], tile_position=TP(e_))
                for e_ in range(2):
                    self.mm(pYv[H(e_), c, :], Ub[H(e_), :], ArbT[H(e_), c, :], start=False, stop=False, r=[Ub, ArbT], w=[pY], tile_position=TP(e_))
                for e_ in range(2):
                    self.mm(pYv[H(e_), c, :], tokT[H(e_), 3, c, :], ArkT[H(e_), c, :], start=False, stop=True, r=[tokT, ArkT], w=[pY], tile_position=TP(e_))
            wcc = wc[:, c:c + 1]
            P.op('dve', lambda e: e.scalar_tensor_tensor(out=Tb[:], in0=pTv, scalar=wcc, in1=Tw[:], op0=ALU.mult, op1=ALU.add), r=[pT, wc, Tw], w=[Tb])
            P.op('dve', lambda e: e.scalar_tensor_tensor(out=Tst[:], in0=pTv, scalar=wcc, in1=Tw[:], op0=ALU.mult, op1=ALU.add), r=[pT, wc, Tw], w=[Tst])
            yield
            if c < 3:
                P.op('dve', lambda e: e.tensor_scalar(out=Tw[:], in0=Tst[:], scalar1=wc[:, c + 1:c + 2], scalar2=None, op0=ALU.mult), r=[Tst, wc], w=[Tw])
        if latent:
            g0 = 256 * g
            if d == 0:
                ysl = slice(g0 - 256, g0); yk = (y0, g0 - 256)
            else:
                ysl = rsl(2303 - g0, 256, -1); yk = (y0, 2048 - g0)
            P.op('dve', lambda e: e.tensor_tensor(out=y0[:, ysl], in0=y0[:, ysl], in1=pY[:, 256 * d:256 * d + 256], op=ALU.add), r=[pY, yk], w=[yk])
        yield

    def rw_rounds(self, hp, C):
        P = self.P
        dirs = [self.rw_alloc_dir() for _ in range(2)]
        for R in range(12):
            gens = []
            if 3 <= R:
                for d in range(2):
                    S = dirs[d]
                    wc = S['rw'][(R - 3) % 4]['wc']
                    P.op('dve', lambda e, S=S, wc=wc: e.tensor_scalar(out=S['Tw'][:], in0=S['Tst'][:], scalar1=wc[:, 0:1], scalar2=None, op0=ALU.mult),
                         r=[S['Tst'], wc], w=[S['Tw']])
                    gens.append(self.gen_Q(hp, d, R - 3, S, C))
            if 2 <= R <= 10:
                for d in range(2):
                    gens.append(self.gen_S2b(hp, d, R - 2, dirs[d], C))
            if 1 <= R <= 9:
                for d in range(2):
                    gens.append(self.gen_S2a(hp, d, R - 1, dirs[d], C))
            if R <= 8:
                for d in range(2):
                    gens.append(self.gen_S1(hp, d, R, dirs[d], C))
            while gens:
                for gn in list(gens):
                    try:
                        next(gn)
                    except StopIteration:
                        gens.remove(gn)

    def rwkv_half(self, hp, d, half, rb, kb, vb, kkb, y0, bacc, Tst, Tb, twd, adb, w2s, a2s, chp, hp2, msk, cmask, CW):
        P, pb, pbh = self.P, self.pb, self.pbh
        h0, W = (0, 1280) if half == 0 else (1280, 1024)
        if d == 0:
            pieces = [(0, 256, 0, 1), (256, 1024, 256, 1)] if half == 0 else [(1280, 1024, 1280, 1)]
        else:
            pieces = [(0, 256, 255, -1), (1280, 1024, 1279, -1)] if half == 0 else [(256, 1024, 2303, -1)]
        sg = lambda t, s0, n, sig0, step, off=0, nn=None: t[:, rsl(sig0 - h0 + step * off, nn if nn is not None else n, step)]
        A1 = P.sb([128, 1280], name='A1'); B1 = P.sb([128, 1280], name='B1'); C1 = P.sb([128, 1280], name='C1')
        rt = P.sb([128, 1280], BF16, name='rt'); at = P.sb([128, 1280], BF16, name='at'); bt = P.sb([128, 1280], BF16, name='bt')
        kt = P.sb([128, 1280], BF16, name='kt'); vs = P.sb([128, 1280], BF16, name='vs')
        wcs = P.sb([128, 20], name='wcs')
        hc = slice(hp * 128, (hp + 1) * 128)
        ds = slice(64 * d, 64 * d + 64)

        def lora(wts, src, bias_col, dst):
            nb = 0
            for (s0, n, sig0, step) in pieces:
                for o in range(0, n, 512):
                    nn = min(512, n - o)
                    pp = pb[nb % 2]; nb += 1
                    self.mm(pp[:, 0:nn], wts[ds, hc], src[ds, s0 + o:s0 + o + nn], r=[wts, src], w=[pp])
                    self.act(sg(dst, s0, n, sig0, step, o, nn), pp[:, 0:nn], AF.Tanh, scale=0.5, bias=hp2[:, hp, bias_col:bias_col + 1], r=[pp, hp2], w=[dst])
        lora(w2s, twd, d, A1)
        P.op('dve', lambda e: e.tensor_scalar(out=A1[:, 0:W], in0=A1[:, 0:W], scalar1=1.0, scalar2=CW, op0=ALU.add, op1=ALU.mult), r=[A1], w=[A1])
        P.op('dve', lambda e: e.tensor_tensor_scan(out=B1[:, 0:W], data0=cmask[:, 0:W], data1=A1[:, 0:W], initial=0.0, op0=ALU.mult, op1=ALU.add),
             r=[A1, cmask], w=[B1])
        P.op('dve', lambda e: e.tensor_tensor(out=A1[:, 0:W], in0=B1[:, 0:W], in1=A1[:, 0:W], op=ALU.subtract), r=[A1, B1], w=[A1])
        self.act(C1[:, 0:W], A1[:, 0:W], AF.Exp, r=[A1], w=[C1])
        for pc in pieces:
            s0, n = pc[0], pc[1]
            P.op('dve', lambda e, pc=pc, s0=s0, n=n: e.scalar_tensor_tensor(out=sg(at, *pc), in0=kkb[:, s0:s0 + n], scalar=-1.0, in1=sg(C1, *pc),
                                                                            op0=ALU.mult, op1=ALU.mult), r=[kkb, C1], w=[at])
        self.act(C1[:, 0:W], B1[:, 0:W], AF.Exp, r=[B1, at], w=[C1])
        P.op('dve', lambda e: e.tensor_copy(out=wcs[:, 0:W // 64], in_=C1[:, 63:W:64]), r=[C1], w=[wcs])
        for pc in pieces:
            s0, n = pc[0], pc[1]
            P.op('dve', lambda e, pc=pc, s0=s0, n=n: e.tensor_tensor(out=sg(rt, *pc), in0=rb[:, s0:s0 + n], in1=sg(C1, *pc), op=ALU.mult), r=[rb, C1], w=[rt])
        self.act(C1[:, 0:W], B1[:, 0:W], AF.Exp, scale=-1.0, r=[B1, rt, wcs], w=[C1])
        lora(a2s, adb, 2 + d, A1)
        P.op('dve', lambda e: e.tensor_scalar(out=B1[:, 0:W], in0=A1[:, 0:W], scalar1=0.5, scalar2=0.5, op0=ALU.mult, op1=ALU.add), r=[A1], w=[B1])
        for pc in pieces:
            s0, n = pc[0], pc[1]
            P.op('dve', lambda e, pc=pc, s0=s0, n=n: e.tensor_tensor(out=sg(B1, *pc), in0=sg(B1, *pc), in1=kkb[:, s0:s0 + n], op=ALU.mult), r=[B1, kkb], w=[B1])
        P.op('dve', lambda e: e.tensor_tensor(out=bt[:, 0:W], in0=B1[:, 0:W], in1=C1[:, 0:W], op=ALU.mult), r=[B1, C1], w=[bt])
        P.op('dve', lambda e: e.tensor_scalar(out=A1[:, 0:W], in0=A1[:, 0:W], scalar1=hp2[:, hp, 4:5], scalar2=hp2[:, hp, 5:6], op0=ALU.mult, op1=ALU.add),
             r=[A1, hp2], w=[A1])
        for pc in pieces:
            s0, n = pc[0], pc[1]
            P.op('dve', lambda e, pc=pc, s0=s0, n=n: e.tensor_tensor(out=sg(A1, *pc), in0=sg(A1, *pc), in1=kb[:, s0:s0 + n], op=ALU.mult), r=[A1, kb], w=[A1])
        P.op('dve', lambda e: e.tensor_tensor(out=kt[:, 0:W], in0=A1[:, 0:W], in1=C1[:, 0:W], op=ALU.mult), r=[A1, C1], w=[kt])
        nb = 0
        for pc in pieces:
            s0, n, sig0, step = pc
            if s0 < 256:
                continue
            P.op('dve', lambda e, pc=pc, s0=s0, n=n: e.scalar_tensor_tensor(out=B1[:, 0:n], in0=sg(A1, *pc), scalar=chp[:, hp, 6:7], in1=rb[:, s0:s0 + n],
                                                                            op0=ALU.mult, op1=ALU.mult), r=[A1, chp, rb, bt], w=[B1])
            for o in range(0, n, 512):
                pp = pb[nb % 2]; nb += 1
                self.mm(pp[:], self.bones[:], B1[:, o:o + 512], r=[B1, self.bones], w=[pp])
                bo = s0 - 256 + o
                if d == 0:
                    P.op('act', lambda e, pp=pp, bo=bo: e.copy(out=bacc[:, bo:bo + 512], in_=pp[:]), r=[pp], w=[bacc])
                else:
                    P.op('dve', lambda e, pp=pp, bo=bo: e.tensor_tensor(out=bacc[:, bo:bo + 512], in0=bacc[:, bo:bo + 512], in1=pp[:], op=ALU.add), r=[pp, bacc], w=[bacc])
        for pc in pieces:
            s0, n = pc[0], pc[1]
            P.op('pool', lambda e, pc=pc, s0=s0, n=n: e.tensor_copy(out=sg(vs, *pc), in_=vb[:, s0:s0 + n]), r=[vb], w=[vs])

        if hp == 0 and half == 0:
            for nm, t_ in (('rt', rt), ('at', at), ('bt', bt), ('kt', kt), ('vs', vs)):
                self.tap(nm + str(d), t_[:], [128, 1280], [t_], dt=BF16)
            self.tap('wcs' + str(d), wcs[:], [128, 20], [wcs])
            self.tap('kd' + str(d), A1[:], [128, 1280], [A1])
        tokT = P.sb([64, 4, 4, 128], BF16, name='tokT')
        Qs = [P.sb([64, 8, 64], BF16, name='Q') for _ in range(2)]; QTs = [P.sb([64, 8, 64], BF16, name='QT') for _ in range(2)]
        Xs = [P.sb([64, 8, 64], BF16, name='X') for _ in range(2)]
        AakT = P.sb([64, 8, 64], BF16, name='AakT'); ArbT = P.sb([64, 8, 64], BF16, name='ArbT'); ArkT = P.sb([64, 8, 64], BF16, name='ArkT')
        Xak = P.sb([64, 8, 64], BF16, name='Xak'); AhT = P.sb([128, 4, 64], BF16, name='AhT')
        Ub = P.sb([64, 2, 64], BF16, name='Ub'); Ts = P.sb([128, 64], name='Ts')
        pU, pT, pY, pA = pb[4], pb[5], pb[6], pb[0]
        pYv = pb[6][:, 0:256].rearrange("p (c t) -> p c t", t=64)
        pAv = pb[0][:, 0:256].rearrange("p (c t) -> p c t", t=64)
        pUv = pb[4][0:64, 0:128].rearrange("p (e v) -> p e v", v=64)
        pTv = pb[5][:, 0:64]
        bank = [0]

        def nextbank():
            bank[0] = (bank[0] + 1) % 2
            return pb[2 + bank[0]]
        v3 = lambda p: p[0:64, :].rearrange("p (i t) -> p i t", t=64)
        for gi in range(W // 256):
            loc = 256 * gi
            g0 = h0 + loc
            latent = g0 >= 256
            for qi, q in enumerate((at, bt, kt, vs)):
                for c in range(4):
                    P.op('pe', lambda e, qi=qi, q=q, c=c: e.transpose(out=pbh[0:64, (qi % 2) * 512 + c * 128:(qi % 2) * 512 + (c + 1) * 128],
                                                                      in_=q[:, loc + 64 * c:loc + 64 * c + 64], identity=self.identb[:]),
                         r=[q, self.identb], w=[pbh])
                if qi % 2 == 1:
                    P.op('act', lambda e, qi=qi: e.copy(out=tokT[:, qi - 1:qi + 1, :, :].rearrange("p q c j -> p (q c j)"), in_=pbh[0:64, :]), r=[pbh], w=[tokT])
            cs = lambda q, c, e_: q[64 * e_:64 * e_ + 64, loc + 64 * c:loc + 64 * c + 64]

            def score(L, Rr, mk, dst):
                pp = nextbank()
                for c in range(4):
                    for e_ in range(2):
                        self.mm(v3(pp)[:, 2 * c + e_, :], cs(L, c, e_), cs(Rr, c, e_), r=[L, Rr], w=[pp])
                P.op('dve', lambda e: e.tensor_tensor(out=dst[:], in0=v3(pp), in1=msk[mk][:], op=ALU.mult), r=[pp, msk[mk]], w=[dst])
            Q, QT, X = Qs[0], QTs[0], Xs[0]
            score(bt, at, 'su', Q)
            score(at, bt, 'sl', QT)
            score(kt, at, 'su', AakT)
            score(bt, rt, 'iu', ArbT)
            score(kt, rt, 'iu', ArkT)
            P.op('dve', lambda e, Q=Q, X=X: e.tensor_tensor(out=X[:], in0=Q[:], in1=msk['id'][:], op=ALU.add), r=[Q, msk['id']], w=[X])
            for lvl in range(2, 7):
                Qn, QTn, Xn = Qs[(lvl + 1) % 2], QTs[(lvl + 1) % 2], Xs[(lvl + 1) % 2]
                if lvl < 6:
                    pq = nextbank()
                    for i in range(8):
                        self.mm(v3(pq)[:, i, :], QT[:, i, :], Q[:, i, :], r=[Q, QT], w=[pq])
                    P.op('act', lambda e, pq=pq, Qn=Qn: e.copy(out=Qn[:], in_=v3(pq)), r=[pq], w=[Qn])
                pqt = nextbank()
                for i in range(8):
                    self.mm(v3(pqt)[:, i, :], Q[:, i, :], QT[:, i, :], r=[Q, QT], w=[pqt])
                P.op('act', lambda e, pqt=pqt, QTn=QTn: e.copy(out=QTn[:], in_=v3(pqt)), r=[pqt], w=[QTn])
                px = nextbank()
                for i in range(8):
                    self.mm(v3(px)[:, i, :], QTn[:, i, :], X[:, i, :], r=[QTn, X], w=[px])
                P.op('dve', lambda e, px=px, X=X, Xn=Xn: e.tensor_tensor(out=Xn[:], in0=v3(px), in1=X[:], op=ALU.add), r=[px, X], w=[Xn])
                Q, QT, X = Qn, QTn, Xn
            MT = X
            pxa = nextbank()
            for c in range(4):
                for e_ in range(2):
                    self.mm(v3(pxa)[:, 2 * c + e_, :], AakT[:, 2 * c + e_, :], tokT[:, 3, c, 64 * e_:64 * e_ + 64], r=[AakT, tokT], w=[pxa])
            P.op('act', lambda e, pxa=pxa: e.copy(out=Xak[:], in_=v3(pxa)), r=[pxa], w=[Xak])
            for c in range(4):
                for e_ in range(2):
                    self.mm(pAv[64 * e_:64 * e_ + 64, c, :], tokT[:, 0, c, 64 * e_:64 * e_ + 64], MT[:, 2 * c + e_, :], r=[tokT, MT], w=[pA],
                            tile_position=(0, 64 * e_))
            P.op('act', lambda e: e.copy(out=AhT[:], in_=pAv), r=[pA], w=[AhT])
            if hp == 0 and half == 0 and d == 0 and gi == 0:
                self.tap('MT', MT[:], [64, 8, 64], [MT], dt=BF16)
                self.tap('AhT', AhT[:], [128, 4, 64], [AhT], dt=BF16)
                self.tap('Xak', Xak[:], [64, 8, 64], [Xak], dt=BF16)
                self.tap('tokT', tokT[:], [64, 4, 4, 128], [tokT], dt=BF16)
                self.tap('ArbT', ArbT[:], [64, 8, 64], [ArbT], dt=BF16)
            for c in range(4):
                for e_ in range(2):
                    i = 2 * c + e_
                    es = slice(64 * e_, 64 * e_ + 64)
                    self.mm(pUv[:, e_, :], MT[:, i, :], Xak[:, i, :], start=True, stop=False, r=[MT, Xak], w=[pU])
                    self.mm(pUv[:, e_, :], AhT[es, c, :], Tb[es, :], start=False, stop=True, r=[AhT, Tb], w=[pU], tile_position=(64 * e_, 0))
                P.op('act', lambda e: e.copy(out=Ub[:], in_=pUv), r=[pU], w=[Ub])
                for e_ in range(2):
                    i = 2 * c + e_
                    es = slice(64 * e_, 64 * e_ + 64)
                    if latent:
                        self.mm(pYv[es, c, :], Tb[es, :], rt[es, loc + 64 * c:loc + 64 * c + 64], start=True, stop=False, r=[Tb, rt], w=[pY],
                                tile_position=(64 * e_, 64 * e_))
                        self.mm(pYv[es, c, :], Ub[:, e_, :], ArbT[:, i, :], start=False, stop=False, r=[Ub, ArbT], w=[pY], tile_position=(0, 64 * e_))
                        self.mm(pYv[es, c, :], tokT[:, 3, c, es], ArkT[:, i, :], start=False, stop=True, r=[tokT, ArkT], w=[pY], tile_position=(0, 64 * e_))
                    self.mm(pTv[es, :], tokT[:, 1, c, es], Ub[:, e_, :], start=True, stop=False, r=[tokT, Ub], w=[pT], tile_position=(0, 64 * e_))
                    self.mm(pTv[es, :], tokT[:, 2, c, es], tokT[:, 3, c, es], start=False, stop=True, r=[tokT], w=[pT], tile_position=(0, 64 * e_))
                P.op('dve', lambda e: e.tensor_tensor(out=Ts[:], in0=pTv, in1=Tst[:], op=ALU.add), r=[pT, Tst], w=[Ts])
                wc = wcs[:, 4 * gi + c:4 * gi + c + 1]
                P.op('dve', lambda e, wc=wc: e.tensor_scalar(out=Tst[:], in0=Ts[:], scalar1=wc, scalar2=None, op0=ALU.mult), r=[Ts, wcs], w=[Tst])
                self.act(Tb[:], Ts[:], AF.Identity, scale=wc, r=[Ts, wcs], w=[Tb])
            if latent:
                if d == 0:
                    P.op('act', lambda e, g0=g0: e.copy(out=y0[:, g0 - 256:g0], in_=pb[6][:, 0:256]), r=[pY], w=[y0])
                else:
                    ysl = rsl(2303 - g0, 256, -1)
                    P.op('dve', lambda e, ysl=ysl: e.tensor_tensor(out=y0[:, ysl], in0=y0[:, ysl], in1=pb[6][:, 0:256], op=ALU.add), r=[pY, y0], w=[y0])

    def wload(self, pool, src, npart, K, ncol, q='sp'):
        P = self.P
        i = pool['i'] = pool['i'] + 1
        wf, wb = pool['f'][i % len(pool['f'])], pool['b'][i % len(pool['b'])]
        P.dma(q, wf[0:npart, 0:K, 0:ncol], src, w=[wf], group=pool['name'] + str(i % len(pool['f'])))
        ceng = 'pool' if (self._castn % 2 == 0) else 'dve'
        self._castn += 1
        P.op(ceng, lambda e: e.tensor_copy(out=wb[0:npart, 0:K, 0:ncol], in_=wf[0:npart, 0:K, 0:ncol]), r=[wf], w=[wb])
        return wb

    def mkpool(self, name, npart, K, ncol, n=2):
        P = self.P
        return {'name': name, 'i': 0, 'f': [P.sb([npart, K, ncol], name=name + 'f') for _ in range(n)],
                'b': [P.sb([npart, K, ncol], BF16, name=name + 'b') for _ in range(n)]}

    def lru(self):
        P, pb, hT, din = self.P, self.pb, self.hT, self.din
        lruT = self.lruT
        cw_, cb_, ba_, bx_, lam_ = din['lru_conv_w'], din['lru_conv_b'], din['lru_ba'], din['lru_bx'], din['lru_lambda']
        rows = [cw_[d, j] for d in range(2) for j in range(4)] + [cb_[0], cb_[1], ba_[0], ba_[1], bx_[0], bx_[1], lam_[0], lam_[1]]
        lp = self.cols(rows, LW, 80, 'lp')
        hb = P.sb([80, 16, 4], name='hb')
        P.op('dve', lambda e: e.tensor_scalar(out=hb[:], in0=lp[:, :, 10:14], scalar1=0.5, scalar2=None, op0=ALU.mult), r=[lp], w=[hb])
        cs = P.sb([80, 16, 4], name='cs')
        one1 = P.sb([80, 1], name='one1')
        P.op('dve', lambda e: e.memset(one1[:], 1.0), w=[one1])
        self.act(cs[:, :, 0:2], lp[:, :, 14:16], AF.Exp, scale=-1.0, r=[lp], w=[cs])
        self.act(cs[:, :, 0:2], cs[:, :, 0:2], AF.Ln, bias=one1[:], r=[cs, one1], w=[cs])
        P.op('dve', lambda e: e.tensor_scalar(out=cs[:, :, 2:4], in0=cs[:, :, 0:2], scalar1=-4.0, scalar2=None, op0=ALU.mult), r=[cs], w=[cs])
        P.op('dve', lambda e: e.tensor_scalar(out=cs[:, :, 0:2], in0=cs[:, :, 0:2], scalar1=-8.0, scalar2=None, op0=ALU.mult), r=[cs], w=[cs])
        q25 = P.sb([80, 1], name='q25')
        P.op('dve', lambda e: e.memset(q25[:], 0.25), w=[q25])
        gwa = P.sb([80, 32, 80], BF16, name='gwa'); gwx = P.sb([80, 32, 80], BF16, name='gwx')
        with P.scope():
            st = P.sb([80, 32, 80], name='gst')
            for src, dst in ((din['lru_wa'], gwa), (din['lru_wx'], gwx)):
                P.dma('sp', st[:], src.rearrange("d n c e -> c (d n) e"), w=[st], group='gst')
                P.op('dve', lambda e, dst=dst: e.tensor_copy(out=dst[:], in_=st[:]), r=[st], w=[dst])
        U = P.sb([80, 2313], name='U'); guy = P.sb([80, NLAT], BF16, name='guy')
        xcs = [P.sb([80, 2313], name='xc') for _ in range(2)]
        xcb = P.sb([80, 2313], BF16, name='xcb')
        thr = P.sb([80, 2313], name='thr'); thi = P.sb([80, 2313], name='thi'); aa = P.sb([80, 2313], name='aa')
        lrub = P.sb([80, NLAT], BF16, name='lrub')
        wp = self.mkpool('lw', 128, 8, 80, n=2)
        P.op('dve', lambda e: e.memset(U[:], 0.0), w=[U])
        for xc in xcs:
            P.op('dve', lambda e, xc=xc: e.memset(xc[:], 0.0), w=[xc])
        P.op('dve', lambda e: e.memset(thr[:], 0.0), w=[thr])
        P.op('dve', lambda e: e.memset(thi[:], 0.0), w=[thi])
        wv = din['w_in'].rearrange("(k p) n -> p k n", p=128)
        hk = [(hT, j) for j in range(5)]
        tbs = [(0, 256, 3)] + [(256 + 512 * i, 512, 262 + 512 * i) for i in range(4)]
        nb = 0
        for n in range(NBLK):
            wx = self.wload(wp, wv[:, :, 80 * n:80 * n + 80], 128, 8, 80)
            for (hc, nn, uc) in tbs:
                pp = pb[nb % 6]; nb += 1
                for k in range(8):
                    self.mm(pp[0:80, 0:nn], wx[:, k, :], hT[:, k, hc:hc + nn], start=(k == 0), stop=(k == 7), r=[wx] + hk, w=[pp])
                P.op('act', lambda e: e.copy(out=U[:, uc:uc + nn], in_=pp[0:80, 0:nn]), r=[pp], w=[U])
            wy = self.wload(wp, wv[:, :, 1280 + 80 * n:1280 + 80 * n + 80], 128, 8, 80, q='act')
            for (hc, nn, uc) in tbs[1:]:
                pp = pb[nb % 6]; nb += 1
                for k in range(8):
                    self.mm(pp[0:80, 0:nn], wy[:, k, :], hT[:, k, hc:hc + nn], start=(k == 0), stop=(k == 7), r=[wy] + hk, w=[pp])
                self.act(guy[:, hc - 256:hc - 256 + nn], pp[0:80, 0:nn], AF.Gelu_apprx_tanh, r=[pp], w=[guy])
            for d in range(2):
                xc = xcs[d]
                sgn = -1 if d == 0 else 1
                cwj = lambda j: lp[:, n, 4 * d + j:4 * d + j + 1]

                def chunk(ci, uc, nn, blocks):
                    nonlocal nb
                    cs_ = slice(uc, uc + nn)
                    P.op('dve', lambda e: e.tensor_scalar(out=xc[:, cs_], in0=U[:, cs_], scalar1=cwj(3), scalar2=lp[:, n, 8 + d:9 + d],
                                                          op0=ALU.mult, op1=ALU.add), r=[U, lp], w=[(xc, ci)])
                    for j in range(3):
                        o = uc + sgn * (3 - j)
                        P.op('dve', lambda e: e.scalar_tensor_tensor(out=xc[:, cs_], in0=U[:, o:o + nn], scalar=cwj(j), in1=xc[:, cs_],
                                                                     op0=ALU.mult, op1=ALU.add), r=[U, lp, (xc, ci)], w=[(xc, ci)])
                    yield
                    P.op('act', lambda e: e.copy(out=xcb[:, cs_], in_=xc[:, cs_]), r=[(xc, ci)], w=[(xcb, ci)])
                    yield
                    for (hc, bn, bc) in blocks:
                        for gw, dst, bcol in ((gwa, thr, d), (gwx, thi, 2 + d)):
                            pp = pb[nb % 6]; nb += 1
                            self.mm(pp[0:80, 0:bn], gw[:, 16 * d + n, :], xcb[:, bc:bc + bn], r=[gw, (xcb, ci)], w=[pp])
                            self.act(dst[:, bc:bc + bn], pp[0:80, 0:bn], AF.Tanh, scale=0.5, bias=hb[:, n, bcol:bcol + 1], r=[pp, hb], w=[(dst, ci)])
                    yield
                    self.act(aa[:, cs_], thr[:, cs_], AF.Exp, scale=cs[:, n, 2 + d:3 + d], bias=cs[:, n, 2 + d:3 + d], r=[(thr, ci), cs], w=[(aa, ci)])
                    self.act(thr[:, cs_], thr[:, cs_], AF.Exp, scale=cs[:, n, d:d + 1], bias=cs[:, n, d:d + 1], r=[(thr, ci), cs], w=[(thr, ci)])
                    P.op('dve', lambda e: e.scalar_tensor_tensor(out=thi[:, cs_], in0=thi[:, cs_], scalar=1.0, in1=xc[:, cs_], op0=ALU.add, op1=ALU.mult),
                         r=[(thi, ci), (xc, ci)], w=[(thi, ci)])
                    yield
                    self.act(thr[:, cs_], thr[:, cs_], AF.Sqrt, scale=-0.25, bias=q25[:], r=[(thr, ci), q25], w=[(thr, ci)])
                    yield
                    P.op('dve', lambda e: e.tensor_tensor(out=thi[:, cs_], in0=thi[:, cs_], in1=thr[:, cs_], op=ALU.mult), r=[(thi, ci), (thr, ci)], w=[(thi, ci)])
                    yield
                gens = [chunk(0, 3, 1283, tbs[0:3]), chunk(1, 1286, 1024, tbs[3:5])]
                while gens:
                    for gn in list(gens):
                        try:
                            next(gn)
                        except StopIteration:
                            gens.remove(gn)
                allk = lambda t_: [(t_, ci) for ci in range(2)]
                if d == 0:
                    P.op('dve', lambda e: e.tensor_tensor_scan(out=xc[:, 3:259], data0=aa[:, 3:259], data1=thi[:, 3:259], initial=0.0, op0=ALU.mult, op1=ALU.add),
                         r=allk(aa) + allk(thi), w=allk(xc))
                    P.op('dve', lambda e: e.tensor_tensor_scan(out=xc[:, 262:2310], data0=aa[:, 262:2310], data1=thi[:, 262:2310], initial=xc[:, 258:259],
                                                               op0=ALU.mult, op1=ALU.add), r=allk(aa) + allk(thi) + allk(xc), w=allk(xc))
                else:
                    rv = lambda t_, a_, b_: t_[:, rsl(b_ - 1, b_ - a_, -1)]
                    P.op('dve', lambda e: e.tensor_tensor_scan(out=rv(xc, 3, 259), data0=rv(aa, 3, 259), data1=rv(thi, 3, 259), initial=0.0, op0=ALU.mult, op1=ALU.add),
                         r=allk(aa) + allk(thi), w=allk(xc))
                    P.op('dve', lambda e: e.tensor_tensor_scan(out=rv(xc, 262, 2310), data0=rv(aa, 262, 2310), data1=rv(thi, 262, 2310), initial=xc[:, 3:4],
                                                               op0=ALU.mult, op1=ALU.add), r=allk(aa) + allk(thi) + allk(xc), w=allk(xc))
            allk = lambda t_: [(t_, ci) for ci in range(2)]
            P.op('dve', lambda e: e.tensor_tensor(out=aa[:, 0:NLAT], in0=xcs[0][:, 262:2310], in1=xcs[1][:, 262:2310], op=ALU.add), r=allk(xcs[0]) + allk(xcs[1]) + allk(aa), w=allk(aa))
            P.op('dve', lambda e: e.tensor_tensor(out=lrub[:], in0=aa[:, 0:NLAT], in1=guy[:], op=ALU.mult), r=allk(aa) + [guy], w=[lrub])
            p0, c0 = (80 * n) % 128, (80 * n) // 128
            n1 = min(80, 128 - p0)
            P.dma('sp', lruT[p0:p0 + n1, c0, :], lrub[0:n1, :], r=[lrub], w=[lruT], group='lruT')
            if n1 < 80:
                P.dma('sp', lruT[0:80 - n1, c0 + 1, :], lrub[n1:80, :], r=[lrub], w=[lruT], group='lruT')

    def merge(self):
        P, pb, hT, din = self.P, self.pb, self.hT, self.din
        lruT, rwT, mT = self.lruT, self.rwT, self.mT
        wv = din['w_in'].rearrange("(k p) n -> p k n", p=128)
        wol = din['w_o_lru'].rearrange("(k p) n -> p k n", p=128)
        wor = din['w_o_rwkv'].rearrange("(k p) n -> p k n", p=128)
        pl = self.mkpool('wl', 128, 10, 128); pr = self.mkpool('wr', 128, 8, 128); pg = self.mkpool('wg', 128, 8, 128, n=3)
        thl = P.sb([128, 512], name='thl'); thr = P.sb([128, 512], name='thr2'); t1 = P.sb([128, 512], name='t1'); t2 = P.sb([128, 512], name='t2')
        hk = [(hT, j) for j in range(5)]
        for dc in range(8):
            cs_ = slice(dc * 128, dc * 128 + 128)
            wl = self.wload(pl, wol[:, :, cs_], 128, 10, 128)
            wr = self.wload(pr, wor[:, :, cs_], 128, 8, 128, q='act')
            wgl = self.wload(pg, wv[:, :, 6048 + dc * 128:6048 + dc * 128 + 128], 128, 8, 128)
            wgr = self.wload(pg, wv[:, :, 7072 + dc * 128:7072 + dc * 128 + 128], 128, 8, 128, q='act')
            for tb in range(4):
                ts_ = slice(512 * tb, 512 * tb + 512); hs_ = slice(256 + 512 * tb, 256 + 512 * tb + 512)
                it = dc * 4 + tb
                p1, p2, p3, p4 = (pb[(4 * it + j_) % 7] for j_ in range(4))
                for c in range(10):
                    self.mm(p1[:], wl[:, c, :], lruT[:, c, ts_], start=(c == 0), stop=(c == 9), r=[wl, lruT], w=[p1])
                for c in range(8):
                    self.mm(p2[:], wr[:, c, :], rwT[:, c, ts_], start=(c == 0), stop=(c == 7), r=[wr, rwT], w=[p2])
                for c in range(8):
                    self.mm(p3[:], wgl[:, c, :], hT[:, c, hs_], start=(c == 0), stop=(c == 7), r=[wgl] + hk, w=[p3])
                for c in range(8):
                    self.mm(p4[:], wgr[:, c, :], hT[:, c, hs_], start=(c == 0), stop=(c == 7), r=[wgr] + hk, w=[p4])
                self.act(thl[:], p3[:], AF.Tanh, scale=0.5, r=[p3], w=[thl])
                self.act(thr[:], p4[:], AF.Tanh, scale=0.5, r=[p4], w=[thr])
                P.op('dve', lambda e: e.scalar_tensor_tensor(out=t1[:], in0=thl[:], scalar=1.0, in1=p1[:], op0=ALU.add, op1=ALU.mult), r=[thl, p1], w=[t1])
                P.op('dve', lambda e: e.scalar_tensor_tensor(out=t2[:], in0=thr[:], scalar=1.0, in1=p2[:], op0=ALU.add, op1=ALU.mult), r=[thr, p2], w=[t2])
                P.op('dve', lambda e: e.tensor_tensor(out=mT[:, dc, ts_], in0=t1[:], in1=t2[:], op=ALU.add), r=[t1, t2], w=[mT])

    def resid1(self):
        P, pb, din, mod = self.P, self.pb, self.din, self.mod
        mT, x1T = self.mT, self.x1T
        hg = P.sb([128, 8], name='hg')
        P.op('dve', lambda e: e.tensor_scalar(out=hg[:], in0=mod[:, 16:24, 0], scalar1=0.5, scalar2=None, op0=ALU.mult), r=[mod], w=[hg])
        wo = P.sb([128, 8, D], BF16, name='wo')
        with P.scope():
            st = P.sb([128, 8, 256], name='wost')
            for j in range(4):
                P.dma('sp', st[:], din['w_out'].rearrange("(k p) n -> p k n", p=128)[:, :, 256 * j:256 * j + 256], w=[st], group='wost')
                P.op('pool', lambda e: e.tensor_copy(out=wo[:, :, 256 * j:256 * j + 256], in_=st[:]), r=[st], w=[wo])
        xv = din['xT'].rearrange("(k p) t -> p k t", p=128)
        for tb in range(4):
            ts_ = slice(512 * tb, 512 * tb + 512)
            P.dma('sp', x1T[:, :, ts_], xv[:, :, ts_], w=[x1T], group='x1ld')
            for dc in range(8):
                pp = pb[(tb * 8 + dc) % 6]
                for c in range(8):
                    self.mm(pp[:], wo[:, c, dc * 128:dc * 128 + 128], mT[:, c, ts_], start=(c == 0), stop=(c == 7), r=[wo, mT], w=[pp])
                P.op('dve', lambda e: e.scalar_tensor_tensor(out=x1T[:, dc, ts_], in0=pp[:], scalar=hg[:, dc:dc + 1], in1=x1T[:, dc, ts_], op0=ALU.mult, op1=ALU.add),
                     r=[pp, hg, x1T], w=[x1T])

    def ffn(self):
        P, pb, din, mod = self.P, self.pb, self.din, self.mod
        x1T, h2T = self.x1T, self.h2T
        wi = din['w_ffn_in'].rearrange("(k p) n -> p k n", p=128)
        wo_ = din['w_ffn_out'].rearrange("(f p) n -> p f n", p=128)
        actT = P.sb([128, 22, 1024], BF16, name='actT')
        pin = self.mkpool('fi', 128, 8, 128, n=3); pout = self.mkpool('fo', 128, 22, 128, n=2)
        sl = P.sb([128, 512], name='sl')
        hk = [(h2T, j) for j in range(4)]
        nb = 0
        for half in range(2):
            for f in range(22):
                wg = self.wload(pin, wi[:, :, f * 128:f * 128 + 128], 128, 8, 128)
                wu = self.wload(pin, wi[:, :, DFF + f * 128:DFF + f * 128 + 128], 128, 8, 128, q='act')
                for t2 in range(2):
                    tok = slice(1024 * half + 512 * t2, 1024 * half + 512 * t2 + 512)
                    pg_, pu_ = pb[nb % 4], pb[(nb + 1) % 4]; nb += 2
                    for k in range(8):
                        self.mm(pg_[:], wg[:, k, :], h2T[:, k, tok], start=(k == 0), stop=(k == 7), r=[wg] + hk, w=[pg_])
                    for k in range(8):
                        self.mm(pu_[:], wu[:, k, :], h2T[:, k, tok], start=(k == 0), stop=(k == 7), r=[wu] + hk, w=[pu_])
                    self.act(sl[:], pg_[:], AF.Silu, r=[pg_], w=[sl])
                    P.op('dve', lambda e: e.tensor_tensor(out=actT[:, f, 512 * t2:512 * t2 + 512], in0=sl[:], in1=pu_[:], op=ALU.mult), r=[sl, pu_], w=[(actT, f)])
            for dc in range(8):
                wo = self.wload(pout, wo_[:, :, dc * 128:dc * 128 + 128], 128, 22, 128)
                for t2 in range(2):
                    tok = slice(1024 * half + 512 * t2, 1024 * half + 512 * t2 + 512)
                    pp = pb[4 + (nb % 2)]; nb += 1
                    for f in range(22):
                        self.mm(pp[:], wo[:, f, :], actT[:, f, 512 * t2:512 * t2 + 512], start=(f == 0), stop=(f == 21), r=[wo, (actT, f)], w=[pp])
                    P.op('dve', lambda e: e.scalar_tensor_tensor(out=x1T[:, dc, tok], in0=pp[:], scalar=mod[:, 40 + dc, 0:1], in1=x1T[:, dc, tok], op0=ALU.mult, op1=ALU.add),
                         r=[pp, mod, x1T], w=[x1T])

    def final(self, outT):
        P, pb, x = self.P, self.pb, self.x1T
        gains = self.gains
        sq = P.sb([128, 8, 512], name='fsq'); rs = P.sb([128, 512], name='frs')
        epst = P.sb([128, 1], name='fepst')
        P.op('dve', lambda e: e.memset(epst[:], RMS_EPS), w=[epst])
        ov = outT.rearrange("(k p) t -> p k t", p=128)
        for tb in range(4):
            ts_ = slice(512 * tb, 512 * tb + 512)
            self.act(sq[:], x[:, :, ts_], AF.Square, r=[x], w=[sq])
            pp = pb[tb % 2]
            for k in range(8):
                self.mm(pp[:], self.ones[:], sq[:, k, :], start=(k == 0), stop=(k == 7), r=[sq, self.ones], w=[pp])
            self.act(rs[:], pp[:], AF.Ln, scale=1.0 / D, bias=epst[:], r=[pp, epst], w=[rs])
            self.act(rs[:], rs[:], AF.Exp, scale=-0.5, r=[rs], w=[rs])
            for k in range(8):
                P.op('dve', lambda e: e.scalar_tensor_tensor(out=sq[:, k, :], in0=x[:, k, ts_], scalar=gains[:, k, 2:3], in1=rs[:], op0=ALU.mult, op1=ALU.mult),
                     r=[x, gains, rs], w=[sq])
            P.op('dve', lambda e: e.tensor_copy(out=epst[:], in_=epst[:]), r=[sq, epst], w=[sq, epst])
            P.dma('sp', ov[:, :, ts_], sq[:], r=[sq], group='out')

    def finish(self):
        P = self.P
        for gname in list(P.dsem):
            if gname.startswith('out'):
                P.wait_group('pool', gname)
        P.emit()
        return self.nc


_CACHE = {}


def _prep(inputs, b):
    f = lambda a: np.ascontiguousarray(a, dtype=np.float32)
    m = {
        'xT': f(inputs['x'][b].T), 'ctxT': f(inputs['ctx'][b].T),
        'cvec': f(np.stack([inputs['c'][b], inputs['c_ctx']])),
        'w_mod': f(inputs['w_mod'][0]), 'b_mod': f(inputs['b_mod'][0]),
        'norm_mix_g': f(inputs['norm_mix_g'][0]), 'norm_ffn_g': f(inputs['norm_ffn_g'][0]), 'norm_final_g': f(inputs['norm_final_g']),
        'w_in': f(inputs['w_in'][0]),
        'lru_conv_w': f(inputs['lru_conv_w'][0]), 'lru_conv_b': f(inputs['lru_conv_b'][0]),
        'lru_wa': f(inputs['lru_wa'][0]), 'lru_ba': f(inputs['lru_ba'][0]), 'lru_wx': f(inputs['lru_wx'][0]), 'lru_bx': f(inputs['lru_bx'][0]),
        'lru_lambda': f(inputs['lru_lambda'][0]), 'w_o_lru': f(inputs['w_o_lru'][0]),
        'rwkv_mu': f(inputs['rwkv_mu'][0]), 'rwkv_w0': f(inputs['rwkv_w0'][0]), 'rwkv_w2': f(inputs['rwkv_w2'][0]),
        'rwkv_a0': f(inputs['rwkv_a0'][0]), 'rwkv_a2': f(inputs['rwkv_a2'][0]), 'rwkv_g2': f(inputs['rwkv_g2'][0]),
        'rwkv_k_k': f(inputs['rwkv_k_k'][0]), 'rwkv_k_a': f(inputs['rwkv_k_a'][0]), 'rwkv_r_k': f(inputs['rwkv_r_k'][0].reshape(-1)),
        'rwkv_ln_g': f(inputs['rwkv_ln_g'][0]), 'rwkv_ln_b': f(inputs['rwkv_ln_b'][0]),
        'w_o_rwkv': f(inputs['w_o_rwkv'][0]), 'w_out': f(inputs['w_out'][0]),
        'w_ffn_in': f(inputs['w_ffn_in'][0]), 'w_ffn_out': f(inputs['w_ffn_out'][0]),
    }
    return m


def kernel(**inputs):
    if 'nc' not in _CACHE:
        _CACHE['nc'] = Builder().build()
    nc = _CACHE['nc']
    shared = _prep(inputs, 0)
    in_maps = []
    for b in range(8):
        m = dict(shared)
        m['xT'] = np.ascontiguousarray(np.asarray(inputs['x'][b], dtype=np.float32).T)
        m['ctxT'] = np.ascontiguousarray(np.asarray(inputs['ctx'][b], dtype=np.float32).T)
        m['cvec'] = np.ascontiguousarray(np.stack([inputs['c'][b], inputs['c_ctx']]).astype(np.float32))
        in_maps.append(m)
    res = run_bass_kernel_spmd(nc, in_maps, core_ids=list(range(8)))
    out = np.stack([np.ascontiguousarray(r['outT'].T) for r in res.results]).astype(np.float32)
    return out
```
